# Optimizing a Trainium2 kernel written in Bass

```python
import math
import jax
import jax.numpy as jnp
from jax import lax
import numpy as np

D_MODEL = 1024
BATCH = 8
SEQ = 2048
DEPTH = 2

F32 = jnp.float32
CTX_LEN = 256
GRID_W = 64
ATTN_W = 3 * D_MODEL // 4
ATTN_HEAD_DIM = 64
ATTN_HEADS = ATTN_W // (2 * ATTN_HEAD_DIM)
ATTN_QK_W = ATTN_HEADS * 2 * ATTN_HEAD_DIM
ATTN_V_W = ATTN_HEADS * 2 * ATTN_HEAD_DIM
F_W = D_MODEL - ATTN_W
F_GROUP_W = 64
F_GROUPS = F_W // F_GROUP_W
ATTN_IN_W = 2 * ATTN_QK_W + ATTN_V_W + F_W
Q_BLOCK = 128
ROPE_BASE = 10000.0
SSD_W = 3 * D_MODEL // 4
SSD_HEAD_DIM = 64
SSD_HEADS = SSD_W // SSD_HEAD_DIM
SSD_GROUPS = 2
SSD_STATE = 128
SSD_CHUNK = 128
XBC_W = SSD_W + 2 * SSD_GROUPS * SSD_STATE
DT_W = 2 * SSD_HEADS
S5_W = D_MODEL - SSD_W
S5_GROUP_W = 16
S5_GROUPS = S5_W // S5_GROUP_W
S5_STATE = 64
SSM_TAIL_W = XBC_W + DT_W + S5_W
SSM_IN_W = SSD_W + SSM_TAIL_W
CONV_K = 3
D_FF = 2816
ALPHA = (2 * DEPTH) ** 0.25
BETA = (8 * DEPTH) ** -0.25
N_EVEN = (DEPTH + 1) // 2
N_ODD = DEPTH // 2
LN_EPS = 1e-5

kernel_name = 'hybrid_diffattn_fnet_s5_ssd_block'


def _flip(t):
    return jnp.flip(t, axis=1)


def layer_norm(x, g, b):
    xf = x.astype(F32)
    mu = jnp.mean(xf, -1, keepdims=True)
    var = jnp.mean(jnp.square(xf - mu), -1, keepdims=True)
    return ((xf - mu) * lax.rsqrt(var + LN_EPS) * g.astype(F32) + b.astype(F32)).astype(x.dtype)


def rms_norm(x, g):
    xf = x.astype(F32)
    return (xf * lax.rsqrt(jnp.mean(jnp.square(xf), -1, keepdims=True) + LN_EPS) * g.astype(F32)).astype(x.dtype)


def modulate(x, shift, scale):
    return x * (1.0 + scale[:, None]) + shift[:, None]


def dwconv_centered(x, w, b):
    k, ch = w.shape
    pad = (k - 1) // 2
    y = lax.conv_general_dilated(x, w[:, None, :].astype(x.dtype), (1,), [(pad, k - 1 - pad)],
                                 dimension_numbers=('NWC', 'WIO', 'NWC'), feature_group_count=ch)
    return y + b


def conv_ffn(u, w_up, b_up, conv_w, conv_b, w_down, b_down):
    hdn = dwconv_centered(u @ w_up + b_up, conv_w, conv_b)
    val, gate = jnp.split(hdn, 2, axis=-1)
    return (val * jax.nn.silu(gate)) @ w_down + b_down


def axial_rope_tables(n_tokens, dim):
    rows = n_tokens // GRID_W
    row = jnp.repeat(jnp.arange(rows, dtype=F32), GRID_W)
    col = jnp.tile(jnp.arange(GRID_W, dtype=F32), rows)
    n_freq = dim // 4
    inv_freq = ROPE_BASE ** (-jnp.arange(n_freq, dtype=F32) / n_freq)
    ang = jnp.concatenate([row[:, None] * inv_freq, col[:, None] * inv_freq], axis=-1)
    return jnp.cos(ang), jnp.sin(ang)


def apply_rope(t, cos, sin):
    half = t.shape[-1] // 2
    tf = t.astype(F32)
    cs, sn = cos[:, None, None, :], sin[:, None, None, :]
    t1, t2 = tf[..., :half], tf[..., half:]
    return jnp.concatenate([t1 * cs - t2 * sn, t1 * sn + t2 * cs], axis=-1).astype(t.dtype)


def _qk_heads(p):
    return p.reshape(p.shape[0], p.shape[1], ATTN_HEADS, 2, ATTN_HEAD_DIM)


def _v_heads(p):
    return p.reshape(p.shape[0], p.shape[1], ATTN_HEADS, 2 * ATTN_HEAD_DIM)


def diff_attention(q, k, v, lam, subln_g, lam_init):
    bsz, n_q = q.shape[0], q.shape[1]
    nb = n_q // Q_BLOCK
    qb = jnp.moveaxis(q.reshape(bsz, nb, Q_BLOCK, ATTN_HEADS, 2, ATTN_HEAD_DIM), 1, 0)
    scale = ATTN_HEAD_DIM ** -0.5

    def block(q_blk):
        s = jnp.einsum('bqhad,bkhad->bahqk', q_blk, k).astype(F32) * scale
        p = jax.nn.softmax(s, axis=-1)
        w = p[:, 0] - lam * p[:, 1]
        return jnp.einsum('bhqk,bkhe->bqhe', w.astype(v.dtype), v)

    o = lax.map(block, qb)
    o = jnp.moveaxis(o, 0, 1).reshape(bsz, n_q, ATTN_HEADS, 2 * ATTN_HEAD_DIM)
    o = rms_norm(o, subln_g) * (1.0 - lam_init)
    return o.reshape(bsz, n_q, ATTN_V_W)


def fourier_mix(f, w, b):
    bsz, n = f.shape[:2]
    fg = f.astype(F32).reshape(bsz, n, F_GROUPS, F_GROUP_W)
    z = jnp.fft.fft2(fg, axes=(1, 3), norm='ortho').real.astype(f.dtype)
    return jnp.einsum('blgc,gce->blge', z, w).reshape(bsz, n, F_W) + b


def attn_fourier_mixer(u_ctx, u_lat, w_in, lam_vec, subln_g, f_w, f_b, w_out, lam_init, cos, sin, need_ctx):
    lf = lam_vec.astype(F32)
    lam = jnp.exp(jnp.sum(lf[0] * lf[1])) - jnp.exp(jnp.sum(lf[2] * lf[3])) + lam_init
    k0, v0, f0 = ATTN_QK_W, 2 * ATTN_QK_W, 2 * ATTN_QK_W + ATTN_V_W
    p_lat = u_lat @ w_in
    q_l = apply_rope(_qk_heads(p_lat[..., :k0]), cos, sin)
    k_l = apply_rope(_qk_heads(p_lat[..., k0:v0]), cos, sin)
    v_l = _v_heads(p_lat[..., v0:f0])
    kv_c = u_ctx @ w_in[:, k0:f0]
    k_c, v_c = _qk_heads(kv_c[..., :ATTN_QK_W]), _v_heads(kv_c[..., ATTN_QK_W:])
    k_all = jnp.concatenate([k_l, k_c], axis=1)
    v_all = jnp.concatenate([v_l, v_c], axis=1)
    o_l = diff_attention(q_l, k_all, v_all, lam, subln_g, lam_init)
    y_lat = jnp.concatenate([o_l, fourier_mix(p_lat[..., f0:], f_w, f_b)], axis=-1) @ w_out
    y_ctx = None
    if need_ctx:
        q_c = _qk_heads(u_ctx @ w_in[:, :k0])
        o_c = diff_attention(q_c, k_c, v_c, lam, subln_g, lam_init)
        y_ctx = jnp.concatenate([o_c, fourier_mix(u_ctx @ w_in[:, f0:], f_w, f_b)], axis=-1) @ w_out
    return y_ctx, y_lat


def ssd_chunked(x, dt, a, bm, cm, h0, want_y):
    bsz, n, H, P = x.shape
    G, N = bm.shape[2], bm.shape[3]
    R = H // G
    Q = SSD_CHUNK
    nc = n // Q
    xc = x.reshape(bsz, nc, Q, G, R, P)
    dtc = dt.reshape(bsz, nc, Q, G, R)
    bc = bm.reshape(bsz, nc, Q, G, N)
    cc = cm.reshape(bsz, nc, Q, G, N)
    acs = jnp.cumsum(dtc * a.reshape(G, R), axis=2)
    w_end = jnp.exp(acs[:, :, -1:] - acs) * dtc
    states = jnp.einsum('bcjgn,bcjgr,bcjgrp->bcgrpn', bc, w_end, xc)
    chunk_decay = jnp.exp(acs[:, :, -1])
    init = jnp.zeros((bsz, G, R, P, N), F32) if h0 is None else h0

    def step(h, inp):
        s, dec = inp
        return h * dec[..., None, None] + s, h

    h_fin, h_in = lax.scan(step, init, (jnp.moveaxis(states, 1, 0), jnp.moveaxis(chunk_decay, 1, 0)))
    if not want_y:
        return None, h_fin
    h_in = jnp.moveaxis(h_in, 0, 1)
    lower = jnp.tril(jnp.ones((Q, Q), bool))
    seg = acs[:, :, :, None] - acs[:, :, None]
    lmat = jnp.exp(jnp.where(lower[:, :, None, None], seg, -jnp.inf))
    cb = jnp.einsum('bcign,bcjgn->bcijg', cc, bc)
    y_diag = jnp.einsum('bcijgr,bcjgrp->bcigrp', cb[..., None] * lmat * dtc[:, :, None], xc)
    y_off = jnp.einsum('bcign,bcgrpn->bcigrp', cc, h_in) * jnp.exp(acs)[..., None]
    return (y_diag + y_off).reshape(bsz, n, H, P), h_fin


def ssm_tail(tail, conv_w, conv_b, dt_bias):
    bsz, n = tail.shape[:2]
    gn = SSD_GROUPS * SSD_STATE
    xbc = jax.nn.silu(dwconv_centered(tail[..., :XBC_W], conv_w, conv_b)).astype(F32)
    xs = xbc[..., :SSD_W].reshape(bsz, n, SSD_HEADS, SSD_HEAD_DIM)
    bm = xbc[..., SSD_W:SSD_W + gn].reshape(bsz, n, SSD_GROUPS, SSD_STATE)
    cm = xbc[..., SSD_W + gn:].reshape(bsz, n, SSD_GROUPS, SSD_STATE)
    dt = jax.nn.softplus(tail[..., XBC_W:XBC_W + DT_W].astype(F32).reshape(bsz, n, 2, SSD_HEADS)
                         + dt_bias.astype(F32))
    us = tail[..., XBC_W + DT_W:].astype(F32).reshape(bsz, n, S5_GROUPS, S5_GROUP_W)
    return xs, bm, cm, dt, us


def s5_discretize(lam_re, lam_im, log_dt, b_re, b_im):
    lr, li = lam_re.astype(F32), lam_im.astype(F32)
    dt = jnp.exp(log_dt.astype(F32))[..., None]
    mag = jnp.exp(dt * lr)
    ab_re, ab_im = mag * jnp.cos(dt * li), mag * jnp.sin(dt * li)
    den = lr * lr + li * li
    k_re = ((ab_re - 1.0) * lr + ab_im * li) / den
    k_im = (ab_im * lr - (ab_re - 1.0) * li) / den
    br, bi = b_re.astype(F32)[None], b_im.astype(F32)[None]
    bb_re = k_re[..., None] * br - k_im[..., None] * bi
    bb_im = k_re[..., None] * bi + k_im[..., None] * br
    return ab_re, ab_im, bb_re, bb_im


def _cplx_affine_combine(e1, e2):
    a1r, a1i, b1r, b1i = e1
    a2r, a2i, b2r, b2i = e2
    return (a1r * a2r - a1i * a2i, a1r * a2i + a1i * a2r,
            a2r * b1r - a2i * b1i + b2r, a2r * b1i + a2i * b1r + b2i)


def s5_scan(ab_re, ab_im, bu_re, bu_im, h0):
    a_re = jnp.broadcast_to(ab_re, bu_re.shape)
    a_im = jnp.broadcast_to(ab_im, bu_re.shape)
    ca_re, ca_im, h_re, h_im = lax.associative_scan(_cplx_affine_combine, (a_re, a_im, bu_re, bu_im), axis=1)
    if h0 is not None:
        h0r, h0i = h0[0][:, None], h0[1][:, None]
        h_re, h_im = h_re + ca_re * h0r - ca_im * h0i, h_im + ca_re * h0i + ca_im * h0r
    return h_re, h_im


def s5_glu(y, w, b):
    bsz, n = y.shape[:2]
    g = jax.nn.gelu(y.reshape(bsz, n, S5_W))
    return g * jax.nn.sigmoid(g @ w.astype(F32) + b.astype(F32))


def s5_bidir(uc, ul, lam_re, lam_im, log_dt, b_re, b_im, c_re, c_im, s5_dd, glu_w, glu_b, need_ctx):
    ab_re, ab_im, bb_re, bb_im = s5_discretize(lam_re, lam_im, log_dt, b_re, b_im)
    cr, ci = c_re.astype(F32), c_im.astype(F32)
    dd = s5_dd.astype(F32).reshape(S5_GROUPS, S5_GROUP_W)

    def run(u, d, h0):
        bu_re = jnp.einsum('blgh,gph->blgp', u, bb_re[d])
        bu_im = jnp.einsum('blgh,gph->blgp', u, bb_im[d])
        return s5_scan(ab_re[d], ab_im[d], bu_re, bu_im, h0)

    def readout(h, d):
        return jnp.einsum('blgp,ghp->blgh', h[0], cr[d]) - jnp.einsum('blgp,ghp->blgh', h[1], ci[d])

    hc_f = run(uc, 0, None)
    hc_b = run(_flip(uc), 1, None)
    hl_f = run(ul, 0, (hc_f[0][:, -1], hc_f[1][:, -1]))
    hl_b = run(_flip(ul), 1, (hc_b[0][:, -1], hc_b[1][:, -1]))
    out_l = s5_glu(readout(hl_f, 0) + _flip(readout(hl_b, 1)) + dd * ul, glu_w, glu_b)
    out_c = None
    if need_ctx:
        out_c = s5_glu(readout(hc_f, 0) + _flip(readout(hc_b, 1)) + dd * uc, glu_w, glu_b)
    return out_c, out_l


def ssm_merge(y_ssd, z, y_s5, norm_g, w_out):
    bsz, n = z.shape[:2]
    g = rms_norm(y_ssd.reshape(bsz, n, SSD_W) * jax.nn.silu(z.astype(F32)), norm_g)
    return jnp.concatenate([g, y_s5], axis=-1).astype(z.dtype) @ w_out


def ssm_mixer(u_ctx, u_lat, w_in, conv_w, conv_b, a_log, dt_bias, d_skip, norm_g, lam_re, lam_im,
              log_dt, b_re, b_im, c_re, c_im, s5_dd, glu_w, glu_b, w_out, need_ctx):
    p_lat = u_lat @ w_in
    p_ctx = u_ctx @ (w_in if need_ctx else w_in[:, SSD_W:])
    x_l, b_l, c_l, dt_l, us_l = ssm_tail(p_lat[..., SSD_W:], conv_w, conv_b, dt_bias)
    x_c, b_c, c_c, dt_c, us_c = ssm_tail(p_ctx[..., -SSM_TAIL_W:], conv_w, conv_b, dt_bias)
    a = -jnp.exp(a_log.astype(F32))
    dsk = d_skip.astype(F32)[:, None]
    yc_f, hc_f = ssd_chunked(x_c, dt_c[:, :, 0], a[0], b_c, c_c, None, need_ctx)
    yc_b, hc_b = ssd_chunked(_flip(x_c), _flip(dt_c[:, :, 1]), a[1], _flip(b_c), _flip(c_c), None, need_ctx)
    yl_f, _ = ssd_chunked(x_l, dt_l[:, :, 0], a[0], b_l, c_l, hc_f, True)
    yl_b, _ = ssd_chunked(_flip(x_l), _flip(dt_l[:, :, 1]), a[1], _flip(b_l), _flip(c_l), hc_b, True)
    s5_c, s5_l = s5_bidir(us_c, us_l, lam_re, lam_im, log_dt, b_re, b_im, c_re, c_im, s5_dd, glu_w, glu_b, need_ctx)
    y_lat = ssm_merge(yl_f + _flip(yl_b) + dsk * x_l, p_lat[..., :SSD_W], s5_l, norm_g, w_out)
    y_ctx = None
    if need_ctx:
        y_ctx = ssm_merge(yc_f + _flip(yc_b) + dsk * x_c, p_ctx[..., :SSD_W], s5_c, norm_g, w_out)
    return y_ctx, y_lat


def post_norm_layer(h, y_mix, mods, ln_g, ln_b, ffn):
    h = layer_norm(ALPHA * h + mods[2][:, None] * y_mix, ln_g[0], ln_b[0])
    f = conv_ffn(modulate(h, mods[3], mods[4]), *ffn)
    return layer_norm(ALPHA * h + mods[5][:, None] * f, ln_g[1], ln_b[1])


def setup_inputs(seed: int = 0) -> dict:
    key = jax.random.key(seed)
    ks = iter(jax.random.split(key, 64))

    def nrm(shape, std):
        return jax.random.normal(next(ks), shape, F32) * std

    def unif(shape, lo, hi):
        return jax.random.uniform(next(ks), shape, F32, lo, hi)

    D = D_MODEL
    dt_lo, dt_hi = math.log(1e-3), math.log(1e-1)
    ssd_dt0 = jnp.exp(unif((N_ODD, 2, SSD_HEADS), dt_lo, dt_hi))
    n_idx = jnp.arange(S5_STATE, dtype=F32)
    inp = {}
    inp['x'] = nrm((BATCH, SEQ, D), 1.0)
    inp['c'] = nrm((BATCH, D), 1.0)
    inp['ctx'] = nrm((BATCH, CTX_LEN, D), 1.0)
    inp['c_ctx'] = nrm((D,), 1.0)
    inp['ada_w'] = nrm((DEPTH, D, 6 * D), D ** -0.5)
    inp['ada_b'] = nrm((DEPTH, 6 * D), 0.02)
    inp['ln_g'] = 1.0 + nrm((DEPTH, 2, D), 0.02)
    inp['ln_b'] = nrm((DEPTH, 2, D), 0.02)
    inp['ffn_w_up'] = nrm((DEPTH, D, 2 * D_FF), D ** -0.5)
    inp['ffn_b_up'] = nrm((DEPTH, 2 * D_FF), 0.02)
    inp['ffn_conv_w'] = nrm((DEPTH, CONV_K, 2 * D_FF), CONV_K ** -0.5)
    inp['ffn_conv_b'] = nrm((DEPTH, 2 * D_FF), 0.02)
    inp['ffn_w_down'] = nrm((DEPTH, D_FF, D), BETA * D_FF ** -0.5)
    inp['ffn_b_down'] = nrm((DEPTH, D), 0.02)
    inp['attn_w_in'] = nrm((N_EVEN, D, ATTN_IN_W), D ** -0.5)
    inp['attn_lambda'] = nrm((N_EVEN, 4, ATTN_HEAD_DIM), 0.1)
    inp['attn_subln_g'] = 1.0 + nrm((N_EVEN, 2 * ATTN_HEAD_DIM), 0.02)
    inp['fourier_w'] = nrm((N_EVEN, F_GROUPS, F_GROUP_W, F_GROUP_W), F_GROUP_W ** -0.5)
    inp['fourier_b'] = nrm((N_EVEN, F_W), 0.02)
    inp['attn_w_out'] = nrm((N_EVEN, D, D), BETA * D ** -0.5)
    inp['ssm_w_in'] = nrm((N_ODD, D, SSM_IN_W), D ** -0.5)
    inp['ssd_conv_w'] = nrm((N_ODD, CONV_K, XBC_W), CONV_K ** -0.5)
    inp['ssd_conv_b'] = nrm((N_ODD, XBC_W), 0.02)
    inp['ssd_a_log'] = jnp.log(unif((N_ODD, 2, SSD_HEADS), 1.0, 16.0))
    inp['ssd_dt_bias'] = ssd_dt0 + jnp.log(-jnp.expm1(-ssd_dt0))
    inp['ssd_d'] = 1.0 + nrm((N_ODD, SSD_HEADS), 0.02)
    inp['ssd_norm_g'] = 1.0 + nrm((N_ODD, SSD_W), 0.02)
    inp['s5_lambda_re'] = -0.5 + nrm((N_ODD, 2, S5_GROUPS, S5_STATE), 0.01)
    inp['s5_lambda_im'] = math.pi * n_idx + nrm((N_ODD, 2, S5_GROUPS, S5_STATE), 0.01)
    inp['s5_log_dt'] = unif((N_ODD, 2, S5_GROUPS), dt_lo, dt_hi)
    inp['s5_b_re'] = nrm((N_ODD, S5_GROUPS, S5_STATE, S5_GROUP_W), (2.0 * S5_GROUP_W) ** -0.5)
    inp['s5_b_im'] = nrm((N_ODD, S5_GROUPS, S5_STATE, S5_GROUP_W), (2.0 * S5_GROUP_W) ** -0.5)
    inp['s5_c_re'] = nrm((N_ODD, 2, S5_GROUPS, S5_GROUP_W, S5_STATE), S5_STATE ** -0.5)
    inp['s5_c_im'] = nrm((N_ODD, 2, S5_GROUPS, S5_GROUP_W, S5_STATE), S5_STATE ** -0.5)
    inp['s5_d'] = nrm((N_ODD, S5_W), 1.0)
    inp['s5_glu_w'] = nrm((N_ODD, S5_W, S5_W), S5_W ** -0.5)
    inp['s5_glu_b'] = nrm((N_ODD, S5_W), 0.02)
    inp['ssm_w_out'] = nrm((N_ODD, D, D), BETA * D ** -0.5)
    return inp


def reference(x, c, ctx, c_ctx, ada_w, ada_b, ln_g, ln_b, ffn_w_up, ffn_b_up, ffn_conv_w, ffn_conv_b,
              ffn_w_down, ffn_b_down, attn_w_in, attn_lambda, attn_subln_g, fourier_w, fourier_b,
              attn_w_out, ssm_w_in, ssd_conv_w, ssd_conv_b, ssd_a_log, ssd_dt_bias, ssd_d, ssd_norm_g,
              s5_lambda_re, s5_lambda_im, s5_log_dt, s5_b_re, s5_b_im, s5_c_re, s5_c_im, s5_d,
              s5_glu_w, s5_glu_b, ssm_w_out):
    cos, sin = axial_rope_tables(x.shape[1], ATTN_HEAD_DIM)
    s_lat = jax.nn.silu(c)
    s_ctx = jax.nn.silu(c_ctx)[None]
    h_lat, h_ctx = x, ctx
    for l in range(DEPTH):
        last = l == DEPTH - 1
        n_mod = 2 if last else 6
        m_lat = jnp.split(s_lat @ ada_w[l] + ada_b[l], 6, axis=-1)
        m_ctx = jnp.split(s_ctx @ ada_w[l][:, :n_mod * D_MODEL] + ada_b[l][:n_mod * D_MODEL], n_mod, axis=-1)
        u_lat = modulate(h_lat, m_lat[0], m_lat[1])
        u_ctx = modulate(h_ctx, m_ctx[0], m_ctx[1])
        i = l // 2
        if l % 2 == 0:
            y_ctx, y_lat = attn_fourier_mixer(u_ctx, u_lat, attn_w_in[i], attn_lambda[i], attn_subln_g[i],
                                              fourier_w[i], fourier_b[i], attn_w_out[i],
                                              0.8 - 0.6 * math.exp(-0.3 * l), cos, sin, not last)
        else:
            y_ctx, y_lat = ssm_mixer(u_ctx, u_lat, ssm_w_in[i], ssd_conv_w[i], ssd_conv_b[i], ssd_a_log[i],
                                     ssd_dt_bias[i], ssd_d[i], ssd_norm_g[i], s5_lambda_re[i], s5_lambda_im[i],
                                     s5_log_dt[i], s5_b_re[i], s5_b_im[i], s5_c_re[i], s5_c_im[i], s5_d[i],
                                     s5_glu_w[i], s5_glu_b[i], ssm_w_out[i], not last)
        ffn = (ffn_w_up[l], ffn_b_up[l], ffn_conv_w[l], ffn_conv_b[l], ffn_w_down[l], ffn_b_down[l])
        h_lat = post_norm_layer(h_lat, y_lat, m_lat, ln_g[l], ln_b[l], ffn)
        if not last:
            h_ctx = post_norm_layer(h_ctx, y_ctx, m_ctx, ln_g[l], ln_b[l], ffn)
    return h_lat
```

```python
import contextlib
import math
import numpy as np
import ml_dtypes
import concourse.bass as bass
import concourse.mybir as mybir
from concourse.bass_utils import run_bass_kernel_spmd

F32 = mybir.dt.float32
BF16 = mybir.dt.bfloat16
AF = mybir.ActivationFunctionType
ALU = mybir.AluOpType

ENGS = ['pe', 'act', 'dve', 'pool', 'sp']
POOL_TO_DVE = True
NDMASEM = 6

D = 1024
NLAT = 2048
NCTX = 256
NTOK = NLAT + NCTX
NT = NTOK // 128
ALPHA = (2 * 2) ** 0.25
EPS = 1e-5
LAM_INIT0 = 0.8 - 0.6 * math.exp(0.0)


class Prog:
    def __init__(self, nc):
        self.nc = nc
        self.ops = {e: [] for e in ENGS}
        self.last_w = {}
        self.readers = {}
        self.barriers = []
        self.bar_dma_start = {e: 0 for e in ENGS}

    def barrier(self):
        pts = []
        for e in ENGS:
            ops = self.ops[e]
            for i in range(len(ops) - 1, -1, -1):
                if not ops[i]['dma'] and ops[i]['fn'] is not None:
                    pts.append((e, i))
                    ops[i]['needed'] = True
                    break
            for i in range(self.bar_dma_start[e], len(ops)):
                if ops[i]['dma']:
                    pts.append((e, i))
            self.bar_dma_start[e] = len(ops)
        self.barriers.append(pts)

    def op(self, eng, fn, reads=(), writes=(), dma=False, keep=False):
        if eng == 'pool' and not dma and POOL_TO_DVE and not keep:
            eng = 'dve'
        ops = self.ops[eng]
        idx = len(ops)
        deps = set()
        for k in reads:
            w = self.last_w.get(k)
            if w is not None:
                deps.add(w)
            if k.startswith('ps'):
                for r in self.readers.get(k, ()):
                    if r[0] != eng:
                        deps.add(r)
        for k in writes:
            w = self.last_w.get(k)
            if w is not None:
                deps.add(w)
            for r in self.readers.get(k, ()):
                deps.add(r)
        best = {}
        out = []
        for (e, i) in deps:
            d = self.ops[e][i]
            if d['dma']:
                out.append((e, i))
            else:
                if e == eng and not dma and eng == 'pe':
                    continue
                if e not in best or best[e] < i:
                    best[e] = i
        for e, i in best.items():
            out.append((e, i))
        for (e, i) in out:
            self.ops[e][i]['needed'] = True
        ops.append(dict(fn=fn, deps=out, dma=dma, needed=False, sem=None, val=None, prev=None, bar=len(self.barriers)))
        me = (eng, idx)
        for k in reads:
            lst = self.readers.setdefault(k, [])
            if not dma:
                lst[:] = [r for r in lst if not (r[0] == eng and not self.ops[r[0]][r[1]]['dma'])]
            lst.append(me)
        for k in writes:
            self.last_w[k] = me
            self.readers[k] = []
        return me

    def dma(self, eng, out, in_, reads=(), writes=(), **kw):
        return self.op(eng, lambda e: e.dma_start(out=out, in_=in_, **kw), reads, writes, dma=True)

    def act(self, out, in_, func, reads=(), writes=(), eng='act', **kw):
        return self.op(eng, lambda e: e.activation(out=out, in_=in_, func=func, **kw), reads, writes)

    def tt(self, out, in0, in1, op, reads=(), writes=(), eng='dve'):
        return self.op(eng, lambda e: e.tensor_tensor(out=out, in0=in0, in1=in1, op=op), reads, writes)

    def ts(self, out, in0, s1, s2, op0, op1=None, reads=(), writes=(), eng='dve'):
        if op1 is None:
            return self.op(eng, lambda e: e.tensor_scalar(out=out, in0=in0, scalar1=s1, scalar2=None, op0=op0), reads, writes)
        return self.op(eng, lambda e: e.tensor_scalar(out=out, in0=in0, scalar1=s1, scalar2=s2, op0=op0, op1=op1), reads, writes)

    def stt(self, out, in0, scalar, in1, op0, op1, reads=(), writes=()):
        return self.op('dve', lambda e: e.scalar_tensor_tensor(out=out, in0=in0, scalar=scalar, in1=in1, op0=op0, op1=op1), reads, writes)

    def copy(self, out, in_, reads=(), writes=(), eng='dve'):
        if eng == 'act':
            return self.op(eng, lambda e: e.activation(out=out, in_=in_, func=AF.Copy), reads, writes)
        return self.op(eng, lambda e: e.tensor_copy(out=out, in_=in_), reads, writes)

    def mm(self, out, lhsT, rhs, start, stop, reads=(), writes=()):
        return self.op('pe', lambda e: e.matmul(out, lhsT=lhsT, rhs=rhs, start=start, stop=stop), reads, writes)

    def tr(self, out, in_, ident, reads=(), writes=()):
        return self.op('pe', lambda e: e.transpose(out=out, in_=in_, identity=ident), reads, writes)

    def scan(self, out, d0, d1, initial, op0, op1, reads=(), writes=()):
        return self.op('dve', lambda e: e.tensor_tensor_scan(out=out, data0=d0, data1=d1, initial=initial, op0=op0, op1=op1), reads, writes)

    def memset(self, ap, val, writes=(), eng='dve'):
        return self.op(eng, lambda e: e.memset(ap, val), (), writes)

    def pow_(self, out, in0, in1, reads=(), writes=()):
        return self.op('pool', lambda e: e.tensor_tensor(out=out, in0=in0, in1=in1, op=ALU.pow), reads, writes, keep=True)

    def recip(self, out, in_, reads=(), writes=()):
        return self.op('dve', lambda e: e.reciprocal(out=out, in_=in_), reads, writes)

    def bnstats(self, out, in_, reads=(), writes=()):
        return self.op('dve', lambda e: e.bn_stats(out=out, in_=in_), reads, writes)

    def bnaggr(self, out, in_, reads=(), writes=()):
        return self.op('dve', lambda e: e.bn_aggr(out=out, in_=in_), reads, writes)

    def emit(self, final_keys=()):
        nc = self.nc
        self.op('sp', None, reads=list(final_keys), writes=())
        with contextlib.ExitStack() as st:
            csem = {e: st.enter_context(nc.semaphore("c_" + e)) for e in ENGS}
            dsem = {e: [st.enter_context(nc.semaphore("d_%s%d" % (e, i))) for i in range(NDMASEM)] for e in ENGS}
            for e in ENGS:
                cnt = 0
                dcnt = 0
                lastd = [None] * NDMASEM
                dval = [0] * NDMASEM
                for i, o in enumerate(self.ops[e]):
                    if o['dma']:
                        s = dcnt % NDMASEM
                        dcnt += 1
                        dval[s] += 16
                        o['sem'] = dsem[e][s]
                        o['val'] = dval[s]
                        o['prev'] = lastd[s]
                        lastd[s] = i
                    elif o['needed']:
                        cnt += 1
                        o['sem'] = csem[e]
                        o['val'] = cnt
            block = st.enter_context(nc.Block())

            def run(e, eng):
                waited = {}

                def wait(sem, val):
                    k = id(sem)
                    if waited.get(k, 0) < val:
                        eng.wait_ge(sem, val)
                        waited[k] = val
                bar_done = 0
                for o in self.ops[e]:
                    while bar_done < o['bar']:
                        for (de, di) in self.barriers[bar_done]:
                            d = self.ops[de][di]
                            wait(d['sem'], d['val'])
                        bar_done += 1
                    for (de, di) in o['deps']:
                        d = self.ops[de][di]
                        wait(d['sem'], d['val'])
                    if o['dma'] and o['prev'] is not None:
                        p = self.ops[e][o['prev']]
                        wait(p['sem'], p['val'])
                    if o['fn'] is None:
                        continue
                    ins = o['fn'](eng)
                    if o['dma']:
                        ins.then_inc(o['sem'], 16)
                    elif o['needed']:
                        ins.then_inc(o['sem'], 1)

            @block.tensor
            def _(eng):
                run('pe', eng)

            @block.scalar
            def _(eng):
                run('act', eng)

            @block.vector
            def _(eng):
                run('dve', eng)

            @block.gpsimd
            def _(eng):
                run('pool', eng)

            @block.sync
            def _(eng):
                run('sp', eng)


def host_consts():
    c = {}
    c['ident'] = np.eye(128, dtype=np.float32)
    ps = np.zeros((128, 128), np.float32)
    for m in range(128):
        partner = m + 32 if (m % 64) < 32 else m - 32
        ps[partner, m] = 1.0
    c['pswap'] = ps
    rows = NLAT // 64
    row = np.repeat(np.arange(rows, dtype=np.float32), 64)
    col = np.tile(np.arange(64, dtype=np.float32), rows)
    nf = 16
    inv = (10000.0 ** (-np.arange(nf, dtype=np.float32) / nf)).astype(np.float32)
    ang = np.concatenate([row[:, None] * inv, col[:, None] * inv], axis=-1).astype(np.float32)
    cs, sn = np.cos(ang).astype(np.float32), np.sin(ang).astype(np.float32)
    cos4 = np.zeros((128, NLAT), np.float32)
    sin4 = np.zeros((128, NLAT), np.float32)
    for p in range(128):
        cos4[p] = cs[:, p % 32]
        sin4[p] = sn[:, p % 32] * (-1.0 if (p % 64) < 32 else 1.0)
    c['cos4'] = cos4
    c['sin4'] = sin4

    def dft(n):
        k = np.arange(n, dtype=np.int64)
        kk = (k[:, None] * k[None, :]) % n
        a = 2.0 * np.pi * kk.astype(np.float64) / n
        return np.cos(a) / np.sqrt(n), np.sin(a) / np.sqrt(n)
    C, S = dft(NLAT)
    c['dftc'] = C.astype(ml_dtypes.bfloat16)
    c['dfts'] = S.astype(ml_dtypes.bfloat16)
    C, S = dft(NCTX)
    c['dftc_c'] = C.astype(ml_dtypes.bfloat16)
    c['dfts_c'] = S.astype(ml_dtypes.bfloat16)
    C, S = dft(64)
    cb = np.zeros((128, 128), np.float32)
    sb = np.zeros((128, 128), np.float32)
    for g in range(2):
        cb[g * 64:(g + 1) * 64, g * 64:(g + 1) * 64] = C
        sb[g * 64:(g + 1) * 64, g * 64:(g + 1) * 64] = -S
    c['c64blk'] = cb
    c['ns64blk'] = sb
    sel = np.zeros((2, 2, 128), np.float32)
    sel[0, 0, :] = 1.0
    sel[1, 1, :] = 1.0
    c['sel'] = sel
    k = np.arange(128)
    vd = np.zeros((128, 2, 128), np.float32)
    ud = np.zeros((128, 2, 128), np.float32)
    vd[:, 0, :] = (k[:, None] <= k[None, :])
    vd[:, 1, :] = (k[:, None] >= k[None, :])
    ud[:, 0, :] = (k[:, None] > k[None, :])
    ud[:, 1, :] = (k[:, None] < k[None, :])
    c['vd'] = vd
    c['ud'] = ud
    c['ones'] = np.ones((128, 512), np.float32)
    return c


def colvec(v, n):
    return np.ascontiguousarray(np.asarray(v, np.float32).reshape(n, 128).T)


def bcast(v, p=128):
    v = np.asarray(v, np.float32).reshape(1, -1)
    return np.ascontiguousarray(np.broadcast_to(v, (p, v.shape[1])))


def host_layout(inp, b):
    m = {}
    m['xin'] = np.ascontiguousarray(np.concatenate([inp['x'][b], inp['ctx'][b]], axis=0))
    cv = np.stack([colvec(inp['c'][b], 8), colvec(inp['c_ctx'], 8)], axis=-1)
    m['cvec'] = np.ascontiguousarray(cv)
    m['ada_w'] = inp['ada_w']
    m['ada_bc'] = np.ascontiguousarray(np.stack([colvec(inp['ada_b'][l], 48) for l in range(2)]))
    m['ada_b2'] = np.ascontiguousarray(np.stack([np.stack([inp['ada_b'][l]] * 2) for l in range(2)]))
    m['ln_g'] = np.ascontiguousarray(np.stack([np.stack([bcast(inp['ln_g'][l][i]) for i in range(2)]) for l in range(2)]))
    m['ln_b'] = np.ascontiguousarray(np.stack([np.stack([bcast(inp['ln_b'][l][i]) for i in range(2)]) for l in range(2)]))
    m['w_up'] = inp['ffn_w_up']
    m['b_up'] = np.ascontiguousarray(np.stack([colvec(inp['ffn_b_up'][l], 44) for l in range(2)]))
    m['conv_w'] = np.ascontiguousarray(np.stack([np.stack([colvec(inp['ffn_conv_w'][l][k], 44) for k in range(3)], axis=1) for l in range(2)]))
    m['conv_b'] = np.ascontiguousarray(np.stack([colvec(inp['ffn_conv_b'][l], 44) for l in range(2)]))
    m['w_down'] = inp['ffn_w_down']
    m['b_down'] = np.ascontiguousarray(np.stack([bcast(inp['ffn_b_down'][l]) for l in range(2)]))
    m['a_w_in'] = inp['attn_w_in'][0]
    m['a_w_out'] = inp['attn_w_out'][0]
    m['a_lam'] = bcast(inp['attn_lambda'][0].reshape(-1))
    m['a_subg'] = bcast(inp['attn_subln_g'][0])
    fw = inp['fourier_w'][0]
    wb = np.zeros((2, 128, 128), np.float32)
    for ch in range(2):
        for g in range(2):
            wb[ch, g * 64:(g + 1) * 64, g * 64:(g + 1) * 64] = fw[ch * 2 + g]
    m['f_wblk'] = wb
    m['f_b'] = colvec(inp['fourier_b'][0], 2)
    m['s_w_in'] = inp['ssm_w_in'][0]
    m['s_w_out'] = inp['ssm_w_out'][0]
    m['sd_cw'] = np.ascontiguousarray(np.stack([colvec(inp['ssd_conv_w'][0][k], 10) for k in range(3)], axis=1))
    m['sd_cb'] = colvec(inp['ssd_conv_b'][0], 10)
    m['sd_alog'] = bcast(inp['ssd_a_log'][0].reshape(-1))
    m['sd_dtb'] = bcast(inp['ssd_dt_bias'][0].reshape(-1))
    m['sd_d'] = bcast(inp['ssd_d'][0])
    m['sd_ng'] = bcast(inp['ssd_norm_g'][0])

    def dsc(a):
        out = np.zeros((128, 16), np.float32)
        for d in range(2):
            for sc in range(8):
                for half in range(2):
                    out[half * 64:(half + 1) * 64, d * 8 + sc] = a[d, 2 * sc + half, :]
        return out
    m['s5_lre'] = dsc(inp['s5_lambda_re'][0])
    m['s5_lim'] = dsc(inp['s5_lambda_im'][0])
    m['s5_ldt'] = dsc(np.broadcast_to(inp['s5_log_dt'][0][:, :, None], (2, 16, 64)))

    def bexp(b):
        out = np.zeros((128, 8, 128), np.float32)
        for sc in range(8):
            for half in range(2):
                g = 2 * sc + half
                col = (g % 8) * 16
                out[half * 64:(half + 1) * 64, sc, col:col + 16] = b[g]
        return out
    m['s5_bre'] = bexp(inp['s5_b_re'][0])
    m['s5_bim'] = bexp(inp['s5_b_im'][0])

    def cexp(c):
        out = np.zeros((2, 128, 8, 128), np.float32)
        for d in range(2):
            for sc in range(8):
                for half in range(2):
                    g = 2 * sc + half
                    col = (g % 8) * 16
                    out[d, half * 64:(half + 1) * 64, sc, col:col + 16] = c[d, g].T
        return out
    m['s5_cre'] = cexp(inp['s5_c_re'][0])
    m['s5_cim'] = cexp(inp['s5_c_im'][0])
    m['s5_dd'] = colvec(inp['s5_d'][0], 2)
    m['s5_glu_w'] = inp['s5_glu_w'][0]
    m['s5_glu_b'] = colvec(inp['s5_glu_b'][0], 2)
    return m


class Builder:
    def __init__(self, debug=()):
        self.debug = set(debug)
        self.nc = bass.Bass("TRN2", target_bir_lowering=False)
        self.P = Prog(self.nc)
        self.din = {}
        self.dout = {}
        self.final = []
        self.uid = 0
        self.ps = [self.nc.alloc_psum_tensor("ps%d" % i, [128, 512], F32) for i in range(8)]
        self.psrr = 0

    def inp(self, name, shape, dt=F32):
        self.din[name] = self.nc.dram_tensor(name, list(shape), dt, kind="ExternalInput").ap()
        return self.din[name]

    def outp(self, name, shape, dt=F32):
        self.dout[name] = self.nc.dram_tensor(name, list(shape), dt, kind="ExternalOutput").ap()
        self.final.append(name)
        return self.dout[name]

    def scratch(self, name, shape, dt=F32):
        return self.nc.dram_tensor(name, list(shape), dt, kind="Internal").ap()

    def sb(self, st, name, shape, dt=F32):
        self.uid += 1
        return st.enter_context(self.nc.sbuf_tensor("s%d_%s" % (self.uid, name), list(shape), dt))

    @contextlib.contextmanager
    def scope(self):
        with contextlib.ExitStack() as s2:
            yield s2
        self.P.barrier()

    def nextps(self, lo=0, hi=8):
        n = hi - lo
        i = lo + (self.psrr % n)
        self.psrr += 1
        return i

    def dbg(self, name, tile_ap, key, shape, dt=F32):
        if name in self.debug:
            o = self.outp("dbg_" + name, shape, dt)
            self.P.dma('sp', o, tile_ap, reads=[key], writes=["dbg_" + name])

    def load_consts(self, st):
        P = self.P
        self.ident = self.sb(st, "ident", [128, 128])
        P.dma('sp', self.ident[:], self.inp("ident", [128, 128]), writes=['ident'])
        self.mhalf = self.sb(st, "mhalf", [128, 4])
        P.memset(self.mhalf[:], -0.5, writes=['mhalf'])
        self.sel = self.sb(st, "sel", [2, 2, 128])
        P.dma('sp', self.sel[:], self.inp("sel", [2, 2, 128]), writes=['sel'])

    def mods(self, st, layers):
        P, nc = self.P, self.nc
        if 'cvec' not in self.din:
            self.inp("cvec", [128, 8, 2])
            self.inp("ada_w", [2, D, 6 * D])
            self.inp("ada_bc", [2, 128, 48])
            self.inp("ada_b2", [2, 2, 6 * D])
            self.modc = [self.sb(st, "modc%d" % l, [128, 48, 2]) for l in range(2)]
            g = self.sb(st, "grow", [2, 2, 1024])
            self.grow = [g[:], g[:]]
            for l in range(2):
                P.memset(self.modc[l][:], 0.0, writes=['modc%d' % l])
        cvec, ada_w, ada_bc, ada_b2 = self.din['cvec'], self.din['ada_w'], self.din['ada_bc'], self.din['ada_b2']
        with self.scope() as s2:
            cv = self.sb(s2, "cv", [128, 8, 2])
            sT = self.sb(s2, "sT", [128, 8, 2], BF16)
            abc = self.sb(s2, "abc", [128, 2, 48])
            ab2 = self.sb(s2, "ab2", [2, 2, 2, 1024])
            P.dma('sp', cv[:], cvec, writes=['cv'])
            P.dma('sp', abc[:], ada_bc.rearrange("l p f -> p l f"), writes=['abc'])
            for l in layers:
                for gi, m in enumerate((2, 5)):
                    P.dma('sp', ab2[:, l, gi, :], ada_b2[l, :, m * 1024:(m + 1) * 1024], writes=['ab2'])
            P.act(sT[:], cv[:], AF.Silu, reads=['cv'], writes=['sT'])
            NWB = 4
            wbuf = [self.sb(s2, "adaw%d" % i, [128, 8, 1024], BF16) for i in range(NWB)]
            n = 0
            for l in layers:
                for m in range(6):
                    wb = wbuf[n % NWB]
                    wk = 'adaw%d' % (n % NWB)
                    n += 1
                    P.dma('pool', wb[:], ada_w[l, :, m * 1024:(m + 1) * 1024].rearrange("(kc p) f -> p kc f", p=128), writes=[wk])
                    if m in (0, 1, 3, 4):
                        bk = self.nextps()
                        pst = self.ps[bk]
                        for j in range(8):
                            for kc in range(8):
                                P.mm(pst[:, 2 * j:2 * j + 2], wb[:, kc, j * 128:(j + 1) * 128], sT[:, kc, :], kc == 0, kc == 7,
                                     reads=[wk, 'sT'], writes=['ps%d' % bk])
                        P.tt(self.modc[l][:, m * 8:(m + 1) * 8, :], pst[:, 0:16].rearrange("p (j w) -> p j w", w=2),
                             abc[:, l, m * 8:(m + 1) * 8].unsqueeze(2).to_broadcast([128, 8, 2]), ALU.add,
                             reads=['ps%d' % bk, 'abc'], writes=['modc%d' % l])
                        if m in (1, 4):
                            P.ts(self.modc[l][:, m * 8:(m + 1) * 8, :], self.modc[l][:, m * 8:(m + 1) * 8, :], 1.0, None, ALU.add,
                                 reads=['modc%d' % l], writes=['modc%d' % l])
                    else:
                        gi = 0 if m == 2 else 1
                        for half in range(2):
                            bk = self.nextps()
                            pst = self.ps[bk]
                            for kc in range(8):
                                P.mm(pst[0:2, :], sT[:, kc, :], wb[:, kc, half * 512:(half + 1) * 512], kc == 0, kc == 7,
                                     reads=[wk, 'sT'], writes=['ps%d' % bk])
                            P.tt(self.grow[l][:, gi, half * 512:(half + 1) * 512], pst[0:2, :], ab2[:, l, gi, half * 512:(half + 1) * 512], ALU.add,
                                 reads=['ps%d' % bk, 'ab2'], writes=['grow'])

    def gate_bcast(self, dst, dkey, l, gi, w):
        P = self.P
        for half in range(2):
            bk = self.nextps()
            P.mm(self.ps[bk][:, :], self.sel[:, w, :], self.grow[l][:, gi, half * 512:(half + 1) * 512], True, True,
                 reads=['sel', 'grow'], writes=['ps%d' % bk])
            P.copy(dst[:, half * 512:(half + 1) * 512], self.ps[bk][:, :], reads=['ps%d' % bk], writes=[dkey], eng='act')

    def transpose_mod(self, src_tile, skey, UT, ukey, t, l, m_shift, m_scale):
        P = self.P
        w = 0 if t < 16 else 1
        for half in range(2):
            bk = self.nextps()
            for q in range(4):
                j = half * 4 + q
                P.tr(self.ps[bk][:, q * 128:(q + 1) * 128], src_tile[:, j * 128:(j + 1) * 128], self.ident[:],
                     reads=[skey, 'ident'], writes=['ps%d' % bk])
            for q in range(4):
                j = half * 4 + q
                P.act(UT[:, j, t * 128:(t + 1) * 128], self.ps[bk][:, q * 128:(q + 1) * 128], AF.Identity,
                      reads=['ps%d' % bk, 'modc%d' % l], writes=[ukey],
                      scale=self.modc[l][:, m_scale * 8 + j, w:w + 1], bias=self.modc[l][:, m_shift * 8 + j, w:w + 1])

    def phase_A(self, st, src, l, ntiles):
        P = self.P
        UT = self.UT
        with self.scope() as s2:
            hb = [self.sb(s2, "hA%d" % i, [128, D]) for i in range(2)]
            for t in range(ntiles):
                h = hb[t % 2]
                hk = 'hA%d' % (t % 2)
                P.dma('sp', h[:], src[t * 128:(t + 1) * 128, :], reads=['hin%d_%d' % (l, t)], writes=[hk])
                self.transpose_mod(h, hk, UT, 'UT', t, l, 0, 1)

    def layer_norm(self, r, rkey, g, b, gbkey, out, okey, s2tmp):
        P = self.P
        st6, mv, rstd, nmr, tk = s2tmp
        for c in range(2):
            P.bnstats(st6[:, c, :], r[:, c * 512:(c + 1) * 512], reads=[rkey], writes=['lnst' + tk])
        P.bnaggr(mv[:], st6[:].rearrange("p a b -> p (a b)"), reads=['lnst' + tk], writes=['lnmv' + tk])
        P.ts(rstd[:], mv[:, 1:2], EPS, None, ALU.add, reads=['lnmv' + tk], writes=['lnrstd' + tk])
        P.pow_(rstd[:], rstd[:], self.mhalf[:, 0:1], reads=['lnrstd' + tk, 'mhalf'], writes=['lnrstd' + tk])
        P.stt(nmr[:], mv[:, 0:1], -1.0, rstd[:], ALU.mult, ALU.mult, reads=['lnmv' + tk, 'lnrstd' + tk], writes=['lnnmr' + tk])
        P.act(out[:], r[:], AF.Identity, reads=[rkey, 'lnrstd' + tk, 'lnnmr' + tk], writes=[okey], scale=rstd[:, 0:1], bias=nmr[:, 0:1])
        P.tt(out[:], out[:], g, ALU.mult, reads=[okey, gbkey], writes=[okey], eng='pool')
        P.tt(out[:], out[:], b, ALU.add, reads=[okey, gbkey], writes=[okey], eng='pool')

    def outproj_ln1(self, st, l, w_out_d, hsrc, ntiles, H1, lhs=None):
        P = self.P
        OT, UT = getattr(self, 'OT', None), self.UT
        with self.scope() as s2:
            wo = self.sb(s2, "wo", [128, 8, D], BF16)
            P.dma('pool', wo[:], w_out_d.rearrange("(kc p) f -> p kc f", p=128), writes=['wo'])
            lng = self.sb(s2, "lng", [128, D])
            lnb = self.sb(s2, "lnb", [128, D])
            P.dma('sp', lng[:], self.din['ln_g'][l, 0], writes=['lngb'])
            P.dma('sp', lnb[:], self.din['ln_b'][l, 0], writes=['lngb'])
            gts = []
            for w in range(2 if ntiles > 16 else 1):
                gt = self.sb(s2, "gmix%d" % w, [128, D])
                self.gate_bcast(gt, 'gmix%d' % w, l, 0, w)
                gts.append(gt)
            hb = [self.sb(s2, "hB%d" % i, [128, D]) for i in range(2)]
            rb = [self.sb(s2, "rB%d" % i, [128, D]) for i in range(2)]
            ob = [self.sb(s2, "oB%d" % i, [128, D]) for i in range(2)]
            tmps = [(self.sb(s2, "lnst6", [128, 2, 6]), self.sb(s2, "lnmv", [128, 2]), self.sb(s2, "lnrstd", [128, 1]), self.sb(s2, "lnnmr", [128, 1]), 'A%d' % i) for i in range(2)]
            mmb = {}

            def mm_pe(t):
                h, hk = hb[t % 2], 'hB%d' % (t % 2)
                P.dma('sp', h[:], hsrc[t * 128:(t + 1) * 128, :], reads=['hin%d_%d' % (l, t)], writes=[hk])
                mmb[t] = []
                for half in range(2):
                    bk = self.nextps()
                    mmb[t].append(bk)
                    for j in range(8):
                        la, lk = lhs(j, t) if lhs is not None else (OT[:, j, t * 128:(t + 1) * 128], 'OT')
                        P.mm(self.ps[bk][:, :], la, wo[:, j, half * 512:(half + 1) * 512], j == 0, j == 7,
                             reads=[lk, 'wo'], writes=['ps%d' % bk])

            def mm_dve(t):
                w = 0 if t < 16 else 1
                h, hk = hb[t % 2], 'hB%d' % (t % 2)
                r, rk = rb[t % 2], 'rB%d' % (t % 2)
                for half in range(2):
                    bk = mmb[t][half]
                    P.tt(r[:, half * 512:(half + 1) * 512], self.ps[bk][:, :], gts[w][:, half * 512:(half + 1) * 512], ALU.mult,
                         reads=['ps%d' % bk, 'gmix%d' % w], writes=[rk])
                P.stt(r[:], h[:], ALPHA, r[:], ALU.mult, ALU.add, reads=[hk, rk], writes=[rk])

            def ln_tile(t):
                r, rk = rb[t % 2], 'rB%d' % (t % 2)
                o, ok = ob[t % 2], 'oB%d' % (t % 2)
                self.layer_norm(r, rk, lng[:], lnb[:], 'lngb', o, ok, tmps[t % 2])
                P.dma('sp', H1[t * 128:(t + 1) * 128, :], o[:], reads=[ok], writes=['H1_%d_%d' % (l, t)])
                if t == 0:
                    self.dbg('h1_%d' % l, o[:], ok, [128, D])
                self.transpose_mod(o, ok, UT, 'UT', t, l, 3, 4)
            mm_pe(0)
            mm_dve(0)
            for t in range(ntiles):
                if t + 1 < ntiles:
                    mm_pe(t + 1)
                ln_tile(t)
                if t + 1 < ntiles:
                    mm_dve(t + 1)

    def ffn(self, st, l, ntiles, H1, dst, dkey, final=False):
        P = self.P
        UT = self.UT
        w_up = self.din['w_up'][l].rearrange("(kc p) f -> p kc f", p=128)
        w_dn = self.din['w_down'][l].rearrange("(j p) f -> p j f", p=128)
        if ntiles > 16:
            sbs = [[(0, 6, False, True)], [(6, 6, True, True)], [(12, 4, True, False), (16, 2, False, False)]]
        else:
            sbs = [[(0, 6, False, True)], [(6, 6, True, True)], [(12, 4, True, False)]]
        with self.scope() as s2:
            wd = self.sb(s2, "wd", [128, 22, D], BF16)
            for q in range(2):
                P.dma('pool', wd[:, q * 11:(q + 1) * 11, :], w_dn[:, q * 11:(q + 1) * 11, :], writes=['wd'])
            bup = self.sb(s2, "bup", [128, 44])
            cw = self.sb(s2, "cw", [128, 3, 44])
            cb = self.sb(s2, "cb", [128, 44])
            P.dma('sp', bup[:], self.din['b_up'][l], writes=['ffc'])
            P.dma('sp', cw[:], self.din['conv_w'][l], writes=['ffc'])
            P.dma('sp', cb[:], self.din['conv_b'][l], writes=['ffc'])
            lng = self.sb(s2, "lng2", [128, D])
            lnb = self.sb(s2, "lnb2", [128, D])
            bdn = self.sb(s2, "bdn", [128, D])
            P.dma('sp', lng[:], self.din['ln_g'][l, 1], writes=['lngb2'])
            P.dma('sp', lnb[:], self.din['ln_b'][l, 1], writes=['lngb2'])
            P.dma('sp', bdn[:], self.din['b_down'][l], writes=['bdn'])
            gts = []
            for w in range(2 if ntiles > 16 else 1):
                gt = self.sb(s2, "gffn%d" % w, [128, D])
                self.gate_bcast(gt, 'gffn%d' % w, l, 1, w)
                gts.append(gt)
            W = 774 + 2
            GT = self.sb(s2, "GT", [128, 22, W], BF16)
            xs = [self.sb(s2, "xs%d" % i, [128, W], BF16) for i in range(4)]
            dg = [self.sb(s2, "dg%d" % i, [128, 6, 128], BF16) for i in range(2)]
            sgb = [self.sb(s2, "sg%d" % i, [128, 512]) for i in range(2)]
            idb = self.sb(s2, "identb", [128, 128], BF16)
            P.copy(idb[:], self.ident[:], reads=['ident'], writes=['identb'])
            wub = [self.sb(s2, "wu%d" % i, [128, 8, 256], BF16) for i in range(3)]
            hb = [self.sb(s2, "hF%d" % i, [128, D]) for i in range(2)]
            rb = [self.sb(s2, "rF%d" % i, [128, D]) for i in range(2)]
            ob = [self.sb(s2, "oF%d" % i, [128, D]) for i in range(2)]
            tmps = [(self.sb(s2, "lnst6b", [128, 2, 6]), self.sb(s2, "lnmvb", [128, 2]), self.sb(s2, "lnrstdb", [128, 1]), self.sb(s2, "lnnmrb", [128, 1]), 'B%d' % i) for i in range(2)]
            for i in range(4):
                P.memset(xs[i][:], 0.0, writes=['xs%d' % i], eng='pool')
            nw = 0
            nsg = 0
            njobs = len(sbs) * 22

            def load_w(n):
                j = n % 22
                wu, wuk = wub[n % 3], 'wu%d' % (n % 3)
                P.dma('pool', wu[:, :, 0:128], w_up[:, :, j * 128:(j + 1) * 128], writes=[wuk])
                P.dma('pool', wu[:, :, 128:256], w_up[:, :, 2816 + j * 128:2816 + (j + 1) * 128], writes=[wuk])
            load_w(0)
            load_w(1)
            for sbi, segs in enumerate(sbs):
                cols = []
                c0 = 0
                for (t0, nt_, hl, hr) in segs:
                    cols.append(c0)
                    c0 += nt_ * 128 + 2
                width = c0
                if sbi > 0:
                    for i in range(4):
                        P.memset(xs[i][:, 0:width], 0.0, writes=['xs%d' % i], eng='pool')
                ogroups = []
                for si, (t0, nt_, hl, hr) in enumerate(segs):
                    n = nt_ * 128
                    off = 0
                    while off < n:
                        g = min(512, n - off)
                        ogroups.append((cols[si] + 1 + off, g))
                        off += g

                def conv_job(nwj, j):
                    nonlocal nsg
                    d_, dk_ = dg[nwj % 2], 'dg%d' % (nwj % 2)
                    for (cpos, g) in ogroups:
                        pb = []
                        for vg in (1, 0):
                            xi = (nwj % 2) * 2 + vg
                            x, xk = xs[xi], 'xs%d' % xi
                            bk = self.nextps()
                            pb.append(bk)
                            for tap in range(3):
                                P.mm(self.ps[bk][:, 0:g], d_[:, vg * 3 + tap, :], x[:, cpos + tap - 1:cpos + tap - 1 + g], tap == 0, tap == 2,
                                     reads=[dk_, xk], writes=['ps%d' % bk])
                        sg, sgk = sgb[nsg % 2], 'sg%d' % (nsg % 2)
                        nsg += 1
                        P.act(sg[:, 0:g], self.ps[pb[0]][:, 0:g], AF.Silu, reads=['ps%d' % pb[0], 'ffc'], writes=[sgk], bias=cb[:, 22 + j:22 + j + 1])
                        P.stt(GT[:, j, cpos:cpos + g], self.ps[pb[1]][:, 0:g], cb[:, j:j + 1], sg[:, 0:g], ALU.add, ALU.mult,
                              reads=['ps%d' % pb[1], 'ffc', sgk], writes=['GT'])
                pend = None
                for j in range(22):
                    wu, wuk = wub[nw % 3], 'wu%d' % (nw % 3)
                    if nw + 2 < njobs:
                        load_w(nw + 2)
                    d_, dk_ = dg[nw % 2], 'dg%d' % (nw % 2)
                    for vg in range(2):
                        ch = vg * 22 + j
                        for tap in range(3):
                            P.act(d_[:, vg * 3 + tap, :], idb[:], AF.Identity, reads=['identb', 'ffc'], writes=[dk_], scale=cw[:, tap, ch:ch + 1], bias=0.0)
                    for vg in range(2):
                        xi = (nw % 2) * 2 + vg
                        x, xk = xs[xi], 'xs%d' % xi
                        ch = vg * 22 + j
                        for si, (t0, nt_, hl, hr) in enumerate(segs):
                            base = cols[si]
                            tok0 = t0 * 128
                            groups = []
                            n = nt_ * 128
                            off = 0
                            while off < n:
                                g = min(512, n - off)
                                groups.append((tok0 + off, g, base + 1 + off))
                                off += g
                            if hl:
                                groups.append((tok0 - 1, 1, base))
                            if hr:
                                groups.append((tok0 + n, 1, base + 1 + n))
                            for (tk, g, cpos) in groups:
                                bk = self.nextps()
                                for kc in range(8):
                                    P.mm(self.ps[bk][:, 0:g], wu[:, kc, vg * 128:(vg + 1) * 128], UT[:, kc, tk:tk + g], kc == 0, kc == 7,
                                         reads=[wuk, 'UT'], writes=['ps%d' % bk])
                                P.act(x[:, cpos:cpos + g], self.ps[bk][:, 0:g], AF.Identity, reads=['ps%d' % bk, 'ffc'], writes=[xk],
                                      bias=bup[:, ch:ch + 1])
                    if pend is not None:
                        conv_job(*pend)
                    pend = (nw, j)
                    nw += 1
                conv_job(*pend)
                for si, (t0, nt_, hl, hr) in enumerate(segs):
                    base = cols[si]
                    for ti in range(nt_):
                        t = t0 + ti
                        w = 0 if t < 16 else 1
                        h, hk = hb[t % 2], 'hF%d' % (t % 2)
                        r, rk = rb[t % 2], 'rF%d' % (t % 2)
                        o, ok = ob[t % 2], 'oF%d' % (t % 2)
                        P.dma('sp', h[:], H1[t * 128:(t + 1) * 128, :], reads=['H1_%d_%d' % (l, t)], writes=[hk])
                        c1 = base + 1 + ti * 128
                        for half in range(2):
                            bk = self.nextps()
                            for j in range(22):
                                P.mm(self.ps[bk][:, :], GT[:, j, c1:c1 + 128], wd[:, j, half * 512:(half + 1) * 512], j == 0, j == 21,
                                     reads=['GT', 'wd'], writes=['ps%d' % bk])
                            P.tt(r[:, half * 512:(half + 1) * 512], self.ps[bk][:, :], bdn[:, half * 512:(half + 1) * 512], ALU.add,
                                 reads=['ps%d' % bk, 'bdn'], writes=[rk])
                        P.tt(r[:], r[:], gts[w][:], ALU.mult, reads=[rk, 'gffn%d' % w], writes=[rk], eng='pool')
                        P.stt(r[:], h[:], ALPHA, r[:], ALU.mult, ALU.add, reads=[hk, rk], writes=[rk])
                        self.layer_norm(r, rk, lng[:], lnb[:], 'lngb2', o, ok, tmps[t % 2])
                        P.dma('sp', dst[t * 128:(t + 1) * 128, :], o[:], reads=[ok], writes=['%s_%d' % (dkey, t)])
                        if final:
                            self.final.append('%s_%d' % (dkey, t))

    def attn_mixer(self, ntiles=NT):
        P = self.P
        UT, OT = self.UT, self.OT
        w_in = self.din['a_w_in'].rearrange("(kc p) f -> p kc f", p=128)
        blocks = [(0, 512), (512, 512), (1024, 512), (1536, 512), (2048, 256)]
        with self.scope() as s1:
            VA = self.sb(s1, "VA", [128, NT, 6, 130], BF16)
            FT = self.sb(s1, "FT", [128, 2, NTOK], BF16)
            P.memset(VA[:, :, :, 128:130], 1.0, writes=['VA'], eng='pool')
            with self.scope() as s2:
                wv = self.sb(s2, "wv", [128, 8, 1024], BF16)
                for q in range(2):
                    P.dma('pool', wv[:, :, q * 512:(q + 1) * 512], w_in[:, :, 1536 + q * 512:1536 + (q + 1) * 512], writes=['wv'])
                for t in range(NT):
                    for half in range(2):
                        bk = self.nextps()
                        for kc in range(8):
                            P.mm(self.ps[bk][:, 0:384], UT[:, kc, t * 128:(t + 1) * 128], wv[:, kc, half * 384:(half + 1) * 384], kc == 0, kc == 7,
                                 reads=['UT', 'wv'], writes=['ps%d' % bk])
                        P.copy(VA[:, t, half * 3:(half + 1) * 3, 0:128], self.ps[bk][:, 0:384].rearrange("p (a b) -> p a b", a=3),
                               reads=['ps%d' % bk], writes=['VA'], eng=('act' if half == 0 else 'dve'))
                for c in range(2):
                    for (t0, n) in blocks:
                        bk = self.nextps()
                        for kc in range(8):
                            P.mm(self.ps[bk][:, 0:n], wv[:, kc, 768 + c * 128:768 + (c + 1) * 128], UT[:, kc, t0:t0 + n], kc == 0, kc == 7,
                                 reads=['UT', 'wv'], writes=['ps%d' % bk])
                        P.copy(FT[:, c, t0:t0 + n], self.ps[bk][:, 0:n], reads=['ps%d' % bk], writes=['FT'], eng='act')
            with self.scope() as s2:
                cos4 = self.sb(s2, "cos4", [128, NLAT])
                sin4 = self.sb(s2, "sin4", [128, NLAT])
                psw = self.sb(s2, "pswap", [128, 128])
                P.dma('sp', cos4[:], self.din['cos4'], writes=['cos4'])
                P.dma('sp', sin4[:], self.din['sin4'], writes=['sin4'])
                P.dma('sp', psw[:], self.din['pswap'], writes=['pswap'])
                lamt = self.sb(s2, "lamt", [128, 256])
                gsub = self.sb(s2, "gsub", [128, 128])
                P.dma('sp', lamt[:], self.din['a_lam'], writes=['lamt'])
                P.dma('sp', gsub[:], self.din['a_subg'], writes=['gsub'])
                P.ts(gsub[:], gsub[:], 1.0 - LAM_INIT0, None, ALU.mult, reads=['gsub'], writes=['gsub'])
                lprod = self.sb(s2, "lprod", [128, 128])
                lsum = self.sb(s2, "lsum", [128, 2])
                nlam = self.sb(s2, "nlam", [128, 1])
                P.tt(lprod[:, 0:64], lamt[:, 0:64], lamt[:, 64:128], ALU.mult, reads=['lamt'], writes=['lprod'])
                P.tt(lprod[:, 64:128], lamt[:, 128:192], lamt[:, 192:256], ALU.mult, reads=['lamt'], writes=['lprod'])
                P.op('dve', lambda e: e.tensor_reduce(out=lsum[:], in_=lprod[:].rearrange("p (a b) -> p a b", a=2), axis=mybir.AxisListType.X, op=ALU.add),
                     reads=['lprod'], writes=['lsum'])
                P.act(lsum[:], lsum[:], AF.Exp, reads=['lsum'], writes=['lsum'])
                P.tt(nlam[:], lsum[:, 1:2], lsum[:, 0:1], ALU.subtract, reads=['lsum'], writes=['nlam'])
                P.ts(nlam[:], nlam[:], -LAM_INIT0, None, ALU.add, reads=['nlam'], writes=['nlam'])
                wqk = [self.sb(s2, "wqk%d" % i, [128, 8, 256], BF16) for i in range(2)]
                QT = [self.sb(s2, "QT%d" % i, [128, NTOK], BF16) for i in range(2)]
                KT = [self.sb(s2, "KT%d" % i, [128, 2, NTOK], BF16) for i in range(2)]
                for i in range(2):
                    P.memset(KT[i][:], 0.0, writes=['KT%d' % i], eng='pool')
                PT = [self.sb(s2, "PT%d" % i, [128, 512], BF16) for i in range(4)]
                qtmp = [self.sb(s2, "qtmp%d" % i, [128, 512]) for i in range(2)]
                ropA = [self.sb(s2, "ropA%d" % i, [128, 512]) for i in range(2)]
                otmp = self.sb(s2, "otmp", [128, 4, 128])
                ofin = [self.sb(s2, "ofin%d" % i, [128, 128]) for i in range(4)]
                osq = self.sb(s2, "osq", [128, 128])
                sm = self.sb(s2, "sm", [128, 4, 4])
                nrope = 0
                npt = 0

                def load_wqk(h):
                    wq, wqkey = wqk[h % 2], 'wqk%d' % (h % 2)
                    P.dma('pool', wq[:, :, 0:128], w_in[:, :, h * 128:(h + 1) * 128], writes=[wqkey])
                    P.dma('pool', wq[:, :, 128:256], w_in[:, :, 768 + h * 128:768 + (h + 1) * 128], writes=[wqkey])
                def proj(h):
                    nonlocal nrope
                    wq, wqkey = wqk[h % 2], 'wqk%d' % (h % 2)
                    dsts = ((QT[h % 2], 'QT%d' % (h % 2)), (KT[h % 2], 'KT%d' % (h % 2)))
                    for qk in range(2):
                        dst, dk = dsts[qk]
                        for (t0, n) in blocks:
                            bk = self.nextps(4, 8)
                            for kc in range(8):
                                P.mm(self.ps[bk][:, 0:n], wq[:, kc, qk * 128:(qk + 1) * 128], UT[:, kc, t0:t0 + n], kc == 0, kc == 7,
                                     reads=[wqkey, 'UT'], writes=['ps%d' % bk])
                            if t0 < NLAT:
                                qt_, qtk = qtmp[nrope % 2], 'qtmp%d' % (nrope % 2)
                                ra, rak = ropA[nrope % 2], 'ropA%d' % (nrope % 2)
                                nrope += 1
                                P.copy(qt_[:, 0:n], self.ps[bk][:, 0:n], reads=['ps%d' % bk], writes=[qtk], eng='act')
                                b2 = self.nextps(4, 8)
                                P.mm(self.ps[b2][:, 0:n], psw[:], qt_[:, 0:n], True, True, reads=['pswap', qtk], writes=['ps%d' % b2])
                                P.tt(ra[:, 0:n], qt_[:, 0:n], cos4[:, t0:t0 + n], ALU.mult, reads=[qtk, 'cos4'], writes=[rak], eng='pool')
                                P.tt(qt_[:, 0:n], self.ps[b2][:, 0:n], sin4[:, t0:t0 + n], ALU.mult, reads=['ps%d' % b2, 'sin4', qtk], writes=[qtk])
                                if qk == 0:
                                    P.tt(dst[:, t0:t0 + n], ra[:, 0:n], qt_[:, 0:n], ALU.add, reads=[rak, qtk], writes=[dk])
                                else:
                                    for a_ in range(2):
                                        P.tt(dst[a_ * 64:(a_ + 1) * 64, a_, t0:t0 + n], ra[a_ * 64:(a_ + 1) * 64, 0:n], qt_[a_ * 64:(a_ + 1) * 64, 0:n], ALU.add,
                                             reads=[rak, qtk], writes=[dk])
                            else:
                                if qk == 0:
                                    P.copy(dst[:, t0:t0 + n], self.ps[bk][:, 0:n], reads=['ps%d' % bk], writes=[dk], eng='act')
                                else:
                                    for a_ in range(2):
                                        P.copy(dst[a_ * 64:(a_ + 1) * 64, a_, t0:t0 + n], self.ps[bk][a_ * 64:(a_ + 1) * 64, 0:n], reads=['ps%d' % bk], writes=[dk], eng='act')
                load_wqk(0)
                load_wqk(1)
                proj(0)
                for h in range(6):
                    dsts = ((QT[h % 2], 'QT%d' % (h % 2)), (KT[h % 2], 'KT%d' % (h % 2)))
                    Qh, qkey = dsts[0]
                    Kh, kkey = dsts[1]
                    qblocks = [(q0, 512, list(range(NT))) for q0 in (0, 512, 1024, 1536)] + [(2048, 256, [16, 17])]
                    for qbi, (q0, nq, ktiles) in enumerate(qblocks):
                        nqt = nq // 128
                        nk = len(ktiles)
                        if qbi == 2 and h + 1 < 6:
                            proj(h + 1)
                            if h + 2 < 6:
                                load_wqk(h + 2)
                        for a in range(2):
                            sbank = {}

                            def emit_S(i):
                                kt = ktiles[i]
                                bk = self.nextps(4, 8)
                                sbank[i] = bk
                                P.mm(self.ps[bk][:, 0:nq], Kh[:, a, kt * 128:(kt + 1) * 128], Qh[:, q0:q0 + nq], True, True,
                                     reads=[qkey, kkey], writes=['ps%d' % bk])
                            emit_S(0)
                            if nk > 1:
                                emit_S(1)
                            for i in range(nk):
                                kt = ktiles[i]
                                bk = sbank[i]
                                pt, ptk = PT[npt % 4], 'PT%d' % (npt % 4)
                                npt += 1
                                P.act(pt[:, 0:nq], self.ps[bk][:, 0:nq], AF.Exp, reads=['ps%d' % bk], writes=[ptk], scale=0.125)
                                if i + 2 < nk:
                                    emit_S(i + 2)
                                for qi in range(nqt):
                                    P.mm(self.ps[qi][:, 0:129], pt[:, qi * 128:(qi + 1) * 128], VA[:, kt, h, 0:129], i == 0, i == nk - 1,
                                         reads=[ptk, 'VA'], writes=['ps%d' % qi])
                            if a == 0:
                                for qi in range(nqt):
                                    acc = self.ps[qi]
                                    sk = 'sm%d' % qi
                                    P.recip(sm[:, qi, 0:1], acc[:, 128:129], reads=['ps%d' % qi], writes=[sk])
                                    P.ts(otmp[:, qi, :], acc[:, 0:128], sm[:, qi, 0:1], None, ALU.mult, reads=['ps%d' % qi, sk], writes=['otmp%d' % qi])
                            else:
                                for qi in range(nqt):
                                    acc = self.ps[qi]
                                    sk = 'sm%d' % qi
                                    of, ofk = ofin[qi], 'ofin%d' % qi
                                    P.recip(sm[:, qi, 1:2], acc[:, 128:129], reads=['ps%d' % qi], writes=[sk])
                                    P.tt(sm[:, qi, 1:2], sm[:, qi, 1:2], nlam[:], ALU.mult, reads=[sk, 'nlam'], writes=[sk])
                                    P.stt(of[:], acc[:, 0:128], sm[:, qi, 1:2], otmp[:, qi, :], ALU.mult, ALU.add, reads=['ps%d' % qi, sk, 'otmp%d' % qi], writes=[ofk])
                                sks = ['sm%d' % qi for qi in range(nqt)]
                                for qi in range(nqt):
                                    sk = 'sm%d' % qi
                                    of, ofk = ofin[qi], 'ofin%d' % qi
                                    P.tt(osq[:], of[:], of[:], ALU.mult, reads=[ofk], writes=['osq'])
                                    P.op('dve', (lambda e, qi=qi: e.tensor_reduce(out=sm[:, qi, 2:3], in_=osq[:], axis=mybir.AxisListType.X, op=ALU.add)),
                                         reads=['osq'], writes=[sk])
                                P.ts(sm[:, 0:nqt, 2:3], sm[:, 0:nqt, 2:3], 1.0 / 128.0, EPS, ALU.mult, ALU.add, reads=sks, writes=sks)
                                P.pow_(sm[:, 0:nqt, 2:3], sm[:, 0:nqt, 2:3], self.mhalf[:, 0:nqt].unsqueeze(2), reads=sks + ['mhalf'], writes=sks)
                                for qi in range(nqt):
                                    sk = 'sm%d' % qi
                                    of, ofk = ofin[qi], 'ofin%d' % qi
                                    P.stt(of[:], of[:], sm[:, qi, 2:3], gsub[:], ALU.mult, ALU.mult, reads=[ofk, sk, 'gsub'], writes=[ofk])
                                    bk = self.nextps(4, 8)
                                    P.tr(self.ps[bk][:, 0:128], of[:], self.ident[:], reads=[ofk, 'ident'], writes=['ps%d' % bk])
                                    P.copy(OT[:, h, q0 + qi * 128:q0 + (qi + 1) * 128], self.ps[bk][:, 0:128], reads=['ps%d' % bk], writes=['OT'], eng='act')
            with self.scope() as s2:
                cb = self.sb(s2, "c64", [128, 2, 128])
                fw = self.sb(s2, "fwb", [128, 2, 128])
                fb = self.sb(s2, "fb", [128, 2])
                P.dma('sp', cb[:, 0, :], self.din['c64blk'], writes=['c64'])
                P.dma('sp', cb[:, 1, :], self.din['ns64blk'], writes=['c64'])
                P.dma('sp', fw[:], self.din['f_wblk'].rearrange("c p e -> p c e"), writes=['fwb'])
                P.dma('sp', fb[:], self.din['f_b'], writes=['fb'])
                ABm = self.sb(s2, "ABm", [128, 2, 2, 128], BF16)
                for c in range(2):
                    for cs in range(2):
                        bk = self.nextps()
                        P.mm(self.ps[bk][:, 0:128], cb[:, cs, :], fw[:, c, :], True, True, reads=['c64', 'fwb'], writes=['ps%d' % bk])
                        P.copy(ABm[:, c, cs, :], self.ps[bk][:, 0:128], reads=['ps%d' % bk], writes=['ABm'])
                Ycs = self.sb(s2, "Ycs", [128, NT, 2, 256], BF16)
                for t in range(NT):
                    for c in range(2):
                        bk = self.nextps(4, 8)
                        P.mm(self.ps[bk][:, 0:256], FT[:, c, t * 128:(t + 1) * 128], ABm[:, c, :, :].rearrange("p a b -> p (a b)"), True, True,
                             reads=['FT', 'ABm'], writes=['ps%d' % bk])
                        P.copy(Ycs[:, t, c, :], self.ps[bk][:, 0:256], reads=['ps%d' % bk], writes=['Ycs'], eng=('act' if c == 0 else 'dve'))
                dbuf = [self.sb(s2, "dft%d" % i, [128, NLAT], BF16) for i in range(3)]
                nd = 0
                dsrc = (self.din['dftc'], self.din['dfts'])
                for tn in range(16):
                    for cs in range(2):
                        db, dbk = dbuf[nd % 3], 'dft%d' % (nd % 3)
                        nd += 1
                        P.dma('sp', db[:], dsrc[cs][tn * 128:(tn + 1) * 128, :], writes=[dbk])
                        first = (tn == 0 and cs == 0)
                        last = (tn == 15 and cs == 1)
                        for c in range(2):
                            for nb in range(4):
                                bq = c * 4 + nb
                                P.mm(self.ps[bq][:, :], Ycs[:, tn, c, cs * 128:(cs + 1) * 128], db[:, nb * 512:(nb + 1) * 512], first, last,
                                     reads=['Ycs', dbk], writes=['ps%d' % bq])
                for c in range(2):
                    for nb in range(4):
                        bq = c * 4 + nb
                        P.act(OT[:, 6 + c, nb * 512:(nb + 1) * 512], self.ps[bq][:, :], AF.Identity, reads=['ps%d' % bq, 'fb'], writes=['OT'], bias=fb[:, c:c + 1])
                dcc = self.sb(s2, "dftcc", [128, 2, 2, 256], BF16)
                P.dma('sp', dcc[:, 0], self.din['dftc_c'].rearrange("(t p) n -> p t n", p=128), writes=['dftcc'])
                P.dma('sp', dcc[:, 1], self.din['dfts_c'].rearrange("(t p) n -> p t n", p=128), writes=['dftcc'])
                for c in range(2):
                    bk = self.nextps(4, 8)
                    n = 0
                    for tn in range(2):
                        for cs in range(2):
                            P.mm(self.ps[bk][:, 0:256], Ycs[:, 16 + tn, c, cs * 128:(cs + 1) * 128], dcc[:, cs, tn, :], n == 0, n == 3,
                                 reads=['Ycs', 'dftcc'], writes=['ps%d' % bk])
                            n += 1
                    P.act(OT[:, 6 + c, 2048:2304], self.ps[bk][:, 0:256], AF.Identity, reads=['ps%d' % bk, 'fb'], writes=['OT'], bias=fb[:, c:c + 1])

    def s5_mixer(self, S5O):
        import os
        stop = int(os.environ.get('S5STOP', '99'))
        if stop <= 0:
            self.P.memset(S5O[:], 0.0, writes=['S5O'])
            return
        P = self.P
        UT = self.UT
        w_in = self.din['s_w_in'].rearrange("(kc p) f -> p kc f", p=128)
        blocks = [(0, 512), (512, 512), (1024, 512), (1536, 512), (2048, 256)]
        T = ALU
        with self.scope() as s1:
            usT = self.sb(s1, "usT", [128, 2, NTOK], BF16)
            Y5 = self.sb(s1, "Y5", [128, 2, NLAT])
            dd = self.sb(s1, "s5dd", [128, 2])
            P.dma('sp', dd[:], self.din['s5_dd'], writes=['s5dd'])
            ones = self.sb(s1, "ones", [128, 512])
            P.dma('sp', ones[:], self.din['ones'], writes=['ones'])
            with self.scope() as s2:
                wus = self.sb(s2, "wus", [128, 8, 256], BF16)
                P.dma('pool', wus[:], w_in[:, :, 2072:2328], writes=['wus'])
                for c in range(2):
                    for (t0, n) in blocks:
                        bk = self.nextps()
                        for kc in range(8):
                            P.mm(self.ps[bk][:, 0:n], wus[:, kc, c * 128:(c + 1) * 128], UT[:, kc, t0:t0 + n], kc == 0, kc == 7,
                                 reads=['wus', 'UT'], writes=['ps%d' % bk])
                        P.copy(usT[:, c, t0:t0 + n], self.ps[bk][:, 0:n], reads=['ps%d' % bk], writes=['usT'], eng='act')
                        if t0 < NLAT and os.environ.get('S5SUB', '') != 'a':
                            P.ts(Y5[:, c, t0:t0 + n], self.ps[bk][:, 0:n], dd[:, c:c + 1], None, T.mult, reads=['ps%d' % bk, 's5dd'], writes=['Y5_%d_%d' % (c, t0)])
            if stop <= 1:
                P.memset(S5O[:], 0.0, writes=['S5O'])
                return
            pr = self.sb(s1, "s5pr", [128, 32, 16])
            names = {}

            def V(nm):
                if nm not in names:
                    names[nm] = len(names)
                return pr[:, names[nm], :]
            for nm, src in (('lre', 's5_lre'), ('lim', 's5_lim'), ('ldt', 's5_ldt')):
                P.dma('sp', V(nm), self.din[src], writes=['s5pr'])
            K_ = ['s5pr']

            def tt(o, a, b, op):
                P.tt(V(o), V(a), V(b), op, reads=K_, writes=K_)

            def ts(o, a, s1_, s2_, op0, op1=None):
                P.ts(V(o), V(a), s1_, s2_, op0, op1, reads=K_, writes=K_)
            P.act(V('dt'), V('ldt'), AF.Exp, reads=K_, writes=K_)
            tt('t0', 'dt', 'lre', T.mult)
            P.act(V('mag'), V('t0'), AF.Exp, reads=K_, writes=K_)
            tt('th', 'dt', 'lim', T.mult)
            ts('k', 'th', 1.0 / (2.0 * math.pi), None, T.mult)
            ts('k', 'k', 12582912.0, None, T.add)
            ts('k', 'k', -12582912.0, None, T.add)
            P.stt(V('r'), V('k'), -6.28125, V('th'), T.mult, T.add, reads=K_, writes=K_)
            P.stt(V('r'), V('k'), -(2.0 * math.pi - 6.28125), V('r'), T.mult, T.add, reads=K_, writes=K_)
            ts('x', 'r', 0.125, None, T.mult)
            tt('x2', 'x', 'x', T.mult)
            ts('ps_', 'x2', -1.0 / 5040.0, 1.0 / 120.0, T.mult, T.add)
            tt('ps_', 'ps_', 'x2', T.mult)
            ts('ps_', 'ps_', -1.0 / 6.0, None, T.add)
            tt('ps_', 'ps_', 'x2', T.mult)
            ts('ps_', 'ps_', 1.0, None, T.add)
            tt('sn', 'ps_', 'x', T.mult)
            ts('pc_', 'x2', 1.0 / 40320.0, -1.0 / 720.0, T.mult, T.add)
            tt('pc_', 'pc_', 'x2', T.mult)
            ts('pc_', 'pc_', 1.0 / 24.0, None, T.add)
            tt('pc_', 'pc_', 'x2', T.mult)
            ts('pc_', 'pc_', -0.5, None, T.add)
            tt('pc_', 'pc_', 'x2', T.mult)
            ts('cs', 'pc_', 1.0, None, T.add)
            for _ in range(3):
                tt('cc', 'cs', 'cs', T.mult)
                tt('ss', 'sn', 'sn', T.mult)
                tt('sc2', 'cs', 'sn', T.mult)
                tt('cs', 'cc', 'ss', T.subtract)
                ts('sn', 'sc2', 2.0, None, T.mult)
            tt('abre', 'mag', 'cs', T.mult)
            tt('abim', 'mag', 'sn', T.mult)
            tt('den', 'lre', 'lre', T.mult)
            tt('t0', 'lim', 'lim', T.mult)
            tt('den', 'den', 't0', T.add)
            P.recip(V('den'), V('den'), reads=K_, writes=K_)
            ts('am1', 'abre', -1.0, None, T.add)
            tt('t0', 'am1', 'lre', T.mult)
            tt('t1', 'abim', 'lim', T.mult)
            tt('t0', 't0', 't1', T.add)
            tt('kre', 't0', 'den', T.mult)
            tt('t0', 'abim', 'lre', T.mult)
            tt('t1', 'am1', 'lim', T.mult)
            tt('t0', 't0', 't1', T.subtract)
            tt('kim', 't0', 'den', T.mult)
            if stop <= 2:
                P.memset(S5O[:], 0.0, writes=['S5O'])
                return
            Ec = self.sb(s1, "s5Ec", [128, 10, 16])
            Es = self.sb(s1, "s5Es", [128, 10, 16])
            EK = ['s5E']
            P.copy(Ec[:, 0, :], V('cs'), reads=K_, writes=EK)
            P.copy(Es[:, 0, :], V('sn'), reads=K_, writes=EK)
            e1 = self.sb(s1, "s5e1", [128, 16])
            e2 = self.sb(s1, "s5e2", [128, 16])
            for j in range(9):
                P.tt(e1[:], Ec[:, j, :], Ec[:, j, :], T.mult, reads=EK, writes=['s5e1'])
                P.tt(e2[:], Es[:, j, :], Es[:, j, :], T.mult, reads=EK, writes=['s5e2'])
                P.tt(Ec[:, j + 1, :], e1[:], e2[:], T.subtract, reads=['s5e1', 's5e2'], writes=EK)
                P.tt(e1[:], Ec[:, j, :], Es[:, j, :], T.mult, reads=EK, writes=['s5e1'])
                P.ts(Es[:, j + 1, :], e1[:], 2.0, None, T.mult, reads=['s5e1'], writes=EK)
            if stop <= 3:
                P.memset(S5O[:], 0.0, writes=['S5O'])
                return
            nEs = self.sb(s1, "s5nEs", [128, 10, 16])
            P.ts(nEs[:], Es[:], -1.0, None, T.mult, reads=EK, writes=['s5nEs'])
            BbT = self.sb(s1, "BbT", [128, 2, 8, 2, 128], BF16)
            CmT = self.sb(s1, "CmT", [128, 2, 8, 2, 128], BF16)
            for d in range(2):
                P.dma('pool', CmT[:, d, :, 0, :], self.din['s5_cre'][d], writes=['CmT'])
                P.dma('pool', CmT[:, d, :, 1, :], self.din['s5_cim'][d], writes=['CmT'])
                P.ts(CmT[:, d, :, 1, :], CmT[:, d, :, 1, :], -1.0, None, T.mult, reads=['CmT'], writes=['CmT'])
            with self.scope() as s2:
                bre = self.sb(s2, "s5bre", [128, 8, 128])
                bim = self.sb(s2, "s5bim", [128, 8, 128])
                P.dma('sp', bre[:], self.din['s5_bre'], writes=['s5bre'])
                P.dma('sp', bim[:], self.din['s5_bim'], writes=['s5bim'])
                wt = [self.sb(s2, "s5wt%d" % i, [128, 128]) for i in range(2)]
                nw = 0
                for d in range(2):
                    for sc in range(8):
                        col = d * 8 + sc
                        kre = V('kre')[:, col:col + 1]
                        kim = V('kim')[:, col:col + 1]
                        for ri in range(2):
                            w, wk = wt[nw % 2], 's5wt%d' % (nw % 2)
                            nw += 1
                            if ri == 0:
                                P.ts(w[:], bim[:, sc, :], kim, None, T.mult, reads=['s5bim'] + K_, writes=[wk])
                                P.stt(w[:], bre[:, sc, :], kre, w[:], T.mult, T.subtract, reads=['s5bre', wk] + K_, writes=[wk])
                            else:
                                P.ts(w[:], bre[:, sc, :], kim, None, T.mult, reads=['s5bre'] + K_, writes=[wk])
                                P.stt(w[:], bim[:, sc, :], kre, w[:], T.mult, T.add, reads=['s5bim', wk] + K_, writes=[wk])
                            bk = self.nextps()
                            P.tr(self.ps[bk][:, 0:128], w[:], self.ident[:], reads=[wk, 'ident'], writes=['ps%d' % bk])
                            P.copy(BbT[:, d, sc, ri, :], self.ps[bk][:, 0:128], reads=['ps%d' % bk], writes=['BbT'], eng='act')
            if stop <= 4:
                P.memset(S5O[:], 0.0, writes=['S5O'])
                return
            with self.scope() as s2:
                Tc = self.sb(s2, "s5Tc", [128, 8, 512])
                Ts = self.sb(s2, "s5Ts", [128, 8, 512])
                tq = [self.sb(s2, "s5tq%d" % i, [128, 8, 256]) for i in range(2)]
                rho = self.sb(s2, "s5rho", [128, 512])
                NB = 2
                mt = [[self.sb(s2, "s5m%d_%d" % (i, b), [128, 512]) for i in range(4)] for b in range(NB)]
                bp = [[self.sb(s2, "s5bp%d_%d" % (i, b), [128, 512]) for i in range(2)] for b in range(NB)]
                gg = [[self.sb(s2, "s5g%d_%d" % (i, b), [128, 512]) for i in range(2)] for b in range(NB)]
                hh = [[self.sb(s2, "s5h%d_%d" % (i, b), [128, 512], BF16) for i in range(2)] for b in range(NB)]
                pp = [self.sb(s2, "s5p%d" % i, [128, 512]) for i in range(4)]
                ini = [self.sb(s2, "s5ini%d" % b, [128, 4]) for b in range(NB)]
                nb_ = 0
                for d in range(2):
                    TK = ['s5T']
                    P.memset(Tc[:, :, 0:1], 1.0, writes=TK)
                    P.memset(Ts[:, :, 0:1], 0.0, writes=TK)
                    for j in range(9):
                        m = 1 << j
                        ecb = Ec[:, j, d * 8:(d + 1) * 8].unsqueeze(2).to_broadcast([128, 8, m])
                        esb = Es[:, j, d * 8:(d + 1) * 8].unsqueeze(2).to_broadcast([128, 8, m])
                        P.tt(tq[0][:, :, 0:m], Ts[:, :, 0:m], esb, T.mult, reads=TK + EK, writes=['s5tq0'])
                        P.tt(tq[1][:, :, 0:m], Tc[:, :, 0:m], esb, T.mult, reads=TK + EK, writes=['s5tq1'])
                        P.tt(Tc[:, :, m:2 * m], Tc[:, :, 0:m], ecb, T.mult, reads=TK + EK, writes=TK)
                        P.tt(Ts[:, :, m:2 * m], Ts[:, :, 0:m], ecb, T.mult, reads=TK + EK, writes=TK)
                        P.tt(Tc[:, :, m:2 * m], Tc[:, :, m:2 * m], tq[0][:, :, 0:m], T.subtract, reads=TK + ['s5tq0'], writes=TK)
                        P.tt(Ts[:, :, m:2 * m], Ts[:, :, m:2 * m], tq[1][:, :, 0:m], T.add, reads=TK + ['s5tq1'], writes=TK)
                    order = [blocks[4]] + (blocks[0:4] if d == 0 else blocks[3::-1])
                    if stop <= 5:
                        continue
                    if stop <= 6 and d == 1:
                        continue
                    if d == 1:
                        P.copy(tq[0][:, :, :], Tc[:, :, 0:256], reads=TK, writes=['s5tq0'])
                        P.copy(tq[1][:, :, :], Tc[:, :, 256:512], reads=TK, writes=['s5tq1'])
                        P.copy(Tc[:, :, 0:256], tq[1][:, :, ::-1], reads=['s5tq1'], writes=TK)
                        P.copy(Tc[:, :, 256:512], tq[0][:, :, ::-1], reads=['s5tq0'], writes=TK)
                        P.copy(tq[0][:, :, :], Ts[:, :, 0:256], reads=TK, writes=['s5tq0'])
                        P.copy(tq[1][:, :, :], Ts[:, :, 256:512], reads=TK, writes=['s5tq1'])
                        P.copy(Ts[:, :, 0:256], tq[1][:, :, ::-1], reads=['s5tq1'], writes=TK)
                        P.copy(Ts[:, :, 256:512], tq[0][:, :, ::-1], reads=['s5tq0'], writes=TK)
                    for sc in range(8):
                        col = d * 8 + sc
                        fc = sc // 4
                        P.ts(rho[:], ones[:], V('mag')[:, col:col + 1], None, T.mult, reads=['ones'] + K_, writes=['s5rho'])
                        prev = None
                        pending = []
                        pend_pe = []
                        for (t0, n) in order:
                            b = nb_ % NB
                            nb_ += 1
                            sfx = '_%d' % b
                            bk1 = self.nextps()
                            P.mm(self.ps[bk1][:, 0:n], BbT[:, d, sc, 0, :], usT[:, fc, t0:t0 + n], True, True, reads=['BbT', 'usT'], writes=['ps%d' % bk1])
                            bk2 = self.nextps()
                            P.mm(self.ps[bk2][:, 0:n], BbT[:, d, sc, 1, :], usT[:, fc, t0:t0 + n], True, True, reads=['BbT', 'usT'], writes=['ps%d' % bk2])
                            while pend_pe:
                                pend_pe.pop(0)()
                            pre, pim = self.ps[bk1][:, 0:n], self.ps[bk2][:, 0:n]
                            if d == 0:
                                tc = Tc[:, sc, 0:n]
                                ts_ = Ts[:, sc, 0:n]
                            else:
                                tc = Tc[:, sc, 512 - n:512]
                                ts_ = Ts[:, sc, 512 - n:512]
                            m1, m2, m3, m4 = [mt[b][i][:, 0:n] for i in range(4)]
                            mk = ['s5m%d%s' % (i, sfx) for i in range(4)]
                            P.tt(m1, pre, tc, T.mult, reads=['ps%d' % bk1] + TK, writes=[mk[0]])
                            P.tt(m2, pim, ts_, T.mult, reads=['ps%d' % bk2] + TK, writes=[mk[1]])
                            P.tt(m3, pim, tc, T.mult, reads=['ps%d' % bk2] + TK, writes=[mk[2]])
                            P.tt(m4, pre, ts_, T.mult, reads=['ps%d' % bk1] + TK, writes=[mk[3]])
                            bpr, bpi = bp[b][0][:, 0:n], bp[b][1][:, 0:n]
                            P.tt(bpr, m1, m2, T.add, reads=[mk[0], mk[1]], writes=['s5bp0' + sfx])
                            P.tt(bpi, m3, m4, T.subtract, reads=[mk[2], mk[3]], writes=['s5bp1' + sfx])
                            gre, gim = gg[b][0][:, 0:n], gg[b][1][:, 0:n]
                            if d == 0:
                                go_r, go_i, bi_r, bi_i = gre, gim, bpr, bpi
                                last = n - 1
                            else:
                                go_r, go_i, bi_r, bi_i = gre[:, ::-1], gim[:, ::-1], bpr[:, ::-1], bpi[:, ::-1]
                                last = 0
                            i0 = 0.0 if prev is None else ini[prev][:, 0:1]
                            i1 = 0.0 if prev is None else ini[prev][:, 1:2]
                            rk = [] if prev is None else ['s5ini%d' % prev]
                            P.scan(go_r, rho[:, 0:n], bi_r, i0, T.mult, T.add, reads=['s5rho', 's5bp0' + sfx] + rk, writes=['s5g0' + sfx])
                            P.scan(go_i, rho[:, 0:n], bi_i, i1, T.mult, T.add, reads=['s5rho', 's5bp1' + sfx] + rk, writes=['s5g1' + sfx])
                            j = 9 if n == 512 else 8
                            ec = Ec[:, j, col:col + 1]
                            es = Es[:, j, col:col + 1]
                            ik = 's5ini%d' % b
                            nes = nEs[:, j, col:col + 1]
                            P.act(ini[b][:, 2:3], gg[b][1][:, last:last + 1], AF.Identity, reads=['s5g1' + sfx, 's5nEs'], writes=[ik], scale=nes, bias=0.0)
                            P.act(ini[b][:, 0:1], gg[b][0][:, last:last + 1], AF.Identity, reads=['s5g0' + sfx, ik] + EK, writes=[ik], scale=ec, bias=ini[b][:, 2:3])
                            P.act(ini[b][:, 3:4], gg[b][0][:, last:last + 1], AF.Identity, reads=['s5g0' + sfx] + EK, writes=[ik], scale=es, bias=0.0)
                            P.act(ini[b][:, 1:2], gg[b][1][:, last:last + 1], AF.Identity, reads=['s5g1' + sfx, ik] + EK, writes=[ik], scale=ec, bias=ini[b][:, 3:4])
                            prev = b
                            while pending:
                                pending.pop(0)()
                            if t0 < NLAT:
                                p1, p2, p3, p4 = [pp[i][:, 0:n] for i in range(4)]
                                hre, him = hh[b][0][:, 0:n], hh[b][1][:, 0:n]
                                P.tt(p1, gre, tc, T.mult, reads=['s5g0' + sfx] + TK, writes=['s5p0'], eng='pool')
                                P.tt(p2, gim, ts_, T.mult, reads=['s5g1' + sfx] + TK, writes=['s5p1'], eng='pool')
                                P.tt(hre, p1, p2, T.subtract, reads=['s5p0', 's5p1'], writes=['s5h0' + sfx], eng='pool')
                                P.tt(p3, gre, ts_, T.mult, reads=['s5g0' + sfx] + TK, writes=['s5p2'], eng='pool')
                                P.tt(p4, gim, tc, T.mult, reads=['s5g1' + sfx] + TK, writes=['s5p3'], eng='pool')
                                P.tt(him, p3, p4, T.add, reads=['s5p2', 's5p3'], writes=['s5h1' + sfx], eng='pool')
                                yk = 'Y5_%d_%d' % (fc, t0)
                                cell = {}

                                def _ro(cell=cell, hre=hre, him=him, sfx=sfx, n=n, d=d, sc=sc):
                                    bk = self.nextps()
                                    cell['bk'] = bk
                                    P.mm(self.ps[bk][:, 0:n], CmT[:, d, sc, 0, :], hre, True, False, reads=['CmT', 's5h0' + sfx], writes=['ps%d' % bk])
                                    P.mm(self.ps[bk][:, 0:n], CmT[:, d, sc, 1, :], him, False, True, reads=['CmT', 's5h1' + sfx], writes=['ps%d' % bk])

                                def _acc(cell=cell, yk=yk, fc=fc, t0=t0, n=n):
                                    bk = cell['bk']
                                    P.tt(Y5[:, fc, t0:t0 + n], Y5[:, fc, t0:t0 + n], self.ps[bk][:, 0:n], T.add, reads=[yk, 'ps%d' % bk], writes=[yk])
                                pend_pe.append(_ro)
                                pending.append(_acc)
                        while pend_pe:
                            pend_pe.pop(0)()
                        while pending:
                            pending.pop(0)()
            if stop <= 7:
                P.memset(S5O[:], 0.0, writes=['S5O'])
                return
            with self.scope() as s2:
                gw = self.sb(s2, "gluw", [128, 2, 256], BF16)
                gb_ = self.sb(s2, "glub", [128, 2])
                P.dma('pool', gw[:], self.din['s5_glu_w'].rearrange("(kc p) f -> p kc f", p=128), writes=['gluw'])
                P.dma('sp', gb_[:], self.din['s5_glu_b'], writes=['glub'])
                gbf = self.sb(s2, "gbf", [128, 2, NLAT], BF16)
                t1 = [self.sb(s2, "glt%d" % i, [128, 512]) for i in range(2)]
                sg = [self.sb(s2, "gls%d" % i, [128, 512]) for i in range(2)]
                n_ = 0
                for c in range(2):
                    for (t0, n) in blocks[0:4]:
                        yk = 'Y5_%d_%d' % (c, t0)
                        x = Y5[:, c, t0:t0 + n]
                        a, ak = t1[n_ % 2], 'glt%d' % (n_ % 2)
                        n_ += 1
                        P.tt(a[:], x, x, T.mult, reads=[yk], writes=[ak], eng='pool')
                        P.ts(a[:], a[:], 0.044715, 1.0, T.mult, T.add, reads=[ak], writes=[ak])
                        P.tt(a[:], a[:], x, T.mult, reads=[ak, yk], writes=[ak])
                        P.act(a[:], a[:], AF.Tanh, reads=[ak], writes=[ak], scale=math.sqrt(2.0 / math.pi))
                        P.stt(a[:], a[:], 1.0, x, T.add, T.mult, reads=[ak, yk], writes=[ak])
                        P.ts(x, a[:], 0.5, None, T.mult, reads=[ak], writes=[yk])
                        P.copy(gbf[:, c, t0:t0 + n], x, reads=[yk], writes=['gbf'], eng='act')
                n_ = 0
                for c in range(2):
                    for (t0, n) in blocks[0:4]:
                        bk = self.nextps()
                        for kc in range(2):
                            P.mm(self.ps[bk][:, 0:n], gw[:, kc, c * 128:(c + 1) * 128], gbf[:, kc, t0:t0 + n], kc == 0, kc == 1,
                                 reads=['gluw', 'gbf'], writes=['ps%d' % bk])
                        a, ak = sg[n_ % 2], 'gls%d' % (n_ % 2)
                        n_ += 1
                        P.act(a[:], self.ps[bk][:, 0:n], AF.Sigmoid, reads=['ps%d' % bk, 'glub'], writes=[ak], bias=gb_[:, c:c + 1])
                        P.tt(S5O[:, c, t0:t0 + n], Y5[:, c, t0:t0 + n], a[:], T.mult, reads=['Y5_%d_%d' % (c, t0), ak], writes=['S5O'])

    def ssd_inproj(self, sB, Xtm, Btm, BT, CT, dtt, dta, ZS):
        P = self.P
        UT = self.UT
        T = ALU
        w_in = self.din['s_w_in'].rearrange("(kc p) f -> p kc f", p=128)
        blocks = [(0, 512), (512, 512), (1024, 512), (1536, 512), (2048, 256)]
        with self.scope() as s2:
            wz = self.sb(s2, "wz", [128, 8, 768], BF16)
            P.dma('pool', wz[:], w_in[:, :, 0:768], writes=['wz'])
            zt = [self.sb(s2, "ztp%d" % i, [128, 768]) for i in range(2)]
            for t in range(16):
                z, zk = zt[t % 2], 'ztp%d' % (t % 2)
                for half in range(2):
                    bk = self.nextps()
                    for kc in range(8):
                        P.mm(self.ps[bk][:, 0:384], UT[:, kc, t * 128:(t + 1) * 128], wz[:, kc, half * 384:(half + 1) * 384], kc == 0, kc == 7,
                             reads=['UT', 'wz'], writes=['ps%d' % bk])
                    P.act(z[:, half * 384:(half + 1) * 384], self.ps[bk][:, 0:384], AF.Silu, reads=['ps%d' % bk], writes=[zk])
                P.dma('sp', ZS[t * 128:(t + 1) * 128, :], z[:], reads=[zk], writes=['ZS_%d' % t])
            wdt = self.sb(s2, "wdt", [128, 8, 24], BF16)
            P.dma('pool', wdt[:], w_in[:, :, 2048:2072], writes=['wdt'])
            dtb = self.sb(s2, "dtb", [128, 24])
            alog = self.sb(s2, "alog", [128, 24])
            P.dma('sp', dtb[:], self.din['sd_dtb'], writes=['dtb'])
            P.dma('sp', alog[:], self.din['sd_alog'], writes=['alog'])
            for t in range(NT):
                bk = self.nextps()
                for kc in range(8):
                    P.mm(self.ps[bk][:, 0:24], UT[:, kc, t * 128:(t + 1) * 128], wdt[:, kc, :], kc == 0, kc == 7, reads=['UT', 'wdt'], writes=['ps%d' % bk])
                P.tt(dtt[:, t, :], self.ps[bk][:, 0:24], dtb[:], T.add, reads=['ps%d' % bk, 'dtb'], writes=['dtt'])
            ax = self.sb(s2, "spax", [128, NT * 24])
            dflat = dtt[:].rearrange("p t f -> p (t f)")
            P.act(ax[:], dflat, AF.Abs, reads=['dtt'], writes=['spax'])
            P.act(ax[:], ax[:], AF.Exp, reads=['spax'], writes=['spax'], scale=-1.0)
            P.act(ax[:], ax[:], AF.Ln, reads=['spax'], writes=['spax'], bias=1.0)
            P.ts(dflat, dflat, 0.0, None, T.max, reads=['dtt'], writes=['dtt'])
            P.tt(dflat, dflat, ax[:], T.add, reads=['dtt', 'spax'], writes=['dtt'])
            P.act(alog[:], alog[:], AF.Exp, reads=['alog'], writes=['alog'])
            P.ts(alog[:], alog[:], -1.0, None, T.mult, reads=['alog'], writes=['alog'])
            P.tt(dta[:], dtt[:], alog[:].unsqueeze(1).to_broadcast([128, NT, 24]), T.mult, reads=['dtt', 'alog'], writes=['dta'])
        with self.scope() as s2:
            cw = self.sb(s2, "sdcw", [128, 3, 10])
            cb = self.sb(s2, "sdcb", [128, 10])
            P.dma('sp', cw[:], self.din['sd_cw'], writes=['sdc'])
            P.dma('sp', cb[:], self.din['sd_cb'], writes=['sdc'])
            W = 2308
            xs = self.sb(s2, "sdxs", [128, W])
            acc = self.sb(s2, "sdacc", [128, W])
            P.memset(xs[:], 0.0, writes=['sdxs'], eng='pool')
            wx = [self.sb(s2, "sdwx%d" % i, [128, 8, 128], BF16) for i in range(2)]

            def colpos(t0):
                return 1 + t0 if t0 < NLAT else 2051 + (t0 - NLAT)
            for c in range(10):
                w, wk = wx[c % 2], 'sdwx%d' % (c % 2)
                P.dma('pool', w[:], w_in[:, :, 768 + c * 128:768 + (c + 1) * 128], writes=[wk])
                for (t0, n) in blocks:
                    bk = self.nextps()
                    for kc in range(8):
                        P.mm(self.ps[bk][:, 0:n], w[:, kc, :], UT[:, kc, t0:t0 + n], kc == 0, kc == 7, reads=[wk, 'UT'], writes=['ps%d' % bk])
                    cp = colpos(t0)
                    P.copy(xs[:, cp:cp + n], self.ps[bk][:, 0:n], reads=['ps%d' % bk], writes=['sdxs'], eng='act')
                n1 = W - 2
                P.ts(acc[:, 1:1 + n1], xs[:, 0:n1], cw[:, 0, c:c + 1], cb[:, c:c + 1], T.mult, T.add, reads=['sdxs', 'sdc'], writes=['sdacc'])
                P.stt(acc[:, 1:1 + n1], xs[:, 1:1 + n1], cw[:, 1, c:c + 1], acc[:, 1:1 + n1], T.mult, T.add, reads=['sdxs', 'sdacc', 'sdc'], writes=['sdacc'])
                P.stt(acc[:, 1:1 + n1], xs[:, 2:2 + n1], cw[:, 2, c:c + 1], acc[:, 1:1 + n1], T.mult, T.add, reads=['sdxs', 'sdacc', 'sdc'], writes=['sdacc'])
                P.act(acc[:, 1:1 + n1], acc[:, 1:1 + n1], AF.Silu, reads=['sdacc'], writes=['sdacc'])
                if 6 <= c < 8:
                    g = c - 6
                    P.copy(BT[:, g, 0:NLAT], acc[:, 1:1 + NLAT], reads=['sdacc'], writes=['BT'], eng='pool')
                    P.copy(BT[:, g, NLAT:NTOK], acc[:, 2051:2051 + NCTX], reads=['sdacc'], writes=['BT'], eng='pool')
                if c >= 8:
                    g = c - 8
                    P.copy(CT[:, g, 0:NLAT], acc[:, 1:1 + NLAT], reads=['sdacc'], writes=['CT'], eng='pool')
                    P.copy(CT[:, g, NLAT:NTOK], acc[:, 2051:2051 + NCTX], reads=['sdacc'], writes=['CT'], eng='pool')
                if c < 8:
                    for t4 in range(0, NT, 4):
                        bk = self.nextps()
                        nq = min(4, NT - t4)
                        for q in range(nq):
                            t = t4 + q
                            cp = colpos(t * 128)
                            P.tr(self.ps[bk][:, q * 128:(q + 1) * 128], acc[:, cp:cp + 128], self.ident[:], reads=['sdacc', 'ident'], writes=['ps%d' % bk])
                        src = self.ps[bk][:, 0:nq * 128].rearrange("p (q f) -> p q f", q=nq)
                        if c < 6:
                            P.copy(Xtm[:, t4:t4 + nq, c * 128:(c + 1) * 128], src, reads=['ps%d' % bk], writes=['Xtm'], eng=('act' if (t4 // 4) % 2 == 0 else 'dve'))
                        else:
                            P.copy(Btm[:, t4:t4 + nq, c - 6, :], src, reads=['ps%d' % bk], writes=['Btm'], eng=('act' if (t4 // 4) % 2 == 0 else 'dve'))

    def ssd_chunks(self, Xtm, Btm, BT, CT, dtt, dta, Yacc):
        P = self.P
        T = ALU
        with self.scope() as s2:
            vd = self.sb(s2, "vd", [128, 2, 128])
            ud = self.sb(s2, "ud", [128, 2, 128])
            ones = self.sb(s2, "ones1", [128, 128])
            dsk = self.sb(s2, "dsk", [128, 12])
            P.dma('sp', vd[:], self.din['vd'], writes=['vd'])
            P.dma('sp', ud[:], self.din['ud'], writes=['ud'])
            P.dma('sp', ones[:], self.din['ones'][:, 0:128], writes=['ones1'])
            P.dma('sp', dsk[:], self.din['sd_d'], writes=['dsk'])
            for t in range(16):
                P.tt(Yacc[:, t, :].rearrange("p (r e) -> p r e", r=12), Xtm[:, t, :].rearrange("p (r e) -> p r e", r=12),
                     dsk[:].unsqueeze(2).to_broadcast([128, 12, 64]), T.mult, reads=['Xtm', 'dsk'], writes=['Yacc%d' % t])
            utflat = self.UT[:].rearrange("p a b -> p (a b)")

            class _V:
                def __init__(self, ap):
                    self.ap = ap

                def __getitem__(self, idx):
                    return self.ap[idx]

            def alias(i):
                return _V(utflat[:, i * 3072:(i + 1) * 3072].bitcast(F32).rearrange("p (a b) -> p a b", a=12))
            rhsV = alias(0)
            Ls = [alias(1), alias(2)]
            CBm_ = [self.sb(s2, "CBm%d" % i, [128, 2, 128]) for i in range(2)]
            M_ = [self.sb(s2, "Mdiag%d" % i, [128, 12, 128], BF16) for i in range(2)]
            xdts = [self.sb(s2, "xdt%d" % i, [128, 12, 64], BF16) for i in range(2)]
            xw_ = [self.sb(s2, "xw%d" % i, [128, 12, 64], BF16) for i in range(2)]
            tmp_ = [self.sb(s2, "ytmp%d" % i, [128, 768]) for i in range(2)]
            Hf_ = [self.sb(s2, "Hf%d" % i, [128, 768]) for i in range(2)]
            Hb_ = [self.sb(s2, "Hb%d" % i, [128, 768], BF16) for i in range(2)]
            eacs_ = [self.sb(s2, "eacs%d" % i, [128, 12]) for i in range(2)]
            cdv_ = [self.sb(s2, "cdv%d" % i, [128, 12]) for i in range(2)]
            jobs = []
            orders = [[16, 17] + list(range(16)), [17, 16] + list(range(15, -1, -1))]
            for ci in range(18):
                for d in range(2):
                    jobs.append((d, ci, orders[d][ci], ci == 17))

            def stageA1(n):
                d, ci, t, islast = jobs[n]
                xdt, xk = xdts[n % 2], 'xdt%d' % (n % 2)
                dta_t = dta[:, t, d * 12:(d + 1) * 12]
                dt_t = dtt[:, t, d * 12:(d + 1) * 12]
                P.tt(rhsV[:], vd[:, d, :].unsqueeze(1).to_broadcast([128, 12, 128]), dta_t.unsqueeze(2).to_broadcast([128, 12, 128]), T.mult,
                     reads=['vd', 'dta'], writes=['rhsV'], eng='pool')
                P.tt(xdt[:], Xtm[:, t, :].rearrange("p (r e) -> p r e", r=12), dt_t.unsqueeze(2).to_broadcast([128, 12, 64]), T.mult,
                     reads=['Xtm', 'dtt'], writes=[xk], eng='pool')

            def stageA2(n):
                d, ci, t, islast = jobs[n]
                L, lk = Ls[n % 2], 'Lseg%d' % (n % 2)
                for q in range(3):
                    bk = self.nextps()
                    P.mm(self.ps[bk][:, :], ud[:, d, :], rhsV[:, 4 * q:4 * q + 4, :].rearrange("p a b -> p (a b)"), True, True,
                         reads=['ud', 'rhsV'], writes=['ps%d' % bk])
                    P.act(L[:, 4 * q:4 * q + 4, :].rearrange("p a b -> p (a b)"), self.ps[bk][:, :], AF.Exp, reads=['ps%d' % bk], writes=[lk])

            def stageB(n):
                d, ci, t, islast = jobs[n]
                L, lk = Ls[n % 2], 'Lseg%d' % (n % 2)
                xdt, xk = xdts[n % 2], 'xdt%d' % (n % 2)
                CBm, M, xw, tmp, Hf, Hb, eacs, cdv = CBm_[d], M_[d], xw_[d], tmp_[d], Hf_[d], Hb_[d], eacs_[d], cdv_[d]
                kCB, kM, kxw, ktmp, kHf, kHb, kea, kcd = ['%s%d' % (k_, d) for k_ in ('CBm', 'Mdiag', 'xw', 'ytmp', 'Hf', 'Hb', 'eacs', 'cdv')]
                iend = 127 if d == 0 else 0
                lat = t < 16
                dta_t = dta[:, t, d * 12:(d + 1) * 12]
                tok = slice(t * 128, (t + 1) * 128)
                if lat:
                    bk = self.nextps()
                    for g in range(2):
                        P.mm(self.ps[bk][:, g * 128:(g + 1) * 128], BT[:, g, tok], CT[:, g, tok], True, True, reads=['BT', 'CT'], writes=['ps%d' % bk])
                    P.tt(CBm[:], self.ps[bk][:, 0:256].rearrange("p (g i) -> p g i", g=2), vd[:, d, :].unsqueeze(1).to_broadcast([128, 2, 128]), T.mult,
                         reads=['ps%d' % bk, 'vd'], writes=[kCB])
                    P.tt(M[:].rearrange("p (g r) i -> p g r i", g=2), L[:].rearrange("p (g r) i -> p g r i", g=2),
                         CBm[:].unsqueeze(2).to_broadcast([128, 2, 6, 128]), T.mult, reads=[lk, kCB], writes=[kM])
                    bA = self.nextps()
                    bB = self.nextps()
                    for r in range(12):
                        dst = self.ps[bA][:, r * 64:(r + 1) * 64] if r < 8 else self.ps[bB][:, (r - 8) * 64:(r - 7) * 64]
                        P.mm(dst, M[:, r, :], xdt[:, r, :], True, True, reads=[kM, xk], writes=['ps%d' % (bA if r < 8 else bB)])
                    bo = [self.nextps(), self.nextps()]
                    for g in range(2):
                        P.mm(self.ps[bo[g]][:, 0:384], CT[:, g, tok], Hb[:, g * 384:(g + 1) * 384], True, True, reads=['CT', kHb], writes=['ps%d' % bo[g]])
                    bk = self.nextps()
                    P.mm(self.ps[bk][:, 0:12], vd[:, d, :], dta_t, True, True, reads=['vd', 'dta'], writes=['ps%d' % bk])
                    P.act(eacs[:], self.ps[bk][:, 0:12], AF.Exp, reads=['ps%d' % bk], writes=[kea])
                    for g in range(2):
                        P.tt(tmp[:, g * 384:(g + 1) * 384].rearrange("p (r e) -> p r e", r=6), self.ps[bo[g]][:, 0:384].rearrange("p (r e) -> p r e", r=6),
                             eacs[:, g * 6:(g + 1) * 6].unsqueeze(2).to_broadcast([128, 6, 64]), T.mult, reads=['ps%d' % bo[g], kea], writes=[ktmp])
                    P.tt(tmp[:, 0:512], tmp[:, 0:512], self.ps[bA][:, :], T.add, reads=[ktmp, 'ps%d' % bA], writes=[ktmp])
                    P.tt(tmp[:, 512:768], tmp[:, 512:768], self.ps[bB][:, 0:256], T.add, reads=[ktmp, 'ps%d' % bB], writes=[ktmp])
                    P.tt(Yacc[:, t, :], Yacc[:, t, :], tmp[:], T.add, reads=['Yacc%d' % t, ktmp], writes=['Yacc%d' % t], eng='pool')
                if not islast:
                    P.tt(xw[:], xdt[:], L[:, :, iend].unsqueeze(2).to_broadcast([128, 12, 64]), T.mult, reads=[xk, lk], writes=[kxw])
                    bs = [self.nextps(), self.nextps()]
                    for g in range(2):
                        P.mm(self.ps[bs[g]][:, 0:384], Btm[:, t, g, :], xw[:, 6 * g:6 * g + 6, :].rearrange("p a b -> p (a b)"), True, True,
                             reads=['Btm', kxw], writes=['ps%d' % bs[g]])
                    if ci == 0:
                        for g in range(2):
                            P.copy(Hf[:, g * 384:(g + 1) * 384], self.ps[bs[g]][:, 0:384], reads=['ps%d' % bs[g]], writes=[kHf])
                    else:
                        bk = self.nextps()
                        P.mm(self.ps[bk][:, 0:12], ones[:], dta_t, True, True, reads=['ones1', 'dta'], writes=['ps%d' % bk])
                        P.act(cdv[:], self.ps[bk][:, 0:12], AF.Exp, reads=['ps%d' % bk], writes=[kcd])
                        P.tt(Hf[:].rearrange("p (r e) -> p r e", r=12), Hf[:].rearrange("p (r e) -> p r e", r=12),
                             cdv[:].unsqueeze(2).to_broadcast([128, 12, 64]), T.mult, reads=[kHf, kcd], writes=[kHf])
                        for g in range(2):
                            P.tt(Hf[:, g * 384:(g + 1) * 384], Hf[:, g * 384:(g + 1) * 384], self.ps[bs[g]][:, 0:384], T.add,
                                 reads=[kHf, 'ps%d' % bs[g]], writes=[kHf])
                    P.copy(Hb[:], Hf[:], reads=[kHf], writes=[kHb], eng='act')
            stageA1(0)
            stageA2(0)
            for n in range(len(jobs)):
                if n + 1 < len(jobs):
                    stageA1(n + 1)
                stageB(n)
                if n + 1 < len(jobs):
                    stageA2(n + 1)

    def ssd_gate(self, Yacc, ZS, OTs):
        P = self.P
        T = ALU
        with self.scope() as s2:
            ng = self.sb(s2, "sdng", [128, 768])
            P.dma('sp', ng[:], self.din['sd_ng'], writes=['sdng'])
            zt = [self.sb(s2, "ztg%d" % i, [128, 768]) for i in range(2)]
            yz = [self.sb(s2, "yz%d" % i, [128, 768]) for i in range(2)]
            sq = self.sb(s2, "gsq", [128, 768])
            sm = self.sb(s2, "gsm", [128, 2])
            for t in range(16):
                z, zk = zt[t % 2], 'ztg%d' % (t % 2)
                y, yk = yz[t % 2], 'yz%d' % (t % 2)
                P.dma('sp', z[:], ZS[t * 128:(t + 1) * 128, :], reads=['ZS_%d' % t], writes=[zk])
                P.tt(y[:], Yacc[:, t, :], z[:], T.mult, reads=['Yacc%d' % t, zk], writes=[yk])
                P.act(sq[:], y[:], AF.Square, reads=[yk], writes=['gsq', 'gsm'], accum_out=sm[:, 0:1])
                P.ts(sm[:, 0:1], sm[:, 0:1], 1.0 / 768.0, EPS, T.mult, T.add, reads=['gsm'], writes=['gsm'])
                P.pow_(sm[:, 0:1], sm[:, 0:1], self.mhalf[:, 0:1], reads=['gsm', 'mhalf'], writes=['gsm'])
                P.stt(y[:], y[:], sm[:, 0:1], ng[:], T.mult, T.mult, reads=[yk, 'gsm', 'sdng'], writes=[yk])
                for half in range(2):
                    bk = self.nextps()
                    for q in range(3):
                        c = half * 3 + q
                        P.tr(self.ps[bk][:, q * 128:(q + 1) * 128], y[:, c * 128:(c + 1) * 128], self.ident[:], reads=[yk, 'ident'], writes=['ps%d' % bk])
                    P.copy(OTs[:, half * 3:(half + 1) * 3, t * 128:(t + 1) * 128], self.ps[bk][:, 0:384].rearrange("p (q f) -> p q f", q=3),
                           reads=['ps%d' % bk], writes=['OTs'], eng=('act' if half == 0 else 'dve'))

    def declare_l1_inputs(self):
        for nm, shp in (('s_w_in', [D, 2328]), ('s_w_out', [D, D]), ('sd_cw', [128, 3, 10]), ('sd_cb', [128, 10]), ('sd_alog', [128, 24]),
                        ('sd_dtb', [128, 24]), ('sd_d', [128, 12]), ('sd_ng', [128, 768]), ('s5_lre', [128, 16]), ('s5_lim', [128, 16]),
                        ('s5_ldt', [128, 16]), ('s5_bre', [128, 8, 128]), ('s5_bim', [128, 8, 128]), ('s5_cre', [2, 128, 8, 128]),
                        ('s5_cim', [2, 128, 8, 128]), ('s5_dd', [128, 2]), ('s5_glu_w', [256, 256]), ('s5_glu_b', [128, 2]),
                        ('vd', [128, 2, 128]), ('ud', [128, 2, 128]), ('ones', [128, 512])):
            self.inp(nm, shp)

    def layer1(self, st, hsrc, dst, dkey, dbg=False, only=None):
        P = self.P
        self.mods(st, [1])
        with self.scope() as sL:
            S5O = self.sb(sL, "S5O", [128, 2, NLAT], BF16)
            self.phase_A(sL, hsrc, 1, NT)
            if only in (None, 's5'):
                self.s5_mixer(S5O)
            else:
                P.memset(S5O[:], 0.0, writes=['S5O'])
            if only == 's5':
                o = self.outp("dbg_S5O", [128, 2, NLAT], BF16)
                P.dma('sp', o, S5O[:], reads=['S5O'], writes=['dbg_S5O'])
                return
            H1 = self.scratch("H1b", [NLAT, D])
            ZS = self.scratch("ZS", [NLAT, 768])
            with self.scope() as sA:
                Yacc = self.sb(sA, "Yacc", [128, 16, 768])
                with self.scope() as sB:
                    Xtm = self.sb(sB, "Xtm", [128, NT, 768], BF16)
                    Btm = self.sb(sB, "Btm", [128, NT, 2, 128], BF16)
                    BT = self.sb(sB, "BT", [128, 2, NTOK], BF16)
                    CT = self.sb(sB, "CT", [128, 2, NTOK], BF16)
                    dtt = self.sb(sB, "dtt", [128, NT, 24])
                    dta = self.sb(sB, "dta", [128, NT, 24])
                    self.ssd_inproj(sB, Xtm, Btm, BT, CT, dtt, dta, ZS)
                    self.ssd_chunks(Xtm, Btm, BT, CT, dtt, dta, Yacc)
                with self.scope() as sC:
                    OTs = self.sb(sC, "OTs", [128, 6, NLAT], BF16)
                    self.ssd_gate(Yacc, ZS, OTs)
                    if only == 'ssd':
                        o = self.outp("dbg_OTs", [128, 6, NLAT], BF16)
                        P.dma('sp', o, OTs[:], reads=['OTs'], writes=['dbg_OTs'])
                        return
                    if dbg:
                        o = self.outp("dbg_S5O", [128, 2, NLAT], BF16)
                        P.dma('sp', o, S5O[:], reads=['S5O'], writes=['dbg_S5O'])
                        o = self.outp("dbg_OTs", [128, 6, NLAT], BF16)
                        P.dma('sp', o, OTs[:], reads=['OTs'], writes=['dbg_OTs'])

                    def lhs(j, t):
                        if j < 6:
                            return OTs[:, j, t * 128:(t + 1) * 128], 'OTs'
                        return S5O[:, j - 6, t * 128:(t + 1) * 128], 'S5O'
                    self.outproj_ln1(sC, 1, self.din['s_w_out'], hsrc, 16, H1, lhs=lhs)
        self.ffn(st, 1, 16, H1, dst, dkey, final=True)

    def build_layer1_test(self, only=None):
        with contextlib.ExitStack() as st:
            self.load_consts(st)
            self.declare_common_inputs()
            self.declare_l1_inputs()
            hin = self.inp("hin", [NTOK, D])
            self.UT = self.sb(st, "UT", [128, 8, NTOK], BF16)
            out = self.outp("out", [NLAT, D])
            self.final.remove("out")
            self.layer1(st, hin, out, 'out', dbg=True, only=only)
            if only is not None:
                self.dout.pop('out')
            self.P.emit(self.final)
        return self.nc

    def declare_common_inputs(self):
        for nm, shp in (('ln_g', [2, 2, 128, D]), ('ln_b', [2, 2, 128, D]), ('w_up', [2, D, 5632]), ('b_up', [2, 128, 44]),
                        ('conv_w', [2, 128, 3, 44]), ('conv_b', [2, 128, 44]), ('w_down', [2, 2816, D]), ('b_down', [2, 128, D]),
                        ('a_w_in', [D, 2560]), ('a_w_out', [D, D]), ('a_lam', [128, 256]), ('a_subg', [128, 128]),
                        ('f_wblk', [2, 128, 128]), ('f_b', [128, 2]), ('cos4', [128, NLAT]), ('sin4', [128, NLAT]), ('pswap', [128, 128]),
                        ('c64blk', [128, 128]), ('ns64blk', [128, 128])):
            self.inp(nm, shp)
        for nm, shp in (('dftc', [NLAT, NLAT]), ('dfts', [NLAT, NLAT]), ('dftc_c', [NCTX, NCTX]), ('dfts_c', [NCTX, NCTX])):
            self.inp(nm, shp, BF16)

    def build_layer0(self, dbg_ot=False):
        with contextlib.ExitStack() as st:
            self.load_consts(st)
            self.declare_common_inputs()
            xin = self.inp("xin", [NTOK, D])
            self.mods(st, [0])
            self.UT = self.sb(st, "UT", [128, 8, NTOK], BF16)
            H1 = self.scratch("H1", [NTOK, D])
            H2 = self.outp("H2", [NTOK, D])
            self.final.remove("H2")
            with self.scope() as s1:
                self.OT = self.sb(s1, "OT", [128, 8, NTOK], BF16)
                self.phase_A(s1, xin, 0, NT)
                self.attn_mixer()
                if dbg_ot:
                    o = self.outp("dbg_OT", [128, 8, NTOK], BF16)
                    self.P.dma('sp', o, self.OT[:], reads=['OT'], writes=['dbg_OT'])
                self.outproj_ln1(s1, 0, self.din['a_w_out'], xin, NT, H1)
            self.ffn(st, 0, NT, H1, H2, 'H2', final=True)
            self.P.emit(self.final)
        return self.nc

    def build_debug_ffn(self):
        with contextlib.ExitStack() as st:
            self.load_consts(st)
            for nm, shp in (('ln_g', [2, 2, 128, D]), ('ln_b', [2, 2, 128, D]), ('w_up', [2, D, 5632]), ('b_up', [2, 128, 44]),
                            ('conv_w', [2, 128, 3, 44]), ('conv_b', [2, 128, 44]), ('w_down', [2, 2816, D]), ('b_down', [2, 128, D]),
                            ('a_w_out', [D, D])):
                self.inp(nm, shp)
            xin = self.inp("xin", [NTOK, D])
            self.mods(st, [0])
            self.UT = self.sb(st, "UT", [128, 8, NTOK], BF16)
            H1 = self.scratch("H1", [NTOK, D])
            H2 = self.outp("H2", [NTOK, D])
            self.final.remove("H2")
            with self.scope() as s1:
                self.OT = self.sb(s1, "OT", [128, 8, NTOK], BF16)
                self.phase_A(s1, xin, 0, NT)
                o = self.outp("dbg_modc0", [128, 48, 2])
                self.P.dma('sp', o, self.modc[0][:], reads=['modc0'], writes=['dbg_modc0'])
                o = self.outp("dbg_grow0", [2, 2, 1024])
                self.P.dma('sp', o, self.grow[0], reads=['grow0'], writes=['dbg_grow0'])
                o = self.outp("dbg_UT", [128, 8, NTOK], BF16)
                self.P.dma('sp', o, self.UT[:], reads=['UT'], writes=['dbg_UT'])
                self.P.copy(self.OT[:], self.UT[:], reads=['UT'], writes=['OT'], eng='pool')
                self.outproj_ln1(s1, 0, self.din['a_w_out'], xin, NT, H1)
            self.ffn(st, 0, NT, H1, H2, 'H2', final=True)
            self.P.emit(self.final)
        return self.nc

    def build_full(self):
        with contextlib.ExitStack() as st:
            self.load_consts(st)
            self.declare_common_inputs()
            self.declare_l1_inputs()
            xin = self.inp("xin", [NTOK, D])
            self.mods(st, [0])
            self.UT = self.sb(st, "UT", [128, 8, NTOK], BF16)
            H1 = self.scratch("H1", [NTOK, D])
            H2 = self.scratch("H2", [NTOK, D])
            with self.scope() as s1:
                self.OT = self.sb(s1, "OT", [128, 8, NTOK], BF16)
                self.phase_A(s1, xin, 0, NT)
                self.attn_mixer()
                self.outproj_ln1(s1, 0, self.din['a_w_out'], xin, NT, H1)
            self.ffn(st, 0, NT, H1, H2, 'hin1')
            out = self.outp("out", [NLAT, D])
            self.final.remove("out")
            self.layer1(st, H2, out, 'out')
            self.P.emit(self.final)
        return self.nc


_CACHE = {}


def kernel(**inputs):
    inp = {k: np.asarray(v) for k, v in inputs.items()}
    if 'b' not in _CACHE:
        B = Builder()
        B.build_full()
        _CACHE['b'] = B
        _CACHE['c'] = host_consts()
    B = _CACHE['b']
    hc = _CACHE['c']
    in_maps = []
    for b in range(8):
        hl = host_layout(inp, b)
        in_maps.append({k: (hc[k] if k in hc else hl[k]) for k in B.din})
    res = run_bass_kernel_spmd(B.nc, in_maps, core_ids=list(range(8)))
    out = np.stack([np.asarray(r['out']) for r in res.results], axis=0)
    return out.astype(np.float32)
```

```python
import contextlib
import math
import numpy as np
import ml_dtypes
import concourse.bass as bass
import concourse.mybir as mybir
from concourse.bass_utils import run_bass_kernel_spmd

F32 = mybir.dt.float32
BF16 = mybir.dt.bfloat16
AF = mybir.ActivationFunctionType
ALU = mybir.AluOpType

ENGS = ['pe', 'act', 'dve', 'pool', 'sp']
POOL_TO_DVE = True
NDMASEM = 6

D = 1024
NLAT = 2048
NCTX = 256
NTOK = NLAT + NCTX
NT = NTOK // 128
ALPHA = (2 * 2) ** 0.25
EPS = 1e-5
LAM_INIT0 = 0.8 - 0.6 * math.exp(0.0)


class Prog:
    def __init__(self, nc):
        self.nc = nc
        self.ops = {e: [] for e in ENGS}
        self.last_w = {}
        self.readers = {}
        self.barriers = []
        self.bar_dma_start = {e: 0 for e in ENGS}

    def barrier(self):
        pts = []
        for e in ENGS:
            ops = self.ops[e]
            for i in range(len(ops) - 1, -1, -1):
                if not ops[i]['dma'] and ops[i]['fn'] is not None:
                    pts.append((e, i))
                    ops[i]['needed'] = True
                    break
            for i in range(self.bar_dma_start[e], len(ops)):
                if ops[i]['dma']:
                    pts.append((e, i))
            self.bar_dma_start[e] = len(ops)
        self.barriers.append(pts)

    def op(self, eng, fn, reads=(), writes=(), dma=False):
        if eng == 'pool' and not dma and POOL_TO_DVE:
            eng = 'dve'
        ops = self.ops[eng]
        idx = len(ops)
        deps = set()
        for k in reads:
            w = self.last_w.get(k)
            if w is not None:
                deps.add(w)
            if k.startswith('ps'):
                for r in self.readers.get(k, ()):
                    if r[0] != eng:
                        deps.add(r)
        for k in writes:
            w = self.last_w.get(k)
            if w is not None:
                deps.add(w)
            for r in self.readers.get(k, ()):
                deps.add(r)
        best = {}
        out = []
        for (e, i) in deps:
            d = self.ops[e][i]
            if d['dma']:
                out.append((e, i))
            else:
                if e == eng and not dma and eng == 'pe':
                    continue
                if e not in best or best[e] < i:
                    best[e] = i
        for e, i in best.items():
            out.append((e, i))
        for (e, i) in out:
            self.ops[e][i]['needed'] = True
        ops.append(dict(fn=fn, deps=out, dma=dma, needed=False, sem=None, val=None, prev=None, bar=len(self.barriers)))
        me = (eng, idx)
        for k in reads:
            lst = self.readers.setdefault(k, [])
            if not dma:
                lst[:] = [r for r in lst if not (r[0] == eng and not self.ops[r[0]][r[1]]['dma'])]
            lst.append(me)
        for k in writes:
            self.last_w[k] = me
            self.readers[k] = []
        return me

    def dma(self, eng, out, in_, reads=(), writes=(), **kw):
        return self.op(eng, lambda e: e.dma_start(out=out, in_=in_, **kw), reads, writes, dma=True)

    def act(self, out, in_, func, reads=(), writes=(), eng='act', **kw):
        return self.op(eng, lambda e: e.activation(out=out, in_=in_, func=func, **kw), reads, writes)

    def tt(self, out, in0, in1, op, reads=(), writes=(), eng='dve'):
        return self.op(eng, lambda e: e.tensor_tensor(out=out, in0=in0, in1=in1, op=op), reads, writes)

    def ts(self, out, in0, s1, s2, op0, op1=None, reads=(), writes=(), eng='dve'):
        if op1 is None:
            return self.op(eng, lambda e: e.tensor_scalar(out=out, in0=in0, scalar1=s1, scalar2=None, op0=op0), reads, writes)
        return self.op(eng, lambda e: e.tensor_scalar(out=out, in0=in0, scalar1=s1, scalar2=s2, op0=op0, op1=op1), reads, writes)

    def stt(self, out, in0, scalar, in1, op0, op1, reads=(), writes=()):
        return self.op('dve', lambda e: e.scalar_tensor_tensor(out=out, in0=in0, scalar=scalar, in1=in1, op0=op0, op1=op1), reads, writes)

    def copy(self, out, in_, reads=(), writes=(), eng='dve'):
        if eng == 'act':
            return self.op(eng, lambda e: e.activation(out=out, in_=in_, func=AF.Copy), reads, writes)
        return self.op(eng, lambda e: e.tensor_copy(out=out, in_=in_), reads, writes)

    def mm(self, out, lhsT, rhs, start, stop, reads=(), writes=()):
        return self.op('pe', lambda e: e.matmul(out, lhsT=lhsT, rhs=rhs, start=start, stop=stop), reads, writes)

    def tr(self, out, in_, ident, reads=(), writes=()):
        return self.op('pe', lambda e: e.transpose(out=out, in_=in_, identity=ident), reads, writes)

    def scan(self, out, d0, d1, initial, op0, op1, reads=(), writes=()):
        return self.op('dve', lambda e: e.tensor_tensor_scan(out=out, data0=d0, data1=d1, initial=initial, op0=op0, op1=op1), reads, writes)

    def memset(self, ap, val, writes=(), eng='dve'):
        return self.op(eng, lambda e: e.memset(ap, val), (), writes)

    def recip(self, out, in_, reads=(), writes=()):
        return self.op('dve', lambda e: e.reciprocal(out=out, in_=in_), reads, writes)

    def bnstats(self, out, in_, reads=(), writes=()):
        return self.op('dve', lambda e: e.bn_stats(out=out, in_=in_), reads, writes)

    def bnaggr(self, out, in_, reads=(), writes=()):
        return self.op('dve', lambda e: e.bn_aggr(out=out, in_=in_), reads, writes)

    def emit(self, final_keys=()):
        nc = self.nc
        self.op('sp', None, reads=list(final_keys), writes=())
        with contextlib.ExitStack() as st:
            csem = {e: st.enter_context(nc.semaphore("c_" + e)) for e in ENGS}
            dsem = {e: [st.enter_context(nc.semaphore("d_%s%d" % (e, i))) for i in range(NDMASEM)] for e in ENGS}
            for e in ENGS:
                cnt = 0
                dcnt = 0
                lastd = [None] * NDMASEM
                dval = [0] * NDMASEM
                for i, o in enumerate(self.ops[e]):
                    if o['dma']:
                        s = dcnt % NDMASEM
                        dcnt += 1
                        dval[s] += 16
                        o['sem'] = dsem[e][s]
                        o['val'] = dval[s]
                        o['prev'] = lastd[s]
                        lastd[s] = i
                    elif o['needed']:
                        cnt += 1
                        o['sem'] = csem[e]
                        o['val'] = cnt
            block = st.enter_context(nc.Block())

            def run(e, eng):
                waited = {}

                def wait(sem, val):
                    k = id(sem)
                    if waited.get(k, 0) < val:
                        eng.wait_ge(sem, val)
                        waited[k] = val
                bar_done = 0
                for o in self.ops[e]:
                    while bar_done < o['bar']:
                        for (de, di) in self.barriers[bar_done]:
                            d = self.ops[de][di]
                            wait(d['sem'], d['val'])
                        bar_done += 1
                    for (de, di) in o['deps']:
                        d = self.ops[de][di]
                        wait(d['sem'], d['val'])
                    if o['dma'] and o['prev'] is not None:
                        p = self.ops[e][o['prev']]
                        wait(p['sem'], p['val'])
                    if o['fn'] is None:
                        continue
                    ins = o['fn'](eng)
                    if o['dma']:
                        ins.then_inc(o['sem'], 16)
                    elif o['needed']:
                        ins.then_inc(o['sem'], 1)

            @block.tensor
            def _(eng):
                run('pe', eng)

            @block.scalar
            def _(eng):
                run('act', eng)

            @block.vector
            def _(eng):
                run('dve', eng)

            @block.gpsimd
            def _(eng):
                run('pool', eng)

            @block.sync
            def _(eng):
                run('sp', eng)


def host_consts():
    c = {}
    c['ident'] = np.eye(128, dtype=np.float32)
    ps = np.zeros((128, 128), np.float32)
    for m in range(128):
        partner = m + 32 if (m % 64) < 32 else m - 32
        ps[partner, m] = 1.0
    c['pswap'] = ps
    rows = NLAT // 64
    row = np.repeat(np.arange(rows, dtype=np.float32), 64)
    col = np.tile(np.arange(64, dtype=np.float32), rows)
    nf = 16
    inv = (10000.0 ** (-np.arange(nf, dtype=np.float32) / nf)).astype(np.float32)
    ang = np.concatenate([row[:, None] * inv, col[:, None] * inv], axis=-1).astype(np.float32)
    cs, sn = np.cos(ang).astype(np.float32), np.sin(ang).astype(np.float32)
    cos4 = np.zeros((128, NLAT), np.float32)
    sin4 = np.zeros((128, NLAT), np.float32)
    for p in range(128):
        cos4[p] = cs[:, p % 32]
        sin4[p] = sn[:, p % 32] * (-1.0 if (p % 64) < 32 else 1.0)
    c['cos4'] = cos4
    c['sin4'] = sin4

    def dft(n):
        k = np.arange(n, dtype=np.int64)
        kk = (k[:, None] * k[None, :]) % n
        a = 2.0 * np.pi * kk.astype(np.float64) / n
        return np.cos(a) / np.sqrt(n), np.sin(a) / np.sqrt(n)
    C, S = dft(NLAT)
    c['dftc'] = C.astype(ml_dtypes.bfloat16)
    c['dfts'] = S.astype(ml_dtypes.bfloat16)
    C, S = dft(NCTX)
    c['dftc_c'] = C.astype(ml_dtypes.bfloat16)
    c['dfts_c'] = S.astype(ml_dtypes.bfloat16)
    C, S = dft(64)
    cb = np.zeros((128, 128), np.float32)
    sb = np.zeros((128, 128), np.float32)
    for g in range(2):
        cb[g * 64:(g + 1) * 64, g * 64:(g + 1) * 64] = C
        sb[g * 64:(g + 1) * 64, g * 64:(g + 1) * 64] = -S
    c['c64blk'] = cb
    c['ns64blk'] = sb
    sel = np.zeros((2, 2, 128), np.float32)
    sel[0, 0, :] = 1.0
    sel[1, 1, :] = 1.0
    c['sel'] = sel
    k = np.arange(128)
    vd = np.zeros((128, 2, 128), np.float32)
    ud = np.zeros((128, 2, 128), np.float32)
    vd[:, 0, :] = (k[:, None] <= k[None, :])
    vd[:, 1, :] = (k[:, None] >= k[None, :])
    ud[:, 0, :] = (k[:, None] > k[None, :])
    ud[:, 1, :] = (k[:, None] < k[None, :])
    c['vd'] = vd
    c['ud'] = ud
    c['ones'] = np.ones((128, 512), np.float32)
    return c


def colvec(v, n):
    return np.ascontiguousarray(np.asarray(v, np.float32).reshape(n, 128).T)


def bcast(v, p=128):
    v = np.asarray(v, np.float32).reshape(1, -1)
    return np.ascontiguousarray(np.broadcast_to(v, (p, v.shape[1])))


def host_layout(inp, b):
    m = {}
    m['xin'] = np.ascontiguousarray(np.concatenate([inp['x'][b], inp['ctx'][b]], axis=0))
    cv = np.stack([colvec(inp['c'][b], 8), colvec(inp['c_ctx'], 8)], axis=-1)
    m['cvec'] = np.ascontiguousarray(cv)
    m['ada_w'] = inp['ada_w']
    m['ada_bc'] = np.ascontiguousarray(np.stack([colvec(inp['ada_b'][l], 48) for l in range(2)]))
    m['ada_b2'] = np.ascontiguousarray(np.stack([np.stack([inp['ada_b'][l]] * 2) for l in range(2)]))
    m['ln_g'] = np.ascontiguousarray(np.stack([np.stack([bcast(inp['ln_g'][l][i]) for i in range(2)]) for l in range(2)]))
    m['ln_b'] = np.ascontiguousarray(np.stack([np.stack([bcast(inp['ln_b'][l][i]) for i in range(2)]) for l in range(2)]))
    m['w_up'] = inp['ffn_w_up']
    m['b_up'] = np.ascontiguousarray(np.stack([colvec(inp['ffn_b_up'][l], 44) for l in range(2)]))
    m['conv_w'] = np.ascontiguousarray(np.stack([np.stack([colvec(inp['ffn_conv_w'][l][k], 44) for k in range(3)], axis=1) for l in range(2)]))
    m['conv_b'] = np.ascontiguousarray(np.stack([colvec(inp['ffn_conv_b'][l], 44) for l in range(2)]))
    m['w_down'] = inp['ffn_w_down']
    m['b_down'] = np.ascontiguousarray(np.stack([bcast(inp['ffn_b_down'][l]) for l in range(2)]))
    m['a_w_in'] = inp['attn_w_in'][0]
    m['a_w_out'] = inp['attn_w_out'][0]
    m['a_lam'] = bcast(inp['attn_lambda'][0].reshape(-1))
    m['a_subg'] = bcast(inp['attn_subln_g'][0])
    fw = inp['fourier_w'][0]
    wb = np.zeros((2, 128, 128), np.float32)
    for ch in range(2):
        for g in range(2):
            wb[ch, g * 64:(g + 1) * 64, g * 64:(g + 1) * 64] = fw[ch * 2 + g]
    m['f_wblk'] = wb
    m['f_b'] = colvec(inp['fourier_b'][0], 2)
    m['s_w_in'] = inp['ssm_w_in'][0]
    m['s_w_out'] = inp['ssm_w_out'][0]
    m['sd_cw'] = np.ascontiguousarray(np.stack([colvec(inp['ssd_conv_w'][0][k], 10) for k in range(3)], axis=1))
    m['sd_cb'] = colvec(inp['ssd_conv_b'][0], 10)
    m['sd_alog'] = bcast(inp['ssd_a_log'][0].reshape(-1))
    m['sd_dtb'] = bcast(inp['ssd_dt_bias'][0].reshape(-1))
    m['sd_d'] = bcast(inp['ssd_d'][0])
    m['sd_ng'] = bcast(inp['ssd_norm_g'][0])

    def dsc(a):
        out = np.zeros((128, 16), np.float32)
        for d in range(2):
            for sc in range(8):
                for half in range(2):
                    out[half * 64:(half + 1) * 64, d * 8 + sc] = a[d, 2 * sc + half, :]
        return out
    m['s5_lre'] = dsc(inp['s5_lambda_re'][0])
    m['s5_lim'] = dsc(inp['s5_lambda_im'][0])
    m['s5_ldt'] = dsc(np.broadcast_to(inp['s5_log_dt'][0][:, :, None], (2, 16, 64)))

    def bexp(b):
        out = np.zeros((128, 8, 128), np.float32)
        for sc in range(8):
            for half in range(2):
                g = 2 * sc + half
                col = (g % 8) * 16
                out[half * 64:(half + 1) * 64, sc, col:col + 16] = b[g]
        return out
    m['s5_bre'] = bexp(inp['s5_b_re'][0])
    m['s5_bim'] = bexp(inp['s5_b_im'][0])

    def cexp(c):
        out = np.zeros((2, 128, 8, 128), np.float32)
        for d in range(2):
            for sc in range(8):
                for half in range(2):
                    g = 2 * sc + half
                    col = (g % 8) * 16
                    out[d, half * 64:(half + 1) * 64, sc, col:col + 16] = c[d, g].T
        return out
    m['s5_cre'] = cexp(inp['s5_c_re'][0])
    m['s5_cim'] = cexp(inp['s5_c_im'][0])
    m['s5_dd'] = colvec(inp['s5_d'][0], 2)
    m['s5_glu_w'] = inp['s5_glu_w'][0]
    m['s5_glu_b'] = colvec(inp['s5_glu_b'][0], 2)
    return m


class Builder:
    def __init__(self, debug=()):
        self.debug = set(debug)
        self.nc = bass.Bass("TRN2", target_bir_lowering=False)
        self.P = Prog(self.nc)
        self.din = {}
        self.dout = {}
        self.final = []
        self.uid = 0
        self.ps = [self.nc.alloc_psum_tensor("ps%d" % i, [128, 512], F32) for i in range(8)]
        self.psrr = 0

    def inp(self, name, shape, dt=F32):
        self.din[name] = self.nc.dram_tensor(name, list(shape), dt, kind="ExternalInput").ap()
        return self.din[name]

    def outp(self, name, shape, dt=F32):
        self.dout[name] = self.nc.dram_tensor(name, list(shape), dt, kind="ExternalOutput").ap()
        self.final.append(name)
        return self.dout[name]

    def scratch(self, name, shape, dt=F32):
        return self.nc.dram_tensor(name, list(shape), dt, kind="Internal").ap()

    def sb(self, st, name, shape, dt=F32):
        self.uid += 1
        return st.enter_context(self.nc.sbuf_tensor("s%d_%s" % (self.uid, name), list(shape), dt))

    @contextlib.contextmanager
    def scope(self):
        with contextlib.ExitStack() as s2:
            yield s2
        self.P.barrier()

    def nextps(self, lo=0, hi=8):
        n = hi - lo
        i = lo + (self.psrr % n)
        self.psrr += 1
        return i

    def dbg(self, name, tile_ap, key, shape, dt=F32):
        if name in self.debug:
            o = self.outp("dbg_" + name, shape, dt)
            self.P.dma('sp', o, tile_ap, reads=[key], writes=["dbg_" + name])

    def load_consts(self, st):
        P = self.P
        self.ident = self.sb(st, "ident", [128, 128])
        P.dma('sp', self.ident[:], self.inp("ident", [128, 128]), writes=['ident'])
        self.sel = self.sb(st, "sel", [2, 2, 128])
        P.dma('sp', self.sel[:], self.inp("sel", [2, 2, 128]), writes=['sel'])

    def mods(self, st, layers):
        P, nc = self.P, self.nc
        if 'cvec' not in self.din:
            self.inp("cvec", [128, 8, 2])
            self.inp("ada_w", [2, D, 6 * D])
            self.inp("ada_bc", [2, 128, 48])
            self.inp("ada_b2", [2, 2, 6 * D])
            self.modc = [self.sb(st, "modc%d" % l, [128, 48, 2]) for l in range(2)]
            g = self.sb(st, "grow", [2, 2, 1024])
            self.grow = [g[:], g[:]]
            for l in range(2):
                P.memset(self.modc[l][:], 0.0, writes=['modc%d' % l])
        cvec, ada_w, ada_bc, ada_b2 = self.din['cvec'], self.din['ada_w'], self.din['ada_bc'], self.din['ada_b2']
        with self.scope() as s2:
            cv = self.sb(s2, "cv", [128, 8, 2])
            sT = self.sb(s2, "sT", [128, 8, 2], BF16)
            abc = self.sb(s2, "abc", [128, 2, 48])
            ab2 = self.sb(s2, "ab2", [2, 2, 2, 1024])
            P.dma('sp', cv[:], cvec, writes=['cv'])
            P.dma('sp', abc[:], ada_bc.rearrange("l p f -> p l f"), writes=['abc'])
            for l in layers:
                for gi, m in enumerate((2, 5)):
                    P.dma('sp', ab2[:, l, gi, :], ada_b2[l, :, m * 1024:(m + 1) * 1024], writes=['ab2'])
            P.act(sT[:], cv[:], AF.Silu, reads=['cv'], writes=['sT'])
            NWB = 4
            wbuf = [self.sb(s2, "adaw%d" % i, [128, 8, 1024], BF16) for i in range(NWB)]
            n = 0
            for l in layers:
                for m in range(6):
                    wb = wbuf[n % NWB]
                    wk = 'adaw%d' % (n % NWB)
                    n += 1
                    P.dma('pool', wb[:], ada_w[l, :, m * 1024:(m + 1) * 1024].rearrange("(kc p) f -> p kc f", p=128), writes=[wk])
                    if m in (0, 1, 3, 4):
                        bk = self.nextps()
                        pst = self.ps[bk]
                        for j in range(8):
                            for kc in range(8):
                                P.mm(pst[:, 2 * j:2 * j + 2], wb[:, kc, j * 128:(j + 1) * 128], sT[:, kc, :], kc == 0, kc == 7,
                                     reads=[wk, 'sT'], writes=['ps%d' % bk])
                        P.tt(self.modc[l][:, m * 8:(m + 1) * 8, :], pst[:, 0:16].rearrange("p (j w) -> p j w", w=2),
                             abc[:, l, m * 8:(m + 1) * 8].unsqueeze(2).to_broadcast([128, 8, 2]), ALU.add,
                             reads=['ps%d' % bk, 'abc'], writes=['modc%d' % l])
                        if m in (1, 4):
                            P.ts(self.modc[l][:, m * 8:(m + 1) * 8, :], self.modc[l][:, m * 8:(m + 1) * 8, :], 1.0, None, ALU.add,
                                 reads=['modc%d' % l], writes=['modc%d' % l])
                    else:
                        gi = 0 if m == 2 else 1
                        for half in range(2):
                            bk = self.nextps()
                            pst = self.ps[bk]
                            for kc in range(8):
                                P.mm(pst[0:2, :], sT[:, kc, :], wb[:, kc, half * 512:(half + 1) * 512], kc == 0, kc == 7,
                                     reads=[wk, 'sT'], writes=['ps%d' % bk])
                            P.tt(self.grow[l][:, gi, half * 512:(half + 1) * 512], pst[0:2, :], ab2[:, l, gi, half * 512:(half + 1) * 512], ALU.add,
                                 reads=['ps%d' % bk, 'ab2'], writes=['grow'])

    def gate_bcast(self, dst, dkey, l, gi, w):
        P = self.P
        for half in range(2):
            bk = self.nextps()
            P.mm(self.ps[bk][:, :], self.sel[:, w, :], self.grow[l][:, gi, half * 512:(half + 1) * 512], True, True,
                 reads=['sel', 'grow'], writes=['ps%d' % bk])
            P.copy(dst[:, half * 512:(half + 1) * 512], self.ps[bk][:, :], reads=['ps%d' % bk], writes=[dkey], eng='act')

    def transpose_mod(self, src_tile, skey, UT, ukey, t, l, m_shift, m_scale):
        P = self.P
        w = 0 if t < 16 else 1
        for half in range(2):
            bk = self.nextps()
            for q in range(4):
                j = half * 4 + q
                P.tr(self.ps[bk][:, q * 128:(q + 1) * 128], src_tile[:, j * 128:(j + 1) * 128], self.ident[:],
                     reads=[skey, 'ident'], writes=['ps%d' % bk])
            for q in range(4):
                j = half * 4 + q
                P.act(UT[:, j, t * 128:(t + 1) * 128], self.ps[bk][:, q * 128:(q + 1) * 128], AF.Identity,
                      reads=['ps%d' % bk, 'modc%d' % l], writes=[ukey],
                      scale=self.modc[l][:, m_scale * 8 + j, w:w + 1], bias=self.modc[l][:, m_shift * 8 + j, w:w + 1])

    def phase_A(self, st, src, l, ntiles):
        P = self.P
        UT = self.UT
        with self.scope() as s2:
            hb = [self.sb(s2, "hA%d" % i, [128, D]) for i in range(2)]
            for t in range(ntiles):
                h = hb[t % 2]
                hk = 'hA%d' % (t % 2)
                P.dma('sp', h[:], src[t * 128:(t + 1) * 128, :], reads=['hin%d_%d' % (l, t)], writes=[hk])
                self.transpose_mod(h, hk, UT, 'UT', t, l, 0, 1)

    def layer_norm(self, r, rkey, g, b, gbkey, out, okey, s2tmp):
        P = self.P
        st6, mv, rstd, nmr, tk = s2tmp
        for c in range(2):
            P.bnstats(st6[:, c, :], r[:, c * 512:(c + 1) * 512], reads=[rkey], writes=['lnst' + tk])
        P.bnaggr(mv[:], st6[:].rearrange("p a b -> p (a b)"), reads=['lnst' + tk], writes=['lnmv' + tk])
        P.ts(rstd[:], mv[:, 1:2], EPS, None, ALU.add, reads=['lnmv' + tk], writes=['lnrstd' + tk])
        P.act(rstd[:], rstd[:], AF.Sqrt, reads=['lnrstd' + tk], writes=['lnrstd' + tk])
        P.recip(rstd[:], rstd[:], reads=['lnrstd' + tk], writes=['lnrstd' + tk])
        P.stt(nmr[:], mv[:, 0:1], -1.0, rstd[:], ALU.mult, ALU.mult, reads=['lnmv' + tk, 'lnrstd' + tk], writes=['lnnmr' + tk])
        P.act(out[:], r[:], AF.Identity, reads=[rkey, 'lnrstd' + tk, 'lnnmr' + tk], writes=[okey], scale=rstd[:, 0:1], bias=nmr[:, 0:1])
        P.tt(out[:], out[:], g, ALU.mult, reads=[okey, gbkey], writes=[okey], eng='pool')
        P.tt(out[:], out[:], b, ALU.add, reads=[okey, gbkey], writes=[okey], eng='pool')

    def outproj_ln1(self, st, l, w_out_d, hsrc, ntiles, H1, lhs=None):
        P = self.P
        OT, UT = getattr(self, 'OT', None), self.UT
        with self.scope() as s2:
            wo = self.sb(s2, "wo", [128, 8, D], BF16)
            P.dma('pool', wo[:], w_out_d.rearrange("(kc p) f -> p kc f", p=128), writes=['wo'])
            lng = self.sb(s2, "lng", [128, D])
            lnb = self.sb(s2, "lnb", [128, D])
            P.dma('sp', lng[:], self.din['ln_g'][l, 0], writes=['lngb'])
            P.dma('sp', lnb[:], self.din['ln_b'][l, 0], writes=['lngb'])
            gts = []
            for w in range(2 if ntiles > 16 else 1):
                gt = self.sb(s2, "gmix%d" % w, [128, D])
                self.gate_bcast(gt, 'gmix%d' % w, l, 0, w)
                gts.append(gt)
            hb = [self.sb(s2, "hB%d" % i, [128, D]) for i in range(2)]
            rb = [self.sb(s2, "rB%d" % i, [128, D]) for i in range(2)]
            ob = [self.sb(s2, "oB%d" % i, [128, D]) for i in range(2)]
            tmps = [(self.sb(s2, "lnst6", [128, 2, 6]), self.sb(s2, "lnmv", [128, 2]), self.sb(s2, "lnrstd", [128, 1]), self.sb(s2, "lnnmr", [128, 1]), 'A%d' % i) for i in range(2)]
            mmb = {}

            def mm_pe(t):
                h, hk = hb[t % 2], 'hB%d' % (t % 2)
                P.dma('sp', h[:], hsrc[t * 128:(t + 1) * 128, :], reads=['hin%d_%d' % (l, t)], writes=[hk])
                mmb[t] = []
                for half in range(2):
                    bk = self.nextps()
                    mmb[t].append(bk)
                    for j in range(8):
                        la, lk = lhs(j, t) if lhs is not None else (OT[:, j, t * 128:(t + 1) * 128], 'OT')
                        P.mm(self.ps[bk][:, :], la, wo[:, j, half * 512:(half + 1) * 512], j == 0, j == 7,
                             reads=[lk, 'wo'], writes=['ps%d' % bk])

            def mm_dve(t):
                w = 0 if t < 16 else 1
                h, hk = hb[t % 2], 'hB%d' % (t % 2)
                r, rk = rb[t % 2], 'rB%d' % (t % 2)
                for half in range(2):
                    bk = mmb[t][half]
                    P.tt(r[:, half * 512:(half + 1) * 512], self.ps[bk][:, :], gts[w][:, half * 512:(half + 1) * 512], ALU.mult,
                         reads=['ps%d' % bk, 'gmix%d' % w], writes=[rk])
                P.stt(r[:], h[:], ALPHA, r[:], ALU.mult, ALU.add, reads=[hk, rk], writes=[rk])

            def ln_tile(t):
                r, rk = rb[t % 2], 'rB%d' % (t % 2)
                o, ok = ob[t % 2], 'oB%d' % (t % 2)
                self.layer_norm(r, rk, lng[:], lnb[:], 'lngb', o, ok, tmps[t % 2])
                P.dma('sp', H1[t * 128:(t + 1) * 128, :], o[:], reads=[ok], writes=['H1_%d_%d' % (l, t)])
                if t == 0:
                    self.dbg('h1_%d' % l, o[:], ok, [128, D])
                self.transpose_mod(o, ok, UT, 'UT', t, l, 3, 4)
            mm_pe(0)
            mm_dve(0)
            for t in range(ntiles):
                if t + 1 < ntiles:
                    mm_pe(t + 1)
                ln_tile(t)
                if t + 1 < ntiles:
                    mm_dve(t + 1)

    def ffn(self, st, l, ntiles, H1, dst, dkey, final=False):
        P = self.P
        UT = self.UT
        w_up = self.din['w_up'][l].rearrange("(kc p) f -> p kc f", p=128)
        w_dn = self.din['w_down'][l].rearrange("(j p) f -> p j f", p=128)
        if ntiles > 16:
            sbs = [[(0, 6, False, True)], [(6, 6, True, True)], [(12, 4, True, False), (16, 2, False, False)]]
        else:
            sbs = [[(0, 6, False, True)], [(6, 6, True, True)], [(12, 4, True, False)]]
        with self.scope() as s2:
            wd = self.sb(s2, "wd", [128, 22, D], BF16)
            for q in range(2):
                P.dma('pool', wd[:, q * 11:(q + 1) * 11, :], w_dn[:, q * 11:(q + 1) * 11, :], writes=['wd'])
            bup = self.sb(s2, "bup", [128, 44])
            cw = self.sb(s2, "cw", [128, 3, 44])
            cb = self.sb(s2, "cb", [128, 44])
            P.dma('sp', bup[:], self.din['b_up'][l], writes=['ffc'])
            P.dma('sp', cw[:], self.din['conv_w'][l], writes=['ffc'])
            P.dma('sp', cb[:], self.din['conv_b'][l], writes=['ffc'])
            lng = self.sb(s2, "lng2", [128, D])
            lnb = self.sb(s2, "lnb2", [128, D])
            bdn = self.sb(s2, "bdn", [128, D])
            P.dma('sp', lng[:], self.din['ln_g'][l, 1], writes=['lngb2'])
            P.dma('sp', lnb[:], self.din['ln_b'][l, 1], writes=['lngb2'])
            P.dma('sp', bdn[:], self.din['b_down'][l], writes=['bdn'])
            gts = []
            for w in range(2 if ntiles > 16 else 1):
                gt = self.sb(s2, "gffn%d" % w, [128, D])
                self.gate_bcast(gt, 'gffn%d' % w, l, 1, w)
                gts.append(gt)
            W = 774 + 2
            GT = self.sb(s2, "GT", [128, 22, W], BF16)
            xs = [self.sb(s2, "xs%d" % i, [128, W], BF16) for i in range(4)]
            dg = [self.sb(s2, "dg%d" % i, [128, 6, 128], BF16) for i in range(2)]
            sgb = [self.sb(s2, "sg%d" % i, [128, 512]) for i in range(2)]
            idb = self.sb(s2, "identb", [128, 128], BF16)
            P.copy(idb[:], self.ident[:], reads=['ident'], writes=['identb'])
            wub = [self.sb(s2, "wu%d" % i, [128, 8, 256], BF16) for i in range(3)]
            hb = [self.sb(s2, "hF%d" % i, [128, D]) for i in range(2)]
            rb = [self.sb(s2, "rF%d" % i, [128, D]) for i in range(2)]
            ob = [self.sb(s2, "oF%d" % i, [128, D]) for i in range(2)]
            tmps = [(self.sb(s2, "lnst6b", [128, 2, 6]), self.sb(s2, "lnmvb", [128, 2]), self.sb(s2, "lnrstdb", [128, 1]), self.sb(s2, "lnnmrb", [128, 1]), 'B%d' % i) for i in range(2)]
            for i in range(4):
                P.memset(xs[i][:], 0.0, writes=['xs%d' % i], eng='pool')
            nw = 0
            nsg = 0
            njobs = len(sbs) * 22

            def load_w(n):
                j = n % 22
                wu, wuk = wub[n % 3], 'wu%d' % (n % 3)
                P.dma('pool', wu[:, :, 0:128], w_up[:, :, j * 128:(j + 1) * 128], writes=[wuk])
                P.dma('pool', wu[:, :, 128:256], w_up[:, :, 2816 + j * 128:2816 + (j + 1) * 128], writes=[wuk])
            load_w(0)
            load_w(1)
            for sbi, segs in enumerate(sbs):
                cols = []
                c0 = 0
                for (t0, nt_, hl, hr) in segs:
                    cols.append(c0)
                    c0 += nt_ * 128 + 2
                width = c0
                if sbi > 0:
                    for i in range(4):
                        P.memset(xs[i][:, 0:width], 0.0, writes=['xs%d' % i], eng='pool')
                ogroups = []
                for si, (t0, nt_, hl, hr) in enumerate(segs):
                    n = nt_ * 128
                    off = 0
                    while off < n:
                        g = min(512, n - off)
                        ogroups.append((cols[si] + 1 + off, g))
                        off += g

                def conv_job(nwj, j):
                    nonlocal nsg
                    d_, dk_ = dg[nwj % 2], 'dg%d' % (nwj % 2)
                    for (cpos, g) in ogroups:
                        pb = []
                        for vg in (1, 0):
                            xi = (nwj % 2) * 2 + vg
                            x, xk = xs[xi], 'xs%d' % xi
                            bk = self.nextps()
                            pb.append(bk)
                            for tap in range(3):
                                P.mm(self.ps[bk][:, 0:g], d_[:, vg * 3 + tap, :], x[:, cpos + tap - 1:cpos + tap - 1 + g], tap == 0, tap == 2,
                                     reads=[dk_, xk], writes=['ps%d' % bk])
                        sg, sgk = sgb[nsg % 2], 'sg%d' % (nsg % 2)
                        nsg += 1
                        P.act(sg[:, 0:g], self.ps[pb[0]][:, 0:g], AF.Silu, reads=['ps%d' % pb[0], 'ffc'], writes=[sgk], bias=cb[:, 22 + j:22 + j + 1])
                        P.stt(GT[:, j, cpos:cpos + g], self.ps[pb[1]][:, 0:g], cb[:, j:j + 1], sg[:, 0:g], ALU.add, ALU.mult,
                              reads=['ps%d' % pb[1], 'ffc', sgk], writes=['GT'])
                pend = None
                for j in range(22):
                    wu, wuk = wub[nw % 3], 'wu%d' % (nw % 3)
                    if nw + 2 < njobs:
                        load_w(nw + 2)
                    d_, dk_ = dg[nw % 2], 'dg%d' % (nw % 2)
                    for vg in range(2):
                        ch = vg * 22 + j
                        for tap in range(3):
                            P.act(d_[:, vg * 3 + tap, :], idb[:], AF.Identity, reads=['identb', 'ffc'], writes=[dk_], scale=cw[:, tap, ch:ch + 1], bias=0.0)
                    for vg in range(2):
                        xi = (nw % 2) * 2 + vg
                        x, xk = xs[xi], 'xs%d' % xi
                        ch = vg * 22 + j
                        for si, (t0, nt_, hl, hr) in enumerate(segs):
                            base = cols[si]
                            tok0 = t0 * 128
                            groups = []
                            n = nt_ * 128
                            off = 0
                            while off < n:
                                g = min(512, n - off)
                                groups.append((tok0 + off, g, base + 1 + off))
                                off += g
                            if hl:
                                groups.append((tok0 - 1, 1, base))
                            if hr:
                                groups.append((tok0 + n, 1, base + 1 + n))
                            for (tk, g, cpos) in groups:
                                bk = self.nextps()
                                for kc in range(8):
                                    P.mm(self.ps[bk][:, 0:g], wu[:, kc, vg * 128:(vg + 1) * 128], UT[:, kc, tk:tk + g], kc == 0, kc == 7,
                                         reads=[wuk, 'UT'], writes=['ps%d' % bk])
                                P.act(x[:, cpos:cpos + g], self.ps[bk][:, 0:g], AF.Identity, reads=['ps%d' % bk, 'ffc'], writes=[xk],
                                      bias=bup[:, ch:ch + 1])
                    if pend is not None:
                        conv_job(*pend)
                    pend = (nw, j)
                    nw += 1
                conv_job(*pend)
                for si, (t0, nt_, hl, hr) in enumerate(segs):
                    base = cols[si]
                    for ti in range(nt_):
                        t = t0 + ti
                        w = 0 if t < 16 else 1
                        h, hk = hb[t % 2], 'hF%d' % (t % 2)
                        r, rk = rb[t % 2], 'rF%d' % (t % 2)
                        o, ok = ob[t % 2], 'oF%d' % (t % 2)
                        P.dma('sp', h[:], H1[t * 128:(t + 1) * 128, :], reads=['H1_%d_%d' % (l, t)], writes=[hk])
                        c1 = base + 1 + ti * 128
                        for half in range(2):
                            bk = self.nextps()
                            for j in range(22):
                                P.mm(self.ps[bk][:, :], GT[:, j, c1:c1 + 128], wd[:, j, half * 512:(half + 1) * 512], j == 0, j == 21,
                                     reads=['GT', 'wd'], writes=['ps%d' % bk])
                            P.tt(r[:, half * 512:(half + 1) * 512], self.ps[bk][:, :], bdn[:, half * 512:(half + 1) * 512], ALU.add,
                                 reads=['ps%d' % bk, 'bdn'], writes=[rk])
                        P.tt(r[:], r[:], gts[w][:], ALU.mult, reads=[rk, 'gffn%d' % w], writes=[rk], eng='pool')
                        P.stt(r[:], h[:], ALPHA, r[:], ALU.mult, ALU.add, reads=[hk, rk], writes=[rk])
                        self.layer_norm(r, rk, lng[:], lnb[:], 'lngb2', o, ok, tmps[t % 2])
                        P.dma('sp', dst[t * 128:(t + 1) * 128, :], o[:], reads=[ok], writes=['%s_%d' % (dkey, t)])
                        if final:
                            self.final.append('%s_%d' % (dkey, t))

    def attn_mixer(self, ntiles=NT):
        P = self.P
        UT, OT = self.UT, self.OT
        w_in = self.din['a_w_in'].rearrange("(kc p) f -> p kc f", p=128)
        blocks = [(0, 512), (512, 512), (1024, 512), (1536, 512), (2048, 256)]
        with self.scope() as s1:
            VA = self.sb(s1, "VA", [128, NT, 6, 130], BF16)
            FT = self.sb(s1, "FT", [128, 2, NTOK], BF16)
            P.memset(VA[:, :, :, 128:130], 1.0, writes=['VA'], eng='pool')
            with self.scope() as s2:
                wv = self.sb(s2, "wv", [128, 8, 1024], BF16)
                for q in range(2):
                    P.dma('pool', wv[:, :, q * 512:(q + 1) * 512], w_in[:, :, 1536 + q * 512:1536 + (q + 1) * 512], writes=['wv'])
                for t in range(NT):
                    for half in range(2):
                        bk = self.nextps()
                        for kc in range(8):
                            P.mm(self.ps[bk][:, 0:384], UT[:, kc, t * 128:(t + 1) * 128], wv[:, kc, half * 384:(half + 1) * 384], kc == 0, kc == 7,
                                 reads=['UT', 'wv'], writes=['ps%d' % bk])
                        P.copy(VA[:, t, half * 3:(half + 1) * 3, 0:128], self.ps[bk][:, 0:384].rearrange("p (a b) -> p a b", a=3),
                               reads=['ps%d' % bk], writes=['VA'], eng=('act' if half == 0 else 'dve'))
                for c in range(2):
                    for (t0, n) in blocks:
                        bk = self.nextps()
                        for kc in range(8):
                            P.mm(self.ps[bk][:, 0:n], wv[:, kc, 768 + c * 128:768 + (c + 1) * 128], UT[:, kc, t0:t0 + n], kc == 0, kc == 7,
                                 reads=['UT', 'wv'], writes=['ps%d' % bk])
                        P.copy(FT[:, c, t0:t0 + n], self.ps[bk][:, 0:n], reads=['ps%d' % bk], writes=['FT'], eng='act')
            with self.scope() as s2:
                cos4 = self.sb(s2, "cos4", [128, NLAT])
                sin4 = self.sb(s2, "sin4", [128, NLAT])
                psw = self.sb(s2, "pswap", [128, 128])
                P.dma('sp', cos4[:], self.din['cos4'], writes=['cos4'])
                P.dma('sp', sin4[:], self.din['sin4'], writes=['sin4'])
                P.dma('sp', psw[:], self.din['pswap'], writes=['pswap'])
                lamt = self.sb(s2, "lamt", [128, 256])
                gsub = self.sb(s2, "gsub", [128, 128])
                P.dma('sp', lamt[:], self.din['a_lam'], writes=['lamt'])
                P.dma('sp', gsub[:], self.din['a_subg'], writes=['gsub'])
                P.ts(gsub[:], gsub[:], 1.0 - LAM_INIT0, None, ALU.mult, reads=['gsub'], writes=['gsub'])
                lprod = self.sb(s2, "lprod", [128, 128])
                lsum = self.sb(s2, "lsum", [128, 2])
                nlam = self.sb(s2, "nlam", [128, 1])
                P.tt(lprod[:, 0:64], lamt[:, 0:64], lamt[:, 64:128], ALU.mult, reads=['lamt'], writes=['lprod'])
                P.tt(lprod[:, 64:128], lamt[:, 128:192], lamt[:, 192:256], ALU.mult, reads=['lamt'], writes=['lprod'])
                P.op('dve', lambda e: e.tensor_reduce(out=lsum[:], in_=lprod[:].rearrange("p (a b) -> p a b", a=2), axis=mybir.AxisListType.X, op=ALU.add),
                     reads=['lprod'], writes=['lsum'])
                P.act(lsum[:], lsum[:], AF.Exp, reads=['lsum'], writes=['lsum'])
                P.tt(nlam[:], lsum[:, 1:2], lsum[:, 0:1], ALU.subtract, reads=['lsum'], writes=['nlam'])
                P.ts(nlam[:], nlam[:], -LAM_INIT0, None, ALU.add, reads=['nlam'], writes=['nlam'])
                wqk = [self.sb(s2, "wqk%d" % i, [128, 8, 256], BF16) for i in range(2)]
                QT = [self.sb(s2, "QT%d" % i, [128, NTOK], BF16) for i in range(2)]
                KT = [self.sb(s2, "KT%d" % i, [128, 2, NTOK], BF16) for i in range(2)]
                for i in range(2):
                    P.memset(KT[i][:], 0.0, writes=['KT%d' % i], eng='pool')
                PT = [self.sb(s2, "PT%d" % i, [128, 512], BF16) for i in range(4)]
                qtmp = [self.sb(s2, "qtmp%d" % i, [128, 512]) for i in range(2)]
                ropA = [self.sb(s2, "ropA%d" % i, [128, 512]) for i in range(2)]
                otmp = self.sb(s2, "otmp", [128, 4, 128])
                ofin = [self.sb(s2, "ofin%d" % i, [128, 128]) for i in range(4)]
                osq = self.sb(s2, "osq", [128, 128])
                sm = self.sb(s2, "sm", [128, 4, 4])
                nrope = 0
                npt = 0

                def load_wqk(h):
                    wq, wqkey = wqk[h % 2], 'wqk%d' % (h % 2)
                    P.dma('pool', wq[:, :, 0:128], w_in[:, :, h * 128:(h + 1) * 128], writes=[wqkey])
                    P.dma('pool', wq[:, :, 128:256], w_in[:, :, 768 + h * 128:768 + (h + 1) * 128], writes=[wqkey])
                def proj(h):
                    nonlocal nrope
                    wq, wqkey = wqk[h % 2], 'wqk%d' % (h % 2)
                    dsts = ((QT[h % 2], 'QT%d' % (h % 2)), (KT[h % 2], 'KT%d' % (h % 2)))
                    for qk in range(2):
                        dst, dk = dsts[qk]
                        for (t0, n) in blocks:
                            bk = self.nextps(4, 8)
                            for kc in range(8):
                                P.mm(self.ps[bk][:, 0:n], wq[:, kc, qk * 128:(qk + 1) * 128], UT[:, kc, t0:t0 + n], kc == 0, kc == 7,
                                     reads=[wqkey, 'UT'], writes=['ps%d' % bk])
                            if t0 < NLAT:
                                qt_, qtk = qtmp[nrope % 2], 'qtmp%d' % (nrope % 2)
                                ra, rak = ropA[nrope % 2], 'ropA%d' % (nrope % 2)
                                nrope += 1
                                P.copy(qt_[:, 0:n], self.ps[bk][:, 0:n], reads=['ps%d' % bk], writes=[qtk])
                                b2 = self.nextps(4, 8)
                                P.mm(self.ps[b2][:, 0:n], psw[:], qt_[:, 0:n], True, True, reads=['pswap', qtk], writes=['ps%d' % b2])
                                P.tt(ra[:, 0:n], qt_[:, 0:n], cos4[:, t0:t0 + n], ALU.mult, reads=[qtk, 'cos4'], writes=[rak], eng='pool')
                                P.tt(qt_[:, 0:n], self.ps[b2][:, 0:n], sin4[:, t0:t0 + n], ALU.mult, reads=['ps%d' % b2, 'sin4', qtk], writes=[qtk])
                                if qk == 0:
                                    P.tt(dst[:, t0:t0 + n], ra[:, 0:n], qt_[:, 0:n], ALU.add, reads=[rak, qtk], writes=[dk])
                                else:
                                    for a_ in range(2):
                                        P.tt(dst[a_ * 64:(a_ + 1) * 64, a_, t0:t0 + n], ra[a_ * 64:(a_ + 1) * 64, 0:n], qt_[a_ * 64:(a_ + 1) * 64, 0:n], ALU.add,
                                             reads=[rak, qtk], writes=[dk])
                            else:
                                if qk == 0:
                                    P.copy(dst[:, t0:t0 + n], self.ps[bk][:, 0:n], reads=['ps%d' % bk], writes=[dk])
                                else:
                                    for a_ in range(2):
                                        P.copy(dst[a_ * 64:(a_ + 1) * 64, a_, t0:t0 + n], self.ps[bk][a_ * 64:(a_ + 1) * 64, 0:n], reads=['ps%d' % bk], writes=[dk])
                load_wqk(0)
                load_wqk(1)
                proj(0)
                for h in range(6):
                    dsts = ((QT[h % 2], 'QT%d' % (h % 2)), (KT[h % 2], 'KT%d' % (h % 2)))
                    Qh, qkey = dsts[0]
                    Kh, kkey = dsts[1]
                    qblocks = [(q0, 512, list(range(NT))) for q0 in (0, 512, 1024, 1536)] + [(2048, 256, [16, 17])]
                    for qbi, (q0, nq, ktiles) in enumerate(qblocks):
                        nqt = nq // 128
                        nk = len(ktiles)
                        if qbi == 2 and h + 1 < 6:
                            proj(h + 1)
                            if h + 2 < 6:
                                load_wqk(h + 2)
                        for a in range(2):
                            sbank = {}

                            def emit_S(i):
                                kt = ktiles[i]
                                bk = self.nextps(4, 8)
                                sbank[i] = bk
                                P.mm(self.ps[bk][:, 0:nq], Kh[:, a, kt * 128:(kt + 1) * 128], Qh[:, q0:q0 + nq], True, True,
                                     reads=[qkey, kkey], writes=['ps%d' % bk])
                            emit_S(0)
                            if nk > 1:
                                emit_S(1)
                            for i in range(nk):
                                kt = ktiles[i]
                                bk = sbank[i]
                                pt, ptk = PT[npt % 4], 'PT%d' % (npt % 4)
                                npt += 1
                                P.act(pt[:, 0:nq], self.ps[bk][:, 0:nq], AF.Exp, reads=['ps%d' % bk], writes=[ptk], scale=0.125)
                                if i + 2 < nk:
                                    emit_S(i + 2)
                                for qi in range(nqt):
                                    P.mm(self.ps[qi][:, 0:129], pt[:, qi * 128:(qi + 1) * 128], VA[:, kt, h, 0:129], i == 0, i == nk - 1,
                                         reads=[ptk, 'VA'], writes=['ps%d' % qi])
                            if a == 0:
                                for qi in range(nqt):
                                    acc = self.ps[qi]
                                    sk = 'sm%d' % qi
                                    P.recip(sm[:, qi, 0:1], acc[:, 128:129], reads=['ps%d' % qi], writes=[sk])
                                    P.ts(otmp[:, qi, :], acc[:, 0:128], sm[:, qi, 0:1], None, ALU.mult, reads=['ps%d' % qi, sk], writes=['otmp%d' % qi])
                            else:
                                for qi in range(nqt):
                                    acc = self.ps[qi]
                                    sk = 'sm%d' % qi
                                    of, ofk = ofin[qi], 'ofin%d' % qi
                                    P.recip(sm[:, qi, 1:2], acc[:, 128:129], reads=['ps%d' % qi], writes=[sk])
                                    P.tt(sm[:, qi, 1:2], sm[:, qi, 1:2], nlam[:], ALU.mult, reads=[sk, 'nlam'], writes=[sk])
                                    P.stt(of[:], acc[:, 0:128], sm[:, qi, 1:2], otmp[:, qi, :], ALU.mult, ALU.add, reads=['ps%d' % qi, sk, 'otmp%d' % qi], writes=[ofk])
                                sks = ['sm%d' % qi for qi in range(nqt)]
                                for qi in range(nqt):
                                    sk = 'sm%d' % qi
                                    of, ofk = ofin[qi], 'ofin%d' % qi
                                    P.tt(osq[:], of[:], of[:], ALU.mult, reads=[ofk], writes=['osq'])
                                    P.op('dve', (lambda e, qi=qi: e.tensor_reduce(out=sm[:, qi, 2:3], in_=osq[:], axis=mybir.AxisListType.X, op=ALU.add)),
                                         reads=['osq'], writes=[sk])
                                P.ts(sm[:, 0:nqt, 2:3], sm[:, 0:nqt, 2:3], 1.0 / 128.0, EPS, ALU.mult, ALU.add, reads=sks, writes=sks)
                                P.act(sm[:, 0:nqt, 2:3], sm[:, 0:nqt, 2:3], AF.Ln, reads=sks, writes=sks)
                                P.act(sm[:, 0:nqt, 2:3], sm[:, 0:nqt, 2:3], AF.Exp, reads=sks, writes=sks, scale=-0.5)
                                for qi in range(nqt):
                                    sk = 'sm%d' % qi
                                    of, ofk = ofin[qi], 'ofin%d' % qi
                                    P.stt(of[:], of[:], sm[:, qi, 2:3], gsub[:], ALU.mult, ALU.mult, reads=[ofk, sk, 'gsub'], writes=[ofk])
                                    bk = self.nextps(4, 8)
                                    P.tr(self.ps[bk][:, 0:128], of[:], self.ident[:], reads=[ofk, 'ident'], writes=['ps%d' % bk])
                                    P.copy(OT[:, h, q0 + qi * 128:q0 + (qi + 1) * 128], self.ps[bk][:, 0:128], reads=['ps%d' % bk], writes=['OT'], eng='act')
            with self.scope() as s2:
                cb = self.sb(s2, "c64", [128, 2, 128])
                fw = self.sb(s2, "fwb", [128, 2, 128])
                fb = self.sb(s2, "fb", [128, 2])
                P.dma('sp', cb[:, 0, :], self.din['c64blk'], writes=['c64'])
                P.dma('sp', cb[:, 1, :], self.din['ns64blk'], writes=['c64'])
                P.dma('sp', fw[:], self.din['f_wblk'].rearrange("c p e -> p c e"), writes=['fwb'])
                P.dma('sp', fb[:], self.din['f_b'], writes=['fb'])
                ABm = self.sb(s2, "ABm", [128, 2, 2, 128], BF16)
                for c in range(2):
                    for cs in range(2):
                        bk = self.nextps()
                        P.mm(self.ps[bk][:, 0:128], cb[:, cs, :], fw[:, c, :], True, True, reads=['c64', 'fwb'], writes=['ps%d' % bk])
                        P.copy(ABm[:, c, cs, :], self.ps[bk][:, 0:128], reads=['ps%d' % bk], writes=['ABm'])
                Ycs = self.sb(s2, "Ycs", [128, NT, 2, 256], BF16)
                for t in range(NT):
                    for c in range(2):
                        bk = self.nextps(4, 8)
                        P.mm(self.ps[bk][:, 0:256], FT[:, c, t * 128:(t + 1) * 128], ABm[:, c, :, :].rearrange("p a b -> p (a b)"), True, True,
                             reads=['FT', 'ABm'], writes=['ps%d' % bk])
                        P.copy(Ycs[:, t, c, :], self.ps[bk][:, 0:256], reads=['ps%d' % bk], writes=['Ycs'], eng=('act' if c == 0 else 'dve'))
                dbuf = [self.sb(s2, "dft%d" % i, [128, NLAT], BF16) for i in range(3)]
                nd = 0
                dsrc = (self.din['dftc'], self.din['dfts'])
                for tn in range(16):
                    for cs in range(2):
                        db, dbk = dbuf[nd % 3], 'dft%d' % (nd % 3)
                        nd += 1
                        P.dma('sp', db[:], dsrc[cs][tn * 128:(tn + 1) * 128, :], writes=[dbk])
                        first = (tn == 0 and cs == 0)
                        last = (tn == 15 and cs == 1)
                        for c in range(2):
                            for nb in range(4):
                                bq = c * 4 + nb
                                P.mm(self.ps[bq][:, :], Ycs[:, tn, c, cs * 128:(cs + 1) * 128], db[:, nb * 512:(nb + 1) * 512], first, last,
                                     reads=['Ycs', dbk], writes=['ps%d' % bq])
                for c in range(2):
                    for nb in range(4):
                        bq = c * 4 + nb
                        P.act(OT[:, 6 + c, nb * 512:(nb + 1) * 512], self.ps[bq][:, :], AF.Identity, reads=['ps%d' % bq, 'fb'], writes=['OT'], bias=fb[:, c:c + 1])
                dcc = self.sb(s2, "dftcc", [128, 2, 2, 256], BF16)
                P.dma('sp', dcc[:, 0], self.din['dftc_c'].rearrange("(t p) n -> p t n", p=128), writes=['dftcc'])
                P.dma('sp', dcc[:, 1], self.din['dfts_c'].rearrange("(t p) n -> p t n", p=128), writes=['dftcc'])
                for c in range(2):
                    bk = self.nextps(4, 8)
                    n = 0
                    for tn in range(2):
                        for cs in range(2):
                            P.mm(self.ps[bk][:, 0:256], Ycs[:, 16 + tn, c, cs * 128:(cs + 1) * 128], dcc[:, cs, tn, :], n == 0, n == 3,
                                 reads=['Ycs', 'dftcc'], writes=['ps%d' % bk])
                            n += 1
                    P.act(OT[:, 6 + c, 2048:2304], self.ps[bk][:, 0:256], AF.Identity, reads=['ps%d' % bk, 'fb'], writes=['OT'], bias=fb[:, c:c + 1])

    def s5_mixer(self, S5O):
        import os
        stop = int(os.environ.get('S5STOP', '99'))
        if stop <= 0:
            self.P.memset(S5O[:], 0.0, writes=['S5O'])
            return
        P = self.P
        UT = self.UT
        w_in = self.din['s_w_in'].rearrange("(kc p) f -> p kc f", p=128)
        blocks = [(0, 512), (512, 512), (1024, 512), (1536, 512), (2048, 256)]
        T = ALU
        with self.scope() as s1:
            usT = self.sb(s1, "usT", [128, 2, NTOK], BF16)
            Y5 = self.sb(s1, "Y5", [128, 2, NLAT])
            dd = self.sb(s1, "s5dd", [128, 2])
            P.dma('sp', dd[:], self.din['s5_dd'], writes=['s5dd'])
            ones = self.sb(s1, "ones", [128, 512])
            P.dma('sp', ones[:], self.din['ones'], writes=['ones'])
            with self.scope() as s2:
                wus = self.sb(s2, "wus", [128, 8, 256], BF16)
                P.dma('pool', wus[:], w_in[:, :, 2072:2328], writes=['wus'])
                for c in range(2):
                    for (t0, n) in blocks:
                        bk = self.nextps()
                        for kc in range(8):
                            P.mm(self.ps[bk][:, 0:n], wus[:, kc, c * 128:(c + 1) * 128], UT[:, kc, t0:t0 + n], kc == 0, kc == 7,
                                 reads=['wus', 'UT'], writes=['ps%d' % bk])
                        P.copy(usT[:, c, t0:t0 + n], self.ps[bk][:, 0:n], reads=['ps%d' % bk], writes=['usT'], eng='act')
                        if t0 < NLAT and os.environ.get('S5SUB', '') != 'a':
                            P.ts(Y5[:, c, t0:t0 + n], self.ps[bk][:, 0:n], dd[:, c:c + 1], None, T.mult, reads=['ps%d' % bk, 's5dd'], writes=['Y5_%d_%d' % (c, t0)])
            if stop <= 1:
                P.memset(S5O[:], 0.0, writes=['S5O'])
                return
            pr = self.sb(s1, "s5pr", [128, 32, 16])
            names = {}

            def V(nm):
                if nm not in names:
                    names[nm] = len(names)
                return pr[:, names[nm], :]
            for nm, src in (('lre', 's5_lre'), ('lim', 's5_lim'), ('ldt', 's5_ldt')):
                P.dma('sp', V(nm), self.din[src], writes=['s5pr'])
            K_ = ['s5pr']

            def tt(o, a, b, op):
                P.tt(V(o), V(a), V(b), op, reads=K_, writes=K_)

            def ts(o, a, s1_, s2_, op0, op1=None):
                P.ts(V(o), V(a), s1_, s2_, op0, op1, reads=K_, writes=K_)
            P.act(V('dt'), V('ldt'), AF.Exp, reads=K_, writes=K_)
            tt('t0', 'dt', 'lre', T.mult)
            P.act(V('mag'), V('t0'), AF.Exp, reads=K_, writes=K_)
            tt('th', 'dt', 'lim', T.mult)
            ts('k', 'th', 1.0 / (2.0 * math.pi), None, T.mult)
            ts('k', 'k', 12582912.0, None, T.add)
            ts('k', 'k', -12582912.0, None, T.add)
            P.stt(V('r'), V('k'), -6.28125, V('th'), T.mult, T.add, reads=K_, writes=K_)
            P.stt(V('r'), V('k'), -(2.0 * math.pi - 6.28125), V('r'), T.mult, T.add, reads=K_, writes=K_)
            ts('x', 'r', 0.125, None, T.mult)
            tt('x2', 'x', 'x', T.mult)
            ts('ps_', 'x2', -1.0 / 5040.0, 1.0 / 120.0, T.mult, T.add)
            tt('ps_', 'ps_', 'x2', T.mult)
            ts('ps_', 'ps_', -1.0 / 6.0, None, T.add)
            tt('ps_', 'ps_', 'x2', T.mult)
            ts('ps_', 'ps_', 1.0, None, T.add)
            tt('sn', 'ps_', 'x', T.mult)
            ts('pc_', 'x2', 1.0 / 40320.0, -1.0 / 720.0, T.mult, T.add)
            tt('pc_', 'pc_', 'x2', T.mult)
            ts('pc_', 'pc_', 1.0 / 24.0, None, T.add)
            tt('pc_', 'pc_', 'x2', T.mult)
            ts('pc_', 'pc_', -0.5, None, T.add)
            tt('pc_', 'pc_', 'x2', T.mult)
            ts('cs', 'pc_', 1.0, None, T.add)
            for _ in range(3):
                tt('cc', 'cs', 'cs', T.mult)
                tt('ss', 'sn', 'sn', T.mult)
                tt('sc2', 'cs', 'sn', T.mult)
                tt('cs', 'cc', 'ss', T.subtract)
                ts('sn', 'sc2', 2.0, None, T.mult)
            tt('abre', 'mag', 'cs', T.mult)
            tt('abim', 'mag', 'sn', T.mult)
            tt('den', 'lre', 'lre', T.mult)
            tt('t0', 'lim', 'lim', T.mult)
            tt('den', 'den', 't0', T.add)
            P.recip(V('den'), V('den'), reads=K_, writes=K_)
            ts('am1', 'abre', -1.0, None, T.add)
            tt('t0', 'am1', 'lre', T.mult)
            tt('t1', 'abim', 'lim', T.mult)
            tt('t0', 't0', 't1', T.add)
            tt('kre', 't0', 'den', T.mult)
            tt('t0', 'abim', 'lre', T.mult)
            tt('t1', 'am1', 'lim', T.mult)
            tt('t0', 't0', 't1', T.subtract)
            tt('kim', 't0', 'den', T.mult)
            if stop <= 2:
                P.memset(S5O[:], 0.0, writes=['S5O'])
                return
            Ec = self.sb(s1, "s5Ec", [128, 10, 16])
            Es = self.sb(s1, "s5Es", [128, 10, 16])
            EK = ['s5E']
            P.copy(Ec[:, 0, :], V('cs'), reads=K_, writes=EK)
            P.copy(Es[:, 0, :], V('sn'), reads=K_, writes=EK)
            e1 = self.sb(s1, "s5e1", [128, 16])
            e2 = self.sb(s1, "s5e2", [128, 16])
            for j in range(9):
                P.tt(e1[:], Ec[:, j, :], Ec[:, j, :], T.mult, reads=EK, writes=['s5e1'])
                P.tt(e2[:], Es[:, j, :], Es[:, j, :], T.mult, reads=EK, writes=['s5e2'])
                P.tt(Ec[:, j + 1, :], e1[:], e2[:], T.subtract, reads=['s5e1', 's5e2'], writes=EK)
                P.tt(e1[:], Ec[:, j, :], Es[:, j, :], T.mult, reads=EK, writes=['s5e1'])
                P.ts(Es[:, j + 1, :], e1[:], 2.0, None, T.mult, reads=['s5e1'], writes=EK)
            if stop <= 3:
                P.memset(S5O[:], 0.0, writes=['S5O'])
                return
            nEs = self.sb(s1, "s5nEs", [128, 10, 16])
            P.ts(nEs[:], Es[:], -1.0, None, T.mult, reads=EK, writes=['s5nEs'])
            BbT = self.sb(s1, "BbT", [128, 2, 8, 2, 128], BF16)
            CmT = self.sb(s1, "CmT", [128, 2, 8, 2, 128], BF16)
            for d in range(2):
                P.dma('pool', CmT[:, d, :, 0, :], self.din['s5_cre'][d], writes=['CmT'])
                P.dma('pool', CmT[:, d, :, 1, :], self.din['s5_cim'][d], writes=['CmT'])
                P.ts(CmT[:, d, :, 1, :], CmT[:, d, :, 1, :], -1.0, None, T.mult, reads=['CmT'], writes=['CmT'])
            with self.scope() as s2:
                bre = self.sb(s2, "s5bre", [128, 8, 128])
                bim = self.sb(s2, "s5bim", [128, 8, 128])
                P.dma('sp', bre[:], self.din['s5_bre'], writes=['s5bre'])
                P.dma('sp', bim[:], self.din['s5_bim'], writes=['s5bim'])
                wt = [self.sb(s2, "s5wt%d" % i, [128, 128]) for i in range(2)]
                nw = 0
                for d in range(2):
                    for sc in range(8):
                        col = d * 8 + sc
                        kre = V('kre')[:, col:col + 1]
                        kim = V('kim')[:, col:col + 1]
                        for ri in range(2):
                            w, wk = wt[nw % 2], 's5wt%d' % (nw % 2)
                            nw += 1
                            if ri == 0:
                                P.ts(w[:], bim[:, sc, :], kim, None, T.mult, reads=['s5bim'] + K_, writes=[wk])
                                P.stt(w[:], bre[:, sc, :], kre, w[:], T.mult, T.subtract, reads=['s5bre', wk] + K_, writes=[wk])
                            else:
                                P.ts(w[:], bre[:, sc, :], kim, None, T.mult, reads=['s5bre'] + K_, writes=[wk])
                                P.stt(w[:], bim[:, sc, :], kre, w[:], T.mult, T.add, reads=['s5bim', wk] + K_, writes=[wk])
                            bk = self.nextps()
                            P.tr(self.ps[bk][:, 0:128], w[:], self.ident[:], reads=[wk, 'ident'], writes=['ps%d' % bk])
                            P.copy(BbT[:, d, sc, ri, :], self.ps[bk][:, 0:128], reads=['ps%d' % bk], writes=['BbT'], eng='act')
            if stop <= 4:
                P.memset(S5O[:], 0.0, writes=['S5O'])
                return
            with self.scope() as s2:
                Tc = self.sb(s2, "s5Tc", [128, 8, 512])
                Ts = self.sb(s2, "s5Ts", [128, 8, 512])
                tq = [self.sb(s2, "s5tq%d" % i, [128, 8, 256]) for i in range(2)]
                rho = self.sb(s2, "s5rho", [128, 512])
                NB = 2
                mt = [[self.sb(s2, "s5m%d_%d" % (i, b), [128, 512]) for i in range(4)] for b in range(NB)]
                bp = [[self.sb(s2, "s5bp%d_%d" % (i, b), [128, 512]) for i in range(2)] for b in range(NB)]
                gg = [[self.sb(s2, "s5g%d_%d" % (i, b), [128, 512]) for i in range(2)] for b in range(NB)]
                hh = [[self.sb(s2, "s5h%d_%d" % (i, b), [128, 512], BF16) for i in range(2)] for b in range(NB)]
                pp = [self.sb(s2, "s5p%d" % i, [128, 512]) for i in range(4)]
                ini = [self.sb(s2, "s5ini%d" % b, [128, 4]) for b in range(NB)]
                nb_ = 0
                for d in range(2):
                    TK = ['s5T']
                    P.memset(Tc[:, :, 0:1], 1.0, writes=TK)
                    P.memset(Ts[:, :, 0:1], 0.0, writes=TK)
                    for j in range(9):
                        m = 1 << j
                        ecb = Ec[:, j, d * 8:(d + 1) * 8].unsqueeze(2).to_broadcast([128, 8, m])
                        esb = Es[:, j, d * 8:(d + 1) * 8].unsqueeze(2).to_broadcast([128, 8, m])
                        P.tt(tq[0][:, :, 0:m], Ts[:, :, 0:m], esb, T.mult, reads=TK + EK, writes=['s5tq0'])
                        P.tt(tq[1][:, :, 0:m], Tc[:, :, 0:m], esb, T.mult, reads=TK + EK, writes=['s5tq1'])
                        P.tt(Tc[:, :, m:2 * m], Tc[:, :, 0:m], ecb, T.mult, reads=TK + EK, writes=TK)
                        P.tt(Ts[:, :, m:2 * m], Ts[:, :, 0:m], ecb, T.mult, reads=TK + EK, writes=TK)
                        P.tt(Tc[:, :, m:2 * m], Tc[:, :, m:2 * m], tq[0][:, :, 0:m], T.subtract, reads=TK + ['s5tq0'], writes=TK)
                        P.tt(Ts[:, :, m:2 * m], Ts[:, :, m:2 * m], tq[1][:, :, 0:m], T.add, reads=TK + ['s5tq1'], writes=TK)
                    order = [blocks[4]] + (blocks[0:4] if d == 0 else blocks[3::-1])
                    if stop <= 5:
                        continue
                    if stop <= 6 and d == 1:
                        continue
                    if d == 1:
                        P.copy(tq[0][:, :, :], Tc[:, :, 0:256], reads=TK, writes=['s5tq0'])
                        P.copy(tq[1][:, :, :], Tc[:, :, 256:512], reads=TK, writes=['s5tq1'])
                        P.copy(Tc[:, :, 0:256], tq[1][:, :, ::-1], reads=['s5tq1'], writes=TK)
                        P.copy(Tc[:, :, 256:512], tq[0][:, :, ::-1], reads=['s5tq0'], writes=TK)
                        P.copy(tq[0][:, :, :], Ts[:, :, 0:256], reads=TK, writes=['s5tq0'])
                        P.copy(tq[1][:, :, :], Ts[:, :, 256:512], reads=TK, writes=['s5tq1'])
                        P.copy(Ts[:, :, 0:256], tq[1][:, :, ::-1], reads=['s5tq1'], writes=TK)
                        P.copy(Ts[:, :, 256:512], tq[0][:, :, ::-1], reads=['s5tq0'], writes=TK)
                    for sc in range(8):
                        col = d * 8 + sc
                        fc = sc // 4
                        P.ts(rho[:], ones[:], V('mag')[:, col:col + 1], None, T.mult, reads=['ones'] + K_, writes=['s5rho'])
                        prev = None
                        pending = []
                        pend_pe = []
                        for (t0, n) in order:
                            b = nb_ % NB
                            nb_ += 1
                            sfx = '_%d' % b
                            bk1 = self.nextps()
                            P.mm(self.ps[bk1][:, 0:n], BbT[:, d, sc, 0, :], usT[:, fc, t0:t0 + n], True, True, reads=['BbT', 'usT'], writes=['ps%d' % bk1])
                            bk2 = self.nextps()
                            P.mm(self.ps[bk2][:, 0:n], BbT[:, d, sc, 1, :], usT[:, fc, t0:t0 + n], True, True, reads=['BbT', 'usT'], writes=['ps%d' % bk2])
                            while pend_pe:
                                pend_pe.pop(0)()
                            pre, pim = self.ps[bk1][:, 0:n], self.ps[bk2][:, 0:n]
                            if d == 0:
                                tc = Tc[:, sc, 0:n]
                                ts_ = Ts[:, sc, 0:n]
                            else:
                                tc = Tc[:, sc, 512 - n:512]
                                ts_ = Ts[:, sc, 512 - n:512]
                            m1, m2, m3, m4 = [mt[b][i][:, 0:n] for i in range(4)]
                            mk = ['s5m%d%s' % (i, sfx) for i in range(4)]
                            P.tt(m1, pre, tc, T.mult, reads=['ps%d' % bk1] + TK, writes=[mk[0]])
                            P.tt(m2, pim, ts_, T.mult, reads=['ps%d' % bk2] + TK, writes=[mk[1]])
                            P.tt(m3, pim, tc, T.mult, reads=['ps%d' % bk2] + TK, writes=[mk[2]])
                            P.tt(m4, pre, ts_, T.mult, reads=['ps%d' % bk1] + TK, writes=[mk[3]])
                            bpr, bpi = bp[b][0][:, 0:n], bp[b][1][:, 0:n]
                            P.tt(bpr, m1, m2, T.add, reads=[mk[0], mk[1]], writes=['s5bp0' + sfx])
                            P.tt(bpi, m3, m4, T.subtract, reads=[mk[2], mk[3]], writes=['s5bp1' + sfx])
                            gre, gim = gg[b][0][:, 0:n], gg[b][1][:, 0:n]
                            if d == 0:
                                go_r, go_i, bi_r, bi_i = gre, gim, bpr, bpi
                                last = n - 1
                            else:
                                go_r, go_i, bi_r, bi_i = gre[:, ::-1], gim[:, ::-1], bpr[:, ::-1], bpi[:, ::-1]
                                last = 0
                            i0 = 0.0 if prev is None else ini[prev][:, 0:1]
                            i1 = 0.0 if prev is None else ini[prev][:, 1:2]
                            rk = [] if prev is None else ['s5ini%d' % prev]
                            P.scan(go_r, rho[:, 0:n], bi_r, i0, T.mult, T.add, reads=['s5rho', 's5bp0' + sfx] + rk, writes=['s5g0' + sfx])
                            P.scan(go_i, rho[:, 0:n], bi_i, i1, T.mult, T.add, reads=['s5rho', 's5bp1' + sfx] + rk, writes=['s5g1' + sfx])
                            j = 9 if n == 512 else 8
                            ec = Ec[:, j, col:col + 1]
                            es = Es[:, j, col:col + 1]
                            ik = 's5ini%d' % b
                            nes = nEs[:, j, col:col + 1]
                            P.act(ini[b][:, 2:3], gg[b][1][:, last:last + 1], AF.Identity, reads=['s5g1' + sfx, 's5nEs'], writes=[ik], scale=nes, bias=0.0)
                            P.act(ini[b][:, 0:1], gg[b][0][:, last:last + 1], AF.Identity, reads=['s5g0' + sfx, ik] + EK, writes=[ik], scale=ec, bias=ini[b][:, 2:3])
                            P.act(ini[b][:, 3:4], gg[b][0][:, last:last + 1], AF.Identity, reads=['s5g0' + sfx] + EK, writes=[ik], scale=es, bias=0.0)
                            P.act(ini[b][:, 1:2], gg[b][1][:, last:last + 1], AF.Identity, reads=['s5g1' + sfx, ik] + EK, writes=[ik], scale=ec, bias=ini[b][:, 3:4])
                            prev = b
                            while pending:
                                pending.pop(0)()
                            if t0 < NLAT:
                                p1, p2, p3, p4 = [pp[i][:, 0:n] for i in range(4)]
                                hre, him = hh[b][0][:, 0:n], hh[b][1][:, 0:n]
                                P.tt(p1, gre, tc, T.mult, reads=['s5g0' + sfx] + TK, writes=['s5p0'], eng='pool')
                                P.tt(p2, gim, ts_, T.mult, reads=['s5g1' + sfx] + TK, writes=['s5p1'], eng='pool')
                                P.tt(hre, p1, p2, T.subtract, reads=['s5p0', 's5p1'], writes=['s5h0' + sfx], eng='pool')
                                P.tt(p3, gre, ts_, T.mult, reads=['s5g0' + sfx] + TK, writes=['s5p2'], eng='pool')
                                P.tt(p4, gim, tc, T.mult, reads=['s5g1' + sfx] + TK, writes=['s5p3'], eng='pool')
                                P.tt(him, p3, p4, T.add, reads=['s5p2', 's5p3'], writes=['s5h1' + sfx], eng='pool')
                                yk = 'Y5_%d_%d' % (fc, t0)
                                cell = {}

                                def _ro(cell=cell, hre=hre, him=him, sfx=sfx, n=n, d=d, sc=sc):
                                    bk = self.nextps()
                                    cell['bk'] = bk
                                    P.mm(self.ps[bk][:, 0:n], CmT[:, d, sc, 0, :], hre, True, False, reads=['CmT', 's5h0' + sfx], writes=['ps%d' % bk])
                                    P.mm(self.ps[bk][:, 0:n], CmT[:, d, sc, 1, :], him, False, True, reads=['CmT', 's5h1' + sfx], writes=['ps%d' % bk])

                                def _acc(cell=cell, yk=yk, fc=fc, t0=t0, n=n):
                                    bk = cell['bk']
                                    P.tt(Y5[:, fc, t0:t0 + n], Y5[:, fc, t0:t0 + n], self.ps[bk][:, 0:n], T.add, reads=[yk, 'ps%d' % bk], writes=[yk])
                                pend_pe.append(_ro)
                                pending.append(_acc)
                        while pend_pe:
                            pend_pe.pop(0)()
                        while pending:
                            pending.pop(0)()
            if stop <= 7:
                P.memset(S5O[:], 0.0, writes=['S5O'])
                return
            with self.scope() as s2:
                gw = self.sb(s2, "gluw", [128, 2, 256], BF16)
                gb_ = self.sb(s2, "glub", [128, 2])
                P.dma('pool', gw[:], self.din['s5_glu_w'].rearrange("(kc p) f -> p kc f", p=128), writes=['gluw'])
                P.dma('sp', gb_[:], self.din['s5_glu_b'], writes=['glub'])
                gbf = self.sb(s2, "gbf", [128, 2, NLAT], BF16)
                t1 = [self.sb(s2, "glt%d" % i, [128, 512]) for i in range(2)]
                sg = [self.sb(s2, "gls%d" % i, [128, 512]) for i in range(2)]
                n_ = 0
                for c in range(2):
                    for (t0, n) in blocks[0:4]:
                        yk = 'Y5_%d_%d' % (c, t0)
                        x = Y5[:, c, t0:t0 + n]
                        a, ak = t1[n_ % 2], 'glt%d' % (n_ % 2)
                        n_ += 1
                        P.tt(a[:], x, x, T.mult, reads=[yk], writes=[ak], eng='pool')
                        P.ts(a[:], a[:], 0.044715, 1.0, T.mult, T.add, reads=[ak], writes=[ak])
                        P.tt(a[:], a[:], x, T.mult, reads=[ak, yk], writes=[ak])
                        P.act(a[:], a[:], AF.Tanh, reads=[ak], writes=[ak], scale=math.sqrt(2.0 / math.pi))
                        P.stt(a[:], a[:], 1.0, x, T.add, T.mult, reads=[ak, yk], writes=[ak])
                        P.ts(x, a[:], 0.5, None, T.mult, reads=[ak], writes=[yk])
                        P.copy(gbf[:, c, t0:t0 + n], x, reads=[yk], writes=['gbf'], eng='act')
                n_ = 0
                for c in range(2):
                    for (t0, n) in blocks[0:4]:
                        bk = self.nextps()
                        for kc in range(2):
                            P.mm(self.ps[bk][:, 0:n], gw[:, kc, c * 128:(c + 1) * 128], gbf[:, kc, t0:t0 + n], kc == 0, kc == 1,
                                 reads=['gluw', 'gbf'], writes=['ps%d' % bk])
                        a, ak = sg[n_ % 2], 'gls%d' % (n_ % 2)
                        n_ += 1
                        P.act(a[:], self.ps[bk][:, 0:n], AF.Sigmoid, reads=['ps%d' % bk, 'glub'], writes=[ak], bias=gb_[:, c:c + 1])
                        P.tt(S5O[:, c, t0:t0 + n], Y5[:, c, t0:t0 + n], a[:], T.mult, reads=['Y5_%d_%d' % (c, t0), ak], writes=['S5O'])

    def ssd_inproj(self, sB, Xtm, Btm, BT, CT, dtt, dta, ZS):
        P = self.P
        UT = self.UT
        T = ALU
        w_in = self.din['s_w_in'].rearrange("(kc p) f -> p kc f", p=128)
        blocks = [(0, 512), (512, 512), (1024, 512), (1536, 512), (2048, 256)]
        with self.scope() as s2:
            wz = self.sb(s2, "wz", [128, 8, 768], BF16)
            P.dma('pool', wz[:], w_in[:, :, 0:768], writes=['wz'])
            zt = [self.sb(s2, "ztp%d" % i, [128, 768]) for i in range(2)]
            for t in range(16):
                z, zk = zt[t % 2], 'ztp%d' % (t % 2)
                for half in range(2):
                    bk = self.nextps()
                    for kc in range(8):
                        P.mm(self.ps[bk][:, 0:384], UT[:, kc, t * 128:(t + 1) * 128], wz[:, kc, half * 384:(half + 1) * 384], kc == 0, kc == 7,
                             reads=['UT', 'wz'], writes=['ps%d' % bk])
                    P.act(z[:, half * 384:(half + 1) * 384], self.ps[bk][:, 0:384], AF.Silu, reads=['ps%d' % bk], writes=[zk])
                P.dma('sp', ZS[t * 128:(t + 1) * 128, :], z[:], reads=[zk], writes=['ZS_%d' % t])
            wdt = self.sb(s2, "wdt", [128, 8, 24], BF16)
            P.dma('pool', wdt[:], w_in[:, :, 2048:2072], writes=['wdt'])
            dtb = self.sb(s2, "dtb", [128, 24])
            alog = self.sb(s2, "alog", [128, 24])
            P.dma('sp', dtb[:], self.din['sd_dtb'], writes=['dtb'])
            P.dma('sp', alog[:], self.din['sd_alog'], writes=['alog'])
            for t in range(NT):
                bk = self.nextps()
                for kc in range(8):
                    P.mm(self.ps[bk][:, 0:24], UT[:, kc, t * 128:(t + 1) * 128], wdt[:, kc, :], kc == 0, kc == 7, reads=['UT', 'wdt'], writes=['ps%d' % bk])
                P.tt(dtt[:, t, :], self.ps[bk][:, 0:24], dtb[:], T.add, reads=['ps%d' % bk, 'dtb'], writes=['dtt'])
            ax = self.sb(s2, "spax", [128, NT * 24])
            dflat = dtt[:].rearrange("p t f -> p (t f)")
            P.act(ax[:], dflat, AF.Abs, reads=['dtt'], writes=['spax'])
            P.act(ax[:], ax[:], AF.Exp, reads=['spax'], writes=['spax'], scale=-1.0)
            P.act(ax[:], ax[:], AF.Ln, reads=['spax'], writes=['spax'], bias=1.0)
            P.ts(dflat, dflat, 0.0, None, T.max, reads=['dtt'], writes=['dtt'])
            P.tt(dflat, dflat, ax[:], T.add, reads=['dtt', 'spax'], writes=['dtt'])
            P.act(alog[:], alog[:], AF.Exp, reads=['alog'], writes=['alog'])
            P.ts(alog[:], alog[:], -1.0, None, T.mult, reads=['alog'], writes=['alog'])
            P.tt(dta[:], dtt[:], alog[:].unsqueeze(1).to_broadcast([128, NT, 24]), T.mult, reads=['dtt', 'alog'], writes=['dta'])
        with self.scope() as s2:
            cw = self.sb(s2, "sdcw", [128, 3, 10])
            cb = self.sb(s2, "sdcb", [128, 10])
            P.dma('sp', cw[:], self.din['sd_cw'], writes=['sdc'])
            P.dma('sp', cb[:], self.din['sd_cb'], writes=['sdc'])
            W = 2308
            xs = self.sb(s2, "sdxs", [128, W])
            acc = self.sb(s2, "sdacc", [128, W])
            P.memset(xs[:], 0.0, writes=['sdxs'], eng='pool')
            wx = [self.sb(s2, "sdwx%d" % i, [128, 8, 128], BF16) for i in range(2)]

            def colpos(t0):
                return 1 + t0 if t0 < NLAT else 2051 + (t0 - NLAT)
            for c in range(10):
                w, wk = wx[c % 2], 'sdwx%d' % (c % 2)
                P.dma('pool', w[:], w_in[:, :, 768 + c * 128:768 + (c + 1) * 128], writes=[wk])
                for (t0, n) in blocks:
                    bk = self.nextps()
                    for kc in range(8):
                        P.mm(self.ps[bk][:, 0:n], w[:, kc, :], UT[:, kc, t0:t0 + n], kc == 0, kc == 7, reads=[wk, 'UT'], writes=['ps%d' % bk])
                    cp = colpos(t0)
                    P.copy(xs[:, cp:cp + n], self.ps[bk][:, 0:n], reads=['ps%d' % bk], writes=['sdxs'], eng='act')
                n1 = W - 2
                P.ts(acc[:, 1:1 + n1], xs[:, 0:n1], cw[:, 0, c:c + 1], cb[:, c:c + 1], T.mult, T.add, reads=['sdxs', 'sdc'], writes=['sdacc'])
                P.stt(acc[:, 1:1 + n1], xs[:, 1:1 + n1], cw[:, 1, c:c + 1], acc[:, 1:1 + n1], T.mult, T.add, reads=['sdxs', 'sdacc', 'sdc'], writes=['sdacc'])
                P.stt(acc[:, 1:1 + n1], xs[:, 2:2 + n1], cw[:, 2, c:c + 1], acc[:, 1:1 + n1], T.mult, T.add, reads=['sdxs', 'sdacc', 'sdc'], writes=['sdacc'])
                P.act(acc[:, 1:1 + n1], acc[:, 1:1 + n1], AF.Silu, reads=['sdacc'], writes=['sdacc'])
                if 6 <= c < 8:
                    g = c - 6
                    P.copy(BT[:, g, 0:NLAT], acc[:, 1:1 + NLAT], reads=['sdacc'], writes=['BT'], eng='pool')
                    P.copy(BT[:, g, NLAT:NTOK], acc[:, 2051:2051 + NCTX], reads=['sdacc'], writes=['BT'], eng='pool')
                if c >= 8:
                    g = c - 8
                    P.copy(CT[:, g, 0:NLAT], acc[:, 1:1 + NLAT], reads=['sdacc'], writes=['CT'], eng='pool')
                    P.copy(CT[:, g, NLAT:NTOK], acc[:, 2051:2051 + NCTX], reads=['sdacc'], writes=['CT'], eng='pool')
                if c < 8:
                    for t4 in range(0, NT, 4):
                        bk = self.nextps()
                        nq = min(4, NT - t4)
                        for q in range(nq):
                            t = t4 + q
                            cp = colpos(t * 128)
                            P.tr(self.ps[bk][:, q * 128:(q + 1) * 128], acc[:, cp:cp + 128], self.ident[:], reads=['sdacc', 'ident'], writes=['ps%d' % bk])
                        src = self.ps[bk][:, 0:nq * 128].rearrange("p (q f) -> p q f", q=nq)
                        if c < 6:
                            P.copy(Xtm[:, t4:t4 + nq, c * 128:(c + 1) * 128], src, reads=['ps%d' % bk], writes=['Xtm'], eng=('act' if (t4 // 4) % 2 == 0 else 'dve'))
                        else:
                            P.copy(Btm[:, t4:t4 + nq, c - 6, :], src, reads=['ps%d' % bk], writes=['Btm'], eng=('act' if (t4 // 4) % 2 == 0 else 'dve'))

    def ssd_chunks(self, Xtm, Btm, BT, CT, dtt, dta, Yacc):
        P = self.P
        T = ALU
        with self.scope() as s2:
            vd = self.sb(s2, "vd", [128, 2, 128])
            ud = self.sb(s2, "ud", [128, 2, 128])
            ones = self.sb(s2, "ones1", [128, 128])
            dsk = self.sb(s2, "dsk", [128, 12])
            P.dma('sp', vd[:], self.din['vd'], writes=['vd'])
            P.dma('sp', ud[:], self.din['ud'], writes=['ud'])
            P.dma('sp', ones[:], self.din['ones'][:, 0:128], writes=['ones1'])
            P.dma('sp', dsk[:], self.din['sd_d'], writes=['dsk'])
            for t in range(16):
                P.tt(Yacc[:, t, :].rearrange("p (r e) -> p r e", r=12), Xtm[:, t, :].rearrange("p (r e) -> p r e", r=12),
                     dsk[:].unsqueeze(2).to_broadcast([128, 12, 64]), T.mult, reads=['Xtm', 'dsk'], writes=['Yacc%d' % t])
            utflat = self.UT[:].rearrange("p a b -> p (a b)")

            class _V:
                def __init__(self, ap):
                    self.ap = ap

                def __getitem__(self, idx):
                    return self.ap[idx]

            def alias(i):
                return _V(utflat[:, i * 3072:(i + 1) * 3072].bitcast(F32).rearrange("p (a b) -> p a b", a=12))
            rhsV = alias(0)
            Ls = [alias(1), alias(2)]
            CBm_ = [self.sb(s2, "CBm%d" % i, [128, 2, 128]) for i in range(2)]
            M_ = [self.sb(s2, "Mdiag%d" % i, [128, 12, 128], BF16) for i in range(2)]
            xdts = [self.sb(s2, "xdt%d" % i, [128, 12, 64], BF16) for i in range(2)]
            xw_ = [self.sb(s2, "xw%d" % i, [128, 12, 64], BF16) for i in range(2)]
            tmp_ = [self.sb(s2, "ytmp%d" % i, [128, 768]) for i in range(2)]
            Hf_ = [self.sb(s2, "Hf%d" % i, [128, 768]) for i in range(2)]
            Hb_ = [self.sb(s2, "Hb%d" % i, [128, 768], BF16) for i in range(2)]
            eacs_ = [self.sb(s2, "eacs%d" % i, [128, 12]) for i in range(2)]
            cdv_ = [self.sb(s2, "cdv%d" % i, [128, 12]) for i in range(2)]
            jobs = []
            orders = [[16, 17] + list(range(16)), [17, 16] + list(range(15, -1, -1))]
            for ci in range(18):
                for d in range(2):
                    jobs.append((d, ci, orders[d][ci], ci == 17))

            def stageA1(n):
                d, ci, t, islast = jobs[n]
                xdt, xk = xdts[n % 2], 'xdt%d' % (n % 2)
                dta_t = dta[:, t, d * 12:(d + 1) * 12]
                dt_t = dtt[:, t, d * 12:(d + 1) * 12]
                P.tt(rhsV[:], vd[:, d, :].unsqueeze(1).to_broadcast([128, 12, 128]), dta_t.unsqueeze(2).to_broadcast([128, 12, 128]), T.mult,
                     reads=['vd', 'dta'], writes=['rhsV'], eng='pool')
                P.tt(xdt[:], Xtm[:, t, :].rearrange("p (r e) -> p r e", r=12), dt_t.unsqueeze(2).to_broadcast([128, 12, 64]), T.mult,
                     reads=['Xtm', 'dtt'], writes=[xk], eng='pool')

            def stageA2(n):
                d, ci, t, islast = jobs[n]
                L, lk = Ls[n % 2], 'Lseg%d' % (n % 2)
                for q in range(3):
                    bk = self.nextps()
                    P.mm(self.ps[bk][:, :], ud[:, d, :], rhsV[:, 4 * q:4 * q + 4, :].rearrange("p a b -> p (a b)"), True, True,
                         reads=['ud', 'rhsV'], writes=['ps%d' % bk])
                    P.act(L[:, 4 * q:4 * q + 4, :].rearrange("p a b -> p (a b)"), self.ps[bk][:, :], AF.Exp, reads=['ps%d' % bk], writes=[lk])

            def stageB(n):
                d, ci, t, islast = jobs[n]
                L, lk = Ls[n % 2], 'Lseg%d' % (n % 2)
                xdt, xk = xdts[n % 2], 'xdt%d' % (n % 2)
                CBm, M, xw, tmp, Hf, Hb, eacs, cdv = CBm_[d], M_[d], xw_[d], tmp_[d], Hf_[d], Hb_[d], eacs_[d], cdv_[d]
                kCB, kM, kxw, ktmp, kHf, kHb, kea, kcd = ['%s%d' % (k_, d) for k_ in ('CBm', 'Mdiag', 'xw', 'ytmp', 'Hf', 'Hb', 'eacs', 'cdv')]
                iend = 127 if d == 0 else 0
                lat = t < 16
                dta_t = dta[:, t, d * 12:(d + 1) * 12]
                tok = slice(t * 128, (t + 1) * 128)
                if lat:
                    bk = self.nextps()
                    for g in range(2):
                        P.mm(self.ps[bk][:, g * 128:(g + 1) * 128], BT[:, g, tok], CT[:, g, tok], True, True, reads=['BT', 'CT'], writes=['ps%d' % bk])
                    P.tt(CBm[:], self.ps[bk][:, 0:256].rearrange("p (g i) -> p g i", g=2), vd[:, d, :].unsqueeze(1).to_broadcast([128, 2, 128]), T.mult,
                         reads=['ps%d' % bk, 'vd'], writes=[kCB])
                    P.tt(M[:].rearrange("p (g r) i -> p g r i", g=2), L[:].rearrange("p (g r) i -> p g r i", g=2),
                         CBm[:].unsqueeze(2).to_broadcast([128, 2, 6, 128]), T.mult, reads=[lk, kCB], writes=[kM])
                    bA = self.nextps()
                    bB = self.nextps()
                    for r in range(12):
                        dst = self.ps[bA][:, r * 64:(r + 1) * 64] if r < 8 else self.ps[bB][:, (r - 8) * 64:(r - 7) * 64]
                        P.mm(dst, M[:, r, :], xdt[:, r, :], True, True, reads=[kM, xk], writes=['ps%d' % (bA if r < 8 else bB)])
                    bo = [self.nextps(), self.nextps()]
                    for g in range(2):
                        P.mm(self.ps[bo[g]][:, 0:384], CT[:, g, tok], Hb[:, g * 384:(g + 1) * 384], True, True, reads=['CT', kHb], writes=['ps%d' % bo[g]])
                    bk = self.nextps()
                    P.mm(self.ps[bk][:, 0:12], vd[:, d, :], dta_t, True, True, reads=['vd', 'dta'], writes=['ps%d' % bk])
                    P.act(eacs[:], self.ps[bk][:, 0:12], AF.Exp, reads=['ps%d' % bk], writes=[kea])
                    for g in range(2):
                        P.tt(tmp[:, g * 384:(g + 1) * 384].rearrange("p (r e) -> p r e", r=6), self.ps[bo[g]][:, 0:384].rearrange("p (r e) -> p r e", r=6),
                             eacs[:, g * 6:(g + 1) * 6].unsqueeze(2).to_broadcast([128, 6, 64]), T.mult, reads=['ps%d' % bo[g], kea], writes=[ktmp])
                    P.tt(tmp[:, 0:512], tmp[:, 0:512], self.ps[bA][:, :], T.add, reads=[ktmp, 'ps%d' % bA], writes=[ktmp])
                    P.tt(tmp[:, 512:768], tmp[:, 512:768], self.ps[bB][:, 0:256], T.add, reads=[ktmp, 'ps%d' % bB], writes=[ktmp])
                    P.tt(Yacc[:, t, :], Yacc[:, t, :], tmp[:], T.add, reads=['Yacc%d' % t, ktmp], writes=['Yacc%d' % t], eng='pool')
                if not islast:
                    P.tt(xw[:], xdt[:], L[:, :, iend].unsqueeze(2).to_broadcast([128, 12, 64]), T.mult, reads=[xk, lk], writes=[kxw])
                    bs = [self.nextps(), self.nextps()]
                    for g in range(2):
                        P.mm(self.ps[bs[g]][:, 0:384], Btm[:, t, g, :], xw[:, 6 * g:6 * g + 6, :].rearrange("p a b -> p (a b)"), True, True,
                             reads=['Btm', kxw], writes=['ps%d' % bs[g]])
                    if ci == 0:
                        for g in range(2):
                            P.copy(Hf[:, g * 384:(g + 1) * 384], self.ps[bs[g]][:, 0:384], reads=['ps%d' % bs[g]], writes=[kHf])
                    else:
                        bk = self.nextps()
                        P.mm(self.ps[bk][:, 0:12], ones[:], dta_t, True, True, reads=['ones1', 'dta'], writes=['ps%d' % bk])
                        P.act(cdv[:], self.ps[bk][:, 0:12], AF.Exp, reads=['ps%d' % bk], writes=[kcd])
                        P.tt(Hf[:].rearrange("p (r e) -> p r e", r=12), Hf[:].rearrange("p (r e) -> p r e", r=12),
                             cdv[:].unsqueeze(2).to_broadcast([128, 12, 64]), T.mult, reads=[kHf, kcd], writes=[kHf])
                        for g in range(2):
                            P.tt(Hf[:, g * 384:(g + 1) * 384], Hf[:, g * 384:(g + 1) * 384], self.ps[bs[g]][:, 0:384], T.add,
                                 reads=[kHf, 'ps%d' % bs[g]], writes=[kHf])
                    P.copy(Hb[:], Hf[:], reads=[kHf], writes=[kHb], eng='act')
            stageA1(0)
            stageA2(0)
            for n in range(len(jobs)):
                if n + 1 < len(jobs):
                    stageA1(n + 1)
                stageB(n)
                if n + 1 < len(jobs):
                    stageA2(n + 1)

    def ssd_gate(self, Yacc, ZS, OTs):
        P = self.P
        T = ALU
        with self.scope() as s2:
            ng = self.sb(s2, "sdng", [128, 768])
            P.dma('sp', ng[:], self.din['sd_ng'], writes=['sdng'])
            zt = [self.sb(s2, "ztg%d" % i, [128, 768]) for i in range(2)]
            yz = [self.sb(s2, "yz%d" % i, [128, 768]) for i in range(2)]
            sq = self.sb(s2, "gsq", [128, 768])
            sm = self.sb(s2, "gsm", [128, 2])
            for t in range(16):
                z, zk = zt[t % 2], 'ztg%d' % (t % 2)
                y, yk = yz[t % 2], 'yz%d' % (t % 2)
                P.dma('sp', z[:], ZS[t * 128:(t + 1) * 128, :], reads=['ZS_%d' % t], writes=[zk])
                P.tt(y[:], Yacc[:, t, :], z[:], T.mult, reads=['Yacc%d' % t, zk], writes=[yk])
                P.act(sq[:], y[:], AF.Square, reads=[yk], writes=['gsq', 'gsm'], accum_out=sm[:, 0:1])
                P.ts(sm[:, 0:1], sm[:, 0:1], 1.0 / 768.0, EPS, T.mult, T.add, reads=['gsm'], writes=['gsm'])
                P.act(sm[:, 0:1], sm[:, 0:1], AF.Sqrt, reads=['gsm'], writes=['gsm'])
                P.recip(sm[:, 0:1], sm[:, 0:1], reads=['gsm'], writes=['gsm'])
                P.stt(y[:], y[:], sm[:, 0:1], ng[:], T.mult, T.mult, reads=[yk, 'gsm', 'sdng'], writes=[yk])
                for half in range(2):
                    bk = self.nextps()
                    for q in range(3):
                        c = half * 3 + q
                        P.tr(self.ps[bk][:, q * 128:(q + 1) * 128], y[:, c * 128:(c + 1) * 128], self.ident[:], reads=[yk, 'ident'], writes=['ps%d' % bk])
                    P.copy(OTs[:, half * 3:(half + 1) * 3, t * 128:(t + 1) * 128], self.ps[bk][:, 0:384].rearrange("p (q f) -> p q f", q=3),
                           reads=['ps%d' % bk], writes=['OTs'], eng=('act' if half == 0 else 'dve'))

    def declare_l1_inputs(self):
        for nm, shp in (('s_w_in', [D, 2328]), ('s_w_out', [D, D]), ('sd_cw', [128, 3, 10]), ('sd_cb', [128, 10]), ('sd_alog', [128, 24]),
                        ('sd_dtb', [128, 24]), ('sd_d', [128, 12]), ('sd_ng', [128, 768]), ('s5_lre', [128, 16]), ('s5_lim', [128, 16]),
                        ('s5_ldt', [128, 16]), ('s5_bre', [128, 8, 128]), ('s5_bim', [128, 8, 128]), ('s5_cre', [2, 128, 8, 128]),
                        ('s5_cim', [2, 128, 8, 128]), ('s5_dd', [128, 2]), ('s5_glu_w', [256, 256]), ('s5_glu_b', [128, 2]),
                        ('vd', [128, 2, 128]), ('ud', [128, 2, 128]), ('ones', [128, 512])):
            self.inp(nm, shp)

    def layer1(self, st, hsrc, dst, dkey, dbg=False, only=None):
        P = self.P
        self.mods(st, [1])
        with self.scope() as sL:
            S5O = self.sb(sL, "S5O", [128, 2, NLAT], BF16)
            self.phase_A(sL, hsrc, 1, NT)
            if only in (None, 's5'):
                self.s5_mixer(S5O)
            else:
                P.memset(S5O[:], 0.0, writes=['S5O'])
            if only == 's5':
                o = self.outp("dbg_S5O", [128, 2, NLAT], BF16)
                P.dma('sp', o, S5O[:], reads=['S5O'], writes=['dbg_S5O'])
                return
            H1 = self.scratch("H1b", [NLAT, D])
            ZS = self.scratch("ZS", [NLAT, 768])
            with self.scope() as sA:
                Yacc = self.sb(sA, "Yacc", [128, 16, 768])
                with self.scope() as sB:
                    Xtm = self.sb(sB, "Xtm", [128, NT, 768], BF16)
                    Btm = self.sb(sB, "Btm", [128, NT, 2, 128], BF16)
                    BT = self.sb(sB, "BT", [128, 2, NTOK], BF16)
                    CT = self.sb(sB, "CT", [128, 2, NTOK], BF16)
                    dtt = self.sb(sB, "dtt", [128, NT, 24])
                    dta = self.sb(sB, "dta", [128, NT, 24])
                    self.ssd_inproj(sB, Xtm, Btm, BT, CT, dtt, dta, ZS)
                    self.ssd_chunks(Xtm, Btm, BT, CT, dtt, dta, Yacc)
                with self.scope() as sC:
                    OTs = self.sb(sC, "OTs", [128, 6, NLAT], BF16)
                    self.ssd_gate(Yacc, ZS, OTs)
                    if only == 'ssd':
                        o = self.outp("dbg_OTs", [128, 6, NLAT], BF16)
                        P.dma('sp', o, OTs[:], reads=['OTs'], writes=['dbg_OTs'])
                        return
                    if dbg:
                        o = self.outp("dbg_S5O", [128, 2, NLAT], BF16)
                        P.dma('sp', o, S5O[:], reads=['S5O'], writes=['dbg_S5O'])
                        o = self.outp("dbg_OTs", [128, 6, NLAT], BF16)
                        P.dma('sp', o, OTs[:], reads=['OTs'], writes=['dbg_OTs'])

                    def lhs(j, t):
                        if j < 6:
                            return OTs[:, j, t * 128:(t + 1) * 128], 'OTs'
                        return S5O[:, j - 6, t * 128:(t + 1) * 128], 'S5O'
                    self.outproj_ln1(sC, 1, self.din['s_w_out'], hsrc, 16, H1, lhs=lhs)
        self.ffn(st, 1, 16, H1, dst, dkey, final=True)

    def build_layer1_test(self, only=None):
        with contextlib.ExitStack() as st:
            self.load_consts(st)
            self.declare_common_inputs()
            self.declare_l1_inputs()
            hin = self.inp("hin", [NTOK, D])
            self.UT = self.sb(st, "UT", [128, 8, NTOK], BF16)
            out = self.outp("out", [NLAT, D])
            self.final.remove("out")
            self.layer1(st, hin, out, 'out', dbg=True, only=only)
            if only is not None:
                self.dout.pop('out')
            self.P.emit(self.final)
        return self.nc

    def declare_common_inputs(self):
        for nm, shp in (('ln_g', [2, 2, 128, D]), ('ln_b', [2, 2, 128, D]), ('w_up', [2, D, 5632]), ('b_up', [2, 128, 44]),
                        ('conv_w', [2, 128, 3, 44]), ('conv_b', [2, 128, 44]), ('w_down', [2, 2816, D]), ('b_down', [2, 128, D]),
                        ('a_w_in', [D, 2560]), ('a_w_out', [D, D]), ('a_lam', [128, 256]), ('a_subg', [128, 128]),
                        ('f_wblk', [2, 128, 128]), ('f_b', [128, 2]), ('cos4', [128, NLAT]), ('sin4', [128, NLAT]), ('pswap', [128, 128]),
                        ('c64blk', [128, 128]), ('ns64blk', [128, 128])):
            self.inp(nm, shp)
        for nm, shp in (('dftc', [NLAT, NLAT]), ('dfts', [NLAT, NLAT]), ('dftc_c', [NCTX, NCTX]), ('dfts_c', [NCTX, NCTX])):
            self.inp(nm, shp, BF16)

    def build_layer0(self, dbg_ot=False):
        with contextlib.ExitStack() as st:
            self.load_consts(st)
            self.declare_common_inputs()
            xin = self.inp("xin", [NTOK, D])
            self.mods(st, [0])
            self.UT = self.sb(st, "UT", [128, 8, NTOK], BF16)
            H1 = self.scratch("H1", [NTOK, D])
            H2 = self.outp("H2", [NTOK, D])
            self.final.remove("H2")
            with self.scope() as s1:
                self.OT = self.sb(s1, "OT", [128, 8, NTOK], BF16)
                self.phase_A(s1, xin, 0, NT)
                self.attn_mixer()
                if dbg_ot:
                    o = self.outp("dbg_OT", [128, 8, NTOK], BF16)
                    self.P.dma('sp', o, self.OT[:], reads=['OT'], writes=['dbg_OT'])
                self.outproj_ln1(s1, 0, self.din['a_w_out'], xin, NT, H1)
            self.ffn(st, 0, NT, H1, H2, 'H2', final=True)
            self.P.emit(self.final)
        return self.nc

    def build_debug_ffn(self):
        with contextlib.ExitStack() as st:
            self.load_consts(st)
            for nm, shp in (('ln_g', [2, 2, 128, D]), ('ln_b', [2, 2, 128, D]), ('w_up', [2, D, 5632]), ('b_up', [2, 128, 44]),
                            ('conv_w', [2, 128, 3, 44]), ('conv_b', [2, 128, 44]), ('w_down', [2, 2816, D]), ('b_down', [2, 128, D]),
                            ('a_w_out', [D, D])):
                self.inp(nm, shp)
            xin = self.inp("xin", [NTOK, D])
            self.mods(st, [0])
            self.UT = self.sb(st, "UT", [128, 8, NTOK], BF16)
            H1 = self.scratch("H1", [NTOK, D])
            H2 = self.outp("H2", [NTOK, D])
            self.final.remove("H2")
            with self.scope() as s1:
                self.OT = self.sb(s1, "OT", [128, 8, NTOK], BF16)
                self.phase_A(s1, xin, 0, NT)
                o = self.outp("dbg_modc0", [128, 48, 2])
                self.P.dma('sp', o, self.modc[0][:], reads=['modc0'], writes=['dbg_modc0'])
                o = self.outp("dbg_grow0", [2, 2, 1024])
                self.P.dma('sp', o, self.grow[0], reads=['grow0'], writes=['dbg_grow0'])
                o = self.outp("dbg_UT", [128, 8, NTOK], BF16)
                self.P.dma('sp', o, self.UT[:], reads=['UT'], writes=['dbg_UT'])
                self.P.copy(self.OT[:], self.UT[:], reads=['UT'], writes=['OT'], eng='pool')
                self.outproj_ln1(s1, 0, self.din['a_w_out'], xin, NT, H1)
            self.ffn(st, 0, NT, H1, H2, 'H2', final=True)
            self.P.emit(self.final)
        return self.nc

    def build_full(self):
        with contextlib.ExitStack() as st:
            self.load_consts(st)
            self.declare_common_inputs()
            self.declare_l1_inputs()
            xin = self.inp("xin", [NTOK, D])
            self.mods(st, [0])
            self.UT = self.sb(st, "UT", [128, 8, NTOK], BF16)
            H1 = self.scratch("H1", [NTOK, D])
            H2 = self.scratch("H2", [NTOK, D])
            with self.scope() as s1:
                self.OT = self.sb(s1, "OT", [128, 8, NTOK], BF16)
                self.phase_A(s1, xin, 0, NT)
                self.attn_mixer()
                self.outproj_ln1(s1, 0, self.din['a_w_out'], xin, NT, H1)
            self.ffn(st, 0, NT, H1, H2, 'hin1')
            out = self.outp("out", [NLAT, D])
            self.final.remove("out")
            self.layer1(st, H2, out, 'out')
            self.P.emit(self.final)
        return self.nc


_CACHE = {}


def kernel(**inputs):
    inp = {k: np.asarray(v) for k, v in inputs.items()}
    if 'b' not in _CACHE:
        B = Builder()
        B.build_full()
        _CACHE['b'] = B
        _CACHE['c'] = host_consts()
    B = _CACHE['b']
    hc = _CACHE['c']
    in_maps = []
    for b in range(8):
        hl = host_layout(inp, b)
        in_maps.append({k: (hc[k] if k in hc else hl[k]) for k in B.din})
    res = run_bass_kernel_spmd(B.nc, in_maps, core_ids=list(range(8)))
    out = np.stack([np.asarray(r['out']) for r in res.results], axis=0)
    return out.astype(np.float32)
```

```python
import contextlib
import math
import numpy as np
import ml_dtypes
import concourse.bass as bass
import concourse.mybir as mybir
from concourse.bass_utils import run_bass_kernel_spmd

F32 = mybir.dt.float32
BF16 = mybir.dt.bfloat16
AF = mybir.ActivationFunctionType
ALU = mybir.AluOpType

ENGS = ['pe', 'act', 'dve', 'pool', 'sp']
POOL_TO_DVE = True
NDMASEM = 6

D = 1024
NLAT = 2048
NCTX = 256
NTOK = NLAT + NCTX
NT = NTOK // 128
ALPHA = (2 * 2) ** 0.25
EPS = 1e-5
LAM_INIT0 = 0.8 - 0.6 * math.exp(0.0)


class Prog:
    def __init__(self, nc):
        self.nc = nc
        self.ops = {e: [] for e in ENGS}
        self.last_w = {}
        self.readers = {}
        self.barriers = []
        self.bar_dma_start = {e: 0 for e in ENGS}

    def barrier(self):
        pts = []
        for e in ENGS:
            ops = self.ops[e]
            for i in range(len(ops) - 1, -1, -1):
                if not ops[i]['dma'] and ops[i]['fn'] is not None:
                    pts.append((e, i))
                    ops[i]['needed'] = True
                    break
            for i in range(self.bar_dma_start[e], len(ops)):
                if ops[i]['dma']:
                    pts.append((e, i))
            self.bar_dma_start[e] = len(ops)
        self.barriers.append(pts)

    def op(self, eng, fn, reads=(), writes=(), dma=False):
        if eng == 'pool' and not dma and POOL_TO_DVE:
            eng = 'dve'
        ops = self.ops[eng]
        idx = len(ops)
        deps = set()
        for k in reads:
            w = self.last_w.get(k)
            if w is not None:
                deps.add(w)
            if k.startswith('ps'):
                for r in self.readers.get(k, ()):
                    if r[0] != eng:
                        deps.add(r)
        for k in writes:
            w = self.last_w.get(k)
            if w is not None:
                deps.add(w)
            for r in self.readers.get(k, ()):
                deps.add(r)
        best = {}
        out = []
        for (e, i) in deps:
            d = self.ops[e][i]
            if d['dma']:
                out.append((e, i))
            else:
                if e == eng and not dma and eng == 'pe':
                    continue
                if e not in best or best[e] < i:
                    best[e] = i
        for e, i in best.items():
            out.append((e, i))
        for (e, i) in out:
            self.ops[e][i]['needed'] = True
        ops.append(dict(fn=fn, deps=out, dma=dma, needed=False, sem=None, val=None, prev=None, bar=len(self.barriers)))
        me = (eng, idx)
        for k in reads:
            lst = self.readers.setdefault(k, [])
            if not dma:
                lst[:] = [r for r in lst if not (r[0] == eng and not self.ops[r[0]][r[1]]['dma'])]
            lst.append(me)
        for k in writes:
            self.last_w[k] = me
            self.readers[k] = []
        return me

    def dma(self, eng, out, in_, reads=(), writes=(), **kw):
        return self.op(eng, lambda e: e.dma_start(out=out, in_=in_, **kw), reads, writes, dma=True)

    def act(self, out, in_, func, reads=(), writes=(), eng='act', **kw):
        return self.op(eng, lambda e: e.activation(out=out, in_=in_, func=func, **kw), reads, writes)

    def tt(self, out, in0, in1, op, reads=(), writes=(), eng='dve'):
        return self.op(eng, lambda e: e.tensor_tensor(out=out, in0=in0, in1=in1, op=op), reads, writes)

    def ts(self, out, in0, s1, s2, op0, op1=None, reads=(), writes=(), eng='dve'):
        if op1 is None:
            return self.op(eng, lambda e: e.tensor_scalar(out=out, in0=in0, scalar1=s1, scalar2=None, op0=op0), reads, writes)
        return self.op(eng, lambda e: e.tensor_scalar(out=out, in0=in0, scalar1=s1, scalar2=s2, op0=op0, op1=op1), reads, writes)

    def stt(self, out, in0, scalar, in1, op0, op1, reads=(), writes=()):
        return self.op('dve', lambda e: e.scalar_tensor_tensor(out=out, in0=in0, scalar=scalar, in1=in1, op0=op0, op1=op1), reads, writes)

    def copy(self, out, in_, reads=(), writes=(), eng='dve'):
        if eng == 'act':
            return self.op(eng, lambda e: e.activation(out=out, in_=in_, func=AF.Copy), reads, writes)
        return self.op(eng, lambda e: e.tensor_copy(out=out, in_=in_), reads, writes)

    def mm(self, out, lhsT, rhs, start, stop, reads=(), writes=()):
        return self.op('pe', lambda e: e.matmul(out, lhsT=lhsT, rhs=rhs, start=start, stop=stop), reads, writes)

    def tr(self, out, in_, ident, reads=(), writes=()):
        return self.op('pe', lambda e: e.transpose(out=out, in_=in_, identity=ident), reads, writes)

    def scan(self, out, d0, d1, initial, op0, op1, reads=(), writes=()):
        return self.op('dve', lambda e: e.tensor_tensor_scan(out=out, data0=d0, data1=d1, initial=initial, op0=op0, op1=op1), reads, writes)

    def memset(self, ap, val, writes=(), eng='dve'):
        return self.op(eng, lambda e: e.memset(ap, val), (), writes)

    def recip(self, out, in_, reads=(), writes=()):
        return self.op('dve', lambda e: e.reciprocal(out=out, in_=in_), reads, writes)

    def bnstats(self, out, in_, reads=(), writes=()):
        return self.op('dve', lambda e: e.bn_stats(out=out, in_=in_), reads, writes)

    def bnaggr(self, out, in_, reads=(), writes=()):
        return self.op('dve', lambda e: e.bn_aggr(out=out, in_=in_), reads, writes)

    def emit(self, final_keys=()):
        nc = self.nc
        self.op('sp', None, reads=list(final_keys), writes=())
        with contextlib.ExitStack() as st:
            csem = {e: st.enter_context(nc.semaphore("c_" + e)) for e in ENGS}
            dsem = {e: [st.enter_context(nc.semaphore("d_%s%d" % (e, i))) for i in range(NDMASEM)] for e in ENGS}
            for e in ENGS:
                cnt = 0
                dcnt = 0
                lastd = [None] * NDMASEM
                dval = [0] * NDMASEM
                for i, o in enumerate(self.ops[e]):
                    if o['dma']:
                        s = dcnt % NDMASEM
                        dcnt += 1
                        dval[s] += 16
                        o['sem'] = dsem[e][s]
                        o['val'] = dval[s]
                        o['prev'] = lastd[s]
                        lastd[s] = i
                    elif o['needed']:
                        cnt += 1
                        o['sem'] = csem[e]
                        o['val'] = cnt
            block = st.enter_context(nc.Block())

            def run(e, eng):
                waited = {}

                def wait(sem, val):
                    k = id(sem)
                    if waited.get(k, 0) < val:
                        eng.wait_ge(sem, val)
                        waited[k] = val
                bar_done = 0
                for o in self.ops[e]:
                    while bar_done < o['bar']:
                        for (de, di) in self.barriers[bar_done]:
                            d = self.ops[de][di]
                            wait(d['sem'], d['val'])
                        bar_done += 1
                    for (de, di) in o['deps']:
                        d = self.ops[de][di]
                        wait(d['sem'], d['val'])
                    if o['dma'] and o['prev'] is not None:
                        p = self.ops[e][o['prev']]
                        wait(p['sem'], p['val'])
                    if o['fn'] is None:
                        continue
                    ins = o['fn'](eng)
                    if o['dma']:
                        ins.then_inc(o['sem'], 16)
                    elif o['needed']:
                        ins.then_inc(o['sem'], 1)

            @block.tensor
            def _(eng):
                run('pe', eng)

            @block.scalar
            def _(eng):
                run('act', eng)

            @block.vector
            def _(eng):
                run('dve', eng)

            @block.gpsimd
            def _(eng):
                run('pool', eng)

            @block.sync
            def _(eng):
                run('sp', eng)


def host_consts():
    c = {}
    c['ident'] = np.eye(128, dtype=np.float32)
    ps = np.zeros((128, 128), np.float32)
    for m in range(128):
        partner = m + 32 if (m % 64) < 32 else m - 32
        ps[partner, m] = 1.0
    c['pswap'] = ps
    rows = NLAT // 64
    row = np.repeat(np.arange(rows, dtype=np.float32), 64)
    col = np.tile(np.arange(64, dtype=np.float32), rows)
    nf = 16
    inv = (10000.0 ** (-np.arange(nf, dtype=np.float32) / nf)).astype(np.float32)
    ang = np.concatenate([row[:, None] * inv, col[:, None] * inv], axis=-1).astype(np.float32)
    cs, sn = np.cos(ang).astype(np.float32), np.sin(ang).astype(np.float32)
    cos4 = np.zeros((128, NLAT), np.float32)
    sin4 = np.zeros((128, NLAT), np.float32)
    for p in range(128):
        cos4[p] = cs[:, p % 32]
        sin4[p] = sn[:, p % 32] * (-1.0 if (p % 64) < 32 else 1.0)
    c['cos4'] = cos4
    c['sin4'] = sin4

    def dft(n):
        k = np.arange(n, dtype=np.int64)
        kk = (k[:, None] * k[None, :]) % n
        a = 2.0 * np.pi * kk.astype(np.float64) / n
        return np.cos(a) / np.sqrt(n), np.sin(a) / np.sqrt(n)
    C, S = dft(NLAT)
    c['dftc'] = C.astype(ml_dtypes.bfloat16)
    c['dfts'] = S.astype(ml_dtypes.bfloat16)
    C, S = dft(NCTX)
    c['dftc_c'] = C.astype(ml_dtypes.bfloat16)
    c['dfts_c'] = S.astype(ml_dtypes.bfloat16)
    C, S = dft(64)
    cb = np.zeros((128, 128), np.float32)
    sb = np.zeros((128, 128), np.float32)
    for g in range(2):
        cb[g * 64:(g + 1) * 64, g * 64:(g + 1) * 64] = C
        sb[g * 64:(g + 1) * 64, g * 64:(g + 1) * 64] = -S
    c['c64blk'] = cb
    c['ns64blk'] = sb
    sel = np.zeros((2, 2, 128), np.float32)
    sel[0, 0, :] = 1.0
    sel[1, 1, :] = 1.0
    c['sel'] = sel
    k = np.arange(128)
    vd = np.zeros((128, 2, 128), np.float32)
    ud = np.zeros((128, 2, 128), np.float32)
    vd[:, 0, :] = (k[:, None] <= k[None, :])
    vd[:, 1, :] = (k[:, None] >= k[None, :])
    ud[:, 0, :] = (k[:, None] > k[None, :])
    ud[:, 1, :] = (k[:, None] < k[None, :])
    c['vd'] = vd
    c['ud'] = ud
    c['ones'] = np.ones((128, 512), np.float32)
    return c


def colvec(v, n):
    return np.ascontiguousarray(np.asarray(v, np.float32).reshape(n, 128).T)


def bcast(v, p=128):
    v = np.asarray(v, np.float32).reshape(1, -1)
    return np.ascontiguousarray(np.broadcast_to(v, (p, v.shape[1])))


def host_layout(inp, b):
    m = {}
    m['xin'] = np.ascontiguousarray(np.concatenate([inp['x'][b], inp['ctx'][b]], axis=0))
    cv = np.stack([colvec(inp['c'][b], 8), colvec(inp['c_ctx'], 8)], axis=-1)
    m['cvec'] = np.ascontiguousarray(cv)
    m['ada_w'] = inp['ada_w']
    m['ada_bc'] = np.ascontiguousarray(np.stack([colvec(inp['ada_b'][l], 48) for l in range(2)]))
    m['ada_b2'] = np.ascontiguousarray(np.stack([np.stack([inp['ada_b'][l]] * 2) for l in range(2)]))
    m['ln_g'] = np.ascontiguousarray(np.stack([np.stack([bcast(inp['ln_g'][l][i]) for i in range(2)]) for l in range(2)]))
    m['ln_b'] = np.ascontiguousarray(np.stack([np.stack([bcast(inp['ln_b'][l][i]) for i in range(2)]) for l in range(2)]))
    m['w_up'] = inp['ffn_w_up']
    m['b_up'] = np.ascontiguousarray(np.stack([colvec(inp['ffn_b_up'][l], 44) for l in range(2)]))
    m['conv_w'] = np.ascontiguousarray(np.stack([np.stack([colvec(inp['ffn_conv_w'][l][k], 44) for k in range(3)], axis=1) for l in range(2)]))
    m['conv_b'] = np.ascontiguousarray(np.stack([colvec(inp['ffn_conv_b'][l], 44) for l in range(2)]))
    m['w_down'] = inp['ffn_w_down']
    m['b_down'] = np.ascontiguousarray(np.stack([bcast(inp['ffn_b_down'][l]) for l in range(2)]))
    m['a_w_in'] = inp['attn_w_in'][0]
    m['a_w_out'] = inp['attn_w_out'][0]
    m['a_lam'] = bcast(inp['attn_lambda'][0].reshape(-1))
    m['a_subg'] = bcast(inp['attn_subln_g'][0])
    fw = inp['fourier_w'][0]
    wb = np.zeros((2, 128, 128), np.float32)
    for ch in range(2):
        for g in range(2):
            wb[ch, g * 64:(g + 1) * 64, g * 64:(g + 1) * 64] = fw[ch * 2 + g]
    m['f_wblk'] = wb
    m['f_b'] = colvec(inp['fourier_b'][0], 2)
    m['s_w_in'] = inp['ssm_w_in'][0]
    m['s_w_out'] = inp['ssm_w_out'][0]
    m['sd_cw'] = np.ascontiguousarray(np.stack([colvec(inp['ssd_conv_w'][0][k], 10) for k in range(3)], axis=1))
    m['sd_cb'] = colvec(inp['ssd_conv_b'][0], 10)
    m['sd_alog'] = bcast(inp['ssd_a_log'][0].reshape(-1))
    m['sd_dtb'] = bcast(inp['ssd_dt_bias'][0].reshape(-1))
    m['sd_d'] = bcast(inp['ssd_d'][0])
    m['sd_ng'] = bcast(inp['ssd_norm_g'][0])

    def dsc(a):
        out = np.zeros((128, 16), np.float32)
        for d in range(2):
            for sc in range(8):
                for half in range(2):
                    out[half * 64:(half + 1) * 64, d * 8 + sc] = a[d, 2 * sc + half, :]
        return out
    m['s5_lre'] = dsc(inp['s5_lambda_re'][0])
    m['s5_lim'] = dsc(inp['s5_lambda_im'][0])
    m['s5_ldt'] = dsc(np.broadcast_to(inp['s5_log_dt'][0][:, :, None], (2, 16, 64)))

    def bexp(b):
        out = np.zeros((128, 8, 128), np.float32)
        for sc in range(8):
            for half in range(2):
                g = 2 * sc + half
                col = (g % 8) * 16
                out[half * 64:(half + 1) * 64, sc, col:col + 16] = b[g]
        return out
    m['s5_bre'] = bexp(inp['s5_b_re'][0])
    m['s5_bim'] = bexp(inp['s5_b_im'][0])

    def cexp(c):
        out = np.zeros((2, 128, 8, 128), np.float32)
        for d in range(2):
            for sc in range(8):
                for half in range(2):
                    g = 2 * sc + half
                    col = (g % 8) * 16
                    out[d, half * 64:(half + 1) * 64, sc, col:col + 16] = c[d, g].T
        return out
    m['s5_cre'] = cexp(inp['s5_c_re'][0])
    m['s5_cim'] = cexp(inp['s5_c_im'][0])
    m['s5_dd'] = colvec(inp['s5_d'][0], 2)
    m['s5_glu_w'] = inp['s5_glu_w'][0]
    m['s5_glu_b'] = colvec(inp['s5_glu_b'][0], 2)
    return m


class Builder:
    def __init__(self, debug=()):
        self.debug = set(debug)
        self.nc = bass.Bass("TRN2", target_bir_lowering=False)
        self.P = Prog(self.nc)
        self.din = {}
        self.dout = {}
        self.final = []
        self.uid = 0
        self.ps = [self.nc.alloc_psum_tensor("ps%d" % i, [128, 512], F32) for i in range(8)]
        self.psrr = 0

    def inp(self, name, shape, dt=F32):
        self.din[name] = self.nc.dram_tensor(name, list(shape), dt, kind="ExternalInput").ap()
        return self.din[name]

    def outp(self, name, shape, dt=F32):
        self.dout[name] = self.nc.dram_tensor(name, list(shape), dt, kind="ExternalOutput").ap()
        self.final.append(name)
        return self.dout[name]

    def scratch(self, name, shape, dt=F32):
        return self.nc.dram_tensor(name, list(shape), dt, kind="Internal").ap()

    def sb(self, st, name, shape, dt=F32):
        self.uid += 1
        return st.enter_context(self.nc.sbuf_tensor("s%d_%s" % (self.uid, name), list(shape), dt))

    @contextlib.contextmanager
    def scope(self):
        with contextlib.ExitStack() as s2:
            yield s2
        self.P.barrier()

    def nextps(self, lo=0, hi=8):
        n = hi - lo
        i = lo + (self.psrr % n)
        self.psrr += 1
        return i

    def dbg(self, name, tile_ap, key, shape, dt=F32):
        if name in self.debug:
            o = self.outp("dbg_" + name, shape, dt)
            self.P.dma('sp', o, tile_ap, reads=[key], writes=["dbg_" + name])

    def load_consts(self, st):
        P = self.P
        self.ident = self.sb(st, "ident", [128, 128])
        P.dma('sp', self.ident[:], self.inp("ident", [128, 128]), writes=['ident'])
        self.sel = self.sb(st, "sel", [2, 2, 128])
        P.dma('sp', self.sel[:], self.inp("sel", [2, 2, 128]), writes=['sel'])

    def mods(self, st, layers):
        P, nc = self.P, self.nc
        if 'cvec' not in self.din:
            self.inp("cvec", [128, 8, 2])
            self.inp("ada_w", [2, D, 6 * D])
            self.inp("ada_bc", [2, 128, 48])
            self.inp("ada_b2", [2, 2, 6 * D])
            self.modc = [self.sb(st, "modc%d" % l, [128, 48, 2]) for l in range(2)]
            g = self.sb(st, "grow", [2, 2, 1024])
            self.grow = [g[:], g[:]]
            for l in range(2):
                P.memset(self.modc[l][:], 0.0, writes=['modc%d' % l])
        cvec, ada_w, ada_bc, ada_b2 = self.din['cvec'], self.din['ada_w'], self.din['ada_bc'], self.din['ada_b2']
        with self.scope() as s2:
            cv = self.sb(s2, "cv", [128, 8, 2])
            sT = self.sb(s2, "sT", [128, 8, 2], BF16)
            abc = self.sb(s2, "abc", [128, 2, 48])
            ab2 = self.sb(s2, "ab2", [2, 2, 2, 1024])
            P.dma('sp', cv[:], cvec, writes=['cv'])
            P.dma('sp', abc[:], ada_bc.rearrange("l p f -> p l f"), writes=['abc'])
            for l in layers:
                for gi, m in enumerate((2, 5)):
                    P.dma('sp', ab2[:, l, gi, :], ada_b2[l, :, m * 1024:(m + 1) * 1024], writes=['ab2'])
            P.act(sT[:], cv[:], AF.Silu, reads=['cv'], writes=['sT'])
            NWB = 4
            wbuf = [self.sb(s2, "adaw%d" % i, [128, 8, 1024], BF16) for i in range(NWB)]
            n = 0
            for l in layers:
                for m in range(6):
                    wb = wbuf[n % NWB]
                    wk = 'adaw%d' % (n % NWB)
                    n += 1
                    P.dma('pool', wb[:], ada_w[l, :, m * 1024:(m + 1) * 1024].rearrange("(kc p) f -> p kc f", p=128), writes=[wk])
                    if m in (0, 1, 3, 4):
                        bk = self.nextps()
                        pst = self.ps[bk]
                        for j in range(8):
                            for kc in range(8):
                                P.mm(pst[:, 2 * j:2 * j + 2], wb[:, kc, j * 128:(j + 1) * 128], sT[:, kc, :], kc == 0, kc == 7,
                                     reads=[wk, 'sT'], writes=['ps%d' % bk])
                        P.tt(self.modc[l][:, m * 8:(m + 1) * 8, :], pst[:, 0:16].rearrange("p (j w) -> p j w", w=2),
                             abc[:, l, m * 8:(m + 1) * 8].unsqueeze(2).to_broadcast([128, 8, 2]), ALU.add,
                             reads=['ps%d' % bk, 'abc'], writes=['modc%d' % l])
                        if m in (1, 4):
                            P.ts(self.modc[l][:, m * 8:(m + 1) * 8, :], self.modc[l][:, m * 8:(m + 1) * 8, :], 1.0, None, ALU.add,
                                 reads=['modc%d' % l], writes=['modc%d' % l])
                    else:
                        gi = 0 if m == 2 else 1
                        for half in range(2):
                            bk = self.nextps()
                            pst = self.ps[bk]
                            for kc in range(8):
                                P.mm(pst[0:2, :], sT[:, kc, :], wb[:, kc, half * 512:(half + 1) * 512], kc == 0, kc == 7,
                                     reads=[wk, 'sT'], writes=['ps%d' % bk])
                            P.tt(self.grow[l][:, gi, half * 512:(half + 1) * 512], pst[0:2, :], ab2[:, l, gi, half * 512:(half + 1) * 512], ALU.add,
                                 reads=['ps%d' % bk, 'ab2'], writes=['grow'])

    def gate_bcast(self, dst, dkey, l, gi, w):
        P = self.P
        for half in range(2):
            bk = self.nextps()
            P.mm(self.ps[bk][:, :], self.sel[:, w, :], self.grow[l][:, gi, half * 512:(half + 1) * 512], True, True,
                 reads=['sel', 'grow'], writes=['ps%d' % bk])
            P.copy(dst[:, half * 512:(half + 1) * 512], self.ps[bk][:, :], reads=['ps%d' % bk], writes=[dkey], eng='act')

    def transpose_mod(self, src_tile, skey, UT, ukey, t, l, m_shift, m_scale):
        P = self.P
        w = 0 if t < 16 else 1
        for half in range(2):
            bk = self.nextps()
            for q in range(4):
                j = half * 4 + q
                P.tr(self.ps[bk][:, q * 128:(q + 1) * 128], src_tile[:, j * 128:(j + 1) * 128], self.ident[:],
                     reads=[skey, 'ident'], writes=['ps%d' % bk])
            for q in range(4):
                j = half * 4 + q
                P.act(UT[:, j, t * 128:(t + 1) * 128], self.ps[bk][:, q * 128:(q + 1) * 128], AF.Identity,
                      reads=['ps%d' % bk, 'modc%d' % l], writes=[ukey],
                      scale=self.modc[l][:, m_scale * 8 + j, w:w + 1], bias=self.modc[l][:, m_shift * 8 + j, w:w + 1])

    def phase_A(self, st, src, l, ntiles):
        P = self.P
        UT = self.UT
        with self.scope() as s2:
            hb = [self.sb(s2, "hA%d" % i, [128, D]) for i in range(2)]
            for t in range(ntiles):
                h = hb[t % 2]
                hk = 'hA%d' % (t % 2)
                P.dma('sp', h[:], src[t * 128:(t + 1) * 128, :], reads=['hin%d_%d' % (l, t)], writes=[hk])
                self.transpose_mod(h, hk, UT, 'UT', t, l, 0, 1)

    def layer_norm(self, r, rkey, g, b, gbkey, out, okey, s2tmp):
        P = self.P
        st6, mv, rstd, nmr, tk = s2tmp
        for c in range(2):
            P.bnstats(st6[:, c, :], r[:, c * 512:(c + 1) * 512], reads=[rkey], writes=['lnst' + tk])
        P.bnaggr(mv[:], st6[:].rearrange("p a b -> p (a b)"), reads=['lnst' + tk], writes=['lnmv' + tk])
        P.ts(rstd[:], mv[:, 1:2], EPS, None, ALU.add, reads=['lnmv' + tk], writes=['lnrstd' + tk])
        P.act(rstd[:], rstd[:], AF.Sqrt, reads=['lnrstd' + tk], writes=['lnrstd' + tk])
        P.recip(rstd[:], rstd[:], reads=['lnrstd' + tk], writes=['lnrstd' + tk])
        P.stt(nmr[:], mv[:, 0:1], -1.0, rstd[:], ALU.mult, ALU.mult, reads=['lnmv' + tk, 'lnrstd' + tk], writes=['lnnmr' + tk])
        P.act(out[:], r[:], AF.Identity, reads=[rkey, 'lnrstd' + tk, 'lnnmr' + tk], writes=[okey], scale=rstd[:, 0:1], bias=nmr[:, 0:1])
        P.tt(out[:], out[:], g, ALU.mult, reads=[okey, gbkey], writes=[okey], eng='pool')
        P.tt(out[:], out[:], b, ALU.add, reads=[okey, gbkey], writes=[okey], eng='pool')

    def outproj_ln1(self, st, l, w_out_d, hsrc, ntiles, H1, lhs=None):
        P = self.P
        OT, UT = getattr(self, 'OT', None), self.UT
        with self.scope() as s2:
            wo = self.sb(s2, "wo", [128, 8, D], BF16)
            P.dma('pool', wo[:], w_out_d.rearrange("(kc p) f -> p kc f", p=128), writes=['wo'])
            lng = self.sb(s2, "lng", [128, D])
            lnb = self.sb(s2, "lnb", [128, D])
            P.dma('sp', lng[:], self.din['ln_g'][l, 0], writes=['lngb'])
            P.dma('sp', lnb[:], self.din['ln_b'][l, 0], writes=['lngb'])
            gts = []
            for w in range(2 if ntiles > 16 else 1):
                gt = self.sb(s2, "gmix%d" % w, [128, D])
                self.gate_bcast(gt, 'gmix%d' % w, l, 0, w)
                gts.append(gt)
            hb = [self.sb(s2, "hB%d" % i, [128, D]) for i in range(2)]
            rb = [self.sb(s2, "rB%d" % i, [128, D]) for i in range(2)]
            ob = [self.sb(s2, "oB%d" % i, [128, D]) for i in range(2)]
            tmps = [(self.sb(s2, "lnst6", [128, 2, 6]), self.sb(s2, "lnmv", [128, 2]), self.sb(s2, "lnrstd", [128, 1]), self.sb(s2, "lnnmr", [128, 1]), 'A%d' % i) for i in range(2)]
            mmb = {}

            def mm_pe(t):
                h, hk = hb[t % 2], 'hB%d' % (t % 2)
                P.dma('sp', h[:], hsrc[t * 128:(t + 1) * 128, :], reads=['hin%d_%d' % (l, t)], writes=[hk])
                mmb[t] = []
                for half in range(2):
                    bk = self.nextps()
                    mmb[t].append(bk)
                    for j in range(8):
                        la, lk = lhs(j, t) if lhs is not None else (OT[:, j, t * 128:(t + 1) * 128], 'OT')
                        P.mm(self.ps[bk][:, :], la, wo[:, j, half * 512:(half + 1) * 512], j == 0, j == 7,
                             reads=[lk, 'wo'], writes=['ps%d' % bk])

            def mm_dve(t):
                w = 0 if t < 16 else 1
                h, hk = hb[t % 2], 'hB%d' % (t % 2)
                r, rk = rb[t % 2], 'rB%d' % (t % 2)
                for half in range(2):
                    bk = mmb[t][half]
                    P.tt(r[:, half * 512:(half + 1) * 512], self.ps[bk][:, :], gts[w][:, half * 512:(half + 1) * 512], ALU.mult,
                         reads=['ps%d' % bk, 'gmix%d' % w], writes=[rk])
                P.stt(r[:], h[:], ALPHA, r[:], ALU.mult, ALU.add, reads=[hk, rk], writes=[rk])

            def ln_tile(t):
                r, rk = rb[t % 2], 'rB%d' % (t % 2)
                o, ok = ob[t % 2], 'oB%d' % (t % 2)
                self.layer_norm(r, rk, lng[:], lnb[:], 'lngb', o, ok, tmps[t % 2])
                P.dma('sp', H1[t * 128:(t + 1) * 128, :], o[:], reads=[ok], writes=['H1_%d_%d' % (l, t)])
                if t == 0:
                    self.dbg('h1_%d' % l, o[:], ok, [128, D])
                self.transpose_mod(o, ok, UT, 'UT', t, l, 3, 4)
            mm_pe(0)
            mm_dve(0)
            for t in range(ntiles):
                if t + 1 < ntiles:
                    mm_pe(t + 1)
                ln_tile(t)
                if t + 1 < ntiles:
                    mm_dve(t + 1)

    def ffn(self, st, l, ntiles, H1, dst, dkey, final=False):
        P = self.P
        UT = self.UT
        w_up = self.din['w_up'][l].rearrange("(kc p) f -> p kc f", p=128)
        w_dn = self.din['w_down'][l].rearrange("(j p) f -> p j f", p=128)
        if ntiles > 16:
            sbs = [[(0, 6, False, True)], [(6, 6, True, True)], [(12, 4, True, False), (16, 2, False, False)]]
        else:
            sbs = [[(0, 6, False, True)], [(6, 6, True, True)], [(12, 4, True, False)]]
        with self.scope() as s2:
            wd = self.sb(s2, "wd", [128, 22, D], BF16)
            for q in range(2):
                P.dma('pool', wd[:, q * 11:(q + 1) * 11, :], w_dn[:, q * 11:(q + 1) * 11, :], writes=['wd'])
            bup = self.sb(s2, "bup", [128, 44])
            cw = self.sb(s2, "cw", [128, 3, 44])
            cb = self.sb(s2, "cb", [128, 44])
            P.dma('sp', bup[:], self.din['b_up'][l], writes=['ffc'])
            P.dma('sp', cw[:], self.din['conv_w'][l], writes=['ffc'])
            P.dma('sp', cb[:], self.din['conv_b'][l], writes=['ffc'])
            lng = self.sb(s2, "lng2", [128, D])
            lnb = self.sb(s2, "lnb2", [128, D])
            bdn = self.sb(s2, "bdn", [128, D])
            P.dma('sp', lng[:], self.din['ln_g'][l, 1], writes=['lngb2'])
            P.dma('sp', lnb[:], self.din['ln_b'][l, 1], writes=['lngb2'])
            P.dma('sp', bdn[:], self.din['b_down'][l], writes=['bdn'])
            gts = []
            for w in range(2 if ntiles > 16 else 1):
                gt = self.sb(s2, "gffn%d" % w, [128, D])
                self.gate_bcast(gt, 'gffn%d' % w, l, 1, w)
                gts.append(gt)
            W = 774 + 2
            GT = self.sb(s2, "GT", [128, 22, W], BF16)
            xs = [self.sb(s2, "xs%d" % i, [128, W], BF16) for i in range(4)]
            dg = [self.sb(s2, "dg%d" % i, [128, 6, 128], BF16) for i in range(2)]
            sgb = [self.sb(s2, "sg%d" % i, [128, 512]) for i in range(2)]
            idb = self.sb(s2, "identb", [128, 128], BF16)
            P.copy(idb[:], self.ident[:], reads=['ident'], writes=['identb'])
            wub = [self.sb(s2, "wu%d" % i, [128, 8, 256], BF16) for i in range(3)]
            hb = [self.sb(s2, "hF%d" % i, [128, D]) for i in range(2)]
            rb = [self.sb(s2, "rF%d" % i, [128, D]) for i in range(2)]
            ob = [self.sb(s2, "oF%d" % i, [128, D]) for i in range(2)]
            tmps = [(self.sb(s2, "lnst6b", [128, 2, 6]), self.sb(s2, "lnmvb", [128, 2]), self.sb(s2, "lnrstdb", [128, 1]), self.sb(s2, "lnnmrb", [128, 1]), 'B%d' % i) for i in range(2)]
            for i in range(4):
                P.memset(xs[i][:], 0.0, writes=['xs%d' % i], eng='pool')
            nw = 0
            nsg = 0
            njobs = len(sbs) * 22

            def load_w(n):
                j = n % 22
                wu, wuk = wub[n % 3], 'wu%d' % (n % 3)
                P.dma('pool', wu[:, :, 0:128], w_up[:, :, j * 128:(j + 1) * 128], writes=[wuk])
                P.dma('pool', wu[:, :, 128:256], w_up[:, :, 2816 + j * 128:2816 + (j + 1) * 128], writes=[wuk])
            load_w(0)
            load_w(1)
            for sbi, segs in enumerate(sbs):
                cols = []
                c0 = 0
                for (t0, nt_, hl, hr) in segs:
                    cols.append(c0)
                    c0 += nt_ * 128 + 2
                width = c0
                if sbi > 0:
                    for i in range(4):
                        P.memset(xs[i][:, 0:width], 0.0, writes=['xs%d' % i], eng='pool')
                ogroups = []
                for si, (t0, nt_, hl, hr) in enumerate(segs):
                    n = nt_ * 128
                    off = 0
                    while off < n:
                        g = min(512, n - off)
                        ogroups.append((cols[si] + 1 + off, g))
                        off += g

                def conv_job(nwj, j):
                    nonlocal nsg
                    d_, dk_ = dg[nwj % 2], 'dg%d' % (nwj % 2)
                    for (cpos, g) in ogroups:
                        pb = []
                        for vg in (1, 0):
                            xi = (nwj % 2) * 2 + vg
                            x, xk = xs[xi], 'xs%d' % xi
                            bk = self.nextps()
                            pb.append(bk)
                            for tap in range(3):
                                P.mm(self.ps[bk][:, 0:g], d_[:, vg * 3 + tap, :], x[:, cpos + tap - 1:cpos + tap - 1 + g], tap == 0, tap == 2,
                                     reads=[dk_, xk], writes=['ps%d' % bk])
                        sg, sgk = sgb[nsg % 2], 'sg%d' % (nsg % 2)
                        nsg += 1
                        P.act(sg[:, 0:g], self.ps[pb[0]][:, 0:g], AF.Silu, reads=['ps%d' % pb[0], 'ffc'], writes=[sgk], bias=cb[:, 22 + j:22 + j + 1])
                        P.stt(GT[:, j, cpos:cpos + g], self.ps[pb[1]][:, 0:g], cb[:, j:j + 1], sg[:, 0:g], ALU.add, ALU.mult,
                              reads=['ps%d' % pb[1], 'ffc', sgk], writes=['GT'])
                pend = None
                for j in range(22):
                    wu, wuk = wub[nw % 3], 'wu%d' % (nw % 3)
                    if nw + 2 < njobs:
                        load_w(nw + 2)
                    d_, dk_ = dg[nw % 2], 'dg%d' % (nw % 2)
                    for vg in range(2):
                        ch = vg * 22 + j
                        for tap in range(3):
                            P.act(d_[:, vg * 3 + tap, :], idb[:], AF.Identity, reads=['identb', 'ffc'], writes=[dk_], scale=cw[:, tap, ch:ch + 1], bias=0.0)
                    for vg in range(2):
                        xi = (nw % 2) * 2 + vg
                        x, xk = xs[xi], 'xs%d' % xi
                        ch = vg * 22 + j
                        for si, (t0, nt_, hl, hr) in enumerate(segs):
                            base = cols[si]
                            tok0 = t0 * 128
                            groups = []
                            n = nt_ * 128
                            off = 0
                            while off < n:
                                g = min(512, n - off)
                                groups.append((tok0 + off, g, base + 1 + off))
                                off += g
                            if hl:
                                groups.append((tok0 - 1, 1, base))
                            if hr:
                                groups.append((tok0 + n, 1, base + 1 + n))
                            for (tk, g, cpos) in groups:
                                bk = self.nextps()
                                for kc in range(8):
                                    P.mm(self.ps[bk][:, 0:g], wu[:, kc, vg * 128:(vg + 1) * 128], UT[:, kc, tk:tk + g], kc == 0, kc == 7,
                                         reads=[wuk, 'UT'], writes=['ps%d' % bk])
                                P.act(x[:, cpos:cpos + g], self.ps[bk][:, 0:g], AF.Identity, reads=['ps%d' % bk, 'ffc'], writes=[xk],
                                      bias=bup[:, ch:ch + 1])
                    if pend is not None:
                        conv_job(*pend)
                    pend = (nw, j)
                    nw += 1
                conv_job(*pend)
                for si, (t0, nt_, hl, hr) in enumerate(segs):
                    base = cols[si]
                    for ti in range(nt_):
                        t = t0 + ti
                        w = 0 if t < 16 else 1
                        h, hk = hb[t % 2], 'hF%d' % (t % 2)
                        r, rk = rb[t % 2], 'rF%d' % (t % 2)
                        o, ok = ob[t % 2], 'oF%d' % (t % 2)
                        P.dma('sp', h[:], H1[t * 128:(t + 1) * 128, :], reads=['H1_%d_%d' % (l, t)], writes=[hk])
                        c1 = base + 1 + ti * 128
                        for half in range(2):
                            bk = self.nextps()
                            for j in range(22):
                                P.mm(self.ps[bk][:, :], GT[:, j, c1:c1 + 128], wd[:, j, half * 512:(half + 1) * 512], j == 0, j == 21,
                                     reads=['GT', 'wd'], writes=['ps%d' % bk])
                            P.tt(r[:, half * 512:(half + 1) * 512], self.ps[bk][:, :], bdn[:, half * 512:(half + 1) * 512], ALU.add,
                                 reads=['ps%d' % bk, 'bdn'], writes=[rk])
                        P.tt(r[:], r[:], gts[w][:], ALU.mult, reads=[rk, 'gffn%d' % w], writes=[rk], eng='pool')
                        P.stt(r[:], h[:], ALPHA, r[:], ALU.mult, ALU.add, reads=[hk, rk], writes=[rk])
                        self.layer_norm(r, rk, lng[:], lnb[:], 'lngb2', o, ok, tmps[t % 2])
                        P.dma('sp', dst[t * 128:(t + 1) * 128, :], o[:], reads=[ok], writes=['%s_%d' % (dkey, t)])
                        if final:
                            self.final.append('%s_%d' % (dkey, t))

    def attn_mixer(self, ntiles=NT):
        P = self.P
        UT, OT = self.UT, self.OT
        w_in = self.din['a_w_in'].rearrange("(kc p) f -> p kc f", p=128)
        blocks = [(0, 512), (512, 512), (1024, 512), (1536, 512), (2048, 256)]
        with self.scope() as s1:
            VA = self.sb(s1, "VA", [128, NT, 6, 130], BF16)
            FT = self.sb(s1, "FT", [128, 2, NTOK], BF16)
            P.memset(VA[:, :, :, 128:130], 1.0, writes=['VA'], eng='pool')
            with self.scope() as s2:
                wv = self.sb(s2, "wv", [128, 8, 1024], BF16)
                for q in range(2):
                    P.dma('pool', wv[:, :, q * 512:(q + 1) * 512], w_in[:, :, 1536 + q * 512:1536 + (q + 1) * 512], writes=['wv'])
                for t in range(NT):
                    for half in range(2):
                        bk = self.nextps()
                        for kc in range(8):
                            P.mm(self.ps[bk][:, 0:384], UT[:, kc, t * 128:(t + 1) * 128], wv[:, kc, half * 384:(half + 1) * 384], kc == 0, kc == 7,
                                 reads=['UT', 'wv'], writes=['ps%d' % bk])
                        P.copy(VA[:, t, half * 3:(half + 1) * 3, 0:128], self.ps[bk][:, 0:384].rearrange("p (a b) -> p a b", a=3),
                               reads=['ps%d' % bk], writes=['VA'], eng=('act' if half == 0 else 'dve'))
                for c in range(2):
                    for (t0, n) in blocks:
                        bk = self.nextps()
                        for kc in range(8):
                            P.mm(self.ps[bk][:, 0:n], wv[:, kc, 768 + c * 128:768 + (c + 1) * 128], UT[:, kc, t0:t0 + n], kc == 0, kc == 7,
                                 reads=['UT', 'wv'], writes=['ps%d' % bk])
                        P.copy(FT[:, c, t0:t0 + n], self.ps[bk][:, 0:n], reads=['ps%d' % bk], writes=['FT'], eng='act')
            with self.scope() as s2:
                cos4 = self.sb(s2, "cos4", [128, NLAT])
                sin4 = self.sb(s2, "sin4", [128, NLAT])
                psw = self.sb(s2, "pswap", [128, 128])
                P.dma('sp', cos4[:], self.din['cos4'], writes=['cos4'])
                P.dma('sp', sin4[:], self.din['sin4'], writes=['sin4'])
                P.dma('sp', psw[:], self.din['pswap'], writes=['pswap'])
                lamt = self.sb(s2, "lamt", [128, 256])
                gsub = self.sb(s2, "gsub", [128, 128])
                P.dma('sp', lamt[:], self.din['a_lam'], writes=['lamt'])
                P.dma('sp', gsub[:], self.din['a_subg'], writes=['gsub'])
                P.ts(gsub[:], gsub[:], 1.0 - LAM_INIT0, None, ALU.mult, reads=['gsub'], writes=['gsub'])
                lprod = self.sb(s2, "lprod", [128, 128])
                lsum = self.sb(s2, "lsum", [128, 2])
                nlam = self.sb(s2, "nlam", [128, 1])
                P.tt(lprod[:, 0:64], lamt[:, 0:64], lamt[:, 64:128], ALU.mult, reads=['lamt'], writes=['lprod'])
                P.tt(lprod[:, 64:128], lamt[:, 128:192], lamt[:, 192:256], ALU.mult, reads=['lamt'], writes=['lprod'])
                P.op('dve', lambda e: e.tensor_reduce(out=lsum[:], in_=lprod[:].rearrange("p (a b) -> p a b", a=2), axis=mybir.AxisListType.X, op=ALU.add),
                     reads=['lprod'], writes=['lsum'])
                P.act(lsum[:], lsum[:], AF.Exp, reads=['lsum'], writes=['lsum'])
                P.tt(nlam[:], lsum[:, 1:2], lsum[:, 0:1], ALU.subtract, reads=['lsum'], writes=['nlam'])
                P.ts(nlam[:], nlam[:], -LAM_INIT0, None, ALU.add, reads=['nlam'], writes=['nlam'])
                wqk = [self.sb(s2, "wqk%d" % i, [128, 8, 256], BF16) for i in range(2)]
                QT = [self.sb(s2, "QT%d" % i, [128, NTOK], BF16) for i in range(2)]
                KT = [self.sb(s2, "KT%d" % i, [128, 2, NTOK], BF16) for i in range(2)]
                for i in range(2):
                    P.memset(KT[i][:], 0.0, writes=['KT%d' % i], eng='pool')
                PT = [self.sb(s2, "PT%d" % i, [128, 512], BF16) for i in range(4)]
                qtmp = [self.sb(s2, "qtmp%d" % i, [128, 512]) for i in range(2)]
                ropA = [self.sb(s2, "ropA%d" % i, [128, 512]) for i in range(2)]
                otmp = self.sb(s2, "otmp", [128, 4, 128])
                ofin = [self.sb(s2, "ofin%d" % i, [128, 128]) for i in range(4)]
                osq = self.sb(s2, "osq", [128, 128])
                sm = self.sb(s2, "sm", [128, 4, 4])
                nrope = 0
                npt = 0

                def load_wqk(h):
                    wq, wqkey = wqk[h % 2], 'wqk%d' % (h % 2)
                    P.dma('pool', wq[:, :, 0:128], w_in[:, :, h * 128:(h + 1) * 128], writes=[wqkey])
                    P.dma('pool', wq[:, :, 128:256], w_in[:, :, 768 + h * 128:768 + (h + 1) * 128], writes=[wqkey])
                def proj(h):
                    nonlocal nrope
                    wq, wqkey = wqk[h % 2], 'wqk%d' % (h % 2)
                    dsts = ((QT[h % 2], 'QT%d' % (h % 2)), (KT[h % 2], 'KT%d' % (h % 2)))
                    for qk in range(2):
                        dst, dk = dsts[qk]
                        for (t0, n) in blocks:
                            bk = self.nextps(4, 8)
                            for kc in range(8):
                                P.mm(self.ps[bk][:, 0:n], wq[:, kc, qk * 128:(qk + 1) * 128], UT[:, kc, t0:t0 + n], kc == 0, kc == 7,
                                     reads=[wqkey, 'UT'], writes=['ps%d' % bk])
                            if t0 < NLAT:
                                qt_, qtk = qtmp[nrope % 2], 'qtmp%d' % (nrope % 2)
                                ra, rak = ropA[nrope % 2], 'ropA%d' % (nrope % 2)
                                nrope += 1
                                P.copy(qt_[:, 0:n], self.ps[bk][:, 0:n], reads=['ps%d' % bk], writes=[qtk], eng='act')
                                b2 = self.nextps(4, 8)
                                P.mm(self.ps[b2][:, 0:n], psw[:], qt_[:, 0:n], True, True, reads=['pswap', qtk], writes=['ps%d' % b2])
                                P.tt(ra[:, 0:n], qt_[:, 0:n], cos4[:, t0:t0 + n], ALU.mult, reads=[qtk, 'cos4'], writes=[rak], eng='pool')
                                P.tt(qt_[:, 0:n], self.ps[b2][:, 0:n], sin4[:, t0:t0 + n], ALU.mult, reads=['ps%d' % b2, 'sin4', qtk], writes=[qtk])
                                if qk == 0:
                                    P.tt(dst[:, t0:t0 + n], ra[:, 0:n], qt_[:, 0:n], ALU.add, reads=[rak, qtk], writes=[dk])
                                else:
                                    for a_ in range(2):
                                        P.tt(dst[a_ * 64:(a_ + 1) * 64, a_, t0:t0 + n], ra[a_ * 64:(a_ + 1) * 64, 0:n], qt_[a_ * 64:(a_ + 1) * 64, 0:n], ALU.add,
                                             reads=[rak, qtk], writes=[dk])
                            else:
                                if qk == 0:
                                    P.copy(dst[:, t0:t0 + n], self.ps[bk][:, 0:n], reads=['ps%d' % bk], writes=[dk], eng='act')
                                else:
                                    for a_ in range(2):
                                        P.copy(dst[a_ * 64:(a_ + 1) * 64, a_, t0:t0 + n], self.ps[bk][a_ * 64:(a_ + 1) * 64, 0:n], reads=['ps%d' % bk], writes=[dk], eng='act')
                load_wqk(0)
                load_wqk(1)
                proj(0)
                for h in range(6):
                    dsts = ((QT[h % 2], 'QT%d' % (h % 2)), (KT[h % 2], 'KT%d' % (h % 2)))
                    Qh, qkey = dsts[0]
                    Kh, kkey = dsts[1]
                    qblocks = [(q0, 512, list(range(NT))) for q0 in (0, 512, 1024, 1536)] + [(2048, 256, [16, 17])]
                    for qbi, (q0, nq, ktiles) in enumerate(qblocks):
                        nqt = nq // 128
                        nk = len(ktiles)
                        if qbi == 2 and h + 1 < 6:
                            proj(h + 1)
                            if h + 2 < 6:
                                load_wqk(h + 2)
                        for a in range(2):
                            sbank = {}

                            def emit_S(i):
                                kt = ktiles[i]
                                bk = self.nextps(4, 8)
                                sbank[i] = bk
                                P.mm(self.ps[bk][:, 0:nq], Kh[:, a, kt * 128:(kt + 1) * 128], Qh[:, q0:q0 + nq], True, True,
                                     reads=[qkey, kkey], writes=['ps%d' % bk])
                            emit_S(0)
                            if nk > 1:
                                emit_S(1)
                            for i in range(nk):
                                kt = ktiles[i]
                                bk = sbank[i]
                                pt, ptk = PT[npt % 4], 'PT%d' % (npt % 4)
                                npt += 1
                                P.act(pt[:, 0:nq], self.ps[bk][:, 0:nq], AF.Exp, reads=['ps%d' % bk], writes=[ptk], scale=0.125)
                                if i + 2 < nk:
                                    emit_S(i + 2)
                                for qi in range(nqt):
                                    P.mm(self.ps[qi][:, 0:129], pt[:, qi * 128:(qi + 1) * 128], VA[:, kt, h, 0:129], i == 0, i == nk - 1,
                                         reads=[ptk, 'VA'], writes=['ps%d' % qi])
                            if a == 0:
                                for qi in range(nqt):
                                    acc = self.ps[qi]
                                    sk = 'sm%d' % qi
                                    P.recip(sm[:, qi, 0:1], acc[:, 128:129], reads=['ps%d' % qi], writes=[sk])
                                    P.ts(otmp[:, qi, :], acc[:, 0:128], sm[:, qi, 0:1], None, ALU.mult, reads=['ps%d' % qi, sk], writes=['otmp%d' % qi])
                            else:
                                for qi in range(nqt):
                                    acc = self.ps[qi]
                                    sk = 'sm%d' % qi
                                    of, ofk = ofin[qi], 'ofin%d' % qi
                                    P.recip(sm[:, qi, 1:2], acc[:, 128:129], reads=['ps%d' % qi], writes=[sk])
                                    P.tt(sm[:, qi, 1:2], sm[:, qi, 1:2], nlam[:], ALU.mult, reads=[sk, 'nlam'], writes=[sk])
                                    P.stt(of[:], acc[:, 0:128], sm[:, qi, 1:2], otmp[:, qi, :], ALU.mult, ALU.add, reads=['ps%d' % qi, sk, 'otmp%d' % qi], writes=[ofk])
                                sks = ['sm%d' % qi for qi in range(nqt)]
                                for qi in range(nqt):
                                    sk = 'sm%d' % qi
                                    of, ofk = ofin[qi], 'ofin%d' % qi
                                    P.tt(osq[:], of[:], of[:], ALU.mult, reads=[ofk], writes=['osq'])
                                    P.op('dve', (lambda e, qi=qi: e.tensor_reduce(out=sm[:, qi, 2:3], in_=osq[:], axis=mybir.AxisListType.X, op=ALU.add)),
                                         reads=['osq'], writes=[sk])
                                P.ts(sm[:, 0:nqt, 2:3], sm[:, 0:nqt, 2:3], 1.0 / 128.0, EPS, ALU.mult, ALU.add, reads=sks, writes=sks)
                                P.act(sm[:, 0:nqt, 2:3], sm[:, 0:nqt, 2:3], AF.Ln, reads=sks, writes=sks)
                                P.act(sm[:, 0:nqt, 2:3], sm[:, 0:nqt, 2:3], AF.Exp, reads=sks, writes=sks, scale=-0.5)
                                for qi in range(nqt):
                                    sk = 'sm%d' % qi
                                    of, ofk = ofin[qi], 'ofin%d' % qi
                                    P.stt(of[:], of[:], sm[:, qi, 2:3], gsub[:], ALU.mult, ALU.mult, reads=[ofk, sk, 'gsub'], writes=[ofk])
                                    bk = self.nextps(4, 8)
                                    P.tr(self.ps[bk][:, 0:128], of[:], self.ident[:], reads=[ofk, 'ident'], writes=['ps%d' % bk])
                                    P.copy(OT[:, h, q0 + qi * 128:q0 + (qi + 1) * 128], self.ps[bk][:, 0:128], reads=['ps%d' % bk], writes=['OT'], eng='act')
            with self.scope() as s2:
                cb = self.sb(s2, "c64", [128, 2, 128])
                fw = self.sb(s2, "fwb", [128, 2, 128])
                fb = self.sb(s2, "fb", [128, 2])
                P.dma('sp', cb[:, 0, :], self.din['c64blk'], writes=['c64'])
                P.dma('sp', cb[:, 1, :], self.din['ns64blk'], writes=['c64'])
                P.dma('sp', fw[:], self.din['f_wblk'].rearrange("c p e -> p c e"), writes=['fwb'])
                P.dma('sp', fb[:], self.din['f_b'], writes=['fb'])
                ABm = self.sb(s2, "ABm", [128, 2, 2, 128], BF16)
                for c in range(2):
                    for cs in range(2):
                        bk = self.nextps()
                        P.mm(self.ps[bk][:, 0:128], cb[:, cs, :], fw[:, c, :], True, True, reads=['c64', 'fwb'], writes=['ps%d' % bk])
                        P.copy(ABm[:, c, cs, :], self.ps[bk][:, 0:128], reads=['ps%d' % bk], writes=['ABm'])
                Ycs = self.sb(s2, "Ycs", [128, NT, 2, 256], BF16)
                for t in range(NT):
                    for c in range(2):
                        bk = self.nextps(4, 8)
                        P.mm(self.ps[bk][:, 0:256], FT[:, c, t * 128:(t + 1) * 128], ABm[:, c, :, :].rearrange("p a b -> p (a b)"), True, True,
                             reads=['FT', 'ABm'], writes=['ps%d' % bk])
                        P.copy(Ycs[:, t, c, :], self.ps[bk][:, 0:256], reads=['ps%d' % bk], writes=['Ycs'], eng=('act' if c == 0 else 'dve'))
                dbuf = [self.sb(s2, "dft%d" % i, [128, NLAT], BF16) for i in range(3)]
                nd = 0
                dsrc = (self.din['dftc'], self.din['dfts'])
                for tn in range(16):
                    for cs in range(2):
                        db, dbk = dbuf[nd % 3], 'dft%d' % (nd % 3)
                        nd += 1
                        P.dma('sp', db[:], dsrc[cs][tn * 128:(tn + 1) * 128, :], writes=[dbk])
                        first = (tn == 0 and cs == 0)
                        last = (tn == 15 and cs == 1)
                        for c in range(2):
                            for nb in range(4):
                                bq = c * 4 + nb
                                P.mm(self.ps[bq][:, :], Ycs[:, tn, c, cs * 128:(cs + 1) * 128], db[:, nb * 512:(nb + 1) * 512], first, last,
                                     reads=['Ycs', dbk], writes=['ps%d' % bq])
                for c in range(2):
                    for nb in range(4):
                        bq = c * 4 + nb
                        P.act(OT[:, 6 + c, nb * 512:(nb + 1) * 512], self.ps[bq][:, :], AF.Identity, reads=['ps%d' % bq, 'fb'], writes=['OT'], bias=fb[:, c:c + 1])
                dcc = self.sb(s2, "dftcc", [128, 2, 2, 256], BF16)
                P.dma('sp', dcc[:, 0], self.din['dftc_c'].rearrange("(t p) n -> p t n", p=128), writes=['dftcc'])
                P.dma('sp', dcc[:, 1], self.din['dfts_c'].rearrange("(t p) n -> p t n", p=128), writes=['dftcc'])
                for c in range(2):
                    bk = self.nextps(4, 8)
                    n = 0
                    for tn in range(2):
                        for cs in range(2):
                            P.mm(self.ps[bk][:, 0:256], Ycs[:, 16 + tn, c, cs * 128:(cs + 1) * 128], dcc[:, cs, tn, :], n == 0, n == 3,
                                 reads=['Ycs', 'dftcc'], writes=['ps%d' % bk])
                            n += 1
                    P.act(OT[:, 6 + c, 2048:2304], self.ps[bk][:, 0:256], AF.Identity, reads=['ps%d' % bk, 'fb'], writes=['OT'], bias=fb[:, c:c + 1])

    def s5_mixer(self, S5O):
        import os
        stop = int(os.environ.get('S5STOP', '99'))
        if stop <= 0:
            self.P.memset(S5O[:], 0.0, writes=['S5O'])
            return
        P = self.P
        UT = self.UT
        w_in = self.din['s_w_in'].rearrange("(kc p) f -> p kc f", p=128)
        blocks = [(0, 512), (512, 512), (1024, 512), (1536, 512), (2048, 256)]
        T = ALU
        with self.scope() as s1:
            usT = self.sb(s1, "usT", [128, 2, NTOK], BF16)
            Y5 = self.sb(s1, "Y5", [128, 2, NLAT])
            dd = self.sb(s1, "s5dd", [128, 2])
            P.dma('sp', dd[:], self.din['s5_dd'], writes=['s5dd'])
            ones = self.sb(s1, "ones", [128, 512])
            P.dma('sp', ones[:], self.din['ones'], writes=['ones'])
            with self.scope() as s2:
                wus = self.sb(s2, "wus", [128, 8, 256], BF16)
                P.dma('pool', wus[:], w_in[:, :, 2072:2328], writes=['wus'])
                for c in range(2):
                    for (t0, n) in blocks:
                        bk = self.nextps()
                        for kc in range(8):
                            P.mm(self.ps[bk][:, 0:n], wus[:, kc, c * 128:(c + 1) * 128], UT[:, kc, t0:t0 + n], kc == 0, kc == 7,
                                 reads=['wus', 'UT'], writes=['ps%d' % bk])
                        P.copy(usT[:, c, t0:t0 + n], self.ps[bk][:, 0:n], reads=['ps%d' % bk], writes=['usT'], eng='act')
                        if t0 < NLAT and os.environ.get('S5SUB', '') != 'a':
                            P.ts(Y5[:, c, t0:t0 + n], self.ps[bk][:, 0:n], dd[:, c:c + 1], None, T.mult, reads=['ps%d' % bk, 's5dd'], writes=['Y5_%d_%d' % (c, t0)])
            if stop <= 1:
                P.memset(S5O[:], 0.0, writes=['S5O'])
                return
            pr = self.sb(s1, "s5pr", [128, 32, 16])
            names = {}

            def V(nm):
                if nm not in names:
                    names[nm] = len(names)
                return pr[:, names[nm], :]
            for nm, src in (('lre', 's5_lre'), ('lim', 's5_lim'), ('ldt', 's5_ldt')):
                P.dma('sp', V(nm), self.din[src], writes=['s5pr'])
            K_ = ['s5pr']

            def tt(o, a, b, op):
                P.tt(V(o), V(a), V(b), op, reads=K_, writes=K_)

            def ts(o, a, s1_, s2_, op0, op1=None):
                P.ts(V(o), V(a), s1_, s2_, op0, op1, reads=K_, writes=K_)
            P.act(V('dt'), V('ldt'), AF.Exp, reads=K_, writes=K_)
            tt('t0', 'dt', 'lre', T.mult)
            P.act(V('mag'), V('t0'), AF.Exp, reads=K_, writes=K_)
            tt('th', 'dt', 'lim', T.mult)
            ts('k', 'th', 1.0 / (2.0 * math.pi), None, T.mult)
            ts('k', 'k', 12582912.0, None, T.add)
            ts('k', 'k', -12582912.0, None, T.add)
            P.stt(V('r'), V('k'), -6.28125, V('th'), T.mult, T.add, reads=K_, writes=K_)
            P.stt(V('r'), V('k'), -(2.0 * math.pi - 6.28125), V('r'), T.mult, T.add, reads=K_, writes=K_)
            ts('x', 'r', 0.125, None, T.mult)
            tt('x2', 'x', 'x', T.mult)
            ts('ps_', 'x2', -1.0 / 5040.0, 1.0 / 120.0, T.mult, T.add)
            tt('ps_', 'ps_', 'x2', T.mult)
            ts('ps_', 'ps_', -1.0 / 6.0, None, T.add)
            tt('ps_', 'ps_', 'x2', T.mult)
            ts('ps_', 'ps_', 1.0, None, T.add)
            tt('sn', 'ps_', 'x', T.mult)
            ts('pc_', 'x2', 1.0 / 40320.0, -1.0 / 720.0, T.mult, T.add)
            tt('pc_', 'pc_', 'x2', T.mult)
            ts('pc_', 'pc_', 1.0 / 24.0, None, T.add)
            tt('pc_', 'pc_', 'x2', T.mult)
            ts('pc_', 'pc_', -0.5, None, T.add)
            tt('pc_', 'pc_', 'x2', T.mult)
            ts('cs', 'pc_', 1.0, None, T.add)
            for _ in range(3):
                tt('cc', 'cs', 'cs', T.mult)
                tt('ss', 'sn', 'sn', T.mult)
                tt('sc2', 'cs', 'sn', T.mult)
                tt('cs', 'cc', 'ss', T.subtract)
                ts('sn', 'sc2', 2.0, None, T.mult)
            tt('abre', 'mag', 'cs', T.mult)
            tt('abim', 'mag', 'sn', T.mult)
            tt('den', 'lre', 'lre', T.mult)
            tt('t0', 'lim', 'lim', T.mult)
            tt('den', 'den', 't0', T.add)
            P.recip(V('den'), V('den'), reads=K_, writes=K_)
            ts('am1', 'abre', -1.0, None, T.add)
            tt('t0', 'am1', 'lre', T.mult)
            tt('t1', 'abim', 'lim', T.mult)
            tt('t0', 't0', 't1', T.add)
            tt('kre', 't0', 'den', T.mult)
            tt('t0', 'abim', 'lre', T.mult)
            tt('t1', 'am1', 'lim', T.mult)
            tt('t0', 't0', 't1', T.subtract)
            tt('kim', 't0', 'den', T.mult)
            if stop <= 2:
                P.memset(S5O[:], 0.0, writes=['S5O'])
                return
            Ec = self.sb(s1, "s5Ec", [128, 10, 16])
            Es = self.sb(s1, "s5Es", [128, 10, 16])
            EK = ['s5E']
            P.copy(Ec[:, 0, :], V('cs'), reads=K_, writes=EK)
            P.copy(Es[:, 0, :], V('sn'), reads=K_, writes=EK)
            e1 = self.sb(s1, "s5e1", [128, 16])
            e2 = self.sb(s1, "s5e2", [128, 16])
            for j in range(9):
                P.tt(e1[:], Ec[:, j, :], Ec[:, j, :], T.mult, reads=EK, writes=['s5e1'])
                P.tt(e2[:], Es[:, j, :], Es[:, j, :], T.mult, reads=EK, writes=['s5e2'])
                P.tt(Ec[:, j + 1, :], e1[:], e2[:], T.subtract, reads=['s5e1', 's5e2'], writes=EK)
                P.tt(e1[:], Ec[:, j, :], Es[:, j, :], T.mult, reads=EK, writes=['s5e1'])
                P.ts(Es[:, j + 1, :], e1[:], 2.0, None, T.mult, reads=['s5e1'], writes=EK)
            if stop <= 3:
                P.memset(S5O[:], 0.0, writes=['S5O'])
                return
            nEs = self.sb(s1, "s5nEs", [128, 10, 16])
            P.ts(nEs[:], Es[:], -1.0, None, T.mult, reads=EK, writes=['s5nEs'])
            BbT = self.sb(s1, "BbT", [128, 2, 8, 2, 128], BF16)
            CmT = self.sb(s1, "CmT", [128, 2, 8, 2, 128], BF16)
            for d in range(2):
                P.dma('pool', CmT[:, d, :, 0, :], self.din['s5_cre'][d], writes=['CmT'])
                P.dma('pool', CmT[:, d, :, 1, :], self.din['s5_cim'][d], writes=['CmT'])
                P.ts(CmT[:, d, :, 1, :], CmT[:, d, :, 1, :], -1.0, None, T.mult, reads=['CmT'], writes=['CmT'])
            with self.scope() as s2:
                bre = self.sb(s2, "s5bre", [128, 8, 128])
                bim = self.sb(s2, "s5bim", [128, 8, 128])
                P.dma('sp', bre[:], self.din['s5_bre'], writes=['s5bre'])
                P.dma('sp', bim[:], self.din['s5_bim'], writes=['s5bim'])
                wt = [self.sb(s2, "s5wt%d" % i, [128, 128]) for i in range(2)]
                nw = 0
                for d in range(2):
                    for sc in range(8):
                        col = d * 8 + sc
                        kre = V('kre')[:, col:col + 1]
                        kim = V('kim')[:, col:col + 1]
                        for ri in range(2):
                            w, wk = wt[nw % 2], 's5wt%d' % (nw % 2)
                            nw += 1
                            if ri == 0:
                                P.ts(w[:], bim[:, sc, :], kim, None, T.mult, reads=['s5bim'] + K_, writes=[wk])
                                P.stt(w[:], bre[:, sc, :], kre, w[:], T.mult, T.subtract, reads=['s5bre', wk] + K_, writes=[wk])
                            else:
                                P.ts(w[:], bre[:, sc, :], kim, None, T.mult, reads=['s5bre'] + K_, writes=[wk])
                                P.stt(w[:], bim[:, sc, :], kre, w[:], T.mult, T.add, reads=['s5bim', wk] + K_, writes=[wk])
                            bk = self.nextps()
                            P.tr(self.ps[bk][:, 0:128], w[:], self.ident[:], reads=[wk, 'ident'], writes=['ps%d' % bk])
                            P.copy(BbT[:, d, sc, ri, :], self.ps[bk][:, 0:128], reads=['ps%d' % bk], writes=['BbT'], eng='act')
            if stop <= 4:
                P.memset(S5O[:], 0.0, writes=['S5O'])
                return
            with self.scope() as s2:
                Tc = self.sb(s2, "s5Tc", [128, 8, 512])
                Ts = self.sb(s2, "s5Ts", [128, 8, 512])
                tq = [self.sb(s2, "s5tq%d" % i, [128, 8, 256]) for i in range(2)]
                rho = self.sb(s2, "s5rho", [128, 512])
                NB = 2
                mt = [[self.sb(s2, "s5m%d_%d" % (i, b), [128, 512]) for i in range(4)] for b in range(NB)]
                bp = [[self.sb(s2, "s5bp%d_%d" % (i, b), [128, 512]) for i in range(2)] for b in range(NB)]
                gg = [[self.sb(s2, "s5g%d_%d" % (i, b), [128, 512]) for i in range(2)] for b in range(NB)]
                hh = [[self.sb(s2, "s5h%d_%d" % (i, b), [128, 512], BF16) for i in range(2)] for b in range(NB)]
                pp = [self.sb(s2, "s5p%d" % i, [128, 512]) for i in range(4)]
                ini = [self.sb(s2, "s5ini%d" % b, [128, 4]) for b in range(NB)]
                nb_ = 0
                for d in range(2):
                    TK = ['s5T']
                    P.memset(Tc[:, :, 0:1], 1.0, writes=TK)
                    P.memset(Ts[:, :, 0:1], 0.0, writes=TK)
                    for j in range(9):
                        m = 1 << j
                        ecb = Ec[:, j, d * 8:(d + 1) * 8].unsqueeze(2).to_broadcast([128, 8, m])
                        esb = Es[:, j, d * 8:(d + 1) * 8].unsqueeze(2).to_broadcast([128, 8, m])
                        P.tt(tq[0][:, :, 0:m], Ts[:, :, 0:m], esb, T.mult, reads=TK + EK, writes=['s5tq0'])
                        P.tt(tq[1][:, :, 0:m], Tc[:, :, 0:m], esb, T.mult, reads=TK + EK, writes=['s5tq1'])
                        P.tt(Tc[:, :, m:2 * m], Tc[:, :, 0:m], ecb, T.mult, reads=TK + EK, writes=TK)
                        P.tt(Ts[:, :, m:2 * m], Ts[:, :, 0:m], ecb, T.mult, reads=TK + EK, writes=TK)
                        P.tt(Tc[:, :, m:2 * m], Tc[:, :, m:2 * m], tq[0][:, :, 0:m], T.subtract, reads=TK + ['s5tq0'], writes=TK)
                        P.tt(Ts[:, :, m:2 * m], Ts[:, :, m:2 * m], tq[1][:, :, 0:m], T.add, reads=TK + ['s5tq1'], writes=TK)
                    order = [blocks[4]] + (blocks[0:4] if d == 0 else blocks[3::-1])
                    if stop <= 5:
                        continue
                    if stop <= 6 and d == 1:
                        continue
                    if d == 1:
                        P.copy(tq[0][:, :, :], Tc[:, :, 0:256], reads=TK, writes=['s5tq0'])
                        P.copy(tq[1][:, :, :], Tc[:, :, 256:512], reads=TK, writes=['s5tq1'])
                        P.copy(Tc[:, :, 0:256], tq[1][:, :, ::-1], reads=['s5tq1'], writes=TK)
                        P.copy(Tc[:, :, 256:512], tq[0][:, :, ::-1], reads=['s5tq0'], writes=TK)
                        P.copy(tq[0][:, :, :], Ts[:, :, 0:256], reads=TK, writes=['s5tq0'])
                        P.copy(tq[1][:, :, :], Ts[:, :, 256:512], reads=TK, writes=['s5tq1'])
                        P.copy(Ts[:, :, 0:256], tq[1][:, :, ::-1], reads=['s5tq1'], writes=TK)
                        P.copy(Ts[:, :, 256:512], tq[0][:, :, ::-1], reads=['s5tq0'], writes=TK)
                    for sc in range(8):
                        col = d * 8 + sc
                        fc = sc // 4
                        P.ts(rho[:], ones[:], V('mag')[:, col:col + 1], None, T.mult, reads=['ones'] + K_, writes=['s5rho'])
                        prev = None
                        pending = []
                        pend_pe = []
                        for (t0, n) in order:
                            b = nb_ % NB
                            nb_ += 1
                            sfx = '_%d' % b
                            bk1 = self.nextps()
                            P.mm(self.ps[bk1][:, 0:n], BbT[:, d, sc, 0, :], usT[:, fc, t0:t0 + n], True, True, reads=['BbT', 'usT'], writes=['ps%d' % bk1])
                            bk2 = self.nextps()
                            P.mm(self.ps[bk2][:, 0:n], BbT[:, d, sc, 1, :], usT[:, fc, t0:t0 + n], True, True, reads=['BbT', 'usT'], writes=['ps%d' % bk2])
                            while pend_pe:
                                pend_pe.pop(0)()
                            pre, pim = self.ps[bk1][:, 0:n], self.ps[bk2][:, 0:n]
                            if d == 0:
                                tc = Tc[:, sc, 0:n]
                                ts_ = Ts[:, sc, 0:n]
                            else:
                                tc = Tc[:, sc, 512 - n:512]
                                ts_ = Ts[:, sc, 512 - n:512]
                            m1, m2, m3, m4 = [mt[b][i][:, 0:n] for i in range(4)]
                            mk = ['s5m%d%s' % (i, sfx) for i in range(4)]
                            P.tt(m1, pre, tc, T.mult, reads=['ps%d' % bk1] + TK, writes=[mk[0]])
                            P.tt(m2, pim, ts_, T.mult, reads=['ps%d' % bk2] + TK, writes=[mk[1]])
                            P.tt(m3, pim, tc, T.mult, reads=['ps%d' % bk2] + TK, writes=[mk[2]])
                            P.tt(m4, pre, ts_, T.mult, reads=['ps%d' % bk1] + TK, writes=[mk[3]])
                            bpr, bpi = bp[b][0][:, 0:n], bp[b][1][:, 0:n]
                            P.tt(bpr, m1, m2, T.add, reads=[mk[0], mk[1]], writes=['s5bp0' + sfx])
                            P.tt(bpi, m3, m4, T.subtract, reads=[mk[2], mk[3]], writes=['s5bp1' + sfx])
                            gre, gim = gg[b][0][:, 0:n], gg[b][1][:, 0:n]
                            if d == 0:
                                go_r, go_i, bi_r, bi_i = gre, gim, bpr, bpi
                                last = n - 1
                            else:
                                go_r, go_i, bi_r, bi_i = gre[:, ::-1], gim[:, ::-1], bpr[:, ::-1], bpi[:, ::-1]
                                last = 0
                            i0 = 0.0 if prev is None else ini[prev][:, 0:1]
                            i1 = 0.0 if prev is None else ini[prev][:, 1:2]
                            rk = [] if prev is None else ['s5ini%d' % prev]
                            P.scan(go_r, rho[:, 0:n], bi_r, i0, T.mult, T.add, reads=['s5rho', 's5bp0' + sfx] + rk, writes=['s5g0' + sfx])
                            P.scan(go_i, rho[:, 0:n], bi_i, i1, T.mult, T.add, reads=['s5rho', 's5bp1' + sfx] + rk, writes=['s5g1' + sfx])
                            j = 9 if n == 512 else 8
                            ec = Ec[:, j, col:col + 1]
                            es = Es[:, j, col:col + 1]
                            ik = 's5ini%d' % b
                            nes = nEs[:, j, col:col + 1]
                            P.act(ini[b][:, 2:3], gg[b][1][:, last:last + 1], AF.Identity, reads=['s5g1' + sfx, 's5nEs'], writes=[ik], scale=nes, bias=0.0)
                            P.act(ini[b][:, 0:1], gg[b][0][:, last:last + 1], AF.Identity, reads=['s5g0' + sfx, ik] + EK, writes=[ik], scale=ec, bias=ini[b][:, 2:3])
                            P.act(ini[b][:, 3:4], gg[b][0][:, last:last + 1], AF.Identity, reads=['s5g0' + sfx] + EK, writes=[ik], scale=es, bias=0.0)
                            P.act(ini[b][:, 1:2], gg[b][1][:, last:last + 1], AF.Identity, reads=['s5g1' + sfx, ik] + EK, writes=[ik], scale=ec, bias=ini[b][:, 3:4])
                            prev = b
                            while pending:
                                pending.pop(0)()
                            if t0 < NLAT:
                                p1, p2, p3, p4 = [pp[i][:, 0:n] for i in range(4)]
                                hre, him = hh[b][0][:, 0:n], hh[b][1][:, 0:n]
                                P.tt(p1, gre, tc, T.mult, reads=['s5g0' + sfx] + TK, writes=['s5p0'], eng='pool')
                                P.tt(p2, gim, ts_, T.mult, reads=['s5g1' + sfx] + TK, writes=['s5p1'], eng='pool')
                                P.tt(hre, p1, p2, T.subtract, reads=['s5p0', 's5p1'], writes=['s5h0' + sfx], eng='pool')
                                P.tt(p3, gre, ts_, T.mult, reads=['s5g0' + sfx] + TK, writes=['s5p2'], eng='pool')
                                P.tt(p4, gim, tc, T.mult, reads=['s5g1' + sfx] + TK, writes=['s5p3'], eng='pool')
                                P.tt(him, p3, p4, T.add, reads=['s5p2', 's5p3'], writes=['s5h1' + sfx], eng='pool')
                                yk = 'Y5_%d_%d' % (fc, t0)
                                cell = {}

                                def _ro(cell=cell, hre=hre, him=him, sfx=sfx, n=n, d=d, sc=sc):
                                    bk = self.nextps()
                                    cell['bk'] = bk
                                    P.mm(self.ps[bk][:, 0:n], CmT[:, d, sc, 0, :], hre, True, False, reads=['CmT', 's5h0' + sfx], writes=['ps%d' % bk])
                                    P.mm(self.ps[bk][:, 0:n], CmT[:, d, sc, 1, :], him, False, True, reads=['CmT', 's5h1' + sfx], writes=['ps%d' % bk])

                                def _acc(cell=cell, yk=yk, fc=fc, t0=t0, n=n):
                                    bk = cell['bk']
                                    P.tt(Y5[:, fc, t0:t0 + n], Y5[:, fc, t0:t0 + n], self.ps[bk][:, 0:n], T.add, reads=[yk, 'ps%d' % bk], writes=[yk])
                                pend_pe.append(_ro)
                                pending.append(_acc)
                        while pend_pe:
                            pend_pe.pop(0)()
                        while pending:
                            pending.pop(0)()
            if stop <= 7:
                P.memset(S5O[:], 0.0, writes=['S5O'])
                return
            with self.scope() as s2:
                gw = self.sb(s2, "gluw", [128, 2, 256], BF16)
                gb_ = self.sb(s2, "glub", [128, 2])
                P.dma('pool', gw[:], self.din['s5_glu_w'].rearrange("(kc p) f -> p kc f", p=128), writes=['gluw'])
                P.dma('sp', gb_[:], self.din['s5_glu_b'], writes=['glub'])
                gbf = self.sb(s2, "gbf", [128, 2, NLAT], BF16)
                t1 = [self.sb(s2, "glt%d" % i, [128, 512]) for i in range(2)]
                sg = [self.sb(s2, "gls%d" % i, [128, 512]) for i in range(2)]
                n_ = 0
                for c in range(2):
                    for (t0, n) in blocks[0:4]:
                        yk = 'Y5_%d_%d' % (c, t0)
                        x = Y5[:, c, t0:t0 + n]
                        a, ak = t1[n_ % 2], 'glt%d' % (n_ % 2)
                        n_ += 1
                        P.tt(a[:], x, x, T.mult, reads=[yk], writes=[ak], eng='pool')
                        P.ts(a[:], a[:], 0.044715, 1.0, T.mult, T.add, reads=[ak], writes=[ak])
                        P.tt(a[:], a[:], x, T.mult, reads=[ak, yk], writes=[ak])
                        P.act(a[:], a[:], AF.Tanh, reads=[ak], writes=[ak], scale=math.sqrt(2.0 / math.pi))
                        P.stt(a[:], a[:], 1.0, x, T.add, T.mult, reads=[ak, yk], writes=[ak])
                        P.ts(x, a[:], 0.5, None, T.mult, reads=[ak], writes=[yk])
                        P.copy(gbf[:, c, t0:t0 + n], x, reads=[yk], writes=['gbf'], eng='act')
                n_ = 0
                for c in range(2):
                    for (t0, n) in blocks[0:4]:
                        bk = self.nextps()
                        for kc in range(2):
                            P.mm(self.ps[bk][:, 0:n], gw[:, kc, c * 128:(c + 1) * 128], gbf[:, kc, t0:t0 + n], kc == 0, kc == 1,
                                 reads=['gluw', 'gbf'], writes=['ps%d' % bk])
                        a, ak = sg[n_ % 2], 'gls%d' % (n_ % 2)
                        n_ += 1
                        P.act(a[:], self.ps[bk][:, 0:n], AF.Sigmoid, reads=['ps%d' % bk, 'glub'], writes=[ak], bias=gb_[:, c:c + 1])
                        P.tt(S5O[:, c, t0:t0 + n], Y5[:, c, t0:t0 + n], a[:], T.mult, reads=['Y5_%d_%d' % (c, t0), ak], writes=['S5O'])

    def ssd_inproj(self, sB, Xtm, Btm, BT, CT, dtt, dta, ZS):
        P = self.P
        UT = self.UT
        T = ALU
        w_in = self.din['s_w_in'].rearrange("(kc p) f -> p kc f", p=128)
        blocks = [(0, 512), (512, 512), (1024, 512), (1536, 512), (2048, 256)]
        with self.scope() as s2:
            wz = self.sb(s2, "wz", [128, 8, 768], BF16)
            P.dma('pool', wz[:], w_in[:, :, 0:768], writes=['wz'])
            zt = [self.sb(s2, "ztp%d" % i, [128, 768]) for i in range(2)]
            for t in range(16):
                z, zk = zt[t % 2], 'ztp%d' % (t % 2)
                for half in range(2):
                    bk = self.nextps()
                    for kc in range(8):
                        P.mm(self.ps[bk][:, 0:384], UT[:, kc, t * 128:(t + 1) * 128], wz[:, kc, half * 384:(half + 1) * 384], kc == 0, kc == 7,
                             reads=['UT', 'wz'], writes=['ps%d' % bk])
                    P.act(z[:, half * 384:(half + 1) * 384], self.ps[bk][:, 0:384], AF.Silu, reads=['ps%d' % bk], writes=[zk])
                P.dma('sp', ZS[t * 128:(t + 1) * 128, :], z[:], reads=[zk], writes=['ZS_%d' % t])
            wdt = self.sb(s2, "wdt", [128, 8, 24], BF16)
            P.dma('pool', wdt[:], w_in[:, :, 2048:2072], writes=['wdt'])
            dtb = self.sb(s2, "dtb", [128, 24])
            alog = self.sb(s2, "alog", [128, 24])
            P.dma('sp', dtb[:], self.din['sd_dtb'], writes=['dtb'])
            P.dma('sp', alog[:], self.din['sd_alog'], writes=['alog'])
            for t in range(NT):
                bk = self.nextps()
                for kc in range(8):
                    P.mm(self.ps[bk][:, 0:24], UT[:, kc, t * 128:(t + 1) * 128], wdt[:, kc, :], kc == 0, kc == 7, reads=['UT', 'wdt'], writes=['ps%d' % bk])
                P.tt(dtt[:, t, :], self.ps[bk][:, 0:24], dtb[:], T.add, reads=['ps%d' % bk, 'dtb'], writes=['dtt'])
            ax = self.sb(s2, "spax", [128, NT * 24])
            dflat = dtt[:].rearrange("p t f -> p (t f)")
            P.act(ax[:], dflat, AF.Abs, reads=['dtt'], writes=['spax'])
            P.act(ax[:], ax[:], AF.Exp, reads=['spax'], writes=['spax'], scale=-1.0)
            P.act(ax[:], ax[:], AF.Ln, reads=['spax'], writes=['spax'], bias=1.0)
            P.ts(dflat, dflat, 0.0, None, T.max, reads=['dtt'], writes=['dtt'])
            P.tt(dflat, dflat, ax[:], T.add, reads=['dtt', 'spax'], writes=['dtt'])
            P.act(alog[:], alog[:], AF.Exp, reads=['alog'], writes=['alog'])
            P.ts(alog[:], alog[:], -1.0, None, T.mult, reads=['alog'], writes=['alog'])
            P.tt(dta[:], dtt[:], alog[:].unsqueeze(1).to_broadcast([128, NT, 24]), T.mult, reads=['dtt', 'alog'], writes=['dta'])
        with self.scope() as s2:
            cw = self.sb(s2, "sdcw", [128, 3, 10])
            cb = self.sb(s2, "sdcb", [128, 10])
            P.dma('sp', cw[:], self.din['sd_cw'], writes=['sdc'])
            P.dma('sp', cb[:], self.din['sd_cb'], writes=['sdc'])
            W = 2308
            xs = self.sb(s2, "sdxs", [128, W])
            acc = self.sb(s2, "sdacc", [128, W])
            P.memset(xs[:], 0.0, writes=['sdxs'], eng='pool')
            wx = [self.sb(s2, "sdwx%d" % i, [128, 8, 128], BF16) for i in range(2)]

            def colpos(t0):
                return 1 + t0 if t0 < NLAT else 2051 + (t0 - NLAT)
            for c in range(10):
                w, wk = wx[c % 2], 'sdwx%d' % (c % 2)
                P.dma('pool', w[:], w_in[:, :, 768 + c * 128:768 + (c + 1) * 128], writes=[wk])
                for (t0, n) in blocks:
                    bk = self.nextps()
                    for kc in range(8):
                        P.mm(self.ps[bk][:, 0:n], w[:, kc, :], UT[:, kc, t0:t0 + n], kc == 0, kc == 7, reads=[wk, 'UT'], writes=['ps%d' % bk])
                    cp = colpos(t0)
                    P.copy(xs[:, cp:cp + n], self.ps[bk][:, 0:n], reads=['ps%d' % bk], writes=['sdxs'], eng='act')
                n1 = W - 2
                P.ts(acc[:, 1:1 + n1], xs[:, 0:n1], cw[:, 0, c:c + 1], cb[:, c:c + 1], T.mult, T.add, reads=['sdxs', 'sdc'], writes=['sdacc'])
                P.stt(acc[:, 1:1 + n1], xs[:, 1:1 + n1], cw[:, 1, c:c + 1], acc[:, 1:1 + n1], T.mult, T.add, reads=['sdxs', 'sdacc', 'sdc'], writes=['sdacc'])
                P.stt(acc[:, 1:1 + n1], xs[:, 2:2 + n1], cw[:, 2, c:c + 1], acc[:, 1:1 + n1], T.mult, T.add, reads=['sdxs', 'sdacc', 'sdc'], writes=['sdacc'])
                P.act(acc[:, 1:1 + n1], acc[:, 1:1 + n1], AF.Silu, reads=['sdacc'], writes=['sdacc'])
                if 6 <= c < 8:
                    g = c - 6
                    P.copy(BT[:, g, 0:NLAT], acc[:, 1:1 + NLAT], reads=['sdacc'], writes=['BT'], eng='pool')
                    P.copy(BT[:, g, NLAT:NTOK], acc[:, 2051:2051 + NCTX], reads=['sdacc'], writes=['BT'], eng='pool')
                if c >= 8:
                    g = c - 8
                    P.copy(CT[:, g, 0:NLAT], acc[:, 1:1 + NLAT], reads=['sdacc'], writes=['CT'], eng='pool')
                    P.copy(CT[:, g, NLAT:NTOK], acc[:, 2051:2051 + NCTX], reads=['sdacc'], writes=['CT'], eng='pool')
                if c < 8:
                    for t4 in range(0, NT, 4):
                        bk = self.nextps()
                        nq = min(4, NT - t4)
                        for q in range(nq):
                            t = t4 + q
                            cp = colpos(t * 128)
                            P.tr(self.ps[bk][:, q * 128:(q + 1) * 128], acc[:, cp:cp + 128], self.ident[:], reads=['sdacc', 'ident'], writes=['ps%d' % bk])
                        src = self.ps[bk][:, 0:nq * 128].rearrange("p (q f) -> p q f", q=nq)
                        if c < 6:
                            P.copy(Xtm[:, t4:t4 + nq, c * 128:(c + 1) * 128], src, reads=['ps%d' % bk], writes=['Xtm'], eng=('act' if (t4 // 4) % 2 == 0 else 'dve'))
                        else:
                            P.copy(Btm[:, t4:t4 + nq, c - 6, :], src, reads=['ps%d' % bk], writes=['Btm'], eng=('act' if (t4 // 4) % 2 == 0 else 'dve'))

    def ssd_chunks(self, Xtm, Btm, BT, CT, dtt, dta, Yacc):
        P = self.P
        T = ALU
        with self.scope() as s2:
            vd = self.sb(s2, "vd", [128, 2, 128])
            ud = self.sb(s2, "ud", [128, 2, 128])
            ones = self.sb(s2, "ones1", [128, 128])
            dsk = self.sb(s2, "dsk", [128, 12])
            P.dma('sp', vd[:], self.din['vd'], writes=['vd'])
            P.dma('sp', ud[:], self.din['ud'], writes=['ud'])
            P.dma('sp', ones[:], self.din['ones'][:, 0:128], writes=['ones1'])
            P.dma('sp', dsk[:], self.din['sd_d'], writes=['dsk'])
            for t in range(16):
                P.tt(Yacc[:, t, :].rearrange("p (r e) -> p r e", r=12), Xtm[:, t, :].rearrange("p (r e) -> p r e", r=12),
                     dsk[:].unsqueeze(2).to_broadcast([128, 12, 64]), T.mult, reads=['Xtm', 'dsk'], writes=['Yacc%d' % t])
            utflat = self.UT[:].rearrange("p a b -> p (a b)")

            class _V:
                def __init__(self, ap):
                    self.ap = ap

                def __getitem__(self, idx):
                    return self.ap[idx]

            def alias(i):
                return _V(utflat[:, i * 3072:(i + 1) * 3072].bitcast(F32).rearrange("p (a b) -> p a b", a=12))
            rhsV = alias(0)
            Ls = [alias(1), alias(2)]
            CBm_ = [self.sb(s2, "CBm%d" % i, [128, 2, 128]) for i in range(2)]
            M_ = [self.sb(s2, "Mdiag%d" % i, [128, 12, 128], BF16) for i in range(2)]
            xdts = [self.sb(s2, "xdt%d" % i, [128, 12, 64], BF16) for i in range(2)]
            xw_ = [self.sb(s2, "xw%d" % i, [128, 12, 64], BF16) for i in range(2)]
            tmp_ = [self.sb(s2, "ytmp%d" % i, [128, 768]) for i in range(2)]
            Hf_ = [self.sb(s2, "Hf%d" % i, [128, 768]) for i in range(2)]
            Hb2 = [[self.sb(s2, "Hb%d_%d" % (i, k), [128, 768], BF16) for k in range(2)] for i in range(2)]
            eacs_ = [self.sb(s2, "eacs%d" % i, [128, 12]) for i in range(2)]
            cdv_ = [self.sb(s2, "cdv%d" % i, [128, 12]) for i in range(2)]
            jobs = []
            orders = [[16, 17] + list(range(16)), [17, 16] + list(range(15, -1, -1))]
            for ci in range(18):
                for d in range(2):
                    jobs.append((d, ci, orders[d][ci], ci == 17))

            def stageA1(n):
                d, ci, t, islast = jobs[n]
                xdt, xk = xdts[n % 2], 'xdt%d' % (n % 2)
                dta_t = dta[:, t, d * 12:(d + 1) * 12]
                dt_t = dtt[:, t, d * 12:(d + 1) * 12]
                P.tt(rhsV[:], vd[:, d, :].unsqueeze(1).to_broadcast([128, 12, 128]), dta_t.unsqueeze(2).to_broadcast([128, 12, 128]), T.mult,
                     reads=['vd', 'dta'], writes=['rhsV'], eng='pool')
                P.tt(xdt[:], Xtm[:, t, :].rearrange("p (r e) -> p r e", r=12), dt_t.unsqueeze(2).to_broadcast([128, 12, 64]), T.mult,
                     reads=['Xtm', 'dtt'], writes=[xk], eng='pool')

            def stageA2(n):
                d, ci, t, islast = jobs[n]
                L, lk = Ls[n % 2], 'Lseg%d' % (n % 2)
                for q in range(3):
                    bk = self.nextps()
                    P.mm(self.ps[bk][:, :], ud[:, d, :], rhsV[:, 4 * q:4 * q + 4, :].rearrange("p a b -> p (a b)"), True, True,
                         reads=['ud', 'rhsV'], writes=['ps%d' % bk])
                    P.act(L[:, 4 * q:4 * q + 4, :].rearrange("p a b -> p (a b)"), self.ps[bk][:, :], AF.Exp, reads=['ps%d' % bk], writes=[lk])

            def stageB(n):
                d, ci, t, islast = jobs[n]
                L, lk = Ls[n % 2], 'Lseg%d' % (n % 2)
                xdt, xk = xdts[n % 2], 'xdt%d' % (n % 2)
                CBm, M, xw, tmp, Hf, eacs, cdv = CBm_[d], M_[d], xw_[d], tmp_[d], Hf_[d], eacs_[d], cdv_[d]
                kCB, kM, kxw, ktmp, kHf, kea, kcd = ['%s%d' % (k_, d) for k_ in ('CBm', 'Mdiag', 'xw', 'ytmp', 'Hf', 'eacs', 'cdv')]
                Hb_cur, kHb_cur = Hb2[d][ci % 2], 'Hb%d_%d' % (d, ci % 2)
                Hb_nxt, kHb_nxt = Hb2[d][(ci + 1) % 2], 'Hb%d_%d' % (d, (ci + 1) % 2)
                iend = 127 if d == 0 else 0
                lat = t < 16
                dta_t = dta[:, t, d * 12:(d + 1) * 12]
                tok = slice(t * 128, (t + 1) * 128)
                bs = None
                if not islast:
                    P.tt(xw[:], xdt[:], L[:, :, iend].unsqueeze(2).to_broadcast([128, 12, 64]), T.mult, reads=[xk, lk], writes=[kxw])
                    bs = [self.nextps(), self.nextps()]
                    for g in range(2):
                        P.mm(self.ps[bs[g]][:, 0:384], Btm[:, t, g, :], xw[:, 6 * g:6 * g + 6, :].rearrange("p a b -> p (a b)"), True, True,
                             reads=['Btm', kxw], writes=['ps%d' % bs[g]])
                    if ci > 0:
                        bk = self.nextps()
                        P.mm(self.ps[bk][:, 0:12], ones[:], dta_t, True, True, reads=['ones1', 'dta'], writes=['ps%d' % bk])
                        P.act(cdv[:], self.ps[bk][:, 0:12], AF.Exp, reads=['ps%d' % bk], writes=[kcd])
                if lat:
                    bk = self.nextps()
                    for g in range(2):
                        P.mm(self.ps[bk][:, g * 128:(g + 1) * 128], BT[:, g, tok], CT[:, g, tok], True, True, reads=['BT', 'CT'], writes=['ps%d' % bk])
                    P.tt(CBm[:], self.ps[bk][:, 0:256].rearrange("p (g i) -> p g i", g=2), vd[:, d, :].unsqueeze(1).to_broadcast([128, 2, 128]), T.mult,
                         reads=['ps%d' % bk, 'vd'], writes=[kCB])
                    P.tt(M[:].rearrange("p (g r) i -> p g r i", g=2), L[:].rearrange("p (g r) i -> p g r i", g=2),
                         CBm[:].unsqueeze(2).to_broadcast([128, 2, 6, 128]), T.mult, reads=[lk, kCB], writes=[kM])
                if not islast:
                    if ci == 0:
                        for g in range(2):
                            P.copy(Hf[:, g * 384:(g + 1) * 384], self.ps[bs[g]][:, 0:384], reads=['ps%d' % bs[g]], writes=[kHf])
                    else:
                        P.tt(Hf[:].rearrange("p (r e) -> p r e", r=12), Hf[:].rearrange("p (r e) -> p r e", r=12),
                             cdv[:].unsqueeze(2).to_broadcast([128, 12, 64]), T.mult, reads=[kHf, kcd], writes=[kHf])
                        for g in range(2):
                            P.tt(Hf[:, g * 384:(g + 1) * 384], Hf[:, g * 384:(g + 1) * 384], self.ps[bs[g]][:, 0:384], T.add,
                                 reads=[kHf, 'ps%d' % bs[g]], writes=[kHf])
                    P.copy(Hb_nxt[:], Hf[:], reads=[kHf], writes=[kHb_nxt], eng='act')
                if lat:
                    bA = self.nextps()
                    bB = self.nextps()
                    for r in range(12):
                        dst = self.ps[bA][:, r * 64:(r + 1) * 64] if r < 8 else self.ps[bB][:, (r - 8) * 64:(r - 7) * 64]
                        P.mm(dst, M[:, r, :], xdt[:, r, :], True, True, reads=[kM, xk], writes=['ps%d' % (bA if r < 8 else bB)])
                    bo = [self.nextps(), self.nextps()]
                    for g in range(2):
                        P.mm(self.ps[bo[g]][:, 0:384], CT[:, g, tok], Hb_cur[:, g * 384:(g + 1) * 384], True, True, reads=['CT', kHb_cur], writes=['ps%d' % bo[g]])
                    bk = self.nextps()
                    P.mm(self.ps[bk][:, 0:12], vd[:, d, :], dta_t, True, True, reads=['vd', 'dta'], writes=['ps%d' % bk])
                    P.act(eacs[:], self.ps[bk][:, 0:12], AF.Exp, reads=['ps%d' % bk], writes=[kea])
                    for g in range(2):
                        P.tt(tmp[:, g * 384:(g + 1) * 384].rearrange("p (r e) -> p r e", r=6), self.ps[bo[g]][:, 0:384].rearrange("p (r e) -> p r e", r=6),
                             eacs[:, g * 6:(g + 1) * 6].unsqueeze(2).to_broadcast([128, 6, 64]), T.mult, reads=['ps%d' % bo[g], kea], writes=[ktmp])
                    P.tt(tmp[:, 0:512], tmp[:, 0:512], self.ps[bA][:, :], T.add, reads=[ktmp, 'ps%d' % bA], writes=[ktmp])
                    P.tt(tmp[:, 512:768], tmp[:, 512:768], self.ps[bB][:, 0:256], T.add, reads=[ktmp, 'ps%d' % bB], writes=[ktmp])
                    P.tt(Yacc[:, t, :], Yacc[:, t, :], tmp[:], T.add, reads=['Yacc%d' % t, ktmp], writes=['Yacc%d' % t], eng='pool')
            stageA1(0)
            stageA2(0)
            for n in range(len(jobs)):
                if n + 1 < len(jobs):
                    stageA1(n + 1)
                stageB(n)
                if n + 1 < len(jobs):
                    stageA2(n + 1)

    def ssd_gate(self, Yacc, ZS, OTs):
        P = self.P
        T = ALU
        with self.scope() as s2:
            ng = self.sb(s2, "sdng", [128, 768])
            P.dma('sp', ng[:], self.din['sd_ng'], writes=['sdng'])
            zt = [self.sb(s2, "ztg%d" % i, [128, 768]) for i in range(2)]
            yz = [self.sb(s2, "yz%d" % i, [128, 768]) for i in range(2)]
            sq = self.sb(s2, "gsq", [128, 768])
            sm = self.sb(s2, "gsm", [128, 2])
            for t in range(16):
                z, zk = zt[t % 2], 'ztg%d' % (t % 2)
                y, yk = yz[t % 2], 'yz%d' % (t % 2)
                P.dma('sp', z[:], ZS[t * 128:(t + 1) * 128, :], reads=['ZS_%d' % t], writes=[zk])
                P.tt(y[:], Yacc[:, t, :], z[:], T.mult, reads=['Yacc%d' % t, zk], writes=[yk])
                P.act(sq[:], y[:], AF.Square, reads=[yk], writes=['gsq', 'gsm'], accum_out=sm[:, 0:1])
                P.ts(sm[:, 0:1], sm[:, 0:1], 1.0 / 768.0, EPS, T.mult, T.add, reads=['gsm'], writes=['gsm'])
                P.act(sm[:, 0:1], sm[:, 0:1], AF.Sqrt, reads=['gsm'], writes=['gsm'])
                P.recip(sm[:, 0:1], sm[:, 0:1], reads=['gsm'], writes=['gsm'])
                P.stt(y[:], y[:], sm[:, 0:1], ng[:], T.mult, T.mult, reads=[yk, 'gsm', 'sdng'], writes=[yk])
                for half in range(2):
                    bk = self.nextps()
                    for q in range(3):
                        c = half * 3 + q
                        P.tr(self.ps[bk][:, q * 128:(q + 1) * 128], y[:, c * 128:(c + 1) * 128], self.ident[:], reads=[yk, 'ident'], writes=['ps%d' % bk])
                    P.copy(OTs[:, half * 3:(half + 1) * 3, t * 128:(t + 1) * 128], self.ps[bk][:, 0:384].rearrange("p (q f) -> p q f", q=3),
                           reads=['ps%d' % bk], writes=['OTs'], eng=('act' if half == 0 else 'dve'))

    def declare_l1_inputs(self):
        for nm, shp in (('s_w_in', [D, 2328]), ('s_w_out', [D, D]), ('sd_cw', [128, 3, 10]), ('sd_cb', [128, 10]), ('sd_alog', [128, 24]),
                        ('sd_dtb', [128, 24]), ('sd_d', [128, 12]), ('sd_ng', [128, 768]), ('s5_lre', [128, 16]), ('s5_lim', [128, 16]),
                        ('s5_ldt', [128, 16]), ('s5_bre', [128, 8, 128]), ('s5_bim', [128, 8, 128]), ('s5_cre', [2, 128, 8, 128]),
                        ('s5_cim', [2, 128, 8, 128]), ('s5_dd', [128, 2]), ('s5_glu_w', [256, 256]), ('s5_glu_b', [128, 2]),
                        ('vd', [128, 2, 128]), ('ud', [128, 2, 128]), ('ones', [128, 512])):
            self.inp(nm, shp)

    def layer1(self, st, hsrc, dst, dkey, dbg=False, only=None):
        P = self.P
        self.mods(st, [1])
        with self.scope() as sL:
            S5O = self.sb(sL, "S5O", [128, 2, NLAT], BF16)
            self.phase_A(sL, hsrc, 1, NT)
            if only in (None, 's5'):
                self.s5_mixer(S5O)
            else:
                P.memset(S5O[:], 0.0, writes=['S5O'])
            if only == 's5':
                o = self.outp("dbg_S5O", [128, 2, NLAT], BF16)
                P.dma('sp', o, S5O[:], reads=['S5O'], writes=['dbg_S5O'])
                return
            H1 = self.scratch("H1b", [NLAT, D])
            ZS = self.scratch("ZS", [NLAT, 768])
            with self.scope() as sA:
                Yacc = self.sb(sA, "Yacc", [128, 16, 768])
                with self.scope() as sB:
                    Xtm = self.sb(sB, "Xtm", [128, NT, 768], BF16)
                    Btm = self.sb(sB, "Btm", [128, NT, 2, 128], BF16)
                    BT = self.sb(sB, "BT", [128, 2, NTOK], BF16)
                    CT = self.sb(sB, "CT", [128, 2, NTOK], BF16)
                    dtt = self.sb(sB, "dtt", [128, NT, 24])
                    dta = self.sb(sB, "dta", [128, NT, 24])
                    self.ssd_inproj(sB, Xtm, Btm, BT, CT, dtt, dta, ZS)
                    self.ssd_chunks(Xtm, Btm, BT, CT, dtt, dta, Yacc)
                with self.scope() as sC:
                    OTs = self.sb(sC, "OTs", [128, 6, NLAT], BF16)
                    self.ssd_gate(Yacc, ZS, OTs)
                    if only == 'ssd':
                        o = self.outp("dbg_OTs", [128, 6, NLAT], BF16)
                        P.dma('sp', o, OTs[:], reads=['OTs'], writes=['dbg_OTs'])
                        return
                    if dbg:
                        o = self.outp("dbg_S5O", [128, 2, NLAT], BF16)
                        P.dma('sp', o, S5O[:], reads=['S5O'], writes=['dbg_S5O'])
                        o = self.outp("dbg_OTs", [128, 6, NLAT], BF16)
                        P.dma('sp', o, OTs[:], reads=['OTs'], writes=['dbg_OTs'])

                    def lhs(j, t):
                        if j < 6:
                            return OTs[:, j, t * 128:(t + 1) * 128], 'OTs'
                        return S5O[:, j - 6, t * 128:(t + 1) * 128], 'S5O'
                    self.outproj_ln1(sC, 1, self.din['s_w_out'], hsrc, 16, H1, lhs=lhs)
        self.ffn(st, 1, 16, H1, dst, dkey, final=True)

    def build_layer1_test(self, only=None):
        with contextlib.ExitStack() as st:
            self.load_consts(st)
            self.declare_common_inputs()
            self.declare_l1_inputs()
            hin = self.inp("hin", [NTOK, D])
            self.UT = self.sb(st, "UT", [128, 8, NTOK], BF16)
            out = self.outp("out", [NLAT, D])
            self.final.remove("out")
            self.layer1(st, hin, out, 'out', dbg=True, only=only)
            if only is not None:
                self.dout.pop('out')
            self.P.emit(self.final)
        return self.nc

    def declare_common_inputs(self):
        for nm, shp in (('ln_g', [2, 2, 128, D]), ('ln_b', [2, 2, 128, D]), ('w_up', [2, D, 5632]), ('b_up', [2, 128, 44]),
                        ('conv_w', [2, 128, 3, 44]), ('conv_b', [2, 128, 44]), ('w_down', [2, 2816, D]), ('b_down', [2, 128, D]),
                        ('a_w_in', [D, 2560]), ('a_w_out', [D, D]), ('a_lam', [128, 256]), ('a_subg', [128, 128]),
                        ('f_wblk', [2, 128, 128]), ('f_b', [128, 2]), ('cos4', [128, NLAT]), ('sin4', [128, NLAT]), ('pswap', [128, 128]),
                        ('c64blk', [128, 128]), ('ns64blk', [128, 128])):
            self.inp(nm, shp)
        for nm, shp in (('dftc', [NLAT, NLAT]), ('dfts', [NLAT, NLAT]), ('dftc_c', [NCTX, NCTX]), ('dfts_c', [NCTX, NCTX])):
            self.inp(nm, shp, BF16)

    def build_layer0(self, dbg_ot=False):
        with contextlib.ExitStack() as st:
            self.load_consts(st)
            self.declare_common_inputs()
            xin = self.inp("xin", [NTOK, D])
            self.mods(st, [0])
            self.UT = self.sb(st, "UT", [128, 8, NTOK], BF16)
            H1 = self.scratch("H1", [NTOK, D])
            H2 = self.outp("H2", [NTOK, D])
            self.final.remove("H2")
            with self.scope() as s1:
                self.OT = self.sb(s1, "OT", [128, 8, NTOK], BF16)
                self.phase_A(s1, xin, 0, NT)
                self.attn_mixer()
                if dbg_ot:
                    o = self.outp("dbg_OT", [128, 8, NTOK], BF16)
                    self.P.dma('sp', o, self.OT[:], reads=['OT'], writes=['dbg_OT'])
                self.outproj_ln1(s1, 0, self.din['a_w_out'], xin, NT, H1)
            self.ffn(st, 0, NT, H1, H2, 'H2', final=True)
            self.P.emit(self.final)
        return self.nc

    def build_debug_ffn(self):
        with contextlib.ExitStack() as st:
            self.load_consts(st)
            for nm, shp in (('ln_g', [2, 2, 128, D]), ('ln_b', [2, 2, 128, D]), ('w_up', [2, D, 5632]), ('b_up', [2, 128, 44]),
                            ('conv_w', [2, 128, 3, 44]), ('conv_b', [2, 128, 44]), ('w_down', [2, 2816, D]), ('b_down', [2, 128, D]),
                            ('a_w_out', [D, D])):
                self.inp(nm, shp)
            xin = self.inp("xin", [NTOK, D])
            self.mods(st, [0])
            self.UT = self.sb(st, "UT", [128, 8, NTOK], BF16)
            H1 = self.scratch("H1", [NTOK, D])
            H2 = self.outp("H2", [NTOK, D])
            self.final.remove("H2")
            with self.scope() as s1:
                self.OT = self.sb(s1, "OT", [128, 8, NTOK], BF16)
                self.phase_A(s1, xin, 0, NT)
                o = self.outp("dbg_modc0", [128, 48, 2])
                self.P.dma('sp', o, self.modc[0][:], reads=['modc0'], writes=['dbg_modc0'])
                o = self.outp("dbg_grow0", [2, 2, 1024])
                self.P.dma('sp', o, self.grow[0], reads=['grow0'], writes=['dbg_grow0'])
                o = self.outp("dbg_UT", [128, 8, NTOK], BF16)
                self.P.dma('sp', o, self.UT[:], reads=['UT'], writes=['dbg_UT'])
                self.P.copy(self.OT[:], self.UT[:], reads=['UT'], writes=['OT'], eng='pool')
                self.outproj_ln1(s1, 0, self.din['a_w_out'], xin, NT, H1)
            self.ffn(st, 0, NT, H1, H2, 'H2', final=True)
            self.P.emit(self.final)
        return self.nc

    def build_full(self):
        with contextlib.ExitStack() as st:
            self.load_consts(st)
            self.declare_common_inputs()
            self.declare_l1_inputs()
            xin = self.inp("xin", [NTOK, D])
            self.mods(st, [0])
            self.UT = self.sb(st, "UT", [128, 8, NTOK], BF16)
            H1 = self.scratch("H1", [NTOK, D])
            H2 = self.scratch("H2", [NTOK, D])
            with self.scope() as s1:
                self.OT = self.sb(s1, "OT", [128, 8, NTOK], BF16)
                self.phase_A(s1, xin, 0, NT)
                self.attn_mixer()
                self.outproj_ln1(s1, 0, self.din['a_w_out'], xin, NT, H1)
            self.ffn(st, 0, NT, H1, H2, 'hin1')
            out = self.outp("out", [NLAT, D])
            self.final.remove("out")
            self.layer1(st, H2, out, 'out')
            self.P.emit(self.final)
        return self.nc


_CACHE = {}


def kernel(**inputs):
    inp = {k: np.asarray(v) for k, v in inputs.items()}
    if 'b' not in _CACHE:
        B = Builder()
        B.build_full()
        _CACHE['b'] = B
        _CACHE['c'] = host_consts()
    B = _CACHE['b']
    hc = _CACHE['c']
    in_maps = []
    for b in range(8):
        hl = host_layout(inp, b)
        in_maps.append({k: (hc[k] if k in hc else hl[k]) for k in B.din})
    res = run_bass_kernel_spmd(B.nc, in_maps, core_ids=list(range(8)))
    out = np.stack([np.asarray(r['out']) for r in res.results], axis=0)
    return out.astype(np.float32)
```

```python
import contextlib
import math
import numpy as np
import ml_dtypes
import concourse.bass as bass
import concourse.mybir as mybir
from concourse.bass_utils import run_bass_kernel_spmd

F32 = mybir.dt.float32
BF16 = mybir.dt.bfloat16
AF = mybir.ActivationFunctionType
ALU = mybir.AluOpType

ENGS = ['pe', 'act', 'dve', 'pool', 'sp']
POOL_TO_DVE = True
NDMASEM = 6

D = 1024
NLAT = 2048
NCTX = 256
NTOK = NLAT + NCTX
NT = NTOK // 128
ALPHA = (2 * 2) ** 0.25
EPS = 1e-5
LAM_INIT0 = 0.8 - 0.6 * math.exp(0.0)


class Prog:
    def __init__(self, nc):
        self.nc = nc
        self.ops = {e: [] for e in ENGS}
        self.last_w = {}
        self.readers = {}
        self.barriers = []
        self.bar_dma_start = {e: 0 for e in ENGS}

    def barrier(self):
        pts = []
        for e in ENGS:
            ops = self.ops[e]
            for i in range(len(ops) - 1, -1, -1):
                if not ops[i]['dma'] and ops[i]['fn'] is not None:
                    pts.append((e, i))
                    ops[i]['needed'] = True
                    break
            for i in range(self.bar_dma_start[e], len(ops)):
                if ops[i]['dma']:
                    pts.append((e, i))
            self.bar_dma_start[e] = len(ops)
        self.barriers.append(pts)

    def op(self, eng, fn, reads=(), writes=(), dma=False):
        if eng == 'pool' and not dma and POOL_TO_DVE:
            eng = 'dve'
        ops = self.ops[eng]
        idx = len(ops)
        deps = set()
        for k in reads:
            w = self.last_w.get(k)
            if w is not None:
                deps.add(w)
            if k.startswith('ps'):
                for r in self.readers.get(k, ()):
                    if r[0] != eng:
                        deps.add(r)
        for k in writes:
            w = self.last_w.get(k)
            if w is not None:
                deps.add(w)
            for r in self.readers.get(k, ()):
                deps.add(r)
        best = {}
        out = []
        for (e, i) in deps:
            d = self.ops[e][i]
            if d['dma']:
                out.append((e, i))
            else:
                if e == eng and not dma and eng == 'pe':
                    continue
                if e not in best or best[e] < i:
                    best[e] = i
        for e, i in best.items():
            out.append((e, i))
        for (e, i) in out:
            self.ops[e][i]['needed'] = True
        ops.append(dict(fn=fn, deps=out, dma=dma, needed=False, sem=None, val=None, prev=None, bar=len(self.barriers)))
        me = (eng, idx)
        for k in reads:
            lst = self.readers.setdefault(k, [])
            if not dma:
                lst[:] = [r for r in lst if not (r[0] == eng and not self.ops[r[0]][r[1]]['dma'])]
            lst.append(me)
        for k in writes:
            self.last_w[k] = me
            self.readers[k] = []
        return me

    def dma(self, eng, out, in_, reads=(), writes=(), **kw):
        return self.op(eng, lambda e: e.dma_start(out=out, in_=in_, **kw), reads, writes, dma=True)

    def act(self, out, in_, func, reads=(), writes=(), eng='act', **kw):
        return self.op(eng, lambda e: e.activation(out=out, in_=in_, func=func, **kw), reads, writes)

    def tt(self, out, in0, in1, op, reads=(), writes=(), eng='dve'):
        return self.op(eng, lambda e: e.tensor_tensor(out=out, in0=in0, in1=in1, op=op), reads, writes)

    def ts(self, out, in0, s1, s2, op0, op1=None, reads=(), writes=(), eng='dve'):
        if op1 is None:
            return self.op(eng, lambda e: e.tensor_scalar(out=out, in0=in0, scalar1=s1, scalar2=None, op0=op0), reads, writes)
        return self.op(eng, lambda e: e.tensor_scalar(out=out, in0=in0, scalar1=s1, scalar2=s2, op0=op0, op1=op1), reads, writes)

    def stt(self, out, in0, scalar, in1, op0, op1, reads=(), writes=()):
        return self.op('dve', lambda e: e.scalar_tensor_tensor(out=out, in0=in0, scalar=scalar, in1=in1, op0=op0, op1=op1), reads, writes)

    def copy(self, out, in_, reads=(), writes=(), eng='dve'):
        if eng == 'act':
            return self.op(eng, lambda e: e.activation(out=out, in_=in_, func=AF.Copy), reads, writes)
        return self.op(eng, lambda e: e.tensor_copy(out=out, in_=in_), reads, writes)

    def mm(self, out, lhsT, rhs, start, stop, reads=(), writes=()):
        return self.op('pe', lambda e: e.matmul(out, lhsT=lhsT, rhs=rhs, start=start, stop=stop), reads, writes)

    def tr(self, out, in_, ident, reads=(), writes=()):
        return self.op('pe', lambda e: e.transpose(out=out, in_=in_, identity=ident), reads, writes)

    def scan(self, out, d0, d1, initial, op0, op1, reads=(), writes=()):
        return self.op('dve', lambda e: e.tensor_tensor_scan(out=out, data0=d0, data1=d1, initial=initial, op0=op0, op1=op1), reads, writes)

    def memset(self, ap, val, writes=(), eng='dve'):
        return self.op(eng, lambda e: e.memset(ap, val), (), writes)

    def recip(self, out, in_, reads=(), writes=()):
        return self.op('dve', lambda e: e.reciprocal(out=out, in_=in_), reads, writes)

    def bnstats(self, out, in_, reads=(), writes=()):
        return self.op('dve', lambda e: e.bn_stats(out=out, in_=in_), reads, writes)

    def bnaggr(self, out, in_, reads=(), writes=()):
        return self.op('dve', lambda e: e.bn_aggr(out=out, in_=in_), reads, writes)

    def emit(self, final_keys=()):
        nc = self.nc
        self.op('sp', None, reads=list(final_keys), writes=())
        with contextlib.ExitStack() as st:
            csem = {e: st.enter_context(nc.semaphore("c_" + e)) for e in ENGS}
            dsem = {e: [st.enter_context(nc.semaphore("d_%s%d" % (e, i))) for i in range(NDMASEM)] for e in ENGS}
            for e in ENGS:
                cnt = 0
                dcnt = 0
                lastd = [None] * NDMASEM
                dval = [0] * NDMASEM
                for i, o in enumerate(self.ops[e]):
                    if o['dma']:
                        s = dcnt % NDMASEM
                        dcnt += 1
                        dval[s] += 16
                        o['sem'] = dsem[e][s]
                        o['val'] = dval[s]
                        o['prev'] = lastd[s]
                        lastd[s] = i
                    elif o['needed']:
                        cnt += 1
                        o['sem'] = csem[e]
                        o['val'] = cnt
            block = st.enter_context(nc.Block())

            def run(e, eng):
                waited = {}

                def wait(sem, val):
                    k = id(sem)
                    if waited.get(k, 0) < val:
                        eng.wait_ge(sem, val)
                        waited[k] = val
                bar_done = 0
                for o in self.ops[e]:
                    while bar_done < o['bar']:
                        for (de, di) in self.barriers[bar_done]:
                            d = self.ops[de][di]
                            wait(d['sem'], d['val'])
                        bar_done += 1
                    for (de, di) in o['deps']:
                        d = self.ops[de][di]
                        wait(d['sem'], d['val'])
                    if o['dma'] and o['prev'] is not None:
                        p = self.ops[e][o['prev']]
                        wait(p['sem'], p['val'])
                    if o['fn'] is None:
                        continue
                    ins = o['fn'](eng)
                    if o['dma']:
                        ins.then_inc(o['sem'], 16)
                    elif o['needed']:
                        ins.then_inc(o['sem'], 1)

            @block.tensor
            def _(eng):
                run('pe', eng)

            @block.scalar
            def _(eng):
                run('act', eng)

            @block.vector
            def _(eng):
                run('dve', eng)

            @block.gpsimd
            def _(eng):
                run('pool', eng)

            @block.sync
            def _(eng):
                run('sp', eng)


def host_consts():
    c = {}
    c['ident'] = np.eye(128, dtype=np.float32)
    ps = np.zeros((128, 128), np.float32)
    for m in range(128):
        partner = m + 32 if (m % 64) < 32 else m - 32
        ps[partner, m] = 1.0
    c['pswap'] = ps
    rows = NLAT // 64
    row = np.repeat(np.arange(rows, dtype=np.float32), 64)
    col = np.tile(np.arange(64, dtype=np.float32), rows)
    nf = 16
    inv = (10000.0 ** (-np.arange(nf, dtype=np.float32) / nf)).astype(np.float32)
    ang = np.concatenate([row[:, None] * inv, col[:, None] * inv], axis=-1).astype(np.float32)
    cs, sn = np.cos(ang).astype(np.float32), np.sin(ang).astype(np.float32)
    cos4 = np.zeros((128, NLAT), np.float32)
    sin4 = np.zeros((128, NLAT), np.float32)
    for p in range(128):
        cos4[p] = cs[:, p % 32]
        sin4[p] = sn[:, p % 32] * (-1.0 if (p % 64) < 32 else 1.0)
    c['cos4'] = cos4
    c['sin4'] = sin4

    def dft(n):
        k = np.arange(n, dtype=np.int64)
        kk = (k[:, None] * k[None, :]) % n
        a = 2.0 * np.pi * kk.astype(np.float64) / n
        return np.cos(a) / np.sqrt(n), np.sin(a) / np.sqrt(n)
    C, S = dft(NLAT)
    c['dftc'] = C.astype(ml_dtypes.bfloat16)
    c['dfts'] = S.astype(ml_dtypes.bfloat16)
    C, S = dft(NCTX)
    c['dftc_c'] = C.astype(ml_dtypes.bfloat16)
    c['dfts_c'] = S.astype(ml_dtypes.bfloat16)
    C, S = dft(64)
    cb = np.zeros((128, 128), np.float32)
    sb = np.zeros((128, 128), np.float32)
    for g in range(2):
        cb[g * 64:(g + 1) * 64, g * 64:(g + 1) * 64] = C
        sb[g * 64:(g + 1) * 64, g * 64:(g + 1) * 64] = -S
    c['c64blk'] = cb
    c['ns64blk'] = sb
    sel = np.zeros((2, 2, 128), np.float32)
    sel[0, 0, :] = 1.0
    sel[1, 1, :] = 1.0
    c['sel'] = sel
    k = np.arange(128)
    vd = np.zeros((128, 2, 128), np.float32)
    ud = np.zeros((128, 2, 128), np.float32)
    vd[:, 0, :] = (k[:, None] <= k[None, :])
    vd[:, 1, :] = (k[:, None] >= k[None, :])
    ud[:, 0, :] = (k[:, None] > k[None, :])
    ud[:, 1, :] = (k[:, None] < k[None, :])
    c['vd'] = vd
    c['ud'] = ud
    c['ones'] = np.ones((128, 512), np.float32)
    return c


def colvec(v, n):
    return np.ascontiguousarray(np.asarray(v, np.float32).reshape(n, 128).T)


def bcast(v, p=128):
    v = np.asarray(v, np.float32).reshape(1, -1)
    return np.ascontiguousarray(np.broadcast_to(v, (p, v.shape[1])))


def host_layout(inp, b):
    m = {}
    m['xin'] = np.ascontiguousarray(np.concatenate([inp['x'][b], inp['ctx'][b]], axis=0))
    cv = np.stack([colvec(inp['c'][b], 8), colvec(inp['c_ctx'], 8)], axis=-1)
    m['cvec'] = np.ascontiguousarray(cv)
    m['ada_w'] = inp['ada_w']
    m['ada_bc'] = np.ascontiguousarray(np.stack([colvec(inp['ada_b'][l], 48) for l in range(2)]))
    m['ada_b2'] = np.ascontiguousarray(np.stack([np.stack([inp['ada_b'][l]] * 2) for l in range(2)]))
    m['ln_g'] = np.ascontiguousarray(np.stack([np.stack([bcast(inp['ln_g'][l][i]) for i in range(2)]) for l in range(2)]))
    m['ln_b'] = np.ascontiguousarray(np.stack([np.stack([bcast(inp['ln_b'][l][i]) for i in range(2)]) for l in range(2)]))
    m['w_up'] = inp['ffn_w_up']
    m['b_up'] = np.ascontiguousarray(np.stack([colvec(inp['ffn_b_up'][l], 44) for l in range(2)]))
    m['conv_w'] = np.ascontiguousarray(np.stack([np.stack([colvec(inp['ffn_conv_w'][l][k], 44) for k in range(3)], axis=1) for l in range(2)]))
    m['conv_b'] = np.ascontiguousarray(np.stack([colvec(inp['ffn_conv_b'][l], 44) for l in range(2)]))
    m['w_down'] = inp['ffn_w_down']
    m['b_down'] = np.ascontiguousarray(np.stack([bcast(inp['ffn_b_down'][l]) for l in range(2)]))
    m['a_w_in'] = inp['attn_w_in'][0]
    m['a_w_out'] = inp['attn_w_out'][0]
    m['a_lam'] = bcast(inp['attn_lambda'][0].reshape(-1))
    m['a_subg'] = bcast(inp['attn_subln_g'][0])
    fw = inp['fourier_w'][0]
    wb = np.zeros((2, 128, 128), np.float32)
    for ch in range(2):
        for g in range(2):
            wb[ch, g * 64:(g + 1) * 64, g * 64:(g + 1) * 64] = fw[ch * 2 + g]
    m['f_wblk'] = wb
    m['f_b'] = colvec(inp['fourier_b'][0], 2)
    m['s_w_in'] = inp['ssm_w_in'][0]
    m['s_w_out'] = inp['ssm_w_out'][0]
    m['sd_cw'] = np.ascontiguousarray(np.stack([colvec(inp['ssd_conv_w'][0][k], 10) for k in range(3)], axis=1))
    m['sd_cb'] = colvec(inp['ssd_conv_b'][0], 10)
    m['sd_alog'] = bcast(inp['ssd_a_log'][0].reshape(-1))
    m['sd_dtb'] = bcast(inp['ssd_dt_bias'][0].reshape(-1))
    m['sd_d'] = bcast(inp['ssd_d'][0])
    m['sd_ng'] = bcast(inp['ssd_norm_g'][0])

    def dsc(a):
        out = np.zeros((128, 16), np.float32)
        for d in range(2):
            for sc in range(8):
                for half in range(2):
                    out[half * 64:(half + 1) * 64, d * 8 + sc] = a[d, 2 * sc + half, :]
        return out
    m['s5_lre'] = dsc(inp['s5_lambda_re'][0])
    m['s5_lim'] = dsc(inp['s5_lambda_im'][0])
    m['s5_ldt'] = dsc(np.broadcast_to(inp['s5_log_dt'][0][:, :, None], (2, 16, 64)))

    def bexp(b):
        out = np.zeros((128, 8, 128), np.float32)
        for sc in range(8):
            for half in range(2):
                g = 2 * sc + half
                col = (g % 8) * 16
                out[half * 64:(half + 1) * 64, sc, col:col + 16] = b[g]
        return out
    m['s5_bre'] = bexp(inp['s5_b_re'][0])
    m['s5_bim'] = bexp(inp['s5_b_im'][0])

    def cexp(c):
        out = np.zeros((2, 128, 8, 128), np.float32)
        for d in range(2):
            for sc in range(8):
                for half in range(2):
                    g = 2 * sc + half
                    col = (g % 8) * 16
                    out[d, half * 64:(half + 1) * 64, sc, col:col + 16] = c[d, g].T
        return out
    m['s5_cre'] = cexp(inp['s5_c_re'][0])
    m['s5_cim'] = cexp(inp['s5_c_im'][0])
    m['s5_dd'] = colvec(inp['s5_d'][0], 2)
    m['s5_glu_w'] = inp['s5_glu_w'][0]
    m['s5_glu_b'] = colvec(inp['s5_glu_b'][0], 2)
    return m


class Builder:
    def __init__(self, debug=()):
        self.debug = set(debug)
        self.nc = bass.Bass("TRN2", target_bir_lowering=False)
        self.P = Prog(self.nc)
        self.din = {}
        self.dout = {}
        self.final = []
        self.uid = 0
        self.ps = [self.nc.alloc_psum_tensor("ps%d" % i, [128, 512], F32) for i in range(8)]
        self.psrr = 0

    def inp(self, name, shape, dt=F32):
        self.din[name] = self.nc.dram_tensor(name, list(shape), dt, kind="ExternalInput").ap()
        return self.din[name]

    def outp(self, name, shape, dt=F32):
        self.dout[name] = self.nc.dram_tensor(name, list(shape), dt, kind="ExternalOutput").ap()
        self.final.append(name)
        return self.dout[name]

    def scratch(self, name, shape, dt=F32):
        return self.nc.dram_tensor(name, list(shape), dt, kind="Internal").ap()

    def sb(self, st, name, shape, dt=F32):
        self.uid += 1
        return st.enter_context(self.nc.sbuf_tensor("s%d_%s" % (self.uid, name), list(shape), dt))

    @contextlib.contextmanager
    def scope(self):
        with contextlib.ExitStack() as s2:
            yield s2
        self.P.barrier()

    def nextps(self, lo=0, hi=8):
        n = hi - lo
        i = lo + (self.psrr % n)
        self.psrr += 1
        return i

    def dbg(self, name, tile_ap, key, shape, dt=F32):
        if name in self.debug:
            o = self.outp("dbg_" + name, shape, dt)
            self.P.dma('sp', o, tile_ap, reads=[key], writes=["dbg_" + name])

    def load_consts(self, st):
        P = self.P
        self.ident = self.sb(st, "ident", [128, 128])
        P.dma('sp', self.ident[:], self.inp("ident", [128, 128]), writes=['ident'])
        self.sel = self.sb(st, "sel", [2, 2, 128])
        P.dma('sp', self.sel[:], self.inp("sel", [2, 2, 128]), writes=['sel'])

    def mods(self, st, layers):
        P, nc = self.P, self.nc
        if 'cvec' not in self.din:
            self.inp("cvec", [128, 8, 2])
            self.inp("ada_w", [2, D, 6 * D])
            self.inp("ada_bc", [2, 128, 48])
            self.inp("ada_b2", [2, 2, 6 * D])
            self.modc = [self.sb(st, "modc%d" % l, [128, 48, 2]) for l in range(2)]
            g = self.sb(st, "grow", [2, 2, 1024])
            self.grow = [g[:], g[:]]
            for l in range(2):
                P.memset(self.modc[l][:], 0.0, writes=['modc%d' % l])
        cvec, ada_w, ada_bc, ada_b2 = self.din['cvec'], self.din['ada_w'], self.din['ada_bc'], self.din['ada_b2']
        with self.scope() as s2:
            cv = self.sb(s2, "cv", [128, 8, 2])
            sT = self.sb(s2, "sT", [128, 8, 2], BF16)
            abc = self.sb(s2, "abc", [128, 2, 48])
            ab2 = self.sb(s2, "ab2", [2, 2, 2, 1024])
            P.dma('sp', cv[:], cvec, writes=['cv'])
            P.dma('sp', abc[:], ada_bc.rearrange("l p f -> p l f"), writes=['abc'])
            for l in layers:
                for gi, m in enumerate((2, 5)):
                    P.dma('sp', ab2[:, l, gi, :], ada_b2[l, :, m * 1024:(m + 1) * 1024], writes=['ab2'])
            P.act(sT[:], cv[:], AF.Silu, reads=['cv'], writes=['sT'])
            NWB = 4
            wbuf = [self.sb(s2, "adaw%d" % i, [128, 8, 1024], BF16) for i in range(NWB)]
            n = 0
            for l in layers:
                for m in range(6):
                    wb = wbuf[n % NWB]
                    wk = 'adaw%d' % (n % NWB)
                    n += 1
                    P.dma('pool', wb[:], ada_w[l, :, m * 1024:(m + 1) * 1024].rearrange("(kc p) f -> p kc f", p=128), writes=[wk])
                    if m in (0, 1, 3, 4):
                        bk = self.nextps()
                        pst = self.ps[bk]
                        for j in range(8):
                            for kc in range(8):
                                P.mm(pst[:, 2 * j:2 * j + 2], wb[:, kc, j * 128:(j + 1) * 128], sT[:, kc, :], kc == 0, kc == 7,
                                     reads=[wk, 'sT'], writes=['ps%d' % bk])
                        P.tt(self.modc[l][:, m * 8:(m + 1) * 8, :], pst[:, 0:16].rearrange("p (j w) -> p j w", w=2),
                             abc[:, l, m * 8:(m + 1) * 8].unsqueeze(2).to_broadcast([128, 8, 2]), ALU.add,
                             reads=['ps%d' % bk, 'abc'], writes=['modc%d' % l])
                        if m in (1, 4):
                            P.ts(self.modc[l][:, m * 8:(m + 1) * 8, :], self.modc[l][:, m * 8:(m + 1) * 8, :], 1.0, None, ALU.add,
                                 reads=['modc%d' % l], writes=['modc%d' % l])
                    else:
                        gi = 0 if m == 2 else 1
                        for half in range(2):
                            bk = self.nextps()
                            pst = self.ps[bk]
                            for kc in range(8):
                                P.mm(pst[0:2, :], sT[:, kc, :], wb[:, kc, half * 512:(half + 1) * 512], kc == 0, kc == 7,
                                     reads=[wk, 'sT'], writes=['ps%d' % bk])
                            P.tt(self.grow[l][:, gi, half * 512:(half + 1) * 512], pst[0:2, :], ab2[:, l, gi, half * 512:(half + 1) * 512], ALU.add,
                                 reads=['ps%d' % bk, 'ab2'], writes=['grow'])

    def gate_bcast(self, dst, dkey, l, gi, w):
        P = self.P
        for half in range(2):
            bk = self.nextps()
            P.mm(self.ps[bk][:, :], self.sel[:, w, :], self.grow[l][:, gi, half * 512:(half + 1) * 512], True, True,
                 reads=['sel', 'grow'], writes=['ps%d' % bk])
            P.copy(dst[:, half * 512:(half + 1) * 512], self.ps[bk][:, :], reads=['ps%d' % bk], writes=[dkey], eng='act')

    def transpose_mod(self, src_tile, skey, UT, ukey, t, l, m_shift, m_scale):
        P = self.P
        w = 0 if t < 16 else 1
        for half in range(2):
            bk = self.nextps()
            for q in range(4):
                j = half * 4 + q
                P.tr(self.ps[bk][:, q * 128:(q + 1) * 128], src_tile[:, j * 128:(j + 1) * 128], self.ident[:],
                     reads=[skey, 'ident'], writes=['ps%d' % bk])
            for q in range(4):
                j = half * 4 + q
                P.act(UT[:, j, t * 128:(t + 1) * 128], self.ps[bk][:, q * 128:(q + 1) * 128], AF.Identity,
                      reads=['ps%d' % bk, 'modc%d' % l], writes=[ukey],
                      scale=self.modc[l][:, m_scale * 8 + j, w:w + 1], bias=self.modc[l][:, m_shift * 8 + j, w:w + 1])

    def phase_A(self, st, src, l, ntiles):
        P = self.P
        UT = self.UT
        with self.scope() as s2:
            hb = [self.sb(s2, "hA%d" % i, [128, D]) for i in range(2)]
            for t in range(ntiles):
                h = hb[t % 2]
                hk = 'hA%d' % (t % 2)
                P.dma('sp', h[:], src[t * 128:(t + 1) * 128, :], reads=['hin%d_%d' % (l, t)], writes=[hk])
                self.transpose_mod(h, hk, UT, 'UT', t, l, 0, 1)

    def layer_norm(self, r, rkey, g, b, gbkey, out, okey, s2tmp):
        P = self.P
        st6, mv, rstd, nmr, tk = s2tmp
        for c in range(2):
            P.bnstats(st6[:, c, :], r[:, c * 512:(c + 1) * 512], reads=[rkey], writes=['lnst' + tk])
        P.bnaggr(mv[:], st6[:].rearrange("p a b -> p (a b)"), reads=['lnst' + tk], writes=['lnmv' + tk])
        P.ts(rstd[:], mv[:, 1:2], EPS, None, ALU.add, reads=['lnmv' + tk], writes=['lnrstd' + tk])
        P.act(rstd[:], rstd[:], AF.Sqrt, reads=['lnrstd' + tk], writes=['lnrstd' + tk])
        P.recip(rstd[:], rstd[:], reads=['lnrstd' + tk], writes=['lnrstd' + tk])
        P.stt(nmr[:], mv[:, 0:1], -1.0, rstd[:], ALU.mult, ALU.mult, reads=['lnmv' + tk, 'lnrstd' + tk], writes=['lnnmr' + tk])
        P.act(out[:], r[:], AF.Identity, reads=[rkey, 'lnrstd' + tk, 'lnnmr' + tk], writes=[okey], scale=rstd[:, 0:1], bias=nmr[:, 0:1])
        P.tt(out[:], out[:], g, ALU.mult, reads=[okey, gbkey], writes=[okey], eng='pool')
        P.tt(out[:], out[:], b, ALU.add, reads=[okey, gbkey], writes=[okey], eng='pool')

    def outproj_ln1(self, st, l, w_out_d, hsrc, ntiles, H1, lhs=None):
        P = self.P
        OT, UT = getattr(self, 'OT', None), self.UT
        with self.scope() as s2:
            wo = self.sb(s2, "wo", [128, 8, D], BF16)
            P.dma('pool', wo[:], w_out_d.rearrange("(kc p) f -> p kc f", p=128), writes=['wo'])
            lng = self.sb(s2, "lng", [128, D])
            lnb = self.sb(s2, "lnb", [128, D])
            P.dma('sp', lng[:], self.din['ln_g'][l, 0], writes=['lngb'])
            P.dma('sp', lnb[:], self.din['ln_b'][l, 0], writes=['lngb'])
            gts = []
            for w in range(2 if ntiles > 16 else 1):
                gt = self.sb(s2, "gmix%d" % w, [128, D])
                self.gate_bcast(gt, 'gmix%d' % w, l, 0, w)
                gts.append(gt)
            hb = [self.sb(s2, "hB%d" % i, [128, D]) for i in range(2)]
            rb = [self.sb(s2, "rB%d" % i, [128, D]) for i in range(2)]
            ob = [self.sb(s2, "oB%d" % i, [128, D]) for i in range(2)]
            tmps = [(self.sb(s2, "lnst6", [128, 2, 6]), self.sb(s2, "lnmv", [128, 2]), self.sb(s2, "lnrstd", [128, 1]), self.sb(s2, "lnnmr", [128, 1]), 'A%d' % i) for i in range(2)]
            mmb = {}

            def mm_pe(t):
                h, hk = hb[t % 2], 'hB%d' % (t % 2)
                P.dma('sp', h[:], hsrc[t * 128:(t + 1) * 128, :], reads=['hin%d_%d' % (l, t)], writes=[hk])
                mmb[t] = []
                for half in range(2):
                    bk = self.nextps()
                    mmb[t].append(bk)
                    for j in range(8):
                        la, lk = lhs(j, t) if lhs is not None else (OT[:, j, t * 128:(t + 1) * 128], 'OT')
                        P.mm(self.ps[bk][:, :], la, wo[:, j, half * 512:(half + 1) * 512], j == 0, j == 7,
                             reads=[lk, 'wo'], writes=['ps%d' % bk])

            def mm_dve(t):
                w = 0 if t < 16 else 1
                h, hk = hb[t % 2], 'hB%d' % (t % 2)
                r, rk = rb[t % 2], 'rB%d' % (t % 2)
                for half in range(2):
                    bk = mmb[t][half]
                    P.tt(r[:, half * 512:(half + 1) * 512], self.ps[bk][:, :], gts[w][:, half * 512:(half + 1) * 512], ALU.mult,
                         reads=['ps%d' % bk, 'gmix%d' % w], writes=[rk])
                P.stt(r[:], h[:], ALPHA, r[:], ALU.mult, ALU.add, reads=[hk, rk], writes=[rk])

            def ln_tile(t):
                r, rk = rb[t % 2], 'rB%d' % (t % 2)
                o, ok = ob[t % 2], 'oB%d' % (t % 2)
                self.layer_norm(r, rk, lng[:], lnb[:], 'lngb', o, ok, tmps[t % 2])
                P.dma('sp', H1[t * 128:(t + 1) * 128, :], o[:], reads=[ok], writes=['H1_%d_%d' % (l, t)])
                if t == 0:
                    self.dbg('h1_%d' % l, o[:], ok, [128, D])
                self.transpose_mod(o, ok, UT, 'UT', t, l, 3, 4)
            mm_pe(0)
            mm_dve(0)
            for t in range(ntiles):
                if t + 1 < ntiles:
                    mm_pe(t + 1)
                ln_tile(t)
                if t + 1 < ntiles:
                    mm_dve(t + 1)

    def ffn(self, st, l, ntiles, H1, dst, dkey, final=False):
        P = self.P
        UT = self.UT
        w_up = self.din['w_up'][l].rearrange("(kc p) f -> p kc f", p=128)
        w_dn = self.din['w_down'][l].rearrange("(j p) f -> p j f", p=128)
        if ntiles > 16:
            sbs = [[(0, 6, False, True)], [(6, 6, True, True)], [(12, 4, True, False), (16, 2, False, False)]]
        else:
            sbs = [[(0, 6, False, True)], [(6, 6, True, True)], [(12, 4, True, False)]]
        with self.scope() as s2:
            wd = self.sb(s2, "wd", [128, 22, D], BF16)
            for q in range(2):
                P.dma('pool', wd[:, q * 11:(q + 1) * 11, :], w_dn[:, q * 11:(q + 1) * 11, :], writes=['wd'])
            bup = self.sb(s2, "bup", [128, 44])
            cw = self.sb(s2, "cw", [128, 3, 44])
            cb = self.sb(s2, "cb", [128, 44])
            P.dma('sp', bup[:], self.din['b_up'][l], writes=['ffc'])
            P.dma('sp', cw[:], self.din['conv_w'][l], writes=['ffc'])
            P.dma('sp', cb[:], self.din['conv_b'][l], writes=['ffc'])
            lng = self.sb(s2, "lng2", [128, D])
            lnb = self.sb(s2, "lnb2", [128, D])
            bdn = self.sb(s2, "bdn", [128, D])
            P.dma('sp', lng[:], self.din['ln_g'][l, 1], writes=['lngb2'])
            P.dma('sp', lnb[:], self.din['ln_b'][l, 1], writes=['lngb2'])
            P.dma('sp', bdn[:], self.din['b_down'][l], writes=['bdn'])
            gts = []
            for w in range(2 if ntiles > 16 else 1):
                gt = self.sb(s2, "gffn%d" % w, [128, D])
                self.gate_bcast(gt, 'gffn%d' % w, l, 1, w)
                gts.append(gt)
            W = 774 + 2
            GT = self.sb(s2, "GT", [128, 22, W], BF16)
            xs = [self.sb(s2, "xs%d" % i, [128, W], BF16) for i in range(4)]
            dg = [self.sb(s2, "dg%d" % i, [128, 6, 128], BF16) for i in range(2)]
            sgb = [self.sb(s2, "sg%d" % i, [128, 512]) for i in range(2)]
            idb = self.sb(s2, "identb", [128, 128], BF16)
            P.copy(idb[:], self.ident[:], reads=['ident'], writes=['identb'])
            wub = [self.sb(s2, "wu%d" % i, [128, 8, 256], BF16) for i in range(3)]
            hb = [self.sb(s2, "hF%d" % i, [128, D]) for i in range(2)]
            rb = [self.sb(s2, "rF%d" % i, [128, D]) for i in range(2)]
            ob = [self.sb(s2, "oF%d" % i, [128, D]) for i in range(2)]
            tmps = [(self.sb(s2, "lnst6b", [128, 2, 6]), self.sb(s2, "lnmvb", [128, 2]), self.sb(s2, "lnrstdb", [128, 1]), self.sb(s2, "lnnmrb", [128, 1]), 'B%d' % i) for i in range(2)]
            for i in range(4):
                P.memset(xs[i][:], 0.0, writes=['xs%d' % i], eng='pool')
            nw = 0
            nsg = 0
            njobs = len(sbs) * 22

            def load_w(n):
                j = n % 22
                wu, wuk = wub[n % 3], 'wu%d' % (n % 3)
                P.dma('pool', wu[:, :, 0:128], w_up[:, :, j * 128:(j + 1) * 128], writes=[wuk])
                P.dma('pool', wu[:, :, 128:256], w_up[:, :, 2816 + j * 128:2816 + (j + 1) * 128], writes=[wuk])
            load_w(0)
            load_w(1)
            for sbi, segs in enumerate(sbs):
                cols = []
                c0 = 0
                for (t0, nt_, hl, hr) in segs:
                    cols.append(c0)
                    c0 += nt_ * 128 + 2
                width = c0
                if sbi > 0:
                    for i in range(4):
                        P.memset(xs[i][:, 0:width], 0.0, writes=['xs%d' % i], eng='pool')
                ogroups = []
                for si, (t0, nt_, hl, hr) in enumerate(segs):
                    n = nt_ * 128
                    off = 0
                    while off < n:
                        g = min(512, n - off)
                        ogroups.append((cols[si] + 1 + off, g))
                        off += g

                def conv_job(nwj, j):
                    nonlocal nsg
                    d_, dk_ = dg[nwj % 2], 'dg%d' % (nwj % 2)
                    for (cpos, g) in ogroups:
                        pb = []
                        for vg in (1, 0):
                            xi = (nwj % 2) * 2 + vg
                            x, xk = xs[xi], 'xs%d' % xi
                            bk = self.nextps()
                            pb.append(bk)
                            for tap in range(3):
                                P.mm(self.ps[bk][:, 0:g], d_[:, vg * 3 + tap, :], x[:, cpos + tap - 1:cpos + tap - 1 + g], tap == 0, tap == 2,
                                     reads=[dk_, xk], writes=['ps%d' % bk])
                        sg, sgk = sgb[nsg % 2], 'sg%d' % (nsg % 2)
                        nsg += 1
                        P.act(sg[:, 0:g], self.ps[pb[0]][:, 0:g], AF.Silu, reads=['ps%d' % pb[0], 'ffc'], writes=[sgk], bias=cb[:, 22 + j:22 + j + 1])
                        P.stt(GT[:, j, cpos:cpos + g], self.ps[pb[1]][:, 0:g], cb[:, j:j + 1], sg[:, 0:g], ALU.add, ALU.mult,
                              reads=['ps%d' % pb[1], 'ffc', sgk], writes=['GT'])
                pend = None
                for j in range(22):
                    wu, wuk = wub[nw % 3], 'wu%d' % (nw % 3)
                    if nw + 2 < njobs:
                        load_w(nw + 2)
                    d_, dk_ = dg[nw % 2], 'dg%d' % (nw % 2)
                    for vg in range(2):
                        ch = vg * 22 + j
                        for tap in range(3):
                            P.act(d_[:, vg * 3 + tap, :], idb[:], AF.Identity, reads=['identb', 'ffc'], writes=[dk_], scale=cw[:, tap, ch:ch + 1], bias=0.0)
                    for vg in range(2):
                        xi = (nw % 2) * 2 + vg
                        x, xk = xs[xi], 'xs%d' % xi
                        ch = vg * 22 + j
                        for si, (t0, nt_, hl, hr) in enumerate(segs):
                            base = cols[si]
                            tok0 = t0 * 128
                            groups = []
                            n = nt_ * 128
                            off = 0
                            while off < n:
                                g = min(512, n - off)
                                groups.append((tok0 + off, g, base + 1 + off))
                                off += g
                            if hl:
                                groups.append((tok0 - 1, 1, base))
                            if hr:
                                groups.append((tok0 + n, 1, base + 1 + n))
                            for (tk, g, cpos) in groups:
                                bk = self.nextps()
                                for kc in range(8):
                                    P.mm(self.ps[bk][:, 0:g], wu[:, kc, vg * 128:(vg + 1) * 128], UT[:, kc, tk:tk + g], kc == 0, kc == 7,
                                         reads=[wuk, 'UT'], writes=['ps%d' % bk])
                                P.act(x[:, cpos:cpos + g], self.ps[bk][:, 0:g], AF.Identity, reads=['ps%d' % bk, 'ffc'], writes=[xk],
                                      bias=bup[:, ch:ch + 1])
                    if pend is not None:
                        conv_job(*pend)
                    pend = (nw, j)
                    nw += 1
                conv_job(*pend)
                for si, (t0, nt_, hl, hr) in enumerate(segs):
                    base = cols[si]
                    for ti in range(nt_):
                        t = t0 + ti
                        w = 0 if t < 16 else 1
                        h, hk = hb[t % 2], 'hF%d' % (t % 2)
                        r, rk = rb[t % 2], 'rF%d' % (t % 2)
                        o, ok = ob[t % 2], 'oF%d' % (t % 2)
                        P.dma('sp', h[:], H1[t * 128:(t + 1) * 128, :], reads=['H1_%d_%d' % (l, t)], writes=[hk])
                        c1 = base + 1 + ti * 128
                        for half in range(2):
                            bk = self.nextps()
                            for j in range(22):
                                P.mm(self.ps[bk][:, :], GT[:, j, c1:c1 + 128], wd[:, j, half * 512:(half + 1) * 512], j == 0, j == 21,
                                     reads=['GT', 'wd'], writes=['ps%d' % bk])
                            P.tt(r[:, half * 512:(half + 1) * 512], self.ps[bk][:, :], bdn[:, half * 512:(half + 1) * 512], ALU.add,
                                 reads=['ps%d' % bk, 'bdn'], writes=[rk])
                        P.tt(r[:], r[:], gts[w][:], ALU.mult, reads=[rk, 'gffn%d' % w], writes=[rk], eng='pool')
                        P.stt(r[:], h[:], ALPHA, r[:], ALU.mult, ALU.add, reads=[hk, rk], writes=[rk])
                        self.layer_norm(r, rk, lng[:], lnb[:], 'lngb2', o, ok, tmps[t % 2])
                        P.dma('sp', dst[t * 128:(t + 1) * 128, :], o[:], reads=[ok], writes=['%s_%d' % (dkey, t)])
                        if final:
                            self.final.append('%s_%d' % (dkey, t))

    def attn_mixer(self, ntiles=NT):
        P = self.P
        UT, OT = self.UT, self.OT
        w_in = self.din['a_w_in'].rearrange("(kc p) f -> p kc f", p=128)
        blocks = [(0, 512), (512, 512), (1024, 512), (1536, 512), (2048, 256)]
        with self.scope() as s1:
            VA = self.sb(s1, "VA", [128, NT, 6, 130], BF16)
            FT = self.sb(s1, "FT", [128, 2, NTOK], BF16)
            P.memset(VA[:, :, :, 128:130], 1.0, writes=['VA'], eng='pool')
            with self.scope() as s2:
                wv = self.sb(s2, "wv", [128, 8, 1024], BF16)
                for q in range(2):
                    P.dma('pool', wv[:, :, q * 512:(q + 1) * 512], w_in[:, :, 1536 + q * 512:1536 + (q + 1) * 512], writes=['wv'])
                for t in range(NT):
                    for half in range(2):
                        bk = self.nextps()
                        for kc in range(8):
                            P.mm(self.ps[bk][:, 0:384], UT[:, kc, t * 128:(t + 1) * 128], wv[:, kc, half * 384:(half + 1) * 384], kc == 0, kc == 7,
                                 reads=['UT', 'wv'], writes=['ps%d' % bk])
                        P.copy(VA[:, t, half * 3:(half + 1) * 3, 0:128], self.ps[bk][:, 0:384].rearrange("p (a b) -> p a b", a=3),
                               reads=['ps%d' % bk], writes=['VA'], eng=('act' if half == 0 else 'dve'))
                for c in range(2):
                    for (t0, n) in blocks:
                        bk = self.nextps()
                        for kc in range(8):
                            P.mm(self.ps[bk][:, 0:n], wv[:, kc, 768 + c * 128:768 + (c + 1) * 128], UT[:, kc, t0:t0 + n], kc == 0, kc == 7,
                                 reads=['UT', 'wv'], writes=['ps%d' % bk])
                        P.copy(FT[:, c, t0:t0 + n], self.ps[bk][:, 0:n], reads=['ps%d' % bk], writes=['FT'], eng='act')
            with self.scope() as s2:
                cos4 = self.sb(s2, "cos4", [128, NLAT])
                sin4 = self.sb(s2, "sin4", [128, NLAT])
                psw = self.sb(s2, "pswap", [128, 128])
                P.dma('sp', cos4[:], self.din['cos4'], writes=['cos4'])
                P.dma('sp', sin4[:], self.din['sin4'], writes=['sin4'])
                P.dma('sp', psw[:], self.din['pswap'], writes=['pswap'])
                lamt = self.sb(s2, "lamt", [128, 256])
                gsub = self.sb(s2, "gsub", [128, 128])
                P.dma('sp', lamt[:], self.din['a_lam'], writes=['lamt'])
                P.dma('sp', gsub[:], self.din['a_subg'], writes=['gsub'])
                P.ts(gsub[:], gsub[:], 1.0 - LAM_INIT0, None, ALU.mult, reads=['gsub'], writes=['gsub'])
                lprod = self.sb(s2, "lprod", [128, 128])
                lsum = self.sb(s2, "lsum", [128, 2])
                nlam = self.sb(s2, "nlam", [128, 1])
                P.tt(lprod[:, 0:64], lamt[:, 0:64], lamt[:, 64:128], ALU.mult, reads=['lamt'], writes=['lprod'])
                P.tt(lprod[:, 64:128], lamt[:, 128:192], lamt[:, 192:256], ALU.mult, reads=['lamt'], writes=['lprod'])
                P.op('dve', lambda e: e.tensor_reduce(out=lsum[:], in_=lprod[:].rearrange("p (a b) -> p a b", a=2), axis=mybir.AxisListType.X, op=ALU.add),
                     reads=['lprod'], writes=['lsum'])
                P.act(lsum[:], lsum[:], AF.Exp, reads=['lsum'], writes=['lsum'])
                P.tt(nlam[:], lsum[:, 1:2], lsum[:, 0:1], ALU.subtract, reads=['lsum'], writes=['nlam'])
                P.ts(nlam[:], nlam[:], -LAM_INIT0, None, ALU.add, reads=['nlam'], writes=['nlam'])
                wqk = [self.sb(s2, "wqk%d" % i, [128, 8, 256], BF16) for i in range(2)]
                QT = [self.sb(s2, "QT%d" % i, [128, NTOK], BF16) for i in range(2)]
                KT = [self.sb(s2, "KT%d" % i, [128, 2, NTOK], BF16) for i in range(2)]
                for i in range(2):
                    P.memset(KT[i][:], 0.0, writes=['KT%d' % i], eng='pool')
                PT = [self.sb(s2, "PT%d" % i, [128, 512], BF16) for i in range(4)]
                qtmp = [self.sb(s2, "qtmp%d" % i, [128, 512]) for i in range(2)]
                ropA = [self.sb(s2, "ropA%d" % i, [128, 512]) for i in range(2)]
                otmp = self.sb(s2, "otmp", [128, 4, 128])
                ofin = [self.sb(s2, "ofin%d" % i, [128, 128]) for i in range(4)]
                osq = self.sb(s2, "osq", [128, 128])
                sm = self.sb(s2, "sm", [128, 4, 4])
                nrope = 0
                npt = 0

                def load_wqk(h):
                    wq, wqkey = wqk[h % 2], 'wqk%d' % (h % 2)
                    P.dma('pool', wq[:, :, 0:128], w_in[:, :, h * 128:(h + 1) * 128], writes=[wqkey])
                    P.dma('pool', wq[:, :, 128:256], w_in[:, :, 768 + h * 128:768 + (h + 1) * 128], writes=[wqkey])
                def proj(h):
                    nonlocal nrope
                    wq, wqkey = wqk[h % 2], 'wqk%d' % (h % 2)
                    dsts = ((QT[h % 2], 'QT%d' % (h % 2)), (KT[h % 2], 'KT%d' % (h % 2)))
                    for qk in range(2):
                        dst, dk = dsts[qk]
                        for (t0, n) in blocks:
                            bk = self.nextps(4, 8)
                            for kc in range(8):
                                P.mm(self.ps[bk][:, 0:n], wq[:, kc, qk * 128:(qk + 1) * 128], UT[:, kc, t0:t0 + n], kc == 0, kc == 7,
                                     reads=[wqkey, 'UT'], writes=['ps%d' % bk])
                            if t0 < NLAT:
                                qt_, qtk = qtmp[nrope % 2], 'qtmp%d' % (nrope % 2)
                                ra, rak = ropA[nrope % 2], 'ropA%d' % (nrope % 2)
                                nrope += 1
                                P.copy(qt_[:, 0:n], self.ps[bk][:, 0:n], reads=['ps%d' % bk], writes=[qtk], eng='act')
                                b2 = self.nextps(4, 8)
                                P.mm(self.ps[b2][:, 0:n], psw[:], qt_[:, 0:n], True, True, reads=['pswap', qtk], writes=['ps%d' % b2])
                                P.tt(ra[:, 0:n], qt_[:, 0:n], cos4[:, t0:t0 + n], ALU.mult, reads=[qtk, 'cos4'], writes=[rak], eng='pool')
                                P.tt(qt_[:, 0:n], self.ps[b2][:, 0:n], sin4[:, t0:t0 + n], ALU.mult, reads=['ps%d' % b2, 'sin4', qtk], writes=[qtk])
                                if qk == 0:
                                    P.tt(dst[:, t0:t0 + n], ra[:, 0:n], qt_[:, 0:n], ALU.add, reads=[rak, qtk], writes=[dk])
                                else:
                                    for a_ in range(2):
                                        P.tt(dst[a_ * 64:(a_ + 1) * 64, a_, t0:t0 + n], ra[a_ * 64:(a_ + 1) * 64, 0:n], qt_[a_ * 64:(a_ + 1) * 64, 0:n], ALU.add,
                                             reads=[rak, qtk], writes=[dk])
                            else:
                                if qk == 0:
                                    P.copy(dst[:, t0:t0 + n], self.ps[bk][:, 0:n], reads=['ps%d' % bk], writes=[dk], eng='act')
                                else:
                                    for a_ in range(2):
                                        P.copy(dst[a_ * 64:(a_ + 1) * 64, a_, t0:t0 + n], self.ps[bk][a_ * 64:(a_ + 1) * 64, 0:n], reads=['ps%d' % bk], writes=[dk], eng='act')
                load_wqk(0)
                load_wqk(1)
                proj(0)
                for h in range(6):
                    dsts = ((QT[h % 2], 'QT%d' % (h % 2)), (KT[h % 2], 'KT%d' % (h % 2)))
                    Qh, qkey = dsts[0]
                    Kh, kkey = dsts[1]
                    qblocks = [(q0, 512, list(range(NT))) for q0 in (0, 512, 1024, 1536)] + [(2048, 256, [16, 17])]
                    for qbi, (q0, nq, ktiles) in enumerate(qblocks):
                        nqt = nq // 128
                        nk = len(ktiles)
                        if qbi == 2 and h + 1 < 6:
                            proj(h + 1)
                            if h + 2 < 6:
                                load_wqk(h + 2)
                        for a in range(2):
                            sbank = {}

                            def emit_S(i):
                                kt = ktiles[i]
                                bk = self.nextps(4, 8)
                                sbank[i] = bk
                                P.mm(self.ps[bk][:, 0:nq], Kh[:, a, kt * 128:(kt + 1) * 128], Qh[:, q0:q0 + nq], True, True,
                                     reads=[qkey, kkey], writes=['ps%d' % bk])
                            emit_S(0)
                            if nk > 1:
                                emit_S(1)
                            for i in range(nk):
                                kt = ktiles[i]
                                bk = sbank[i]
                                pt, ptk = PT[npt % 4], 'PT%d' % (npt % 4)
                                npt += 1
                                P.act(pt[:, 0:nq], self.ps[bk][:, 0:nq], AF.Exp, reads=['ps%d' % bk], writes=[ptk], scale=0.125)
                                if i + 2 < nk:
                                    emit_S(i + 2)
                                for qi in range(nqt):
                                    P.mm(self.ps[qi][:, 0:129], pt[:, qi * 128:(qi + 1) * 128], VA[:, kt, h, 0:129], i == 0, i == nk - 1,
                                         reads=[ptk, 'VA'], writes=['ps%d' % qi])
                            if a == 0:
                                for qi in range(nqt):
                                    acc = self.ps[qi]
                                    sk = 'sm%d' % qi
                                    P.recip(sm[:, qi, 0:1], acc[:, 128:129], reads=['ps%d' % qi], writes=[sk])
                                    P.ts(otmp[:, qi, :], acc[:, 0:128], sm[:, qi, 0:1], None, ALU.mult, reads=['ps%d' % qi, sk], writes=['otmp%d' % qi])
                            else:
                                for qi in range(nqt):
                                    acc = self.ps[qi]
                                    sk = 'sm%d' % qi
                                    of, ofk = ofin[qi], 'ofin%d' % qi
                                    P.recip(sm[:, qi, 1:2], acc[:, 128:129], reads=['ps%d' % qi], writes=[sk])
                                    P.tt(sm[:, qi, 1:2], sm[:, qi, 1:2], nlam[:], ALU.mult, reads=[sk, 'nlam'], writes=[sk])
                                    P.stt(of[:], acc[:, 0:128], sm[:, qi, 1:2], otmp[:, qi, :], ALU.mult, ALU.add, reads=['ps%d' % qi, sk, 'otmp%d' % qi], writes=[ofk])
                                sks = ['sm%d' % qi for qi in range(nqt)]
                                for qi in range(nqt):
                                    sk = 'sm%d' % qi
                                    of, ofk = ofin[qi], 'ofin%d' % qi
                                    P.tt(osq[:], of[:], of[:], ALU.mult, reads=[ofk], writes=['osq'])
                                    P.op('dve', (lambda e, qi=qi: e.tensor_reduce(out=sm[:, qi, 2:3], in_=osq[:], axis=mybir.AxisListType.X, op=ALU.add)),
                                         reads=['osq'], writes=[sk])
                                P.ts(sm[:, 0:nqt, 2:3], sm[:, 0:nqt, 2:3], 1.0 / 128.0, EPS, ALU.mult, ALU.add, reads=sks, writes=sks)
                                P.act(sm[:, 0:nqt, 2:3], sm[:, 0:nqt, 2:3], AF.Ln, reads=sks, writes=sks)
                                P.act(sm[:, 0:nqt, 2:3], sm[:, 0:nqt, 2:3], AF.Exp, reads=sks, writes=sks, scale=-0.5)
                                for qi in range(nqt):
                                    sk = 'sm%d' % qi
                                    of, ofk = ofin[qi], 'ofin%d' % qi
                                    P.stt(of[:], of[:], sm[:, qi, 2:3], gsub[:], ALU.mult, ALU.mult, reads=[ofk, sk, 'gsub'], writes=[ofk])
                                    bk = self.nextps(4, 8)
                                    P.tr(self.ps[bk][:, 0:128], of[:], self.ident[:], reads=[ofk, 'ident'], writes=['ps%d' % bk])
                                    P.copy(OT[:, h, q0 + qi * 128:q0 + (qi + 1) * 128], self.ps[bk][:, 0:128], reads=['ps%d' % bk], writes=['OT'], eng='act')
            with self.scope() as s2:
                cb = self.sb(s2, "c64", [128, 2, 128])
                fw = self.sb(s2, "fwb", [128, 2, 128])
                fb = self.sb(s2, "fb", [128, 2])
                P.dma('sp', cb[:, 0, :], self.din['c64blk'], writes=['c64'])
                P.dma('sp', cb[:, 1, :], self.din['ns64blk'], writes=['c64'])
                P.dma('sp', fw[:], self.din['f_wblk'].rearrange("c p e -> p c e"), writes=['fwb'])
                P.dma('sp', fb[:], self.din['f_b'], writes=['fb'])
                ABm = self.sb(s2, "ABm", [128, 2, 2, 128], BF16)
                for c in range(2):
                    for cs in range(2):
                        bk = self.nextps()
                        P.mm(self.ps[bk][:, 0:128], cb[:, cs, :], fw[:, c, :], True, True, reads=['c64', 'fwb'], writes=['ps%d' % bk])
                        P.copy(ABm[:, c, cs, :], self.ps[bk][:, 0:128], reads=['ps%d' % bk], writes=['ABm'])
                Ycs = self.sb(s2, "Ycs", [128, NT, 2, 256], BF16)
                for t in range(NT):
                    for c in range(2):
                        bk = self.nextps(4, 8)
                        P.mm(self.ps[bk][:, 0:256], FT[:, c, t * 128:(t + 1) * 128], ABm[:, c, :, :].rearrange("p a b -> p (a b)"), True, True,
                             reads=['FT', 'ABm'], writes=['ps%d' % bk])
                        P.copy(Ycs[:, t, c, :], self.ps[bk][:, 0:256], reads=['ps%d' % bk], writes=['Ycs'], eng=('act' if c == 0 else 'dve'))
                dbuf = [self.sb(s2, "dft%d" % i, [128, NLAT], BF16) for i in range(3)]
                nd = 0
                dsrc = (self.din['dftc'], self.din['dfts'])
                for tn in range(16):
                    for cs in range(2):
                        db, dbk = dbuf[nd % 3], 'dft%d' % (nd % 3)
                        nd += 1
                        P.dma('sp', db[:], dsrc[cs][tn * 128:(tn + 1) * 128, :], writes=[dbk])
                        first = (tn == 0 and cs == 0)
                        last = (tn == 15 and cs == 1)
                        for c in range(2):
                            for nb in range(4):
                                bq = c * 4 + nb
                                P.mm(self.ps[bq][:, :], Ycs[:, tn, c, cs * 128:(cs + 1) * 128], db[:, nb * 512:(nb + 1) * 512], first, last,
                                     reads=['Ycs', dbk], writes=['ps%d' % bq])
                for c in range(2):
                    for nb in range(4):
                        bq = c * 4 + nb
                        P.act(OT[:, 6 + c, nb * 512:(nb + 1) * 512], self.ps[bq][:, :], AF.Identity, reads=['ps%d' % bq, 'fb'], writes=['OT'], bias=fb[:, c:c + 1])
                dcc = self.sb(s2, "dftcc", [128, 2, 2, 256], BF16)
                P.dma('sp', dcc[:, 0], self.din['dftc_c'].rearrange("(t p) n -> p t n", p=128), writes=['dftcc'])
                P.dma('sp', dcc[:, 1], self.din['dfts_c'].rearrange("(t p) n -> p t n", p=128), writes=['dftcc'])
                for c in range(2):
                    bk = self.nextps(4, 8)
                    n = 0
                    for tn in range(2):
                        for cs in range(2):
                            P.mm(self.ps[bk][:, 0:256], Ycs[:, 16 + tn, c, cs * 128:(cs + 1) * 128], dcc[:, cs, tn, :], n == 0, n == 3,
                                 reads=['Ycs', 'dftcc'], writes=['ps%d' % bk])
                            n += 1
                    P.act(OT[:, 6 + c, 2048:2304], self.ps[bk][:, 0:256], AF.Identity, reads=['ps%d' % bk, 'fb'], writes=['OT'], bias=fb[:, c:c + 1])

    def s5_mixer(self, S5O):
        import os
        stop = int(os.environ.get('S5STOP', '99'))
        if stop <= 0:
            self.P.memset(S5O[:], 0.0, writes=['S5O'])
            return
        P = self.P
        UT = self.UT
        w_in = self.din['s_w_in'].rearrange("(kc p) f -> p kc f", p=128)
        blocks = [(0, 512), (512, 512), (1024, 512), (1536, 512), (2048, 256)]
        T = ALU
        with self.scope() as s1:
            usT = self.sb(s1, "usT", [128, 2, NTOK], BF16)
            Y5 = self.sb(s1, "Y5", [128, 2, NLAT])
            dd = self.sb(s1, "s5dd", [128, 2])
            P.dma('sp', dd[:], self.din['s5_dd'], writes=['s5dd'])
            ones = self.sb(s1, "ones", [128, 512])
            P.dma('sp', ones[:], self.din['ones'], writes=['ones'])
            with self.scope() as s2:
                wus = self.sb(s2, "wus", [128, 8, 256], BF16)
                P.dma('pool', wus[:], w_in[:, :, 2072:2328], writes=['wus'])
                for c in range(2):
                    for (t0, n) in blocks:
                        bk = self.nextps()
                        for kc in range(8):
                            P.mm(self.ps[bk][:, 0:n], wus[:, kc, c * 128:(c + 1) * 128], UT[:, kc, t0:t0 + n], kc == 0, kc == 7,
                                 reads=['wus', 'UT'], writes=['ps%d' % bk])
                        P.copy(usT[:, c, t0:t0 + n], self.ps[bk][:, 0:n], reads=['ps%d' % bk], writes=['usT'], eng='act')
                        if t0 < NLAT and os.environ.get('S5SUB', '') != 'a':
                            P.ts(Y5[:, c, t0:t0 + n], self.ps[bk][:, 0:n], dd[:, c:c + 1], None, T.mult, reads=['ps%d' % bk, 's5dd'], writes=['Y5_%d_%d' % (c, t0)])
            if stop <= 1:
                P.memset(S5O[:], 0.0, writes=['S5O'])
                return
            pr = self.sb(s1, "s5pr", [128, 32, 16])
            names = {}

            def V(nm):
                if nm not in names:
                    names[nm] = len(names)
                return pr[:, names[nm], :]
            for nm, src in (('lre', 's5_lre'), ('lim', 's5_lim'), ('ldt', 's5_ldt')):
                P.dma('sp', V(nm), self.din[src], writes=['s5pr'])
            K_ = ['s5pr']

            def tt(o, a, b, op):
                P.tt(V(o), V(a), V(b), op, reads=K_, writes=K_)

            def ts(o, a, s1_, s2_, op0, op1=None):
                P.ts(V(o), V(a), s1_, s2_, op0, op1, reads=K_, writes=K_)
            P.act(V('dt'), V('ldt'), AF.Exp, reads=K_, writes=K_)
            tt('t0', 'dt', 'lre', T.mult)
            P.act(V('mag'), V('t0'), AF.Exp, reads=K_, writes=K_)
            tt('th', 'dt', 'lim', T.mult)
            ts('k', 'th', 1.0 / (2.0 * math.pi), None, T.mult)
            ts('k', 'k', 12582912.0, None, T.add)
            ts('k', 'k', -12582912.0, None, T.add)
            P.stt(V('r'), V('k'), -6.28125, V('th'), T.mult, T.add, reads=K_, writes=K_)
            P.stt(V('r'), V('k'), -(2.0 * math.pi - 6.28125), V('r'), T.mult, T.add, reads=K_, writes=K_)
            ts('x', 'r', 0.125, None, T.mult)
            tt('x2', 'x', 'x', T.mult)
            ts('ps_', 'x2', -1.0 / 5040.0, 1.0 / 120.0, T.mult, T.add)
            tt('ps_', 'ps_', 'x2', T.mult)
            ts('ps_', 'ps_', -1.0 / 6.0, None, T.add)
            tt('ps_', 'ps_', 'x2', T.mult)
            ts('ps_', 'ps_', 1.0, None, T.add)
            tt('sn', 'ps_', 'x', T.mult)
            ts('pc_', 'x2', 1.0 / 40320.0, -1.0 / 720.0, T.mult, T.add)
            tt('pc_', 'pc_', 'x2', T.mult)
            ts('pc_', 'pc_', 1.0 / 24.0, None, T.add)
            tt('pc_', 'pc_', 'x2', T.mult)
            ts('pc_', 'pc_', -0.5, None, T.add)
            tt('pc_', 'pc_', 'x2', T.mult)
            ts('cs', 'pc_', 1.0, None, T.add)
            for _ in range(3):
                tt('cc', 'cs', 'cs', T.mult)
                tt('ss', 'sn', 'sn', T.mult)
                tt('sc2', 'cs', 'sn', T.mult)
                tt('cs', 'cc', 'ss', T.subtract)
                ts('sn', 'sc2', 2.0, None, T.mult)
            tt('abre', 'mag', 'cs', T.mult)
            tt('abim', 'mag', 'sn', T.mult)
            tt('den', 'lre', 'lre', T.mult)
            tt('t0', 'lim', 'lim', T.mult)
            tt('den', 'den', 't0', T.add)
            P.recip(V('den'), V('den'), reads=K_, writes=K_)
            ts('am1', 'abre', -1.0, None, T.add)
            tt('t0', 'am1', 'lre', T.mult)
            tt('t1', 'abim', 'lim', T.mult)
            tt('t0', 't0', 't1', T.add)
            tt('kre', 't0', 'den', T.mult)
            tt('t0', 'abim', 'lre', T.mult)
            tt('t1', 'am1', 'lim', T.mult)
            tt('t0', 't0', 't1', T.subtract)
            tt('kim', 't0', 'den', T.mult)
            if stop <= 2:
                P.memset(S5O[:], 0.0, writes=['S5O'])
                return
            Ec = self.sb(s1, "s5Ec", [128, 10, 16])
            Es = self.sb(s1, "s5Es", [128, 10, 16])
            EK = ['s5E']
            P.copy(Ec[:, 0, :], V('cs'), reads=K_, writes=EK)
            P.copy(Es[:, 0, :], V('sn'), reads=K_, writes=EK)
            e1 = self.sb(s1, "s5e1", [128, 16])
            e2 = self.sb(s1, "s5e2", [128, 16])
            for j in range(9):
                P.tt(e1[:], Ec[:, j, :], Ec[:, j, :], T.mult, reads=EK, writes=['s5e1'])
                P.tt(e2[:], Es[:, j, :], Es[:, j, :], T.mult, reads=EK, writes=['s5e2'])
                P.tt(Ec[:, j + 1, :], e1[:], e2[:], T.subtract, reads=['s5e1', 's5e2'], writes=EK)
                P.tt(e1[:], Ec[:, j, :], Es[:, j, :], T.mult, reads=EK, writes=['s5e1'])
                P.ts(Es[:, j + 1, :], e1[:], 2.0, None, T.mult, reads=['s5e1'], writes=EK)
            if stop <= 3:
                P.memset(S5O[:], 0.0, writes=['S5O'])
                return
            nEs = self.sb(s1, "s5nEs", [128, 10, 16])
            P.ts(nEs[:], Es[:], -1.0, None, T.mult, reads=EK, writes=['s5nEs'])
            BbT = self.sb(s1, "BbT", [128, 2, 8, 2, 128], BF16)
            CmT = self.sb(s1, "CmT", [128, 2, 8, 2, 128], BF16)
            for d in range(2):
                P.dma('pool', CmT[:, d, :, 0, :], self.din['s5_cre'][d], writes=['CmT'])
                P.dma('pool', CmT[:, d, :, 1, :], self.din['s5_cim'][d], writes=['CmT'])
                P.ts(CmT[:, d, :, 1, :], CmT[:, d, :, 1, :], -1.0, None, T.mult, reads=['CmT'], writes=['CmT'])
            with self.scope() as s2:
                bre = self.sb(s2, "s5bre", [128, 8, 128])
                bim = self.sb(s2, "s5bim", [128, 8, 128])
                P.dma('sp', bre[:], self.din['s5_bre'], writes=['s5bre'])
                P.dma('sp', bim[:], self.din['s5_bim'], writes=['s5bim'])
                wt = [self.sb(s2, "s5wt%d" % i, [128, 128]) for i in range(2)]
                nw = 0
                for d in range(2):
                    for sc in range(8):
                        col = d * 8 + sc
                        kre = V('kre')[:, col:col + 1]
                        kim = V('kim')[:, col:col + 1]
                        for ri in range(2):
                            w, wk = wt[nw % 2], 's5wt%d' % (nw % 2)
                            nw += 1
                            if ri == 0:
                                P.ts(w[:], bim[:, sc, :], kim, None, T.mult, reads=['s5bim'] + K_, writes=[wk])
                                P.stt(w[:], bre[:, sc, :], kre, w[:], T.mult, T.subtract, reads=['s5bre', wk] + K_, writes=[wk])
                            else:
                                P.ts(w[:], bre[:, sc, :], kim, None, T.mult, reads=['s5bre'] + K_, writes=[wk])
                                P.stt(w[:], bim[:, sc, :], kre, w[:], T.mult, T.add, reads=['s5bim', wk] + K_, writes=[wk])
                            bk = self.nextps()
                            P.tr(self.ps[bk][:, 0:128], w[:], self.ident[:], reads=[wk, 'ident'], writes=['ps%d' % bk])
                            P.copy(BbT[:, d, sc, ri, :], self.ps[bk][:, 0:128], reads=['ps%d' % bk], writes=['BbT'], eng='act')
            if stop <= 4:
                P.memset(S5O[:], 0.0, writes=['S5O'])
                return
            with self.scope() as s2:
                Tc = self.sb(s2, "s5Tc", [128, 8, 512])
                Ts = self.sb(s2, "s5Ts", [128, 8, 512])
                tq = [self.sb(s2, "s5tq%d" % i, [128, 8, 256]) for i in range(2)]
                rho = self.sb(s2, "s5rho", [128, 512])
                NB = 2
                mt = [[self.sb(s2, "s5m%d_%d" % (i, b), [128, 512]) for i in range(4)] for b in range(NB)]
                bp = [[self.sb(s2, "s5bp%d_%d" % (i, b), [128, 512]) for i in range(2)] for b in range(NB)]
                gg = [[self.sb(s2, "s5g%d_%d" % (i, b), [128, 512]) for i in range(2)] for b in range(NB)]
                hh = [[self.sb(s2, "s5h%d_%d" % (i, b), [128, 512], BF16) for i in range(2)] for b in range(NB)]
                pp = [self.sb(s2, "s5p%d" % i, [128, 512]) for i in range(4)]
                ini = [self.sb(s2, "s5ini%d" % b, [128, 4]) for b in range(NB)]
                nb_ = 0
                for d in range(2):
                    TK = ['s5T']
                    P.memset(Tc[:, :, 0:1], 1.0, writes=TK)
                    P.memset(Ts[:, :, 0:1], 0.0, writes=TK)
                    for j in range(9):
                        m = 1 << j
                        ecb = Ec[:, j, d * 8:(d + 1) * 8].unsqueeze(2).to_broadcast([128, 8, m])
                        esb = Es[:, j, d * 8:(d + 1) * 8].unsqueeze(2).to_broadcast([128, 8, m])
                        P.tt(tq[0][:, :, 0:m], Ts[:, :, 0:m], esb, T.mult, reads=TK + EK, writes=['s5tq0'])
                        P.tt(tq[1][:, :, 0:m], Tc[:, :, 0:m], esb, T.mult, reads=TK + EK, writes=['s5tq1'])
                        P.tt(Tc[:, :, m:2 * m], Tc[:, :, 0:m], ecb, T.mult, reads=TK + EK, writes=TK)
                        P.tt(Ts[:, :, m:2 * m], Ts[:, :, 0:m], ecb, T.mult, reads=TK + EK, writes=TK)
                        P.tt(Tc[:, :, m:2 * m], Tc[:, :, m:2 * m], tq[0][:, :, 0:m], T.subtract, reads=TK + ['s5tq0'], writes=TK)
                        P.tt(Ts[:, :, m:2 * m], Ts[:, :, m:2 * m], tq[1][:, :, 0:m], T.add, reads=TK + ['s5tq1'], writes=TK)
                    order = [blocks[4]] + (blocks[0:4] if d == 0 else blocks[3::-1])
                    if stop <= 5:
                        continue
                    if stop <= 6 and d == 1:
                        continue
                    if d == 1:
                        P.copy(tq[0][:, :, :], Tc[:, :, 0:256], reads=TK, writes=['s5tq0'])
                        P.copy(tq[1][:, :, :], Tc[:, :, 256:512], reads=TK, writes=['s5tq1'])
                        P.copy(Tc[:, :, 0:256], tq[1][:, :, ::-1], reads=['s5tq1'], writes=TK)
                        P.copy(Tc[:, :, 256:512], tq[0][:, :, ::-1], reads=['s5tq0'], writes=TK)
                        P.copy(tq[0][:, :, :], Ts[:, :, 0:256], reads=TK, writes=['s5tq0'])
                        P.copy(tq[1][:, :, :], Ts[:, :, 256:512], reads=TK, writes=['s5tq1'])
                        P.copy(Ts[:, :, 0:256], tq[1][:, :, ::-1], reads=['s5tq1'], writes=TK)
                        P.copy(Ts[:, :, 256:512], tq[0][:, :, ::-1], reads=['s5tq0'], writes=TK)
                    for sc in range(8):
                        col = d * 8 + sc
                        fc = sc // 4
                        P.ts(rho[:], ones[:], V('mag')[:, col:col + 1], None, T.mult, reads=['ones'] + K_, writes=['s5rho'])
                        prev = None
                        pending = []
                        pend_pe = []
                        for (t0, n) in order:
                            b = nb_ % NB
                            nb_ += 1
                            sfx = '_%d' % b
                            bk1 = self.nextps()
                            P.mm(self.ps[bk1][:, 0:n], BbT[:, d, sc, 0, :], usT[:, fc, t0:t0 + n], True, True, reads=['BbT', 'usT'], writes=['ps%d' % bk1])
                            bk2 = self.nextps()
                            P.mm(self.ps[bk2][:, 0:n], BbT[:, d, sc, 1, :], usT[:, fc, t0:t0 + n], True, True, reads=['BbT', 'usT'], writes=['ps%d' % bk2])
                            while pend_pe:
                                pend_pe.pop(0)()
                            pre, pim = self.ps[bk1][:, 0:n], self.ps[bk2][:, 0:n]
                            if d == 0:
                                tc = Tc[:, sc, 0:n]
                                ts_ = Ts[:, sc, 0:n]
                            else:
                                tc = Tc[:, sc, 512 - n:512]
                                ts_ = Ts[:, sc, 512 - n:512]
                            m1, m2, m3, m4 = [mt[b][i][:, 0:n] for i in range(4)]
                            mk = ['s5m%d%s' % (i, sfx) for i in range(4)]
                            P.tt(m1, pre, tc, T.mult, reads=['ps%d' % bk1] + TK, writes=[mk[0]])
                            P.tt(m2, pim, ts_, T.mult, reads=['ps%d' % bk2] + TK, writes=[mk[1]])
                            P.tt(m3, pim, tc, T.mult, reads=['ps%d' % bk2] + TK, writes=[mk[2]])
                            P.tt(m4, pre, ts_, T.mult, reads=['ps%d' % bk1] + TK, writes=[mk[3]])
                            bpr, bpi = bp[b][0][:, 0:n], bp[b][1][:, 0:n]
                            P.tt(bpr, m1, m2, T.add, reads=[mk[0], mk[1]], writes=['s5bp0' + sfx])
                            P.tt(bpi, m3, m4, T.subtract, reads=[mk[2], mk[3]], writes=['s5bp1' + sfx])
                            gre, gim = gg[b][0][:, 0:n], gg[b][1][:, 0:n]
                            if d == 0:
                                go_r, go_i, bi_r, bi_i = gre, gim, bpr, bpi
                                last = n - 1
                            else:
                                go_r, go_i, bi_r, bi_i = gre[:, ::-1], gim[:, ::-1], bpr[:, ::-1], bpi[:, ::-1]
                                last = 0
                            i0 = 0.0 if prev is None else ini[prev][:, 0:1]
                            i1 = 0.0 if prev is None else ini[prev][:, 1:2]
                            rk = [] if prev is None else ['s5ini%d' % prev]
                            P.scan(go_r, rho[:, 0:n], bi_r, i0, T.mult, T.add, reads=['s5rho', 's5bp0' + sfx] + rk, writes=['s5g0' + sfx])
                            P.scan(go_i, rho[:, 0:n], bi_i, i1, T.mult, T.add, reads=['s5rho', 's5bp1' + sfx] + rk, writes=['s5g1' + sfx])
                            j = 9 if n == 512 else 8
                            ec = Ec[:, j, col:col + 1]
                            es = Es[:, j, col:col + 1]
                            ik = 's5ini%d' % b
                            nes = nEs[:, j, col:col + 1]
                            P.act(ini[b][:, 2:3], gg[b][1][:, last:last + 1], AF.Identity, reads=['s5g1' + sfx, 's5nEs'], writes=[ik], scale=nes, bias=0.0)
                            P.act(ini[b][:, 0:1], gg[b][0][:, last:last + 1], AF.Identity, reads=['s5g0' + sfx, ik] + EK, writes=[ik], scale=ec, bias=ini[b][:, 2:3])
                            P.act(ini[b][:, 3:4], gg[b][0][:, last:last + 1], AF.Identity, reads=['s5g0' + sfx] + EK, writes=[ik], scale=es, bias=0.0)
                            P.act(ini[b][:, 1:2], gg[b][1][:, last:last + 1], AF.Identity, reads=['s5g1' + sfx, ik] + EK, writes=[ik], scale=ec, bias=ini[b][:, 3:4])
                            prev = b
                            while pending:
                                pending.pop(0)()
                            if t0 < NLAT:
                                p1, p2, p3, p4 = [pp[i][:, 0:n] for i in range(4)]
                                hre, him = hh[b][0][:, 0:n], hh[b][1][:, 0:n]
                                P.tt(p1, gre, tc, T.mult, reads=['s5g0' + sfx] + TK, writes=['s5p0'], eng='pool')
                                P.tt(p2, gim, ts_, T.mult, reads=['s5g1' + sfx] + TK, writes=['s5p1'], eng='pool')
                                P.tt(hre, p1, p2, T.subtract, reads=['s5p0', 's5p1'], writes=['s5h0' + sfx], eng='pool')
                                P.tt(p3, gre, ts_, T.mult, reads=['s5g0' + sfx] + TK, writes=['s5p2'], eng='pool')
                                P.tt(p4, gim, tc, T.mult, reads=['s5g1' + sfx] + TK, writes=['s5p3'], eng='pool')
                                P.tt(him, p3, p4, T.add, reads=['s5p2', 's5p3'], writes=['s5h1' + sfx], eng='pool')
                                yk = 'Y5_%d_%d' % (fc, t0)
                                cell = {}

                                def _ro(cell=cell, hre=hre, him=him, sfx=sfx, n=n, d=d, sc=sc):
                                    bk = self.nextps()
                                    cell['bk'] = bk
                                    P.mm(self.ps[bk][:, 0:n], CmT[:, d, sc, 0, :], hre, True, False, reads=['CmT', 's5h0' + sfx], writes=['ps%d' % bk])
                                    P.mm(self.ps[bk][:, 0:n], CmT[:, d, sc, 1, :], him, False, True, reads=['CmT', 's5h1' + sfx], writes=['ps%d' % bk])

                                def _acc(cell=cell, yk=yk, fc=fc, t0=t0, n=n):
                                    bk = cell['bk']
                                    P.tt(Y5[:, fc, t0:t0 + n], Y5[:, fc, t0:t0 + n], self.ps[bk][:, 0:n], T.add, reads=[yk, 'ps%d' % bk], writes=[yk])
                                pend_pe.append(_ro)
                                pending.append(_acc)
                        while pend_pe:
                            pend_pe.pop(0)()
                        while pending:
                            pending.pop(0)()
            if stop <= 7:
                P.memset(S5O[:], 0.0, writes=['S5O'])
                return
            with self.scope() as s2:
                gw = self.sb(s2, "gluw", [128, 2, 256], BF16)
                gb_ = self.sb(s2, "glub", [128, 2])
                P.dma('pool', gw[:], self.din['s5_glu_w'].rearrange("(kc p) f -> p kc f", p=128), writes=['gluw'])
                P.dma('sp', gb_[:], self.din['s5_glu_b'], writes=['glub'])
                gbf = self.sb(s2, "gbf", [128, 2, NLAT], BF16)
                t1 = [self.sb(s2, "glt%d" % i, [128, 512]) for i in range(2)]
                sg = [self.sb(s2, "gls%d" % i, [128, 512]) for i in range(2)]
                n_ = 0
                for c in range(2):
                    for (t0, n) in blocks[0:4]:
                        yk = 'Y5_%d_%d' % (c, t0)
                        x = Y5[:, c, t0:t0 + n]
                        a, ak = t1[n_ % 2], 'glt%d' % (n_ % 2)
                        n_ += 1
                        P.tt(a[:], x, x, T.mult, reads=[yk], writes=[ak], eng='pool')
                        P.ts(a[:], a[:], 0.044715, 1.0, T.mult, T.add, reads=[ak], writes=[ak])
                        P.tt(a[:], a[:], x, T.mult, reads=[ak, yk], writes=[ak])
                        P.act(a[:], a[:], AF.Tanh, reads=[ak], writes=[ak], scale=math.sqrt(2.0 / math.pi))
                        P.stt(a[:], a[:], 1.0, x, T.add, T.mult, reads=[ak, yk], writes=[ak])
                        P.ts(x, a[:], 0.5, None, T.mult, reads=[ak], writes=[yk])
                        P.copy(gbf[:, c, t0:t0 + n], x, reads=[yk], writes=['gbf'], eng='act')
                n_ = 0
                for c in range(2):
                    for (t0, n) in blocks[0:4]:
                        bk = self.nextps()
                        for kc in range(2):
                            P.mm(self.ps[bk][:, 0:n], gw[:, kc, c * 128:(c + 1) * 128], gbf[:, kc, t0:t0 + n], kc == 0, kc == 1,
                                 reads=['gluw', 'gbf'], writes=['ps%d' % bk])
                        a, ak = sg[n_ % 2], 'gls%d' % (n_ % 2)
                        n_ += 1
                        P.act(a[:], self.ps[bk][:, 0:n], AF.Sigmoid, reads=['ps%d' % bk, 'glub'], writes=[ak], bias=gb_[:, c:c + 1])
                        P.tt(S5O[:, c, t0:t0 + n], Y5[:, c, t0:t0 + n], a[:], T.mult, reads=['Y5_%d_%d' % (c, t0), ak], writes=['S5O'])

    def ssd_inproj(self, sB, Xtm, Btm, BT, CT, dtt, dta, ZS):
        P = self.P
        UT = self.UT
        T = ALU
        w_in = self.din['s_w_in'].rearrange("(kc p) f -> p kc f", p=128)
        blocks = [(0, 512), (512, 512), (1024, 512), (1536, 512), (2048, 256)]
        with self.scope() as s2:
            wz = self.sb(s2, "wz", [128, 8, 768], BF16)
            P.dma('pool', wz[:], w_in[:, :, 0:768], writes=['wz'])
            zt = [self.sb(s2, "ztp%d" % i, [128, 768]) for i in range(2)]
            for t in range(16):
                z, zk = zt[t % 2], 'ztp%d' % (t % 2)
                for half in range(2):
                    bk = self.nextps()
                    for kc in range(8):
                        P.mm(self.ps[bk][:, 0:384], UT[:, kc, t * 128:(t + 1) * 128], wz[:, kc, half * 384:(half + 1) * 384], kc == 0, kc == 7,
                             reads=['UT', 'wz'], writes=['ps%d' % bk])
                    P.act(z[:, half * 384:(half + 1) * 384], self.ps[bk][:, 0:384], AF.Silu, reads=['ps%d' % bk], writes=[zk])
                P.dma('sp', ZS[t * 128:(t + 1) * 128, :], z[:], reads=[zk], writes=['ZS_%d' % t])
            wdt = self.sb(s2, "wdt", [128, 8, 24], BF16)
            P.dma('pool', wdt[:], w_in[:, :, 2048:2072], writes=['wdt'])
            dtb = self.sb(s2, "dtb", [128, 24])
            alog = self.sb(s2, "alog", [128, 24])
            P.dma('sp', dtb[:], self.din['sd_dtb'], writes=['dtb'])
            P.dma('sp', alog[:], self.din['sd_alog'], writes=['alog'])
            for t in range(NT):
                bk = self.nextps()
                for kc in range(8):
                    P.mm(self.ps[bk][:, 0:24], UT[:, kc, t * 128:(t + 1) * 128], wdt[:, kc, :], kc == 0, kc == 7, reads=['UT', 'wdt'], writes=['ps%d' % bk])
                P.tt(dtt[:, t, :], self.ps[bk][:, 0:24], dtb[:], T.add, reads=['ps%d' % bk, 'dtb'], writes=['dtt'])
            ax = self.sb(s2, "spax", [128, NT * 24])
            dflat = dtt[:].rearrange("p t f -> p (t f)")
            P.act(ax[:], dflat, AF.Abs, reads=['dtt'], writes=['spax'])
            P.act(ax[:], ax[:], AF.Exp, reads=['spax'], writes=['spax'], scale=-1.0)
            P.act(ax[:], ax[:], AF.Ln, reads=['spax'], writes=['spax'], bias=1.0)
            P.ts(dflat, dflat, 0.0, None, T.max, reads=['dtt'], writes=['dtt'])
            P.tt(dflat, dflat, ax[:], T.add, reads=['dtt', 'spax'], writes=['dtt'])
            P.act(alog[:], alog[:], AF.Exp, reads=['alog'], writes=['alog'])
            P.ts(alog[:], alog[:], -1.0, None, T.mult, reads=['alog'], writes=['alog'])
            P.tt(dta[:], dtt[:], alog[:].unsqueeze(1).to_broadcast([128, NT, 24]), T.mult, reads=['dtt', 'alog'], writes=['dta'])
        with self.scope() as s2:
            cw = self.sb(s2, "sdcw", [128, 3, 10])
            cb = self.sb(s2, "sdcb", [128, 10])
            P.dma('sp', cw[:], self.din['sd_cw'], writes=['sdc'])
            P.dma('sp', cb[:], self.din['sd_cb'], writes=['sdc'])
            W = 2308
            xs = self.sb(s2, "sdxs", [128, W])
            acc = self.sb(s2, "sdacc", [128, W])
            P.memset(xs[:], 0.0, writes=['sdxs'], eng='pool')
            wx = [self.sb(s2, "sdwx%d" % i, [128, 8, 128], BF16) for i in range(2)]

            def colpos(t0):
                return 1 + t0 if t0 < NLAT else 2051 + (t0 - NLAT)
            for c in range(10):
                w, wk = wx[c % 2], 'sdwx%d' % (c % 2)
                P.dma('pool', w[:], w_in[:, :, 768 + c * 128:768 + (c + 1) * 128], writes=[wk])
                for (t0, n) in blocks:
                    bk = self.nextps()
                    for kc in range(8):
                        P.mm(self.ps[bk][:, 0:n], w[:, kc, :], UT[:, kc, t0:t0 + n], kc == 0, kc == 7, reads=[wk, 'UT'], writes=['ps%d' % bk])
                    cp = colpos(t0)
                    P.copy(xs[:, cp:cp + n], self.ps[bk][:, 0:n], reads=['ps%d' % bk], writes=['sdxs'], eng='act')
                n1 = W - 2
                P.ts(acc[:, 1:1 + n1], xs[:, 0:n1], cw[:, 0, c:c + 1], cb[:, c:c + 1], T.mult, T.add, reads=['sdxs', 'sdc'], writes=['sdacc'])
                P.stt(acc[:, 1:1 + n1], xs[:, 1:1 + n1], cw[:, 1, c:c + 1], acc[:, 1:1 + n1], T.mult, T.add, reads=['sdxs', 'sdacc', 'sdc'], writes=['sdacc'])
                P.stt(acc[:, 1:1 + n1], xs[:, 2:2 + n1], cw[:, 2, c:c + 1], acc[:, 1:1 + n1], T.mult, T.add, reads=['sdxs', 'sdacc', 'sdc'], writes=['sdacc'])
                P.act(acc[:, 1:1 + n1], acc[:, 1:1 + n1], AF.Silu, reads=['sdacc'], writes=['sdacc'])
                if 6 <= c < 8:
                    g = c - 6
                    P.copy(BT[:, g, 0:NLAT], acc[:, 1:1 + NLAT], reads=['sdacc'], writes=['BT'], eng='pool')
                    P.copy(BT[:, g, NLAT:NTOK], acc[:, 2051:2051 + NCTX], reads=['sdacc'], writes=['BT'], eng='pool')
                if c >= 8:
                    g = c - 8
                    P.copy(CT[:, g, 0:NLAT], acc[:, 1:1 + NLAT], reads=['sdacc'], writes=['CT'], eng='pool')
                    P.copy(CT[:, g, NLAT:NTOK], acc[:, 2051:2051 + NCTX], reads=['sdacc'], writes=['CT'], eng='pool')
                if c < 8:
                    for t4 in range(0, NT, 4):
                        bk = self.nextps()
                        nq = min(4, NT - t4)
                        for q in range(nq):
                            t = t4 + q
                            cp = colpos(t * 128)
                            P.tr(self.ps[bk][:, q * 128:(q + 1) * 128], acc[:, cp:cp + 128], self.ident[:], reads=['sdacc', 'ident'], writes=['ps%d' % bk])
                        src = self.ps[bk][:, 0:nq * 128].rearrange("p (q f) -> p q f", q=nq)
                        if c < 6:
                            P.copy(Xtm[:, t4:t4 + nq, c * 128:(c + 1) * 128], src, reads=['ps%d' % bk], writes=['Xtm'], eng=('act' if (t4 // 4) % 2 == 0 else 'dve'))
                        else:
                            P.copy(Btm[:, t4:t4 + nq, c - 6, :], src, reads=['ps%d' % bk], writes=['Btm'], eng=('act' if (t4 // 4) % 2 == 0 else 'dve'))

    def ssd_chunks(self, Xtm, Btm, BT, CT, dtt, dta, Yacc):
        P = self.P
        T = ALU
        with self.scope() as s2:
            vd = self.sb(s2, "vd", [128, 2, 128])
            ud = self.sb(s2, "ud", [128, 2, 128])
            ones = self.sb(s2, "ones1", [128, 128])
            dsk = self.sb(s2, "dsk", [128, 12])
            P.dma('sp', vd[:], self.din['vd'], writes=['vd'])
            P.dma('sp', ud[:], self.din['ud'], writes=['ud'])
            P.dma('sp', ones[:], self.din['ones'][:, 0:128], writes=['ones1'])
            P.dma('sp', dsk[:], self.din['sd_d'], writes=['dsk'])
            for t in range(16):
                P.tt(Yacc[:, t, :].rearrange("p (r e) -> p r e", r=12), Xtm[:, t, :].rearrange("p (r e) -> p r e", r=12),
                     dsk[:].unsqueeze(2).to_broadcast([128, 12, 64]), T.mult, reads=['Xtm', 'dsk'], writes=['Yacc%d' % t])
            utflat = self.UT[:].rearrange("p a b -> p (a b)")

            class _V:
                def __init__(self, ap):
                    self.ap = ap

                def __getitem__(self, idx):
                    return self.ap[idx]

            def alias(i):
                return _V(utflat[:, i * 3072:(i + 1) * 3072].bitcast(F32).rearrange("p (a b) -> p a b", a=12))
            rhsV = alias(0)
            Ls = [alias(1), alias(2)]
            CBm_ = [self.sb(s2, "CBm%d" % i, [128, 2, 128]) for i in range(2)]
            M_ = [self.sb(s2, "Mdiag%d" % i, [128, 12, 128], BF16) for i in range(2)]
            xdts = [self.sb(s2, "xdt%d" % i, [128, 12, 64], BF16) for i in range(2)]
            xw_ = [self.sb(s2, "xw%d" % i, [128, 12, 64], BF16) for i in range(2)]
            tmp_ = [self.sb(s2, "ytmp%d" % i, [128, 768]) for i in range(2)]
            Hf_ = [self.sb(s2, "Hf%d" % i, [128, 768]) for i in range(2)]
            Hb2 = [[self.sb(s2, "Hb%d_%d" % (i, k), [128, 768], BF16) for k in range(2)] for i in range(2)]
            eacs_ = [self.sb(s2, "eacs%d" % i, [128, 12]) for i in range(2)]
            cdv_ = [self.sb(s2, "cdv%d" % i, [128, 12]) for i in range(2)]
            jobs = []
            orders = [[16, 17] + list(range(16)), [17, 16] + list(range(15, -1, -1))]
            for ci in range(18):
                for d in range(2):
                    jobs.append((d, ci, orders[d][ci], ci == 17))

            def stageA1(n):
                d, ci, t, islast = jobs[n]
                xdt, xk = xdts[n % 2], 'xdt%d' % (n % 2)
                dta_t = dta[:, t, d * 12:(d + 1) * 12]
                dt_t = dtt[:, t, d * 12:(d + 1) * 12]
                for r in range(12):
                    P.act(rhsV[:, r, :], vd[:, d, :], AF.Identity, reads=['vd', 'dta'], writes=['rhsV'], scale=dta_t[:, r:r + 1], bias=0.0)
                P.tt(xdt[:], Xtm[:, t, :].rearrange("p (r e) -> p r e", r=12), dt_t.unsqueeze(2).to_broadcast([128, 12, 64]), T.mult,
                     reads=['Xtm', 'dtt'], writes=[xk], eng='pool')

            def stageA2(n):
                d, ci, t, islast = jobs[n]
                L, lk = Ls[n % 2], 'Lseg%d' % (n % 2)
                for q in range(3):
                    bk = self.nextps()
                    P.mm(self.ps[bk][:, :], ud[:, d, :], rhsV[:, 4 * q:4 * q + 4, :].rearrange("p a b -> p (a b)"), True, True,
                         reads=['ud', 'rhsV'], writes=['ps%d' % bk])
                    P.act(L[:, 4 * q:4 * q + 4, :].rearrange("p a b -> p (a b)"), self.ps[bk][:, :], AF.Exp, reads=['ps%d' % bk], writes=[lk])

            def stageB(n):
                d, ci, t, islast = jobs[n]
                L, lk = Ls[n % 2], 'Lseg%d' % (n % 2)
                xdt, xk = xdts[n % 2], 'xdt%d' % (n % 2)
                CBm, M, xw, tmp, Hf, eacs, cdv = CBm_[d], M_[d], xw_[d], tmp_[d], Hf_[d], eacs_[d], cdv_[d]
                kCB, kM, kxw, ktmp, kHf, kea, kcd = ['%s%d' % (k_, d) for k_ in ('CBm', 'Mdiag', 'xw', 'ytmp', 'Hf', 'eacs', 'cdv')]
                Hb_cur, kHb_cur = Hb2[d][ci % 2], 'Hb%d_%d' % (d, ci % 2)
                Hb_nxt, kHb_nxt = Hb2[d][(ci + 1) % 2], 'Hb%d_%d' % (d, (ci + 1) % 2)
                iend = 127 if d == 0 else 0
                lat = t < 16
                dta_t = dta[:, t, d * 12:(d + 1) * 12]
                tok = slice(t * 128, (t + 1) * 128)
                bs = None
                if not islast:
                    P.tt(xw[:], xdt[:], L[:, :, iend].unsqueeze(2).to_broadcast([128, 12, 64]), T.mult, reads=[xk, lk], writes=[kxw])
                    bs = [self.nextps(), self.nextps()]
                    for g in range(2):
                        P.mm(self.ps[bs[g]][:, 0:384], Btm[:, t, g, :], xw[:, 6 * g:6 * g + 6, :].rearrange("p a b -> p (a b)"), True, True,
                             reads=['Btm', kxw], writes=['ps%d' % bs[g]])
                    if ci > 0:
                        bk = self.nextps()
                        P.mm(self.ps[bk][:, 0:12], ones[:], dta_t, True, True, reads=['ones1', 'dta'], writes=['ps%d' % bk])
                        P.act(cdv[:], self.ps[bk][:, 0:12], AF.Exp, reads=['ps%d' % bk], writes=[kcd])
                if lat:
                    bk = self.nextps()
                    for g in range(2):
                        P.mm(self.ps[bk][:, g * 128:(g + 1) * 128], BT[:, g, tok], CT[:, g, tok], True, True, reads=['BT', 'CT'], writes=['ps%d' % bk])
                    P.tt(CBm[:], self.ps[bk][:, 0:256].rearrange("p (g i) -> p g i", g=2), vd[:, d, :].unsqueeze(1).to_broadcast([128, 2, 128]), T.mult,
                         reads=['ps%d' % bk, 'vd'], writes=[kCB])
                    P.tt(M[:].rearrange("p (g r) i -> p g r i", g=2), L[:].rearrange("p (g r) i -> p g r i", g=2),
                         CBm[:].unsqueeze(2).to_broadcast([128, 2, 6, 128]), T.mult, reads=[lk, kCB], writes=[kM])
                if not islast:
                    if ci == 0:
                        for g in range(2):
                            P.copy(Hf[:, g * 384:(g + 1) * 384], self.ps[bs[g]][:, 0:384], reads=['ps%d' % bs[g]], writes=[kHf])
                    else:
                        P.tt(Hf[:].rearrange("p (r e) -> p r e", r=12), Hf[:].rearrange("p (r e) -> p r e", r=12),
                             cdv[:].unsqueeze(2).to_broadcast([128, 12, 64]), T.mult, reads=[kHf, kcd], writes=[kHf])
                        for g in range(2):
                            P.tt(Hf[:, g * 384:(g + 1) * 384], Hf[:, g * 384:(g + 1) * 384], self.ps[bs[g]][:, 0:384], T.add,
                                 reads=[kHf, 'ps%d' % bs[g]], writes=[kHf])
                    P.copy(Hb_nxt[:], Hf[:], reads=[kHf], writes=[kHb_nxt], eng='act')
                if lat:
                    bA = self.nextps()
                    bB = self.nextps()
                    for r in range(12):
                        dst = self.ps[bA][:, r * 64:(r + 1) * 64] if r < 8 else self.ps[bB][:, (r - 8) * 64:(r - 7) * 64]
                        P.mm(dst, M[:, r, :], xdt[:, r, :], True, True, reads=[kM, xk], writes=['ps%d' % (bA if r < 8 else bB)])
                    bo = [self.nextps(), self.nextps()]
                    for g in range(2):
                        P.mm(self.ps[bo[g]][:, 0:384], CT[:, g, tok], Hb_cur[:, g * 384:(g + 1) * 384], True, True, reads=['CT', kHb_cur], writes=['ps%d' % bo[g]])
                    bk = self.nextps()
                    P.mm(self.ps[bk][:, 0:12], vd[:, d, :], dta_t, True, True, reads=['vd', 'dta'], writes=['ps%d' % bk])
                    P.act(eacs[:], self.ps[bk][:, 0:12], AF.Exp, reads=['ps%d' % bk], writes=[kea])
                    for g in range(2):
                        P.tt(tmp[:, g * 384:(g + 1) * 384].rearrange("p (r e) -> p r e", r=6), self.ps[bo[g]][:, 0:384].rearrange("p (r e) -> p r e", r=6),
                             eacs[:, g * 6:(g + 1) * 6].unsqueeze(2).to_broadcast([128, 6, 64]), T.mult, reads=['ps%d' % bo[g], kea], writes=[ktmp])
                    P.tt(tmp[:, 0:512], tmp[:, 0:512], self.ps[bA][:, :], T.add, reads=[ktmp, 'ps%d' % bA], writes=[ktmp])
                    P.tt(tmp[:, 512:768], tmp[:, 512:768], self.ps[bB][:, 0:256], T.add, reads=[ktmp, 'ps%d' % bB], writes=[ktmp])
                    P.tt(Yacc[:, t, :], Yacc[:, t, :], tmp[:], T.add, reads=['Yacc%d' % t, ktmp], writes=['Yacc%d' % t], eng='pool')
            stageA1(0)
            stageA2(0)
            for n in range(len(jobs)):
                if n + 1 < len(jobs):
                    stageA1(n + 1)
                stageB(n)
                if n + 1 < len(jobs):
                    stageA2(n + 1)

    def ssd_gate(self, Yacc, ZS, OTs):
        P = self.P
        T = ALU
        with self.scope() as s2:
            ng = self.sb(s2, "sdng", [128, 768])
            P.dma('sp', ng[:], self.din['sd_ng'], writes=['sdng'])
            zt = [self.sb(s2, "ztg%d" % i, [128, 768]) for i in range(2)]
            yz = [self.sb(s2, "yz%d" % i, [128, 768]) for i in range(2)]
            sq = self.sb(s2, "gsq", [128, 768])
            sm = self.sb(s2, "gsm", [128, 2])
            for t in range(16):
                z, zk = zt[t % 2], 'ztg%d' % (t % 2)
                y, yk = yz[t % 2], 'yz%d' % (t % 2)
                P.dma('sp', z[:], ZS[t * 128:(t + 1) * 128, :], reads=['ZS_%d' % t], writes=[zk])
                P.tt(y[:], Yacc[:, t, :], z[:], T.mult, reads=['Yacc%d' % t, zk], writes=[yk])
                P.act(sq[:], y[:], AF.Square, reads=[yk], writes=['gsq', 'gsm'], accum_out=sm[:, 0:1])
                P.ts(sm[:, 0:1], sm[:, 0:1], 1.0 / 768.0, EPS, T.mult, T.add, reads=['gsm'], writes=['gsm'])
                P.act(sm[:, 0:1], sm[:, 0:1], AF.Sqrt, reads=['gsm'], writes=['gsm'])
                P.recip(sm[:, 0:1], sm[:, 0:1], reads=['gsm'], writes=['gsm'])
                P.stt(y[:], y[:], sm[:, 0:1], ng[:], T.mult, T.mult, reads=[yk, 'gsm', 'sdng'], writes=[yk])
                for half in range(2):
                    bk = self.nextps()
                    for q in range(3):
                        c = half * 3 + q
                        P.tr(self.ps[bk][:, q * 128:(q + 1) * 128], y[:, c * 128:(c + 1) * 128], self.ident[:], reads=[yk, 'ident'], writes=['ps%d' % bk])
                    P.copy(OTs[:, half * 3:(half + 1) * 3, t * 128:(t + 1) * 128], self.ps[bk][:, 0:384].rearrange("p (q f) -> p q f", q=3),
                           reads=['ps%d' % bk], writes=['OTs'], eng=('act' if half == 0 else 'dve'))

    def declare_l1_inputs(self):
        for nm, shp in (('s_w_in', [D, 2328]), ('s_w_out', [D, D]), ('sd_cw', [128, 3, 10]), ('sd_cb', [128, 10]), ('sd_alog', [128, 24]),
                        ('sd_dtb', [128, 24]), ('sd_d', [128, 12]), ('sd_ng', [128, 768]), ('s5_lre', [128, 16]), ('s5_lim', [128, 16]),
                        ('s5_ldt', [128, 16]), ('s5_bre', [128, 8, 128]), ('s5_bim', [128, 8, 128]), ('s5_cre', [2, 128, 8, 128]),
                        ('s5_cim', [2, 128, 8, 128]), ('s5_dd', [128, 2]), ('s5_glu_w', [256, 256]), ('s5_glu_b', [128, 2]),
                        ('vd', [128, 2, 128]), ('ud', [128, 2, 128]), ('ones', [128, 512])):
            self.inp(nm, shp)

    def layer1(self, st, hsrc, dst, dkey, dbg=False, only=None):
        P = self.P
        self.mods(st, [1])
        with self.scope() as sL:
            S5O = self.sb(sL, "S5O", [128, 2, NLAT], BF16)
            self.phase_A(sL, hsrc, 1, NT)
            if only in (None, 's5'):
                self.s5_mixer(S5O)
            else:
                P.memset(S5O[:], 0.0, writes=['S5O'])
            if only == 's5':
                o = self.outp("dbg_S5O", [128, 2, NLAT], BF16)
                P.dma('sp', o, S5O[:], reads=['S5O'], writes=['dbg_S5O'])
                return
            H1 = self.scratch("H1b", [NLAT, D])
            ZS = self.scratch("ZS", [NLAT, 768])
            with self.scope() as sA:
                Yacc = self.sb(sA, "Yacc", [128, 16, 768])
                with self.scope() as sB:
                    Xtm = self.sb(sB, "Xtm", [128, NT, 768], BF16)
                    Btm = self.sb(sB, "Btm", [128, NT, 2, 128], BF16)
                    BT = self.sb(sB, "BT", [128, 2, NTOK], BF16)
                    CT = self.sb(sB, "CT", [128, 2, NTOK], BF16)
                    dtt = self.sb(sB, "dtt", [128, NT, 24])
                    dta = self.sb(sB, "dta", [128, NT, 24])
                    self.ssd_inproj(sB, Xtm, Btm, BT, CT, dtt, dta, ZS)
                    self.ssd_chunks(Xtm, Btm, BT, CT, dtt, dta, Yacc)
                with self.scope() as sC:
                    OTs = self.sb(sC, "OTs", [128, 6, NLAT], BF16)
                    self.ssd_gate(Yacc, ZS, OTs)
                    if only == 'ssd':
                        o = self.outp("dbg_OTs", [128, 6, NLAT], BF16)
                        P.dma('sp', o, OTs[:], reads=['OTs'], writes=['dbg_OTs'])
                        return
                    if dbg:
                        o = self.outp("dbg_S5O", [128, 2, NLAT], BF16)
                        P.dma('sp', o, S5O[:], reads=['S5O'], writes=['dbg_S5O'])
                        o = self.outp("dbg_OTs", [128, 6, NLAT], BF16)
                        P.dma('sp', o, OTs[:], reads=['OTs'], writes=['dbg_OTs'])

                    def lhs(j, t):
                        if j < 6:
                            return OTs[:, j, t * 128:(t + 1) * 128], 'OTs'
                        return S5O[:, j - 6, t * 128:(t + 1) * 128], 'S5O'
                    self.outproj_ln1(sC, 1, self.din['s_w_out'], hsrc, 16, H1, lhs=lhs)
        self.ffn(st, 1, 16, H1, dst, dkey, final=True)

    def build_layer1_test(self, only=None):
        with contextlib.ExitStack() as st:
            self.load_consts(st)
            self.declare_common_inputs()
            self.declare_l1_inputs()
            hin = self.inp("hin", [NTOK, D])
            self.UT = self.sb(st, "UT", [128, 8, NTOK], BF16)
            out = self.outp("out", [NLAT, D])
            self.final.remove("out")
            self.layer1(st, hin, out, 'out', dbg=True, only=only)
            if only is not None:
                self.dout.pop('out')
            self.P.emit(self.final)
        return self.nc

    def declare_common_inputs(self):
        for nm, shp in (('ln_g', [2, 2, 128, D]), ('ln_b', [2, 2, 128, D]), ('w_up', [2, D, 5632]), ('b_up', [2, 128, 44]),
                        ('conv_w', [2, 128, 3, 44]), ('conv_b', [2, 128, 44]), ('w_down', [2, 2816, D]), ('b_down', [2, 128, D]),
                        ('a_w_in', [D, 2560]), ('a_w_out', [D, D]), ('a_lam', [128, 256]), ('a_subg', [128, 128]),
                        ('f_wblk', [2, 128, 128]), ('f_b', [128, 2]), ('cos4', [128, NLAT]), ('sin4', [128, NLAT]), ('pswap', [128, 128]),
                        ('c64blk', [128, 128]), ('ns64blk', [128, 128])):
            self.inp(nm, shp)
        for nm, shp in (('dftc', [NLAT, NLAT]), ('dfts', [NLAT, NLAT]), ('dftc_c', [NCTX, NCTX]), ('dfts_c', [NCTX, NCTX])):
            self.inp(nm, shp, BF16)

    def build_layer0(self, dbg_ot=False):
        with contextlib.ExitStack() as st:
            self.load_consts(st)
            self.declare_common_inputs()
            xin = self.inp("xin", [NTOK, D])
            self.mods(st, [0])
            self.UT = self.sb(st, "UT", [128, 8, NTOK], BF16)
            H1 = self.scratch("H1", [NTOK, D])
            H2 = self.outp("H2", [NTOK, D])
            self.final.remove("H2")
            with self.scope() as s1:
                self.OT = self.sb(s1, "OT", [128, 8, NTOK], BF16)
                self.phase_A(s1, xin, 0, NT)
                self.attn_mixer()
                if dbg_ot:
                    o = self.outp("dbg_OT", [128, 8, NTOK], BF16)
                    self.P.dma('sp', o, self.OT[:], reads=['OT'], writes=['dbg_OT'])
                self.outproj_ln1(s1, 0, self.din['a_w_out'], xin, NT, H1)
            self.ffn(st, 0, NT, H1, H2, 'H2', final=True)
            self.P.emit(self.final)
        return self.nc

    def build_debug_ffn(self):
        with contextlib.ExitStack() as st:
            self.load_consts(st)
            for nm, shp in (('ln_g', [2, 2, 128, D]), ('ln_b', [2, 2, 128, D]), ('w_up', [2, D, 5632]), ('b_up', [2, 128, 44]),
                            ('conv_w', [2, 128, 3, 44]), ('conv_b', [2, 128, 44]), ('w_down', [2, 2816, D]), ('b_down', [2, 128, D]),
                            ('a_w_out', [D, D])):
                self.inp(nm, shp)
            xin = self.inp("xin", [NTOK, D])
            self.mods(st, [0])
            self.UT = self.sb(st, "UT", [128, 8, NTOK], BF16)
            H1 = self.scratch("H1", [NTOK, D])
            H2 = self.outp("H2", [NTOK, D])
            self.final.remove("H2")
            with self.scope() as s1:
                self.OT = self.sb(s1, "OT", [128, 8, NTOK], BF16)
                self.phase_A(s1, xin, 0, NT)
                o = self.outp("dbg_modc0", [128, 48, 2])
                self.P.dma('sp', o, self.modc[0][:], reads=['modc0'], writes=['dbg_modc0'])
                o = self.outp("dbg_grow0", [2, 2, 1024])
                self.P.dma('sp', o, self.grow[0], reads=['grow0'], writes=['dbg_grow0'])
                o = self.outp("dbg_UT", [128, 8, NTOK], BF16)
                self.P.dma('sp', o, self.UT[:], reads=['UT'], writes=['dbg_UT'])
                self.P.copy(self.OT[:], self.UT[:], reads=['UT'], writes=['OT'], eng='pool')
                self.outproj_ln1(s1, 0, self.din['a_w_out'], xin, NT, H1)
            self.ffn(st, 0, NT, H1, H2, 'H2', final=True)
            self.P.emit(self.final)
        return self.nc

    def build_full(self):
        with contextlib.ExitStack() as st:
            self.load_consts(st)
            self.declare_common_inputs()
            self.declare_l1_inputs()
            xin = self.inp("xin", [NTOK, D])
            self.mods(st, [0])
            self.UT = self.sb(st, "UT", [128, 8, NTOK], BF16)
            H1 = self.scratch("H1", [NTOK, D])
            H2 = self.scratch("H2", [NTOK, D])
            with self.scope() as s1:
                self.OT = self.sb(s1, "OT", [128, 8, NTOK], BF16)
                self.phase_A(s1, xin, 0, NT)
                self.attn_mixer()
                self.outproj_ln1(s1, 0, self.din['a_w_out'], xin, NT, H1)
            self.ffn(st, 0, NT, H1, H2, 'hin1')
            out = self.outp("out", [NLAT, D])
            self.final.remove("out")
            self.layer1(st, H2, out, 'out')
            self.P.emit(self.final)
        return self.nc


_CACHE = {}


def kernel(**inputs):
    inp = {k: np.asarray(v) for k, v in inputs.items()}
    if 'b' not in _CACHE:
        B = Builder()
        B.build_full()
        _CACHE['b'] = B
        _CACHE['c'] = host_consts()
    B = _CACHE['b']
    hc = _CACHE['c']
    in_maps = []
    for b in range(8):
        hl = host_layout(inp, b)
        in_maps.append({k: (hc[k] if k in hc else hl[k]) for k in B.din})
    res = run_bass_kernel_spmd(B.nc, in_maps, core_ids=list(range(8)))
    out = np.stack([np.asarray(r['out']) for r in res.results], axis=0)
    return out.astype(np.float32)
```

```python
import contextlib
import math
import numpy as np
import ml_dtypes
import concourse.bass as bass
import concourse.mybir as mybir
from concourse.bass_utils import run_bass_kernel_spmd

F32 = mybir.dt.float32
BF16 = mybir.dt.bfloat16
AF = mybir.ActivationFunctionType
ALU = mybir.AluOpType

ENGS = ['pe', 'act', 'dve', 'pool', 'sp']
POOL_TO_DVE = True
NDMASEM = 6

D = 1024
NLAT = 2048
NCTX = 256
NTOK = NLAT + NCTX
NT = NTOK // 128
ALPHA = (2 * 2) ** 0.25
EPS = 1e-5
LAM_INIT0 = 0.8 - 0.6 * math.exp(0.0)


class Prog:
    def __init__(self, nc):
        self.nc = nc
        self.ops = {e: [] for e in ENGS}
        self.last_w = {}
        self.readers = {}
        self.barriers = []
        self.bar_dma_start = {e: 0 for e in ENGS}

    def barrier(self):
        pts = []
        for e in ENGS:
            ops = self.ops[e]
            for i in range(len(ops) - 1, -1, -1):
                if not ops[i]['dma'] and ops[i]['fn'] is not None:
                    pts.append((e, i))
                    ops[i]['needed'] = True
                    break
            for i in range(self.bar_dma_start[e], len(ops)):
                if ops[i]['dma']:
                    pts.append((e, i))
            self.bar_dma_start[e] = len(ops)
        self.barriers.append(pts)

    def op(self, eng, fn, reads=(), writes=(), dma=False):
        if eng == 'pool' and not dma and POOL_TO_DVE:
            eng = 'dve'
        ops = self.ops[eng]
        idx = len(ops)
        deps = set()
        for k in reads:
            w = self.last_w.get(k)
            if w is not None:
                deps.add(w)
            if k.startswith('ps'):
                for r in self.readers.get(k, ()):
                    if r[0] != eng:
                        deps.add(r)
        for k in writes:
            w = self.last_w.get(k)
            if w is not None:
                deps.add(w)
            for r in self.readers.get(k, ()):
                deps.add(r)
        best = {}
        out = []
        for (e, i) in deps:
            d = self.ops[e][i]
            if d['dma']:
                out.append((e, i))
            else:
                if e == eng and not dma and eng == 'pe':
                    continue
                if e not in best or best[e] < i:
                    best[e] = i
        for e, i in best.items():
            out.append((e, i))
        for (e, i) in out:
            self.ops[e][i]['needed'] = True
        ops.append(dict(fn=fn, deps=out, dma=dma, needed=False, sem=None, val=None, prev=None, bar=len(self.barriers)))
        me = (eng, idx)
        for k in reads:
            lst = self.readers.setdefault(k, [])
            if not dma:
                lst[:] = [r for r in lst if not (r[0] == eng and not self.ops[r[0]][r[1]]['dma'])]
            lst.append(me)
        for k in writes:
            self.last_w[k] = me
            self.readers[k] = []
        return me

    def dma(self, eng, out, in_, reads=(), writes=(), **kw):
        return self.op(eng, lambda e: e.dma_start(out=out, in_=in_, **kw), reads, writes, dma=True)

    def act(self, out, in_, func, reads=(), writes=(), eng='act', **kw):
        return self.op(eng, lambda e: e.activation(out=out, in_=in_, func=func, **kw), reads, writes)

    def tt(self, out, in0, in1, op, reads=(), writes=(), eng='dve'):
        return self.op(eng, lambda e: e.tensor_tensor(out=out, in0=in0, in1=in1, op=op), reads, writes)

    def ts(self, out, in0, s1, s2, op0, op1=None, reads=(), writes=(), eng='dve'):
        if op1 is None:
            return self.op(eng, lambda e: e.tensor_scalar(out=out, in0=in0, scalar1=s1, scalar2=None, op0=op0), reads, writes)
        return self.op(eng, lambda e: e.tensor_scalar(out=out, in0=in0, scalar1=s1, scalar2=s2, op0=op0, op1=op1), reads, writes)

    def stt(self, out, in0, scalar, in1, op0, op1, reads=(), writes=()):
        return self.op('dve', lambda e: e.scalar_tensor_tensor(out=out, in0=in0, scalar=scalar, in1=in1, op0=op0, op1=op1), reads, writes)

    def copy(self, out, in_, reads=(), writes=(), eng='dve'):
        if eng == 'act':
            return self.op(eng, lambda e: e.activation(out=out, in_=in_, func=AF.Copy), reads, writes)
        return self.op(eng, lambda e: e.tensor_copy(out=out, in_=in_), reads, writes)

    def mm(self, out, lhsT, rhs, start, stop, reads=(), writes=()):
        return self.op('pe', lambda e: e.matmul(out, lhsT=lhsT, rhs=rhs, start=start, stop=stop), reads, writes)

    def tr(self, out, in_, ident, reads=(), writes=()):
        return self.op('pe', lambda e: e.transpose(out=out, in_=in_, identity=ident), reads, writes)

    def scan(self, out, d0, d1, initial, op0, op1, reads=(), writes=()):
        return self.op('dve', lambda e: e.tensor_tensor_scan(out=out, data0=d0, data1=d1, initial=initial, op0=op0, op1=op1), reads, writes)

    def memset(self, ap, val, writes=(), eng='dve'):
        return self.op(eng, lambda e: e.memset(ap, val), (), writes)

    def recip(self, out, in_, reads=(), writes=()):
        return self.op('dve', lambda e: e.reciprocal(out=out, in_=in_), reads, writes)

    def bnstats(self, out, in_, reads=(), writes=()):
        return self.op('dve', lambda e: e.bn_stats(out=out, in_=in_), reads, writes)

    def bnaggr(self, out, in_, reads=(), writes=()):
        return self.op('dve', lambda e: e.bn_aggr(out=out, in_=in_), reads, writes)

    def emit(self, final_keys=()):
        nc = self.nc
        self.op('sp', None, reads=list(final_keys), writes=())
        with contextlib.ExitStack() as st:
            csem = {e: st.enter_context(nc.semaphore("c_" + e)) for e in ENGS}
            dsem = {e: [st.enter_context(nc.semaphore("d_%s%d" % (e, i))) for i in range(NDMASEM)] for e in ENGS}
            for e in ENGS:
                cnt = 0
                dcnt = 0
                lastd = [None] * NDMASEM
                dval = [0] * NDMASEM
                for i, o in enumerate(self.ops[e]):
                    if o['dma']:
                        s = dcnt % NDMASEM
                        dcnt += 1
                        dval[s] += 16
                        o['sem'] = dsem[e][s]
                        o['val'] = dval[s]
                        o['prev'] = lastd[s]
                        lastd[s] = i
                    elif o['needed']:
                        cnt += 1
                        o['sem'] = csem[e]
                        o['val'] = cnt
            block = st.enter_context(nc.Block())

            def run(e, eng):
                waited = {}

                def wait(sem, val):
                    k = id(sem)
                    if waited.get(k, 0) < val:
                        eng.wait_ge(sem, val)
                        waited[k] = val
                bar_done = 0
                for o in self.ops[e]:
                    while bar_done < o['bar']:
                        for (de, di) in self.barriers[bar_done]:
                            d = self.ops[de][di]
                            wait(d['sem'], d['val'])
                        bar_done += 1
                    for (de, di) in o['deps']:
                        d = self.ops[de][di]
                        wait(d['sem'], d['val'])
                    if o['dma'] and o['prev'] is not None:
                        p = self.ops[e][o['prev']]
                        wait(p['sem'], p['val'])
                    if o['fn'] is None:
                        continue
                    ins = o['fn'](eng)
                    if o['dma']:
                        ins.then_inc(o['sem'], 16)
                    elif o['needed']:
                        ins.then_inc(o['sem'], 1)

            @block.tensor
            def _(eng):
                run('pe', eng)

            @block.scalar
            def _(eng):
                run('act', eng)

            @block.vector
            def _(eng):
                run('dve', eng)

            @block.gpsimd
            def _(eng):
                run('pool', eng)

            @block.sync
            def _(eng):
                run('sp', eng)


def host_consts():
    c = {}
    c['ident'] = np.eye(128, dtype=np.float32)
    ps = np.zeros((128, 128), np.float32)
    for m in range(128):
        partner = m + 32 if (m % 64) < 32 else m - 32
        ps[partner, m] = 1.0
    c['pswap'] = ps
    rows = NLAT // 64
    row = np.repeat(np.arange(rows, dtype=np.float32), 64)
    col = np.tile(np.arange(64, dtype=np.float32), rows)
    nf = 16
    inv = (10000.0 ** (-np.arange(nf, dtype=np.float32) / nf)).astype(np.float32)
    ang = np.concatenate([row[:, None] * inv, col[:, None] * inv], axis=-1).astype(np.float32)
    cs, sn = np.cos(ang).astype(np.float32), np.sin(ang).astype(np.float32)
    cos4 = np.zeros((128, NLAT), np.float32)
    sin4 = np.zeros((128, NLAT), np.float32)
    for p in range(128):
        cos4[p] = cs[:, p % 32]
        sin4[p] = sn[:, p % 32] * (-1.0 if (p % 64) < 32 else 1.0)
    c['cos4'] = cos4
    c['sin4'] = sin4

    def dft(n):
        k = np.arange(n, dtype=np.int64)
        kk = (k[:, None] * k[None, :]) % n
        a = 2.0 * np.pi * kk.astype(np.float64) / n
        return np.cos(a) / np.sqrt(n), np.sin(a) / np.sqrt(n)
    C, S = dft(NLAT)
    c['dftc'] = C.astype(ml_dtypes.bfloat16)
    c['dfts'] = S.astype(ml_dtypes.bfloat16)
    C, S = dft(NCTX)
    c['dftc_c'] = C.astype(ml_dtypes.bfloat16)
    c['dfts_c'] = S.astype(ml_dtypes.bfloat16)
    C, S = dft(64)
    cb = np.zeros((128, 128), np.float32)
    sb = np.zeros((128, 128), np.float32)
    for g in range(2):
        cb[g * 64:(g + 1) * 64, g * 64:(g + 1) * 64] = C
        sb[g * 64:(g + 1) * 64, g * 64:(g + 1) * 64] = -S
    c['c64blk'] = cb
    c['ns64blk'] = sb
    sel = np.zeros((2, 2, 128), np.float32)
    sel[0, 0, :] = 1.0
    sel[1, 1, :] = 1.0
    c['sel'] = sel
    k = np.arange(128)
    vd = np.zeros((128, 2, 128), np.float32)
    ud = np.zeros((128, 2, 128), np.float32)
    vd[:, 0, :] = (k[:, None] <= k[None, :])
    vd[:, 1, :] = (k[:, None] >= k[None, :])
    ud[:, 0, :] = (k[:, None] > k[None, :])
    ud[:, 1, :] = (k[:, None] < k[None, :])
    c['vd'] = vd
    c['ud'] = ud
    c['ones'] = np.ones((128, 512), np.float32)
    return c


def colvec(v, n):
    return np.ascontiguousarray(np.asarray(v, np.float32).reshape(n, 128).T)


def bcast(v, p=128):
    v = np.asarray(v, np.float32).reshape(1, -1)
    return np.ascontiguousarray(np.broadcast_to(v, (p, v.shape[1])))


def host_layout(inp, b):
    m = {}
    m['xin'] = np.ascontiguousarray(np.concatenate([inp['x'][b], inp['ctx'][b]], axis=0))
    cv = np.stack([colvec(inp['c'][b], 8), colvec(inp['c_ctx'], 8)], axis=-1)
    m['cvec'] = np.ascontiguousarray(cv)
    m['ada_w'] = inp['ada_w']
    m['ada_bc'] = np.ascontiguousarray(np.stack([colvec(inp['ada_b'][l], 48) for l in range(2)]))
    m['ada_b2'] = np.ascontiguousarray(np.stack([np.stack([inp['ada_b'][l]] * 2) for l in range(2)]))
    m['ln_g'] = np.ascontiguousarray(np.stack([np.stack([bcast(inp['ln_g'][l][i]) for i in range(2)]) for l in range(2)]))
    m['ln_b'] = np.ascontiguousarray(np.stack([np.stack([bcast(inp['ln_b'][l][i]) for i in range(2)]) for l in range(2)]))
    m['w_up'] = inp['ffn_w_up']
    m['b_up'] = np.ascontiguousarray(np.stack([colvec(inp['ffn_b_up'][l], 44) for l in range(2)]))
    m['conv_w'] = np.ascontiguousarray(np.stack([np.stack([colvec(inp['ffn_conv_w'][l][k], 44) for k in range(3)], axis=1) for l in range(2)]))
    m['conv_b'] = np.ascontiguousarray(np.stack([colvec(inp['ffn_conv_b'][l], 44) for l in range(2)]))
    m['w_down'] = inp['ffn_w_down']
    m['b_down'] = np.ascontiguousarray(np.stack([bcast(inp['ffn_b_down'][l]) for l in range(2)]))
    m['a_w_in'] = inp['attn_w_in'][0]
    m['a_w_out'] = inp['attn_w_out'][0]
    m['a_lam'] = bcast(inp['attn_lambda'][0].reshape(-1))
    m['a_subg'] = bcast(inp['attn_subln_g'][0])
    fw = inp['fourier_w'][0]
    wb = np.zeros((2, 128, 128), np.float32)
    for ch in range(2):
        for g in range(2):
            wb[ch, g * 64:(g + 1) * 64, g * 64:(g + 1) * 64] = fw[ch * 2 + g]
    m['f_wblk'] = wb
    m['f_b'] = colvec(inp['fourier_b'][0], 2)
    m['s_w_in'] = inp['ssm_w_in'][0]
    m['s_w_out'] = inp['ssm_w_out'][0]
    m['sd_cw'] = np.ascontiguousarray(np.stack([colvec(inp['ssd_conv_w'][0][k], 10) for k in range(3)], axis=1))
    m['sd_cb'] = colvec(inp['ssd_conv_b'][0], 10)
    m['sd_alog'] = bcast(inp['ssd_a_log'][0].reshape(-1))
    m['sd_dtb'] = bcast(inp['ssd_dt_bias'][0].reshape(-1))
    m['sd_d'] = bcast(inp['ssd_d'][0])
    m['sd_ng'] = bcast(inp['ssd_norm_g'][0])

    def dsc(a):
        out = np.zeros((128, 16), np.float32)
        for d in range(2):
            for sc in range(8):
                for half in range(2):
                    out[half * 64:(half + 1) * 64, d * 8 + sc] = a[d, 2 * sc + half, :]
        return out
    m['s5_lre'] = dsc(inp['s5_lambda_re'][0])
    m['s5_lim'] = dsc(inp['s5_lambda_im'][0])
    m['s5_ldt'] = dsc(np.broadcast_to(inp['s5_log_dt'][0][:, :, None], (2, 16, 64)))

    def bexp(b):
        out = np.zeros((128, 8, 128), np.float32)
        for sc in range(8):
            for half in range(2):
                g = 2 * sc + half
                col = (g % 8) * 16
                out[half * 64:(half + 1) * 64, sc, col:col + 16] = b[g]
        return out
    m['s5_bre'] = bexp(inp['s5_b_re'][0])
    m['s5_bim'] = bexp(inp['s5_b_im'][0])

    def cexp(c):
        out = np.zeros((2, 128, 8, 128), np.float32)
        for d in range(2):
            for sc in range(8):
                for half in range(2):
                    g = 2 * sc + half
                    col = (g % 8) * 16
                    out[d, half * 64:(half + 1) * 64, sc, col:col + 16] = c[d, g].T
        return out
    m['s5_cre'] = cexp(inp['s5_c_re'][0])
    m['s5_cim'] = cexp(inp['s5_c_im'][0])
    m['s5_dd'] = colvec(inp['s5_d'][0], 2)
    m['s5_glu_w'] = inp['s5_glu_w'][0]
    m['s5_glu_b'] = colvec(inp['s5_glu_b'][0], 2)
    return m


class Builder:
    def __init__(self, debug=()):
        self.debug = set(debug)
        self.nc = bass.Bass("TRN2", target_bir_lowering=False)
        self.P = Prog(self.nc)
        self.din = {}
        self.dout = {}
        self.final = []
        self.uid = 0
        self.ps = [self.nc.alloc_psum_tensor("ps%d" % i, [128, 512], F32) for i in range(8)]
        self.psrr = 0

    def inp(self, name, shape, dt=F32):
        self.din[name] = self.nc.dram_tensor(name, list(shape), dt, kind="ExternalInput").ap()
        return self.din[name]

    def outp(self, name, shape, dt=F32):
        self.dout[name] = self.nc.dram_tensor(name, list(shape), dt, kind="ExternalOutput").ap()
        self.final.append(name)
        return self.dout[name]

    def scratch(self, name, shape, dt=F32):
        return self.nc.dram_tensor(name, list(shape), dt, kind="Internal").ap()

    def sb(self, st, name, shape, dt=F32):
        self.uid += 1
        return st.enter_context(self.nc.sbuf_tensor("s%d_%s" % (self.uid, name), list(shape), dt))

    @contextlib.contextmanager
    def scope(self):
        with contextlib.ExitStack() as s2:
            yield s2
        self.P.barrier()

    def nextps(self, lo=0, hi=8):
        n = hi - lo
        i = lo + (self.psrr % n)
        self.psrr += 1
        return i

    def dbg(self, name, tile_ap, key, shape, dt=F32):
        if name in self.debug:
            o = self.outp("dbg_" + name, shape, dt)
            self.P.dma('sp', o, tile_ap, reads=[key], writes=["dbg_" + name])

    def load_consts(self, st):
        P = self.P
        self.ident = self.sb(st, "ident", [128, 128])
        P.dma('sp', self.ident[:], self.inp("ident", [128, 128]), writes=['ident'])
        self.sel = self.sb(st, "sel", [2, 2, 128])
        P.dma('sp', self.sel[:], self.inp("sel", [2, 2, 128]), writes=['sel'])

    def mods(self, st, layers):
        P, nc = self.P, self.nc
        if 'cvec' not in self.din:
            self.inp("cvec", [128, 8, 2])
            self.inp("ada_w", [2, D, 6 * D])
            self.inp("ada_bc", [2, 128, 48])
            self.inp("ada_b2", [2, 2, 6 * D])
            self.modc = [self.sb(st, "modc%d" % l, [128, 48, 2]) for l in range(2)]
            g = self.sb(st, "grow", [2, 2, 1024])
            self.grow = [g[:], g[:]]
            for l in range(2):
                P.memset(self.modc[l][:], 0.0, writes=['modc%d' % l])
        cvec, ada_w, ada_bc, ada_b2 = self.din['cvec'], self.din['ada_w'], self.din['ada_bc'], self.din['ada_b2']
        with self.scope() as s2:
            cv = self.sb(s2, "cv", [128, 8, 2])
            sT = self.sb(s2, "sT", [128, 8, 2], BF16)
            abc = self.sb(s2, "abc", [128, 2, 48])
            ab2 = self.sb(s2, "ab2", [2, 2, 2, 1024])
            P.dma('sp', cv[:], cvec, writes=['cv'])
            P.dma('sp', abc[:], ada_bc.rearrange("l p f -> p l f"), writes=['abc'])
            for l in layers:
                for gi, m in enumerate((2, 5)):
                    P.dma('sp', ab2[:, l, gi, :], ada_b2[l, :, m * 1024:(m + 1) * 1024], writes=['ab2'])
            P.act(sT[:], cv[:], AF.Silu, reads=['cv'], writes=['sT'])
            NWB = 4
            wbuf = [self.sb(s2, "adaw%d" % i, [128, 8, 1024], BF16) for i in range(NWB)]
            n = 0
            for l in layers:
                for m in range(6):
                    wb = wbuf[n % NWB]
                    wk = 'adaw%d' % (n % NWB)
                    n += 1
                    P.dma('pool', wb[:], ada_w[l, :, m * 1024:(m + 1) * 1024].rearrange("(kc p) f -> p kc f", p=128), writes=[wk])
                    if m in (0, 1, 3, 4):
                        bk = self.nextps()
                        pst = self.ps[bk]
                        for j in range(8):
                            for kc in range(8):
                                P.mm(pst[:, 2 * j:2 * j + 2], wb[:, kc, j * 128:(j + 1) * 128], sT[:, kc, :], kc == 0, kc == 7,
                                     reads=[wk, 'sT'], writes=['ps%d' % bk])
                        P.tt(self.modc[l][:, m * 8:(m + 1) * 8, :], pst[:, 0:16].rearrange("p (j w) -> p j w", w=2),
                             abc[:, l, m * 8:(m + 1) * 8].unsqueeze(2).to_broadcast([128, 8, 2]), ALU.add,
                             reads=['ps%d' % bk, 'abc'], writes=['modc%d' % l])
                        if m in (1, 4):
                            P.ts(self.modc[l][:, m * 8:(m + 1) * 8, :], self.modc[l][:, m * 8:(m + 1) * 8, :], 1.0, None, ALU.add,
                                 reads=['modc%d' % l], writes=['modc%d' % l])
                    else:
                        gi = 0 if m == 2 else 1
                        for half in range(2):
                            bk = self.nextps()
                            pst = self.ps[bk]
                            for kc in range(8):
                                P.mm(pst[0:2, :], sT[:, kc, :], wb[:, kc, half * 512:(half + 1) * 512], kc == 0, kc == 7,
                                     reads=[wk, 'sT'], writes=['ps%d' % bk])
                            P.tt(self.grow[l][:, gi, half * 512:(half + 1) * 512], pst[0:2, :], ab2[:, l, gi, half * 512:(half + 1) * 512], ALU.add,
                                 reads=['ps%d' % bk, 'ab2'], writes=['grow'])

    def gate_bcast(self, dst, dkey, l, gi, w):
        P = self.P
        for half in range(2):
            bk = self.nextps()
            P.mm(self.ps[bk][:, :], self.sel[:, w, :], self.grow[l][:, gi, half * 512:(half + 1) * 512], True, True,
                 reads=['sel', 'grow'], writes=['ps%d' % bk])
            P.copy(dst[:, half * 512:(half + 1) * 512], self.ps[bk][:, :], reads=['ps%d' % bk], writes=[dkey], eng='act')

    def transpose_mod(self, src_tile, skey, UT, ukey, t, l, m_shift, m_scale):
        P = self.P
        w = 0 if t < 16 else 1
        for half in range(2):
            bk = self.nextps()
            for q in range(4):
                j = half * 4 + q
                P.tr(self.ps[bk][:, q * 128:(q + 1) * 128], src_tile[:, j * 128:(j + 1) * 128], self.ident[:],
                     reads=[skey, 'ident'], writes=['ps%d' % bk])
            for q in range(4):
                j = half * 4 + q
                P.act(UT[:, j, t * 128:(t + 1) * 128], self.ps[bk][:, q * 128:(q + 1) * 128], AF.Identity,
                      reads=['ps%d' % bk, 'modc%d' % l], writes=[ukey],
                      scale=self.modc[l][:, m_scale * 8 + j, w:w + 1], bias=self.modc[l][:, m_shift * 8 + j, w:w + 1])

    def phase_A(self, st, src, l, ntiles):
        P = self.P
        UT = self.UT
        with self.scope() as s2:
            hb = [self.sb(s2, "hA%d" % i, [128, D]) for i in range(2)]
            for t in range(ntiles):
                h = hb[t % 2]
                hk = 'hA%d' % (t % 2)
                P.dma('sp', h[:], src[t * 128:(t + 1) * 128, :], reads=['hin%d_%d' % (l, t)], writes=[hk])
                self.transpose_mod(h, hk, UT, 'UT', t, l, 0, 1)

    def layer_norm(self, r, rkey, g, b, gbkey, out, okey, s2tmp):
        P = self.P
        st6, mv, rstd, nmr, tk = s2tmp
        for c in range(2):
            P.bnstats(st6[:, c, :], r[:, c * 512:(c + 1) * 512], reads=[rkey], writes=['lnst' + tk])
        P.bnaggr(mv[:], st6[:].rearrange("p a b -> p (a b)"), reads=['lnst' + tk], writes=['lnmv' + tk])
        P.ts(rstd[:], mv[:, 1:2], EPS, None, ALU.add, reads=['lnmv' + tk], writes=['lnrstd' + tk])
        P.act(rstd[:], rstd[:], AF.Sqrt, reads=['lnrstd' + tk], writes=['lnrstd' + tk])
        P.recip(rstd[:], rstd[:], reads=['lnrstd' + tk], writes=['lnrstd' + tk])
        P.stt(nmr[:], mv[:, 0:1], -1.0, rstd[:], ALU.mult, ALU.mult, reads=['lnmv' + tk, 'lnrstd' + tk], writes=['lnnmr' + tk])
        P.act(out[:], r[:], AF.Identity, reads=[rkey, 'lnrstd' + tk, 'lnnmr' + tk], writes=[okey], scale=rstd[:, 0:1], bias=nmr[:, 0:1])
        P.tt(out[:], out[:], g, ALU.mult, reads=[okey, gbkey], writes=[okey], eng='pool')
        P.tt(out[:], out[:], b, ALU.add, reads=[okey, gbkey], writes=[okey], eng='pool')

    def outproj_ln1(self, st, l, w_out_d, hsrc, ntiles, H1, lhs=None):
        P = self.P
        OT, UT = getattr(self, 'OT', None), self.UT
        with self.scope() as s2:
            wo = self.sb(s2, "wo", [128, 8, D], BF16)
            P.dma('pool', wo[:], w_out_d.rearrange("(kc p) f -> p kc f", p=128), writes=['wo'])
            lng = self.sb(s2, "lng", [128, D])
            lnb = self.sb(s2, "lnb", [128, D])
            P.dma('sp', lng[:], self.din['ln_g'][l, 0], writes=['lngb'])
            P.dma('sp', lnb[:], self.din['ln_b'][l, 0], writes=['lngb'])
            gts = []
            for w in range(2 if ntiles > 16 else 1):
                gt = self.sb(s2, "gmix%d" % w, [128, D])
                self.gate_bcast(gt, 'gmix%d' % w, l, 0, w)
                gts.append(gt)
            hb = [self.sb(s2, "hB%d" % i, [128, D]) for i in range(2)]
            rb = [self.sb(s2, "rB%d" % i, [128, D]) for i in range(2)]
            ob = [self.sb(s2, "oB%d" % i, [128, D]) for i in range(2)]
            tmps = [(self.sb(s2, "lnst6", [128, 2, 6]), self.sb(s2, "lnmv", [128, 2]), self.sb(s2, "lnrstd", [128, 1]), self.sb(s2, "lnnmr", [128, 1]), 'A%d' % i) for i in range(2)]
            mmb = {}

            def mm_pe(t):
                h, hk = hb[t % 2], 'hB%d' % (t % 2)
                P.dma('sp', h[:], hsrc[t * 128:(t + 1) * 128, :], reads=['hin%d_%d' % (l, t)], writes=[hk])
                mmb[t] = []
                for half in range(2):
                    bk = self.nextps()
                    mmb[t].append(bk)
                    for j in range(8):
                        la, lk = lhs(j, t) if lhs is not None else (OT[:, j, t * 128:(t + 1) * 128], 'OT')
                        P.mm(self.ps[bk][:, :], la, wo[:, j, half * 512:(half + 1) * 512], j == 0, j == 7,
                             reads=[lk, 'wo'], writes=['ps%d' % bk])

            def mm_dve(t):
                w = 0 if t < 16 else 1
                h, hk = hb[t % 2], 'hB%d' % (t % 2)
                r, rk = rb[t % 2], 'rB%d' % (t % 2)
                for half in range(2):
                    bk = mmb[t][half]
                    P.tt(r[:, half * 512:(half + 1) * 512], self.ps[bk][:, :], gts[w][:, half * 512:(half + 1) * 512], ALU.mult,
                         reads=['ps%d' % bk, 'gmix%d' % w], writes=[rk])
                P.stt(r[:], h[:], ALPHA, r[:], ALU.mult, ALU.add, reads=[hk, rk], writes=[rk])

            def ln_tile(t):
                r, rk = rb[t % 2], 'rB%d' % (t % 2)
                o, ok = ob[t % 2], 'oB%d' % (t % 2)
                self.layer_norm(r, rk, lng[:], lnb[:], 'lngb', o, ok, tmps[t % 2])
                P.dma('sp', H1[t * 128:(t + 1) * 128, :], o[:], reads=[ok], writes=['H1_%d_%d' % (l, t)])
                if t == 0:
                    self.dbg('h1_%d' % l, o[:], ok, [128, D])
                self.transpose_mod(o, ok, UT, 'UT', t, l, 3, 4)
            mm_pe(0)
            mm_dve(0)
            for t in range(ntiles):
                if t + 1 < ntiles:
                    mm_pe(t + 1)
                ln_tile(t)
                if t + 1 < ntiles:
                    mm_dve(t + 1)

    def ffn(self, st, l, ntiles, H1, dst, dkey, final=False):
        P = self.P
        UT = self.UT
        w_up = self.din['w_up'][l].rearrange("(kc p) f -> p kc f", p=128)
        w_dn = self.din['w_down'][l].rearrange("(j p) f -> p j f", p=128)
        if ntiles > 16:
            sbs = [[(0, 6, False, True)], [(6, 6, True, True)], [(12, 4, True, False), (16, 2, False, False)]]
        else:
            sbs = [[(0, 6, False, True)], [(6, 6, True, True)], [(12, 4, True, False)]]
        with self.scope() as s2:
            wd = self.sb(s2, "wd", [128, 22, D], BF16)
            for q in range(2):
                P.dma('pool', wd[:, q * 11:(q + 1) * 11, :], w_dn[:, q * 11:(q + 1) * 11, :], writes=['wd'])
            bup = self.sb(s2, "bup", [128, 44])
            cw = self.sb(s2, "cw", [128, 3, 44])
            cb = self.sb(s2, "cb", [128, 44])
            P.dma('sp', bup[:], self.din['b_up'][l], writes=['ffc'])
            P.dma('sp', cw[:], self.din['conv_w'][l], writes=['ffc'])
            P.dma('sp', cb[:], self.din['conv_b'][l], writes=['ffc'])
            lng = self.sb(s2, "lng2", [128, D])
            lnb = self.sb(s2, "lnb2", [128, D])
            bdn = self.sb(s2, "bdn", [128, D])
            P.dma('sp', lng[:], self.din['ln_g'][l, 1], writes=['lngb2'])
            P.dma('sp', lnb[:], self.din['ln_b'][l, 1], writes=['lngb2'])
            P.dma('sp', bdn[:], self.din['b_down'][l], writes=['bdn'])
            gts = []
            for w in range(2 if ntiles > 16 else 1):
                gt = self.sb(s2, "gffn%d" % w, [128, D])
                self.gate_bcast(gt, 'gffn%d' % w, l, 1, w)
                gts.append(gt)
            W = 774 + 2
            GT = self.sb(s2, "GT", [128, 22, W], BF16)
            xs = [self.sb(s2, "xs%d" % i, [128, W], BF16) for i in range(4)]
            dg = [self.sb(s2, "dg%d" % i, [128, 6, 128], BF16) for i in range(2)]
            sgb = [self.sb(s2, "sg%d" % i, [128, 512]) for i in range(2)]
            idb = self.sb(s2, "identb", [128, 128], BF16)
            P.copy(idb[:], self.ident[:], reads=['ident'], writes=['identb'])
            wub = [self.sb(s2, "wu%d" % i, [128, 8, 256], BF16) for i in range(3)]
            hb = [self.sb(s2, "hF%d" % i, [128, D]) for i in range(2)]
            rb = [self.sb(s2, "rF%d" % i, [128, D]) for i in range(2)]
            ob = [self.sb(s2, "oF%d" % i, [128, D]) for i in range(2)]
            tmps = [(self.sb(s2, "lnst6b", [128, 2, 6]), self.sb(s2, "lnmvb", [128, 2]), self.sb(s2, "lnrstdb", [128, 1]), self.sb(s2, "lnnmrb", [128, 1]), 'B%d' % i) for i in range(2)]
            for i in range(4):
                P.memset(xs[i][:], 0.0, writes=['xs%d' % i], eng='pool')
            nw = 0
            nsg = 0
            njobs = len(sbs) * 22

            def load_w(n):
                j = n % 22
                wu, wuk = wub[n % 3], 'wu%d' % (n % 3)
                P.dma('pool', wu[:, :, 0:128], w_up[:, :, j * 128:(j + 1) * 128], writes=[wuk])
                P.dma('pool', wu[:, :, 128:256], w_up[:, :, 2816 + j * 128:2816 + (j + 1) * 128], writes=[wuk])
            load_w(0)
            load_w(1)
            for sbi, segs in enumerate(sbs):
                cols = []
                c0 = 0
                for (t0, nt_, hl, hr) in segs:
                    cols.append(c0)
                    c0 += nt_ * 128 + 2
                width = c0
                if sbi > 0:
                    for i in range(4):
                        P.memset(xs[i][:, 0:width], 0.0, writes=['xs%d' % i], eng='pool')
                ogroups = []
                for si, (t0, nt_, hl, hr) in enumerate(segs):
                    n = nt_ * 128
                    off = 0
                    while off < n:
                        g = min(512, n - off)
                        ogroups.append((cols[si] + 1 + off, g))
                        off += g

                def conv_job(nwj, j):
                    nonlocal nsg
                    d_, dk_ = dg[nwj % 2], 'dg%d' % (nwj % 2)
                    for (cpos, g) in ogroups:
                        pb = []
                        for vg in (1, 0):
                            xi = (nwj % 2) * 2 + vg
                            x, xk = xs[xi], 'xs%d' % xi
                            bk = self.nextps()
                            pb.append(bk)
                            for tap in range(3):
                                P.mm(self.ps[bk][:, 0:g], d_[:, vg * 3 + tap, :], x[:, cpos + tap - 1:cpos + tap - 1 + g], tap == 0, tap == 2,
                                     reads=[dk_, xk], writes=['ps%d' % bk])
                        sg, sgk = sgb[nsg % 2], 'sg%d' % (nsg % 2)
                        nsg += 1
                        P.act(sg[:, 0:g], self.ps[pb[0]][:, 0:g], AF.Silu, reads=['ps%d' % pb[0], 'ffc'], writes=[sgk], bias=cb[:, 22 + j:22 + j + 1])
                        P.stt(GT[:, j, cpos:cpos + g], self.ps[pb[1]][:, 0:g], cb[:, j:j + 1], sg[:, 0:g], ALU.add, ALU.mult,
                              reads=['ps%d' % pb[1], 'ffc', sgk], writes=['GT'])
                pend = None
                for j in range(22):
                    wu, wuk = wub[nw % 3], 'wu%d' % (nw % 3)
                    if nw + 2 < njobs:
                        load_w(nw + 2)
                    d_, dk_ = dg[nw % 2], 'dg%d' % (nw % 2)
                    for vg in range(2):
                        ch = vg * 22 + j
                        for tap in range(3):
                            P.act(d_[:, vg * 3 + tap, :], idb[:], AF.Identity, reads=['identb', 'ffc'], writes=[dk_], scale=cw[:, tap, ch:ch + 1], bias=0.0)
                    for vg in range(2):
                        xi = (nw % 2) * 2 + vg
                        x, xk = xs[xi], 'xs%d' % xi
                        ch = vg * 22 + j
                        for si, (t0, nt_, hl, hr) in enumerate(segs):
                            base = cols[si]
                            tok0 = t0 * 128
                            groups = []
                            n = nt_ * 128
                            lo = tok0 - (1 if hl else 0)
                            tot = n + (1 if hl else 0) + (1 if hr else 0)
                            cbase = base + 1 - (1 if hl else 0)
                            ng = -(-tot // 512)
                            gs = -(-tot // ng)
                            off = 0
                            while off < tot:
                                g = min(gs, tot - off)
                                groups.append((lo + off, g, cbase + off))
                                off += g
                            for (tk, g, cpos) in groups:
                                bk = self.nextps()
                                for kc in range(8):
                                    P.mm(self.ps[bk][:, 0:g], wu[:, kc, vg * 128:(vg + 1) * 128], UT[:, kc, tk:tk + g], kc == 0, kc == 7,
                                         reads=[wuk, 'UT'], writes=['ps%d' % bk])
                                P.act(x[:, cpos:cpos + g], self.ps[bk][:, 0:g], AF.Identity, reads=['ps%d' % bk, 'ffc'], writes=[xk],
                                      bias=bup[:, ch:ch + 1])
                    if pend is not None:
                        conv_job(*pend)
                    pend = (nw, j)
                    nw += 1
                conv_job(*pend)
                for si, (t0, nt_, hl, hr) in enumerate(segs):
                    base = cols[si]
                    for ti in range(nt_):
                        t = t0 + ti
                        w = 0 if t < 16 else 1
                        h, hk = hb[t % 2], 'hF%d' % (t % 2)
                        r, rk = rb[t % 2], 'rF%d' % (t % 2)
                        o, ok = ob[t % 2], 'oF%d' % (t % 2)
                        P.dma('sp', h[:], H1[t * 128:(t + 1) * 128, :], reads=['H1_%d_%d' % (l, t)], writes=[hk])
                        c1 = base + 1 + ti * 128
                        for half in range(2):
                            bk = self.nextps()
                            for j in range(22):
                                P.mm(self.ps[bk][:, :], GT[:, j, c1:c1 + 128], wd[:, j, half * 512:(half + 1) * 512], j == 0, j == 21,
                                     reads=['GT', 'wd'], writes=['ps%d' % bk])
                            P.tt(r[:, half * 512:(half + 1) * 512], self.ps[bk][:, :], bdn[:, half * 512:(half + 1) * 512], ALU.add,
                                 reads=['ps%d' % bk, 'bdn'], writes=[rk])
                        P.tt(r[:], r[:], gts[w][:], ALU.mult, reads=[rk, 'gffn%d' % w], writes=[rk], eng='pool')
                        P.stt(r[:], h[:], ALPHA, r[:], ALU.mult, ALU.add, reads=[hk, rk], writes=[rk])
                        self.layer_norm(r, rk, lng[:], lnb[:], 'lngb2', o, ok, tmps[t % 2])
                        P.dma('sp', dst[t * 128:(t + 1) * 128, :], o[:], reads=[ok], writes=['%s_%d' % (dkey, t)])
                        if final:
                            self.final.append('%s_%d' % (dkey, t))

    def attn_mixer(self, ntiles=NT):
        P = self.P
        UT, OT = self.UT, self.OT
        w_in = self.din['a_w_in'].rearrange("(kc p) f -> p kc f", p=128)
        blocks = [(0, 512), (512, 512), (1024, 512), (1536, 512), (2048, 256)]
        with self.scope() as s1:
            VA = self.sb(s1, "VA", [128, NT, 6, 130], BF16)
            FT = self.sb(s1, "FT", [128, 2, NTOK], BF16)
            P.memset(VA[:, :, :, 128:130], 1.0, writes=['VA'], eng='pool')
            with self.scope() as s2:
                wv = self.sb(s2, "wv", [128, 8, 1024], BF16)
                for q in range(2):
                    P.dma('pool', wv[:, :, q * 512:(q + 1) * 512], w_in[:, :, 1536 + q * 512:1536 + (q + 1) * 512], writes=['wv'])
                for t in range(NT):
                    for half in range(2):
                        bk = self.nextps()
                        for kc in range(8):
                            P.mm(self.ps[bk][:, 0:384], UT[:, kc, t * 128:(t + 1) * 128], wv[:, kc, half * 384:(half + 1) * 384], kc == 0, kc == 7,
                                 reads=['UT', 'wv'], writes=['ps%d' % bk])
                        P.copy(VA[:, t, half * 3:(half + 1) * 3, 0:128], self.ps[bk][:, 0:384].rearrange("p (a b) -> p a b", a=3),
                               reads=['ps%d' % bk], writes=['VA'], eng=('act' if half == 0 else 'dve'))
                for c in range(2):
                    for (t0, n) in blocks:
                        bk = self.nextps()
                        for kc in range(8):
                            P.mm(self.ps[bk][:, 0:n], wv[:, kc, 768 + c * 128:768 + (c + 1) * 128], UT[:, kc, t0:t0 + n], kc == 0, kc == 7,
                                 reads=['UT', 'wv'], writes=['ps%d' % bk])
                        P.copy(FT[:, c, t0:t0 + n], self.ps[bk][:, 0:n], reads=['ps%d' % bk], writes=['FT'], eng='act')
            with self.scope() as s2:
                cos4 = self.sb(s2, "cos4", [128, NLAT])
                sin4 = self.sb(s2, "sin4", [128, NLAT])
                psw = self.sb(s2, "pswap", [128, 128])
                P.dma('sp', cos4[:], self.din['cos4'], writes=['cos4'])
                P.dma('sp', sin4[:], self.din['sin4'], writes=['sin4'])
                P.dma('sp', psw[:], self.din['pswap'], writes=['pswap'])
                lamt = self.sb(s2, "lamt", [128, 256])
                gsub = self.sb(s2, "gsub", [128, 128])
                P.dma('sp', lamt[:], self.din['a_lam'], writes=['lamt'])
                P.dma('sp', gsub[:], self.din['a_subg'], writes=['gsub'])
                P.ts(gsub[:], gsub[:], 1.0 - LAM_INIT0, None, ALU.mult, reads=['gsub'], writes=['gsub'])
                lprod = self.sb(s2, "lprod", [128, 128])
                lsum = self.sb(s2, "lsum", [128, 2])
                nlam = self.sb(s2, "nlam", [128, 1])
                P.tt(lprod[:, 0:64], lamt[:, 0:64], lamt[:, 64:128], ALU.mult, reads=['lamt'], writes=['lprod'])
                P.tt(lprod[:, 64:128], lamt[:, 128:192], lamt[:, 192:256], ALU.mult, reads=['lamt'], writes=['lprod'])
                P.op('dve', lambda e: e.tensor_reduce(out=lsum[:], in_=lprod[:].rearrange("p (a b) -> p a b", a=2), axis=mybir.AxisListType.X, op=ALU.add),
                     reads=['lprod'], writes=['lsum'])
                P.act(lsum[:], lsum[:], AF.Exp, reads=['lsum'], writes=['lsum'])
                P.tt(nlam[:], lsum[:, 1:2], lsum[:, 0:1], ALU.subtract, reads=['lsum'], writes=['nlam'])
                P.ts(nlam[:], nlam[:], -LAM_INIT0, None, ALU.add, reads=['nlam'], writes=['nlam'])
                wqk = [self.sb(s2, "wqk%d" % i, [128, 8, 256], BF16) for i in range(2)]
                QT = [self.sb(s2, "QT%d" % i, [128, NTOK], BF16) for i in range(2)]
                KT = [self.sb(s2, "KT%d" % i, [128, 2, NTOK], BF16) for i in range(2)]
                for i in range(2):
                    P.memset(KT[i][:], 0.0, writes=['KT%d' % i], eng='pool')
                PT = [self.sb(s2, "PT%d" % i, [128, 512], BF16) for i in range(4)]
                qtmp = [self.sb(s2, "qtmp%d" % i, [128, 512]) for i in range(2)]
                ropA = [self.sb(s2, "ropA%d" % i, [128, 512]) for i in range(2)]
                otmp = self.sb(s2, "otmp", [128, 4, 128])
                ofin = [self.sb(s2, "ofin%d" % i, [128, 128]) for i in range(4)]
                osq = self.sb(s2, "osq", [128, 128])
                sm = self.sb(s2, "sm", [128, 4, 4])
                nrope = 0
                npt = 0

                def load_wqk(h):
                    wq, wqkey = wqk[h % 2], 'wqk%d' % (h % 2)
                    P.dma('pool', wq[:, :, 0:128], w_in[:, :, h * 128:(h + 1) * 128], writes=[wqkey])
                    P.dma('pool', wq[:, :, 128:256], w_in[:, :, 768 + h * 128:768 + (h + 1) * 128], writes=[wqkey])
                def proj(h):
                    nonlocal nrope
                    wq, wqkey = wqk[h % 2], 'wqk%d' % (h % 2)
                    dsts = ((QT[h % 2], 'QT%d' % (h % 2)), (KT[h % 2], 'KT%d' % (h % 2)))
                    for qk in range(2):
                        dst, dk = dsts[qk]
                        for (t0, n) in blocks:
                            bk = self.nextps(4, 8)
                            for kc in range(8):
                                P.mm(self.ps[bk][:, 0:n], wq[:, kc, qk * 128:(qk + 1) * 128], UT[:, kc, t0:t0 + n], kc == 0, kc == 7,
                                     reads=[wqkey, 'UT'], writes=['ps%d' % bk])
                            if t0 < NLAT:
                                qt_, qtk = qtmp[nrope % 2], 'qtmp%d' % (nrope % 2)
                                ra, rak = ropA[nrope % 2], 'ropA%d' % (nrope % 2)
                                nrope += 1
                                P.copy(qt_[:, 0:n], self.ps[bk][:, 0:n], reads=['ps%d' % bk], writes=[qtk], eng='act')
                                b2 = self.nextps(4, 8)
                                P.mm(self.ps[b2][:, 0:n], psw[:], qt_[:, 0:n], True, True, reads=['pswap', qtk], writes=['ps%d' % b2])
                                P.tt(ra[:, 0:n], qt_[:, 0:n], cos4[:, t0:t0 + n], ALU.mult, reads=[qtk, 'cos4'], writes=[rak], eng='pool')
                                P.tt(qt_[:, 0:n], self.ps[b2][:, 0:n], sin4[:, t0:t0 + n], ALU.mult, reads=['ps%d' % b2, 'sin4', qtk], writes=[qtk])
                                if qk == 0:
                                    P.tt(dst[:, t0:t0 + n], ra[:, 0:n], qt_[:, 0:n], ALU.add, reads=[rak, qtk], writes=[dk])
                                else:
                                    for a_ in range(2):
                                        P.tt(dst[a_ * 64:(a_ + 1) * 64, a_, t0:t0 + n], ra[a_ * 64:(a_ + 1) * 64, 0:n], qt_[a_ * 64:(a_ + 1) * 64, 0:n], ALU.add,
                                             reads=[rak, qtk], writes=[dk])
                            else:
                                if qk == 0:
                                    P.copy(dst[:, t0:t0 + n], self.ps[bk][:, 0:n], reads=['ps%d' % bk], writes=[dk], eng='act')
                                else:
                                    for a_ in range(2):
                                        P.copy(dst[a_ * 64:(a_ + 1) * 64, a_, t0:t0 + n], self.ps[bk][a_ * 64:(a_ + 1) * 64, 0:n], reads=['ps%d' % bk], writes=[dk], eng='act')
                load_wqk(0)
                load_wqk(1)
                proj(0)
                for h in range(6):
                    dsts = ((QT[h % 2], 'QT%d' % (h % 2)), (KT[h % 2], 'KT%d' % (h % 2)))
                    Qh, qkey = dsts[0]
                    Kh, kkey = dsts[1]
                    qblocks = [(q0, 512, list(range(NT))) for q0 in (0, 512, 1024, 1536)] + [(2048, 256, [16, 17])]
                    for qbi, (q0, nq, ktiles) in enumerate(qblocks):
                        nqt = nq // 128
                        nk = len(ktiles)
                        if qbi == 2 and h + 1 < 6:
                            proj(h + 1)
                            if h + 2 < 6:
                                load_wqk(h + 2)
                        for a in range(2):
                            sbank = {}

                            def emit_S(i):
                                kt = ktiles[i]
                                bk = self.nextps(4, 8)
                                sbank[i] = bk
                                P.mm(self.ps[bk][:, 0:nq], Kh[:, a, kt * 128:(kt + 1) * 128], Qh[:, q0:q0 + nq], True, True,
                                     reads=[qkey, kkey], writes=['ps%d' % bk])
                            emit_S(0)
                            if nk > 1:
                                emit_S(1)
                            for i in range(nk):
                                kt = ktiles[i]
                                bk = sbank[i]
                                pt, ptk = PT[npt % 4], 'PT%d' % (npt % 4)
                                npt += 1
                                P.act(pt[:, 0:nq], self.ps[bk][:, 0:nq], AF.Exp, reads=['ps%d' % bk], writes=[ptk], scale=0.125)
                                if i + 2 < nk:
                                    emit_S(i + 2)
                                for qi in range(nqt):
                                    P.mm(self.ps[qi][:, 0:129], pt[:, qi * 128:(qi + 1) * 128], VA[:, kt, h, 0:129], i == 0, i == nk - 1,
                                         reads=[ptk, 'VA'], writes=['ps%d' % qi])
                            if a == 0:
                                for qi in range(nqt):
                                    acc = self.ps[qi]
                                    sk = 'sm%d' % qi
                                    P.recip(sm[:, qi, 0:1], acc[:, 128:129], reads=['ps%d' % qi], writes=[sk])
                                    P.ts(otmp[:, qi, :], acc[:, 0:128], sm[:, qi, 0:1], None, ALU.mult, reads=['ps%d' % qi, sk], writes=['otmp%d' % qi])
                            else:
                                for qi in range(nqt):
                                    acc = self.ps[qi]
                                    sk = 'sm%d' % qi
                                    of, ofk = ofin[qi], 'ofin%d' % qi
                                    P.recip(sm[:, qi, 1:2], acc[:, 128:129], reads=['ps%d' % qi], writes=[sk])
                                    P.tt(sm[:, qi, 1:2], sm[:, qi, 1:2], nlam[:], ALU.mult, reads=[sk, 'nlam'], writes=[sk])
                                    P.stt(of[:], acc[:, 0:128], sm[:, qi, 1:2], otmp[:, qi, :], ALU.mult, ALU.add, reads=['ps%d' % qi, sk, 'otmp%d' % qi], writes=[ofk])
                                sks = ['sm%d' % qi for qi in range(nqt)]
                                for qi in range(nqt):
                                    sk = 'sm%d' % qi
                                    of, ofk = ofin[qi], 'ofin%d' % qi
                                    P.tt(osq[:], of[:], of[:], ALU.mult, reads=[ofk], writes=['osq'])
                                    P.op('dve', (lambda e, qi=qi: e.tensor_reduce(out=sm[:, qi, 2:3], in_=osq[:], axis=mybir.AxisListType.X, op=ALU.add)),
                                         reads=['osq'], writes=[sk])
                                P.ts(sm[:, 0:nqt, 2:3], sm[:, 0:nqt, 2:3], 1.0 / 128.0, EPS, ALU.mult, ALU.add, reads=sks, writes=sks)
                                P.act(sm[:, 0:nqt, 2:3], sm[:, 0:nqt, 2:3], AF.Ln, reads=sks, writes=sks)
                                P.act(sm[:, 0:nqt, 2:3], sm[:, 0:nqt, 2:3], AF.Exp, reads=sks, writes=sks, scale=-0.5)
                                for qi in range(nqt):
                                    sk = 'sm%d' % qi
                                    of, ofk = ofin[qi], 'ofin%d' % qi
                                    P.stt(of[:], of[:], sm[:, qi, 2:3], gsub[:], ALU.mult, ALU.mult, reads=[ofk, sk, 'gsub'], writes=[ofk])
                                    bk = self.nextps(4, 8)
                                    P.tr(self.ps[bk][:, 0:128], of[:], self.ident[:], reads=[ofk, 'ident'], writes=['ps%d' % bk])
                                    P.copy(OT[:, h, q0 + qi * 128:q0 + (qi + 1) * 128], self.ps[bk][:, 0:128], reads=['ps%d' % bk], writes=['OT'], eng='act')
            with self.scope() as s2:
                cb = self.sb(s2, "c64", [128, 2, 128])
                fw = self.sb(s2, "fwb", [128, 2, 128])
                fb = self.sb(s2, "fb", [128, 2])
                P.dma('sp', cb[:, 0, :], self.din['c64blk'], writes=['c64'])
                P.dma('sp', cb[:, 1, :], self.din['ns64blk'], writes=['c64'])
                P.dma('sp', fw[:], self.din['f_wblk'].rearrange("c p e -> p c e"), writes=['fwb'])
                P.dma('sp', fb[:], self.din['f_b'], writes=['fb'])
                ABm = self.sb(s2, "ABm", [128, 2, 2, 128], BF16)
                for c in range(2):
                    for cs in range(2):
                        bk = self.nextps()
                        P.mm(self.ps[bk][:, 0:128], cb[:, cs, :], fw[:, c, :], True, True, reads=['c64', 'fwb'], writes=['ps%d' % bk])
                        P.copy(ABm[:, c, cs, :], self.ps[bk][:, 0:128], reads=['ps%d' % bk], writes=['ABm'])
                Ycs = self.sb(s2, "Ycs", [128, NT, 2, 256], BF16)
                for t in range(NT):
                    for c in range(2):
                        bk = self.nextps(4, 8)
                        P.mm(self.ps[bk][:, 0:256], FT[:, c, t * 128:(t + 1) * 128], ABm[:, c, :, :].rearrange("p a b -> p (a b)"), True, True,
                             reads=['FT', 'ABm'], writes=['ps%d' % bk])
                        P.copy(Ycs[:, t, c, :], self.ps[bk][:, 0:256], reads=['ps%d' % bk], writes=['Ycs'], eng=('act' if c == 0 else 'dve'))
                dbuf = [self.sb(s2, "dft%d" % i, [128, NLAT], BF16) for i in range(3)]
                nd = 0
                dsrc = (self.din['dftc'], self.din['dfts'])
                for tn in range(16):
                    for cs in range(2):
                        db, dbk = dbuf[nd % 3], 'dft%d' % (nd % 3)
                        nd += 1
                        P.dma('sp', db[:], dsrc[cs][tn * 128:(tn + 1) * 128, :], writes=[dbk])
                        first = (tn == 0 and cs == 0)
                        last = (tn == 15 and cs == 1)
                        for c in range(2):
                            for nb in range(4):
                                bq = c * 4 + nb
                                P.mm(self.ps[bq][:, :], Ycs[:, tn, c, cs * 128:(cs + 1) * 128], db[:, nb * 512:(nb + 1) * 512], first, last,
                                     reads=['Ycs', dbk], writes=['ps%d' % bq])
                for c in range(2):
                    for nb in range(4):
                        bq = c * 4 + nb
                        P.act(OT[:, 6 + c, nb * 512:(nb + 1) * 512], self.ps[bq][:, :], AF.Identity, reads=['ps%d' % bq, 'fb'], writes=['OT'], bias=fb[:, c:c + 1])
                dcc = self.sb(s2, "dftcc", [128, 2, 2, 256], BF16)
                P.dma('sp', dcc[:, 0], self.din['dftc_c'].rearrange("(t p) n -> p t n", p=128), writes=['dftcc'])
                P.dma('sp', dcc[:, 1], self.din['dfts_c'].rearrange("(t p) n -> p t n", p=128), writes=['dftcc'])
                for c in range(2):
                    bk = self.nextps(4, 8)
                    n = 0
                    for tn in range(2):
                        for cs in range(2):
                            P.mm(self.ps[bk][:, 0:256], Ycs[:, 16 + tn, c, cs * 128:(cs + 1) * 128], dcc[:, cs, tn, :], n == 0, n == 3,
                                 reads=['Ycs', 'dftcc'], writes=['ps%d' % bk])
                            n += 1
                    P.act(OT[:, 6 + c, 2048:2304], self.ps[bk][:, 0:256], AF.Identity, reads=['ps%d' % bk, 'fb'], writes=['OT'], bias=fb[:, c:c + 1])

    def s5_mixer(self, S5O):
        import os
        stop = int(os.environ.get('S5STOP', '99'))
        if stop <= 0:
            self.P.memset(S5O[:], 0.0, writes=['S5O'])
            return
        P = self.P
        UT = self.UT
        w_in = self.din['s_w_in'].rearrange("(kc p) f -> p kc f", p=128)
        blocks = [(0, 512), (512, 512), (1024, 512), (1536, 512), (2048, 256)]
        T = ALU
        with self.scope() as s1:
            usT = self.sb(s1, "usT", [128, 2, NTOK], BF16)
            Y5 = self.sb(s1, "Y5", [128, 2, NLAT])
            dd = self.sb(s1, "s5dd", [128, 2])
            P.dma('sp', dd[:], self.din['s5_dd'], writes=['s5dd'])
            ones = self.sb(s1, "ones", [128, 512])
            P.dma('sp', ones[:], self.din['ones'], writes=['ones'])
            with self.scope() as s2:
                wus = self.sb(s2, "wus", [128, 8, 256], BF16)
                P.dma('pool', wus[:], w_in[:, :, 2072:2328], writes=['wus'])
                for c in range(2):
                    for (t0, n) in blocks:
                        bk = self.nextps()
                        for kc in range(8):
                            P.mm(self.ps[bk][:, 0:n], wus[:, kc, c * 128:(c + 1) * 128], UT[:, kc, t0:t0 + n], kc == 0, kc == 7,
                                 reads=['wus', 'UT'], writes=['ps%d' % bk])
                        P.copy(usT[:, c, t0:t0 + n], self.ps[bk][:, 0:n], reads=['ps%d' % bk], writes=['usT'], eng='act')
                        if t0 < NLAT and os.environ.get('S5SUB', '') != 'a':
                            P.ts(Y5[:, c, t0:t0 + n], self.ps[bk][:, 0:n], dd[:, c:c + 1], None, T.mult, reads=['ps%d' % bk, 's5dd'], writes=['Y5_%d_%d' % (c, t0)])
            if stop <= 1:
                P.memset(S5O[:], 0.0, writes=['S5O'])
                return
            pr = self.sb(s1, "s5pr", [128, 32, 16])
            names = {}

            def V(nm):
                if nm not in names:
                    names[nm] = len(names)
                return pr[:, names[nm], :]
            for nm, src in (('lre', 's5_lre'), ('lim', 's5_lim'), ('ldt', 's5_ldt')):
                P.dma('sp', V(nm), self.din[src], writes=['s5pr'])
            K_ = ['s5pr']

            def tt(o, a, b, op):
                P.tt(V(o), V(a), V(b), op, reads=K_, writes=K_)

            def ts(o, a, s1_, s2_, op0, op1=None):
                P.ts(V(o), V(a), s1_, s2_, op0, op1, reads=K_, writes=K_)
            P.act(V('dt'), V('ldt'), AF.Exp, reads=K_, writes=K_)
            tt('t0', 'dt', 'lre', T.mult)
            P.act(V('mag'), V('t0'), AF.Exp, reads=K_, writes=K_)
            tt('th', 'dt', 'lim', T.mult)
            ts('k', 'th', 1.0 / (2.0 * math.pi), None, T.mult)
            ts('k', 'k', 12582912.0, None, T.add)
            ts('k', 'k', -12582912.0, None, T.add)
            P.stt(V('r'), V('k'), -6.28125, V('th'), T.mult, T.add, reads=K_, writes=K_)
            P.stt(V('r'), V('k'), -(2.0 * math.pi - 6.28125), V('r'), T.mult, T.add, reads=K_, writes=K_)
            ts('x', 'r', 0.125, None, T.mult)
            tt('x2', 'x', 'x', T.mult)
            ts('ps_', 'x2', -1.0 / 5040.0, 1.0 / 120.0, T.mult, T.add)
            tt('ps_', 'ps_', 'x2', T.mult)
            ts('ps_', 'ps_', -1.0 / 6.0, None, T.add)
            tt('ps_', 'ps_', 'x2', T.mult)
            ts('ps_', 'ps_', 1.0, None, T.add)
            tt('sn', 'ps_', 'x', T.mult)
            ts('pc_', 'x2', 1.0 / 40320.0, -1.0 / 720.0, T.mult, T.add)
            tt('pc_', 'pc_', 'x2', T.mult)
            ts('pc_', 'pc_', 1.0 / 24.0, None, T.add)
            tt('pc_', 'pc_', 'x2', T.mult)
            ts('pc_', 'pc_', -0.5, None, T.add)
            tt('pc_', 'pc_', 'x2', T.mult)
            ts('cs', 'pc_', 1.0, None, T.add)
            for _ in range(3):
                tt('cc', 'cs', 'cs', T.mult)
                tt('ss', 'sn', 'sn', T.mult)
                tt('sc2', 'cs', 'sn', T.mult)
                tt('cs', 'cc', 'ss', T.subtract)
                ts('sn', 'sc2', 2.0, None, T.mult)
            tt('abre', 'mag', 'cs', T.mult)
            tt('abim', 'mag', 'sn', T.mult)
            tt('den', 'lre', 'lre', T.mult)
            tt('t0', 'lim', 'lim', T.mult)
            tt('den', 'den', 't0', T.add)
            P.recip(V('den'), V('den'), reads=K_, writes=K_)
            ts('am1', 'abre', -1.0, None, T.add)
            tt('t0', 'am1', 'lre', T.mult)
            tt('t1', 'abim', 'lim', T.mult)
            tt('t0', 't0', 't1', T.add)
            tt('kre', 't0', 'den', T.mult)
            tt('t0', 'abim', 'lre', T.mult)
            tt('t1', 'am1', 'lim', T.mult)
            tt('t0', 't0', 't1', T.subtract)
            tt('kim', 't0', 'den', T.mult)
            if stop <= 2:
                P.memset(S5O[:], 0.0, writes=['S5O'])
                return
            Ec = self.sb(s1, "s5Ec", [128, 10, 16])
            Es = self.sb(s1, "s5Es", [128, 10, 16])
            EK = ['s5E']
            P.copy(Ec[:, 0, :], V('cs'), reads=K_, writes=EK)
            P.copy(Es[:, 0, :], V('sn'), reads=K_, writes=EK)
            e1 = self.sb(s1, "s5e1", [128, 16])
            e2 = self.sb(s1, "s5e2", [128, 16])
            for j in range(9):
                P.tt(e1[:], Ec[:, j, :], Ec[:, j, :], T.mult, reads=EK, writes=['s5e1'])
                P.tt(e2[:], Es[:, j, :], Es[:, j, :], T.mult, reads=EK, writes=['s5e2'])
                P.tt(Ec[:, j + 1, :], e1[:], e2[:], T.subtract, reads=['s5e1', 's5e2'], writes=EK)
                P.tt(e1[:], Ec[:, j, :], Es[:, j, :], T.mult, reads=EK, writes=['s5e1'])
                P.ts(Es[:, j + 1, :], e1[:], 2.0, None, T.mult, reads=['s5e1'], writes=EK)
            if stop <= 3:
                P.memset(S5O[:], 0.0, writes=['S5O'])
                return
            nEs = self.sb(s1, "s5nEs", [128, 10, 16])
            P.ts(nEs[:], Es[:], -1.0, None, T.mult, reads=EK, writes=['s5nEs'])
            BbT = self.sb(s1, "BbT", [128, 2, 8, 2, 128], BF16)
            CmT = self.sb(s1, "CmT", [128, 2, 8, 2, 128], BF16)
            for d in range(2):
                P.dma('pool', CmT[:, d, :, 0, :], self.din['s5_cre'][d], writes=['CmT'])
                P.dma('pool', CmT[:, d, :, 1, :], self.din['s5_cim'][d], writes=['CmT'])
                P.ts(CmT[:, d, :, 1, :], CmT[:, d, :, 1, :], -1.0, None, T.mult, reads=['CmT'], writes=['CmT'])
            with self.scope() as s2:
                bre = self.sb(s2, "s5bre", [128, 8, 128])
                bim = self.sb(s2, "s5bim", [128, 8, 128])
                P.dma('sp', bre[:], self.din['s5_bre'], writes=['s5bre'])
                P.dma('sp', bim[:], self.din['s5_bim'], writes=['s5bim'])
                wt = [self.sb(s2, "s5wt%d" % i, [128, 128]) for i in range(2)]
                nw = 0
                for d in range(2):
                    for sc in range(8):
                        col = d * 8 + sc
                        kre = V('kre')[:, col:col + 1]
                        kim = V('kim')[:, col:col + 1]
                        for ri in range(2):
                            w, wk = wt[nw % 2], 's5wt%d' % (nw % 2)
                            nw += 1
                            if ri == 0:
                                P.ts(w[:], bim[:, sc, :], kim, None, T.mult, reads=['s5bim'] + K_, writes=[wk])
                                P.stt(w[:], bre[:, sc, :], kre, w[:], T.mult, T.subtract, reads=['s5bre', wk] + K_, writes=[wk])
                            else:
                                P.ts(w[:], bre[:, sc, :], kim, None, T.mult, reads=['s5bre'] + K_, writes=[wk])
                                P.stt(w[:], bim[:, sc, :], kre, w[:], T.mult, T.add, reads=['s5bim', wk] + K_, writes=[wk])
                            bk = self.nextps()
                            P.tr(self.ps[bk][:, 0:128], w[:], self.ident[:], reads=[wk, 'ident'], writes=['ps%d' % bk])
                            P.copy(BbT[:, d, sc, ri, :], self.ps[bk][:, 0:128], reads=['ps%d' % bk], writes=['BbT'], eng='act')
            if stop <= 4:
                P.memset(S5O[:], 0.0, writes=['S5O'])
                return
            with self.scope() as s2:
                Tc = self.sb(s2, "s5Tc", [128, 8, 512])
                Ts = self.sb(s2, "s5Ts", [128, 8, 512])
                tq = [self.sb(s2, "s5tq%d" % i, [128, 8, 256]) for i in range(2)]
                rho = self.sb(s2, "s5rho", [128, 512])
                NB = 2
                mt = [[self.sb(s2, "s5m%d_%d" % (i, b), [128, 512]) for i in range(4)] for b in range(NB)]
                bp = [[self.sb(s2, "s5bp%d_%d" % (i, b), [128, 512]) for i in range(2)] for b in range(NB)]
                gg = [[self.sb(s2, "s5g%d_%d" % (i, b), [128, 512]) for i in range(2)] for b in range(NB)]
                hh = [[self.sb(s2, "s5h%d_%d" % (i, b), [128, 512], BF16) for i in range(2)] for b in range(NB)]
                pp = [self.sb(s2, "s5p%d" % i, [128, 512]) for i in range(4)]
                ini = [self.sb(s2, "s5ini%d" % b, [128, 4]) for b in range(NB)]
                nb_ = 0
                for d in range(2):
                    TK = ['s5T']
                    P.memset(Tc[:, :, 0:1], 1.0, writes=TK)
                    P.memset(Ts[:, :, 0:1], 0.0, writes=TK)
                    for j in range(9):
                        m = 1 << j
                        ecb = Ec[:, j, d * 8:(d + 1) * 8].unsqueeze(2).to_broadcast([128, 8, m])
                        esb = Es[:, j, d * 8:(d + 1) * 8].unsqueeze(2).to_broadcast([128, 8, m])
                        P.tt(tq[0][:, :, 0:m], Ts[:, :, 0:m], esb, T.mult, reads=TK + EK, writes=['s5tq0'])
                        P.tt(tq[1][:, :, 0:m], Tc[:, :, 0:m], esb, T.mult, reads=TK + EK, writes=['s5tq1'])
                        P.tt(Tc[:, :, m:2 * m], Tc[:, :, 0:m], ecb, T.mult, reads=TK + EK, writes=TK)
                        P.tt(Ts[:, :, m:2 * m], Ts[:, :, 0:m], ecb, T.mult, reads=TK + EK, writes=TK)
                        P.tt(Tc[:, :, m:2 * m], Tc[:, :, m:2 * m], tq[0][:, :, 0:m], T.subtract, reads=TK + ['s5tq0'], writes=TK)
                        P.tt(Ts[:, :, m:2 * m], Ts[:, :, m:2 * m], tq[1][:, :, 0:m], T.add, reads=TK + ['s5tq1'], writes=TK)
                    order = [blocks[4]] + (blocks[0:4] if d == 0 else blocks[3::-1])
                    if stop <= 5:
                        continue
                    if stop <= 6 and d == 1:
                        continue
                    if d == 1:
                        P.copy(tq[0][:, :, :], Tc[:, :, 0:256], reads=TK, writes=['s5tq0'])
                        P.copy(tq[1][:, :, :], Tc[:, :, 256:512], reads=TK, writes=['s5tq1'])
                        P.copy(Tc[:, :, 0:256], tq[1][:, :, ::-1], reads=['s5tq1'], writes=TK)
                        P.copy(Tc[:, :, 256:512], tq[0][:, :, ::-1], reads=['s5tq0'], writes=TK)
                        P.copy(tq[0][:, :, :], Ts[:, :, 0:256], reads=TK, writes=['s5tq0'])
                        P.copy(tq[1][:, :, :], Ts[:, :, 256:512], reads=TK, writes=['s5tq1'])
                        P.copy(Ts[:, :, 0:256], tq[1][:, :, ::-1], reads=['s5tq1'], writes=TK)
                        P.copy(Ts[:, :, 256:512], tq[0][:, :, ::-1], reads=['s5tq0'], writes=TK)
                    for sc in range(8):
                        col = d * 8 + sc
                        fc = sc // 4
                        P.ts(rho[:], ones[:], V('mag')[:, col:col + 1], None, T.mult, reads=['ones'] + K_, writes=['s5rho'])
                        prev = None
                        pending = []
                        pend_pe = []
                        for (t0, n) in order:
                            b = nb_ % NB
                            nb_ += 1
                            sfx = '_%d' % b
                            bk1 = self.nextps()
                            P.mm(self.ps[bk1][:, 0:n], BbT[:, d, sc, 0, :], usT[:, fc, t0:t0 + n], True, True, reads=['BbT', 'usT'], writes=['ps%d' % bk1])
                            bk2 = self.nextps()
                            P.mm(self.ps[bk2][:, 0:n], BbT[:, d, sc, 1, :], usT[:, fc, t0:t0 + n], True, True, reads=['BbT', 'usT'], writes=['ps%d' % bk2])
                            while pend_pe:
                                pend_pe.pop(0)()
                            pre, pim = self.ps[bk1][:, 0:n], self.ps[bk2][:, 0:n]
                            if d == 0:
                                tc = Tc[:, sc, 0:n]
                                ts_ = Ts[:, sc, 0:n]
                            else:
                                tc = Tc[:, sc, 512 - n:512]
                                ts_ = Ts[:, sc, 512 - n:512]
                            m1, m2, m3, m4 = [mt[b][i][:, 0:n] for i in range(4)]
                            mk = ['s5m%d%s' % (i, sfx) for i in range(4)]
                            P.tt(m1, pre, tc, T.mult, reads=['ps%d' % bk1] + TK, writes=[mk[0]])
                            P.tt(m2, pim, ts_, T.mult, reads=['ps%d' % bk2] + TK, writes=[mk[1]])
                            P.tt(m3, pim, tc, T.mult, reads=['ps%d' % bk2] + TK, writes=[mk[2]])
                            P.tt(m4, pre, ts_, T.mult, reads=['ps%d' % bk1] + TK, writes=[mk[3]])
                            bpr, bpi = bp[b][0][:, 0:n], bp[b][1][:, 0:n]
                            P.tt(bpr, m1, m2, T.add, reads=[mk[0], mk[1]], writes=['s5bp0' + sfx])
                            P.tt(bpi, m3, m4, T.subtract, reads=[mk[2], mk[3]], writes=['s5bp1' + sfx])
                            gre, gim = gg[b][0][:, 0:n], gg[b][1][:, 0:n]
                            if d == 0:
                                go_r, go_i, bi_r, bi_i = gre, gim, bpr, bpi
                                last = n - 1
                            else:
                                go_r, go_i, bi_r, bi_i = gre[:, ::-1], gim[:, ::-1], bpr[:, ::-1], bpi[:, ::-1]
                                last = 0
                            i0 = 0.0 if prev is None else ini[prev][:, 0:1]
                            i1 = 0.0 if prev is None else ini[prev][:, 1:2]
                            rk = [] if prev is None else ['s5ini%d' % prev]
                            P.scan(go_r, rho[:, 0:n], bi_r, i0, T.mult, T.add, reads=['s5rho', 's5bp0' + sfx] + rk, writes=['s5g0' + sfx])
                            P.scan(go_i, rho[:, 0:n], bi_i, i1, T.mult, T.add, reads=['s5rho', 's5bp1' + sfx] + rk, writes=['s5g1' + sfx])
                            j = 9 if n == 512 else 8
                            ec = Ec[:, j, col:col + 1]
                            es = Es[:, j, col:col + 1]
                            ik = 's5ini%d' % b
                            nes = nEs[:, j, col:col + 1]
                            P.act(ini[b][:, 2:3], gg[b][1][:, last:last + 1], AF.Identity, reads=['s5g1' + sfx, 's5nEs'], writes=[ik], scale=nes, bias=0.0)
                            P.act(ini[b][:, 0:1], gg[b][0][:, last:last + 1], AF.Identity, reads=['s5g0' + sfx, ik] + EK, writes=[ik], scale=ec, bias=ini[b][:, 2:3])
                            P.act(ini[b][:, 3:4], gg[b][0][:, last:last + 1], AF.Identity, reads=['s5g0' + sfx] + EK, writes=[ik], scale=es, bias=0.0)
                            P.act(ini[b][:, 1:2], gg[b][1][:, last:last + 1], AF.Identity, reads=['s5g1' + sfx, ik] + EK, writes=[ik], scale=ec, bias=ini[b][:, 3:4])
                            prev = b
                            while pending:
                                pending.pop(0)()
                            if t0 < NLAT:
                                p1, p2, p3, p4 = [pp[i][:, 0:n] for i in range(4)]
                                hre, him = hh[b][0][:, 0:n], hh[b][1][:, 0:n]
                                P.tt(p1, gre, tc, T.mult, reads=['s5g0' + sfx] + TK, writes=['s5p0'], eng='pool')
                                P.tt(p2, gim, ts_, T.mult, reads=['s5g1' + sfx] + TK, writes=['s5p1'], eng='pool')
                                P.tt(hre, p1, p2, T.subtract, reads=['s5p0', 's5p1'], writes=['s5h0' + sfx], eng='pool')
                                P.tt(p3, gre, ts_, T.mult, reads=['s5g0' + sfx] + TK, writes=['s5p2'], eng='pool')
                                P.tt(p4, gim, tc, T.mult, reads=['s5g1' + sfx] + TK, writes=['s5p3'], eng='pool')
                                P.tt(him, p3, p4, T.add, reads=['s5p2', 's5p3'], writes=['s5h1' + sfx], eng='pool')
                                yk = 'Y5_%d_%d' % (fc, t0)
                                cell = {}

                                def _ro(cell=cell, hre=hre, him=him, sfx=sfx, n=n, d=d, sc=sc):
                                    bk = self.nextps()
                                    cell['bk'] = bk
                                    P.mm(self.ps[bk][:, 0:n], CmT[:, d, sc, 0, :], hre, True, False, reads=['CmT', 's5h0' + sfx], writes=['ps%d' % bk])
                                    P.mm(self.ps[bk][:, 0:n], CmT[:, d, sc, 1, :], him, False, True, reads=['CmT', 's5h1' + sfx], writes=['ps%d' % bk])

                                def _acc(cell=cell, yk=yk, fc=fc, t0=t0, n=n):
                                    bk = cell['bk']
                                    P.tt(Y5[:, fc, t0:t0 + n], Y5[:, fc, t0:t0 + n], self.ps[bk][:, 0:n], T.add, reads=[yk, 'ps%d' % bk], writes=[yk])
                                pend_pe.append(_ro)
                                pending.append(_acc)
                        while pend_pe:
                            pend_pe.pop(0)()
                        while pending:
                            pending.pop(0)()
            if stop <= 7:
                P.memset(S5O[:], 0.0, writes=['S5O'])
                return
            with self.scope() as s2:
                gw = self.sb(s2, "gluw", [128, 2, 256], BF16)
                gb_ = self.sb(s2, "glub", [128, 2])
                P.dma('pool', gw[:], self.din['s5_glu_w'].rearrange("(kc p) f -> p kc f", p=128), writes=['gluw'])
                P.dma('sp', gb_[:], self.din['s5_glu_b'], writes=['glub'])
                gbf = self.sb(s2, "gbf", [128, 2, NLAT], BF16)
                t1 = [self.sb(s2, "glt%d" % i, [128, 512]) for i in range(2)]
                sg = [self.sb(s2, "gls%d" % i, [128, 512]) for i in range(2)]
                n_ = 0
                for c in range(2):
                    for (t0, n) in blocks[0:4]:
                        yk = 'Y5_%d_%d' % (c, t0)
                        x = Y5[:, c, t0:t0 + n]
                        a, ak = t1[n_ % 2], 'glt%d' % (n_ % 2)
                        n_ += 1
                        P.tt(a[:], x, x, T.mult, reads=[yk], writes=[ak], eng='pool')
                        P.ts(a[:], a[:], 0.044715, 1.0, T.mult, T.add, reads=[ak], writes=[ak])
                        P.tt(a[:], a[:], x, T.mult, reads=[ak, yk], writes=[ak])
                        P.act(a[:], a[:], AF.Tanh, reads=[ak], writes=[ak], scale=math.sqrt(2.0 / math.pi))
                        P.stt(a[:], a[:], 1.0, x, T.add, T.mult, reads=[ak, yk], writes=[ak])
                        P.ts(x, a[:], 0.5, None, T.mult, reads=[ak], writes=[yk])
                        P.copy(gbf[:, c, t0:t0 + n], x, reads=[yk], writes=['gbf'], eng='act')
                n_ = 0
                for c in range(2):
                    for (t0, n) in blocks[0:4]:
                        bk = self.nextps()
                        for kc in range(2):
                            P.mm(self.ps[bk][:, 0:n], gw[:, kc, c * 128:(c + 1) * 128], gbf[:, kc, t0:t0 + n], kc == 0, kc == 1,
                                 reads=['gluw', 'gbf'], writes=['ps%d' % bk])
                        a, ak = sg[n_ % 2], 'gls%d' % (n_ % 2)
                        n_ += 1
                        P.act(a[:], self.ps[bk][:, 0:n], AF.Sigmoid, reads=['ps%d' % bk, 'glub'], writes=[ak], bias=gb_[:, c:c + 1])
                        P.tt(S5O[:, c, t0:t0 + n], Y5[:, c, t0:t0 + n], a[:], T.mult, reads=['Y5_%d_%d' % (c, t0), ak], writes=['S5O'])

    def ssd_inproj(self, sB, Xtm, Btm, BT, CT, dtt, dta, ZS):
        P = self.P
        UT = self.UT
        T = ALU
        w_in = self.din['s_w_in'].rearrange("(kc p) f -> p kc f", p=128)
        blocks = [(0, 512), (512, 512), (1024, 512), (1536, 512), (2048, 256)]
        with self.scope() as s2:
            wz = self.sb(s2, "wz", [128, 8, 768], BF16)
            P.dma('pool', wz[:], w_in[:, :, 0:768], writes=['wz'])
            zt = [self.sb(s2, "ztp%d" % i, [128, 768]) for i in range(2)]
            for t in range(16):
                z, zk = zt[t % 2], 'ztp%d' % (t % 2)
                for half in range(2):
                    bk = self.nextps()
                    for kc in range(8):
                        P.mm(self.ps[bk][:, 0:384], UT[:, kc, t * 128:(t + 1) * 128], wz[:, kc, half * 384:(half + 1) * 384], kc == 0, kc == 7,
                             reads=['UT', 'wz'], writes=['ps%d' % bk])
                    P.act(z[:, half * 384:(half + 1) * 384], self.ps[bk][:, 0:384], AF.Silu, reads=['ps%d' % bk], writes=[zk])
                P.dma('sp', ZS[t * 128:(t + 1) * 128, :], z[:], reads=[zk], writes=['ZS_%d' % t])
            wdt = self.sb(s2, "wdt", [128, 8, 24], BF16)
            P.dma('pool', wdt[:], w_in[:, :, 2048:2072], writes=['wdt'])
            dtb = self.sb(s2, "dtb", [128, 24])
            alog = self.sb(s2, "alog", [128, 24])
            P.dma('sp', dtb[:], self.din['sd_dtb'], writes=['dtb'])
            P.dma('sp', alog[:], self.din['sd_alog'], writes=['alog'])
            for t in range(NT):
                bk = self.nextps()
                for kc in range(8):
                    P.mm(self.ps[bk][:, 0:24], UT[:, kc, t * 128:(t + 1) * 128], wdt[:, kc, :], kc == 0, kc == 7, reads=['UT', 'wdt'], writes=['ps%d' % bk])
                P.tt(dtt[:, t, :], self.ps[bk][:, 0:24], dtb[:], T.add, reads=['ps%d' % bk, 'dtb'], writes=['dtt'])
            ax = self.sb(s2, "spax", [128, NT * 24])
            dflat = dtt[:].rearrange("p t f -> p (t f)")
            P.act(ax[:], dflat, AF.Abs, reads=['dtt'], writes=['spax'])
            P.act(ax[:], ax[:], AF.Exp, reads=['spax'], writes=['spax'], scale=-1.0)
            P.act(ax[:], ax[:], AF.Ln, reads=['spax'], writes=['spax'], bias=1.0)
            P.ts(dflat, dflat, 0.0, None, T.max, reads=['dtt'], writes=['dtt'])
            P.tt(dflat, dflat, ax[:], T.add, reads=['dtt', 'spax'], writes=['dtt'])
            P.act(alog[:], alog[:], AF.Exp, reads=['alog'], writes=['alog'])
            P.ts(alog[:], alog[:], -1.0, None, T.mult, reads=['alog'], writes=['alog'])
            P.tt(dta[:], dtt[:], alog[:].unsqueeze(1).to_broadcast([128, NT, 24]), T.mult, reads=['dtt', 'alog'], writes=['dta'])
        with self.scope() as s2:
            cw = self.sb(s2, "sdcw", [128, 3, 10])
            cb = self.sb(s2, "sdcb", [128, 10])
            P.dma('sp', cw[:], self.din['sd_cw'], writes=['sdc'])
            P.dma('sp', cb[:], self.din['sd_cb'], writes=['sdc'])
            W = 2308
            xs = self.sb(s2, "sdxs", [128, W])
            acc = self.sb(s2, "sdacc", [128, W])
            P.memset(xs[:], 0.0, writes=['sdxs'], eng='pool')
            wx = [self.sb(s2, "sdwx%d" % i, [128, 8, 128], BF16) for i in range(2)]

            def colpos(t0):
                return 1 + t0 if t0 < NLAT else 2051 + (t0 - NLAT)
            for c in range(10):
                w, wk = wx[c % 2], 'sdwx%d' % (c % 2)
                P.dma('pool', w[:], w_in[:, :, 768 + c * 128:768 + (c + 1) * 128], writes=[wk])
                for (t0, n) in blocks:
                    bk = self.nextps()
                    for kc in range(8):
                        P.mm(self.ps[bk][:, 0:n], w[:, kc, :], UT[:, kc, t0:t0 + n], kc == 0, kc == 7, reads=[wk, 'UT'], writes=['ps%d' % bk])
                    cp = colpos(t0)
                    P.copy(xs[:, cp:cp + n], self.ps[bk][:, 0:n], reads=['ps%d' % bk], writes=['sdxs'], eng='act')
                n1 = W - 2
                P.ts(acc[:, 1:1 + n1], xs[:, 0:n1], cw[:, 0, c:c + 1], cb[:, c:c + 1], T.mult, T.add, reads=['sdxs', 'sdc'], writes=['sdacc'])
                P.stt(acc[:, 1:1 + n1], xs[:, 1:1 + n1], cw[:, 1, c:c + 1], acc[:, 1:1 + n1], T.mult, T.add, reads=['sdxs', 'sdacc', 'sdc'], writes=['sdacc'])
                P.stt(acc[:, 1:1 + n1], xs[:, 2:2 + n1], cw[:, 2, c:c + 1], acc[:, 1:1 + n1], T.mult, T.add, reads=['sdxs', 'sdacc', 'sdc'], writes=['sdacc'])
                P.act(acc[:, 1:1 + n1], acc[:, 1:1 + n1], AF.Silu, reads=['sdacc'], writes=['sdacc'])
                if 6 <= c < 8:
                    g = c - 6
                    P.copy(BT[:, g, 0:NLAT], acc[:, 1:1 + NLAT], reads=['sdacc'], writes=['BT'], eng='pool')
                    P.copy(BT[:, g, NLAT:NTOK], acc[:, 2051:2051 + NCTX], reads=['sdacc'], writes=['BT'], eng='pool')
                if c >= 8:
                    g = c - 8
                    P.copy(CT[:, g, 0:NLAT], acc[:, 1:1 + NLAT], reads=['sdacc'], writes=['CT'], eng='pool')
                    P.copy(CT[:, g, NLAT:NTOK], acc[:, 2051:2051 + NCTX], reads=['sdacc'], writes=['CT'], eng='pool')
                if c < 8:
                    for t4 in range(0, NT, 4):
                        bk = self.nextps()
                        nq = min(4, NT - t4)
                        for q in range(nq):
                            t = t4 + q
                            cp = colpos(t * 128)
                            P.tr(self.ps[bk][:, q * 128:(q + 1) * 128], acc[:, cp:cp + 128], self.ident[:], reads=['sdacc', 'ident'], writes=['ps%d' % bk])
                        src = self.ps[bk][:, 0:nq * 128].rearrange("p (q f) -> p q f", q=nq)
                        if c < 6:
                            P.copy(Xtm[:, t4:t4 + nq, c * 128:(c + 1) * 128], src, reads=['ps%d' % bk], writes=['Xtm'], eng=('act' if (t4 // 4) % 2 == 0 else 'dve'))
                        else:
                            P.copy(Btm[:, t4:t4 + nq, c - 6, :], src, reads=['ps%d' % bk], writes=['Btm'], eng=('act' if (t4 // 4) % 2 == 0 else 'dve'))

    def ssd_chunks(self, Xtm, Btm, BT, CT, dtt, dta, Yacc):
        P = self.P
        T = ALU
        with self.scope() as s2:
            vd = self.sb(s2, "vd", [128, 2, 128])
            ud = self.sb(s2, "ud", [128, 2, 128])
            ones = self.sb(s2, "ones1", [128, 128])
            dsk = self.sb(s2, "dsk", [128, 12])
            P.dma('sp', vd[:], self.din['vd'], writes=['vd'])
            P.dma('sp', ud[:], self.din['ud'], writes=['ud'])
            P.dma('sp', ones[:], self.din['ones'][:, 0:128], writes=['ones1'])
            P.dma('sp', dsk[:], self.din['sd_d'], writes=['dsk'])
            for t in range(16):
                P.tt(Yacc[:, t, :].rearrange("p (r e) -> p r e", r=12), Xtm[:, t, :].rearrange("p (r e) -> p r e", r=12),
                     dsk[:].unsqueeze(2).to_broadcast([128, 12, 64]), T.mult, reads=['Xtm', 'dsk'], writes=['Yacc%d' % t])
            utflat = self.UT[:].rearrange("p a b -> p (a b)")

            class _V:
                def __init__(self, ap):
                    self.ap = ap

                def __getitem__(self, idx):
                    return self.ap[idx]

            def alias(i):
                return _V(utflat[:, i * 3072:(i + 1) * 3072].bitcast(F32).rearrange("p (a b) -> p a b", a=12))
            rhsV = alias(0)
            Ls = [alias(1), alias(2)]
            CBm_ = [self.sb(s2, "CBm%d" % i, [128, 2, 128]) for i in range(2)]
            M_ = [self.sb(s2, "Mdiag%d" % i, [128, 12, 128], BF16) for i in range(2)]
            xdts = [self.sb(s2, "xdt%d" % i, [128, 12, 64], BF16) for i in range(2)]
            xw_ = [self.sb(s2, "xw%d" % i, [128, 12, 64], BF16) for i in range(2)]
            tmp_ = [self.sb(s2, "ytmp%d" % i, [128, 768]) for i in range(2)]
            Hf_ = [self.sb(s2, "Hf%d" % i, [128, 768]) for i in range(2)]
            Hb2 = [[self.sb(s2, "Hb%d_%d" % (i, k), [128, 768], BF16) for k in range(2)] for i in range(2)]
            eacs_ = [self.sb(s2, "eacs%d" % i, [128, 12]) for i in range(2)]
            cdv_ = [self.sb(s2, "cdv%d" % i, [128, 12]) for i in range(2)]
            jobs = []
            orders = [[16, 17] + list(range(16)), [17, 16] + list(range(15, -1, -1))]
            for ci in range(18):
                for d in range(2):
                    jobs.append((d, ci, orders[d][ci], ci == 17))

            def stageA1(n):
                d, ci, t, islast = jobs[n]
                xdt, xk = xdts[n % 2], 'xdt%d' % (n % 2)
                dta_t = dta[:, t, d * 12:(d + 1) * 12]
                dt_t = dtt[:, t, d * 12:(d + 1) * 12]
                for r in range(12):
                    P.act(rhsV[:, r, :], vd[:, d, :], AF.Identity, reads=['vd', 'dta'], writes=['rhsV'], scale=dta_t[:, r:r + 1], bias=0.0)
                P.tt(xdt[:], Xtm[:, t, :].rearrange("p (r e) -> p r e", r=12), dt_t.unsqueeze(2).to_broadcast([128, 12, 64]), T.mult,
                     reads=['Xtm', 'dtt'], writes=[xk], eng='pool')

            def stageA2(n):
                d, ci, t, islast = jobs[n]
                L, lk = Ls[n % 2], 'Lseg%d' % (n % 2)
                for q in range(3):
                    bk = self.nextps()
                    P.mm(self.ps[bk][:, :], ud[:, d, :], rhsV[:, 4 * q:4 * q + 4, :].rearrange("p a b -> p (a b)"), True, True,
                         reads=['ud', 'rhsV'], writes=['ps%d' % bk])
                    P.act(L[:, 4 * q:4 * q + 4, :].rearrange("p a b -> p (a b)"), self.ps[bk][:, :], AF.Exp, reads=['ps%d' % bk], writes=[lk])

            def stageB(n):
                d, ci, t, islast = jobs[n]
                L, lk = Ls[n % 2], 'Lseg%d' % (n % 2)
                xdt, xk = xdts[n % 2], 'xdt%d' % (n % 2)
                CBm, M, xw, tmp, Hf, eacs, cdv = CBm_[d], M_[d], xw_[d], tmp_[d], Hf_[d], eacs_[d], cdv_[d]
                kCB, kM, kxw, ktmp, kHf, kea, kcd = ['%s%d' % (k_, d) for k_ in ('CBm', 'Mdiag', 'xw', 'ytmp', 'Hf', 'eacs', 'cdv')]
                Hb_cur, kHb_cur = Hb2[d][ci % 2], 'Hb%d_%d' % (d, ci % 2)
                Hb_nxt, kHb_nxt = Hb2[d][(ci + 1) % 2], 'Hb%d_%d' % (d, (ci + 1) % 2)
                iend = 127 if d == 0 else 0
                lat = t < 16
                dta_t = dta[:, t, d * 12:(d + 1) * 12]
                tok = slice(t * 128, (t + 1) * 128)
                bs = None
                if not islast:
                    P.tt(xw[:], xdt[:], L[:, :, iend].unsqueeze(2).to_broadcast([128, 12, 64]), T.mult, reads=[xk, lk], writes=[kxw])
                    bs = [self.nextps(), self.nextps()]
                    for g in range(2):
                        P.mm(self.ps[bs[g]][:, 0:384], Btm[:, t, g, :], xw[:, 6 * g:6 * g + 6, :].rearrange("p a b -> p (a b)"), True, True,
                             reads=['Btm', kxw], writes=['ps%d' % bs[g]])
                    if ci > 0:
                        bk = self.nextps()
                        P.mm(self.ps[bk][:, 0:12], ones[:], dta_t, True, True, reads=['ones1', 'dta'], writes=['ps%d' % bk])
                        P.act(cdv[:], self.ps[bk][:, 0:12], AF.Exp, reads=['ps%d' % bk], writes=[kcd])
                if lat:
                    bk = self.nextps()
                    for g in range(2):
                        P.mm(self.ps[bk][:, g * 128:(g + 1) * 128], BT[:, g, tok], CT[:, g, tok], True, True, reads=['BT', 'CT'], writes=['ps%d' % bk])
                    P.tt(CBm[:], self.ps[bk][:, 0:256].rearrange("p (g i) -> p g i", g=2), vd[:, d, :].unsqueeze(1).to_broadcast([128, 2, 128]), T.mult,
                         reads=['ps%d' % bk, 'vd'], writes=[kCB])
                    P.tt(M[:].rearrange("p (g r) i -> p g r i", g=2), L[:].rearrange("p (g r) i -> p g r i", g=2),
                         CBm[:].unsqueeze(2).to_broadcast([128, 2, 6, 128]), T.mult, reads=[lk, kCB], writes=[kM])
                if not islast:
                    if ci == 0:
                        for g in range(2):
                            P.copy(Hf[:, g * 384:(g + 1) * 384], self.ps[bs[g]][:, 0:384], reads=['ps%d' % bs[g]], writes=[kHf])
                    else:
                        P.tt(Hf[:].rearrange("p (r e) -> p r e", r=12), Hf[:].rearrange("p (r e) -> p r e", r=12),
                             cdv[:].unsqueeze(2).to_broadcast([128, 12, 64]), T.mult, reads=[kHf, kcd], writes=[kHf])
                        for g in range(2):
                            P.tt(Hf[:, g * 384:(g + 1) * 384], Hf[:, g * 384:(g + 1) * 384], self.ps[bs[g]][:, 0:384], T.add,
                                 reads=[kHf, 'ps%d' % bs[g]], writes=[kHf])
                    P.copy(Hb_nxt[:], Hf[:], reads=[kHf], writes=[kHb_nxt], eng='act')
                if lat:
                    bA = self.nextps()
                    bB = self.nextps()
                    for r in range(12):
                        dst = self.ps[bA][:, r * 64:(r + 1) * 64] if r < 8 else self.ps[bB][:, (r - 8) * 64:(r - 7) * 64]
                        P.mm(dst, M[:, r, :], xdt[:, r, :], True, True, reads=[kM, xk], writes=['ps%d' % (bA if r < 8 else bB)])
                    bo = [self.nextps(), self.nextps()]
                    for g in range(2):
                        P.mm(self.ps[bo[g]][:, 0:384], CT[:, g, tok], Hb_cur[:, g * 384:(g + 1) * 384], True, True, reads=['CT', kHb_cur], writes=['ps%d' % bo[g]])
                    bk = self.nextps()
                    P.mm(self.ps[bk][:, 0:12], vd[:, d, :], dta_t, True, True, reads=['vd', 'dta'], writes=['ps%d' % bk])
                    P.act(eacs[:], self.ps[bk][:, 0:12], AF.Exp, reads=['ps%d' % bk], writes=[kea])
                    for g in range(2):
                        P.tt(tmp[:, g * 384:(g + 1) * 384].rearrange("p (r e) -> p r e", r=6), self.ps[bo[g]][:, 0:384].rearrange("p (r e) -> p r e", r=6),
                             eacs[:, g * 6:(g + 1) * 6].unsqueeze(2).to_broadcast([128, 6, 64]), T.mult, reads=['ps%d' % bo[g], kea], writes=[ktmp])
                    P.tt(tmp[:, 0:512], tmp[:, 0:512], self.ps[bA][:, :], T.add, reads=[ktmp, 'ps%d' % bA], writes=[ktmp])
                    P.tt(tmp[:, 512:768], tmp[:, 512:768], self.ps[bB][:, 0:256], T.add, reads=[ktmp, 'ps%d' % bB], writes=[ktmp])
                    P.tt(Yacc[:, t, :], Yacc[:, t, :], tmp[:], T.add, reads=['Yacc%d' % t, ktmp], writes=['Yacc%d' % t], eng='pool')
            stageA1(0)
            stageA2(0)
            for n in range(len(jobs)):
                if n + 1 < len(jobs):
                    stageA1(n + 1)
                stageB(n)
                if n + 1 < len(jobs):
                    stageA2(n + 1)

    def ssd_gate(self, Yacc, ZS, OTs):
        P = self.P
        T = ALU
        with self.scope() as s2:
            ng = self.sb(s2, "sdng", [128, 768])
            P.dma('sp', ng[:], self.din['sd_ng'], writes=['sdng'])
            zt = [self.sb(s2, "ztg%d" % i, [128, 768]) for i in range(2)]
            yz = [self.sb(s2, "yz%d" % i, [128, 768]) for i in range(2)]
            sq = self.sb(s2, "gsq", [128, 768])
            sm = self.sb(s2, "gsm", [128, 2])
            for t in range(16):
                z, zk = zt[t % 2], 'ztg%d' % (t % 2)
                y, yk = yz[t % 2], 'yz%d' % (t % 2)
                P.dma('sp', z[:], ZS[t * 128:(t + 1) * 128, :], reads=['ZS_%d' % t], writes=[zk])
                P.tt(y[:], Yacc[:, t, :], z[:], T.mult, reads=['Yacc%d' % t, zk], writes=[yk])
                P.act(sq[:], y[:], AF.Square, reads=[yk], writes=['gsq', 'gsm'], accum_out=sm[:, 0:1])
                P.ts(sm[:, 0:1], sm[:, 0:1], 1.0 / 768.0, EPS, T.mult, T.add, reads=['gsm'], writes=['gsm'])
                P.act(sm[:, 0:1], sm[:, 0:1], AF.Sqrt, reads=['gsm'], writes=['gsm'])
                P.recip(sm[:, 0:1], sm[:, 0:1], reads=['gsm'], writes=['gsm'])
                P.stt(y[:], y[:], sm[:, 0:1], ng[:], T.mult, T.mult, reads=[yk, 'gsm', 'sdng'], writes=[yk])
                for half in range(2):
                    bk = self.nextps()
                    for q in range(3):
                        c = half * 3 + q
                        P.tr(self.ps[bk][:, q * 128:(q + 1) * 128], y[:, c * 128:(c + 1) * 128], self.ident[:], reads=[yk, 'ident'], writes=['ps%d' % bk])
                    P.copy(OTs[:, half * 3:(half + 1) * 3, t * 128:(t + 1) * 128], self.ps[bk][:, 0:384].rearrange("p (q f) -> p q f", q=3),
                           reads=['ps%d' % bk], writes=['OTs'], eng=('act' if half == 0 else 'dve'))

    def declare_l1_inputs(self):
        for nm, shp in (('s_w_in', [D, 2328]), ('s_w_out', [D, D]), ('sd_cw', [128, 3, 10]), ('sd_cb', [128, 10]), ('sd_alog', [128, 24]),
                        ('sd_dtb', [128, 24]), ('sd_d', [128, 12]), ('sd_ng', [128, 768]), ('s5_lre', [128, 16]), ('s5_lim', [128, 16]),
                        ('s5_ldt', [128, 16]), ('s5_bre', [128, 8, 128]), ('s5_bim', [128, 8, 128]), ('s5_cre', [2, 128, 8, 128]),
                        ('s5_cim', [2, 128, 8, 128]), ('s5_dd', [128, 2]), ('s5_glu_w', [256, 256]), ('s5_glu_b', [128, 2]),
                        ('vd', [128, 2, 128]), ('ud', [128, 2, 128]), ('ones', [128, 512])):
            self.inp(nm, shp)

    def layer1(self, st, hsrc, dst, dkey, dbg=False, only=None):
        P = self.P
        self.mods(st, [1])
        with self.scope() as sL:
            S5O = self.sb(sL, "S5O", [128, 2, NLAT], BF16)
            self.phase_A(sL, hsrc, 1, NT)
            if only in (None, 's5'):
                self.s5_mixer(S5O)
            else:
                P.memset(S5O[:], 0.0, writes=['S5O'])
            if only == 's5':
                o = self.outp("dbg_S5O", [128, 2, NLAT], BF16)
                P.dma('sp', o, S5O[:], reads=['S5O'], writes=['dbg_S5O'])
                return
            H1 = self.scratch("H1b", [NLAT, D])
            ZS = self.scratch("ZS", [NLAT, 768])
            with self.scope() as sA:
                Yacc = self.sb(sA, "Yacc", [128, 16, 768])
                with self.scope() as sB:
                    Xtm = self.sb(sB, "Xtm", [128, NT, 768], BF16)
                    Btm = self.sb(sB, "Btm", [128, NT, 2, 128], BF16)
                    BT = self.sb(sB, "BT", [128, 2, NTOK], BF16)
                    CT = self.sb(sB, "CT", [128, 2, NTOK], BF16)
                    dtt = self.sb(sB, "dtt", [128, NT, 24])
                    dta = self.sb(sB, "dta", [128, NT, 24])
                    self.ssd_inproj(sB, Xtm, Btm, BT, CT, dtt, dta, ZS)
                    self.ssd_chunks(Xtm, Btm, BT, CT, dtt, dta, Yacc)
                with self.scope() as sC:
                    OTs = self.sb(sC, "OTs", [128, 6, NLAT], BF16)
                    self.ssd_gate(Yacc, ZS, OTs)
                    if only == 'ssd':
                        o = self.outp("dbg_OTs", [128, 6, NLAT], BF16)
                        P.dma('sp', o, OTs[:], reads=['OTs'], writes=['dbg_OTs'])
                        return
                    if dbg:
                        o = self.outp("dbg_S5O", [128, 2, NLAT], BF16)
                        P.dma('sp', o, S5O[:], reads=['S5O'], writes=['dbg_S5O'])
                        o = self.outp("dbg_OTs", [128, 6, NLAT], BF16)
                        P.dma('sp', o, OTs[:], reads=['OTs'], writes=['dbg_OTs'])

                    def lhs(j, t):
                        if j < 6:
                            return OTs[:, j, t * 128:(t + 1) * 128], 'OTs'
                        return S5O[:, j - 6, t * 128:(t + 1) * 128], 'S5O'
                    self.outproj_ln1(sC, 1, self.din['s_w_out'], hsrc, 16, H1, lhs=lhs)
        self.ffn(st, 1, 16, H1, dst, dkey, final=True)

    def build_layer1_test(self, only=None):
        with contextlib.ExitStack() as st:
            self.load_consts(st)
            self.declare_common_inputs()
            self.declare_l1_inputs()
            hin = self.inp("hin", [NTOK, D])
            self.UT = self.sb(st, "UT", [128, 8, NTOK], BF16)
            out = self.outp("out", [NLAT, D])
            self.final.remove("out")
            self.layer1(st, hin, out, 'out', dbg=True, only=only)
            if only is not None:
                self.dout.pop('out')
            self.P.emit(self.final)
        return self.nc

    def declare_common_inputs(self):
        for nm, shp in (('ln_g', [2, 2, 128, D]), ('ln_b', [2, 2, 128, D]), ('w_up', [2, D, 5632]), ('b_up', [2, 128, 44]),
                        ('conv_w', [2, 128, 3, 44]), ('conv_b', [2, 128, 44]), ('w_down', [2, 2816, D]), ('b_down', [2, 128, D]),
                        ('a_w_in', [D, 2560]), ('a_w_out', [D, D]), ('a_lam', [128, 256]), ('a_subg', [128, 128]),
                        ('f_wblk', [2, 128, 128]), ('f_b', [128, 2]), ('cos4', [128, NLAT]), ('sin4', [128, NLAT]), ('pswap', [128, 128]),
                        ('c64blk', [128, 128]), ('ns64blk', [128, 128])):
            self.inp(nm, shp)
        for nm, shp in (('dftc', [NLAT, NLAT]), ('dfts', [NLAT, NLAT]), ('dftc_c', [NCTX, NCTX]), ('dfts_c', [NCTX, NCTX])):
            self.inp(nm, shp, BF16)

    def build_layer0(self, dbg_ot=False):
        with contextlib.ExitStack() as st:
            self.load_consts(st)
            self.declare_common_inputs()
            xin = self.inp("xin", [NTOK, D])
            self.mods(st, [0])
            self.UT = self.sb(st, "UT", [128, 8, NTOK], BF16)
            H1 = self.scratch("H1", [NTOK, D])
            H2 = self.outp("H2", [NTOK, D])
            self.final.remove("H2")
            with self.scope() as s1:
                self.OT = self.sb(s1, "OT", [128, 8, NTOK], BF16)
                self.phase_A(s1, xin, 0, NT)
                self.attn_mixer()
                if dbg_ot:
                    o = self.outp("dbg_OT", [128, 8, NTOK], BF16)
                    self.P.dma('sp', o, self.OT[:], reads=['OT'], writes=['dbg_OT'])
                self.outproj_ln1(s1, 0, self.din['a_w_out'], xin, NT, H1)
            self.ffn(st, 0, NT, H1, H2, 'H2', final=True)
            self.P.emit(self.final)
        return self.nc

    def build_debug_ffn(self):
        with contextlib.ExitStack() as st:
            self.load_consts(st)
            for nm, shp in (('ln_g', [2, 2, 128, D]), ('ln_b', [2, 2, 128, D]), ('w_up', [2, D, 5632]), ('b_up', [2, 128, 44]),
                            ('conv_w', [2, 128, 3, 44]), ('conv_b', [2, 128, 44]), ('w_down', [2, 2816, D]), ('b_down', [2, 128, D]),
                            ('a_w_out', [D, D])):
                self.inp(nm, shp)
            xin = self.inp("xin", [NTOK, D])
            self.mods(st, [0])
            self.UT = self.sb(st, "UT", [128, 8, NTOK], BF16)
            H1 = self.scratch("H1", [NTOK, D])
            H2 = self.outp("H2", [NTOK, D])
            self.final.remove("H2")
            with self.scope() as s1:
                self.OT = self.sb(s1, "OT", [128, 8, NTOK], BF16)
                self.phase_A(s1, xin, 0, NT)
                o = self.outp("dbg_modc0", [128, 48, 2])
                self.P.dma('sp', o, self.modc[0][:], reads=['modc0'], writes=['dbg_modc0'])
                o = self.outp("dbg_grow0", [2, 2, 1024])
                self.P.dma('sp', o, self.grow[0], reads=['grow0'], writes=['dbg_grow0'])
                o = self.outp("dbg_UT", [128, 8, NTOK], BF16)
                self.P.dma('sp', o, self.UT[:], reads=['UT'], writes=['dbg_UT'])
                self.P.copy(self.OT[:], self.UT[:], reads=['UT'], writes=['OT'], eng='pool')
                self.outproj_ln1(s1, 0, self.din['a_w_out'], xin, NT, H1)
            self.ffn(st, 0, NT, H1, H2, 'H2', final=True)
            self.P.emit(self.final)
        return self.nc

    def build_full(self):
        with contextlib.ExitStack() as st:
            self.load_consts(st)
            self.declare_common_inputs()
            self.declare_l1_inputs()
            xin = self.inp("xin", [NTOK, D])
            self.mods(st, [0])
            self.UT = self.sb(st, "UT", [128, 8, NTOK], BF16)
            H1 = self.scratch("H1", [NTOK, D])
            H2 = self.scratch("H2", [NTOK, D])
            with self.scope() as s1:
                self.OT = self.sb(s1, "OT", [128, 8, NTOK], BF16)
                self.phase_A(s1, xin, 0, NT)
                self.attn_mixer()
                self.outproj_ln1(s1, 0, self.din['a_w_out'], xin, NT, H1)
            self.ffn(st, 0, NT, H1, H2, 'hin1')
            out = self.outp("out", [NLAT, D])
            self.final.remove("out")
            self.layer1(st, H2, out, 'out')
            self.P.emit(self.final)
        return self.nc


_CACHE = {}


def kernel(**inputs):
    inp = {k: np.asarray(v) for k, v in inputs.items()}
    if 'b' not in _CACHE:
        B = Builder()
        B.build_full()
        _CACHE['b'] = B
        _CACHE['c'] = host_consts()
    B = _CACHE['b']
    hc = _CACHE['c']
    in_maps = []
    for b in range(8):
        hl = host_layout(inp, b)
        in_maps.append({k: (hc[k] if k in hc else hl[k]) for k in B.din})
    res = run_bass_kernel_spmd(B.nc, in_maps, core_ids=list(range(8)))
    out = np.stack([np.asarray(r['out']) for r in res.results], axis=0)
    return out.astype(np.float32)
```
